# Optimizing a Trainium2 kernel written in Bass

```python
import math
import jax, jax.numpy as jnp
from jax import lax
import numpy as np

D_MODEL = 1024
BATCH = 4
SEQ = 4096
DEPTH = 2

GRID_W = 64
CTX_LEN = 256
N_BRANCH = 3
EPS = 1e-6

HY_WIDTH = D_MODEL
HY_ORDER = 2
HY_SHORT_CONV = 3
HY_BANDS = 16
HY_EMB = 2 * HY_BANDS + 1
HY_FFN = 64
HY_WINDOW_SHIFT = 0.05
HY_DECAY_SHORT = 0.3
HY_DECAY_LONG = 1.5
HY_DECAY_TARGET = 1e-2

SSD_INNER = D_MODEL
SSD_HEADDIM = 64
SSD_HEADS = SSD_INNER // SSD_HEADDIM
SSD_GROUPS = 2
SSD_STATE = 128
SSD_SHORT_CONV = 3
SSD_CHUNK = 128

DA_HEAD_DIM = 64
DA_V_DIM = 2 * DA_HEAD_DIM
DA_HEADS = D_MODEL // DA_V_DIM
DA_WIDTH = DA_HEADS * DA_V_DIM
ROPE_BASE = 10000.0
ROPE_FREQS = DA_HEAD_DIM // 4
Q_BLOCK = 128

DENSE_FF = 4 * D_MODEL
N_EXPERTS = 8
TOP_K = 2
EXPERT_FF = 2 * D_MODEL

HY_COLS = (HY_ORDER + 1) * HY_WIDTH
SSD_XBC = SSD_INNER + 2 * SSD_GROUPS * SSD_STATE
SSD_COLS = SSD_INNER + SSD_XBC + 2 * SSD_HEADS
DA_QK = DA_HEADS * 2 * DA_HEAD_DIM
DA_COLS = 2 * DA_QK + DA_WIDTH
GATE_COLS = N_BRANCH * D_MODEL
IN_COLS = HY_COLS + SSD_COLS + DA_COLS + GATE_COLS

kernel_name = 'hybrid_hyena_ssd_diffattn_moe_block'


def rmsnorm(x, g):
    xf = x.astype(jnp.float32)
    y = xf * lax.rsqrt(jnp.mean(xf * xf, axis=-1, keepdims=True) + EPS)
    return (y * g.astype(jnp.float32)).astype(x.dtype)


def modulate(h, shift, scale):
    return h * (1 + scale) + shift


def dwconv(u, w, b):
    k = w.shape[0]
    y = lax.conv_general_dilated(u, w[:, None, :], window_strides=(1,), padding=[((k - 1) // 2, k // 2)],
                                 dimension_numbers=('NWC', 'WIO', 'NWC'), feature_group_count=u.shape[-1])
    return y + b


def hyena_filters(L, w1, b1, w2, b2, w3, b3, w4, freq):
    f32 = jnp.float32
    t = jnp.linspace(0.0, 1.0, L, dtype=f32)[:, None]
    w = 2.0 * math.pi * jnp.arange(L, dtype=f32)[:, None] / L
    f = jnp.linspace(1e-4, HY_BANDS - 1, HY_BANDS, dtype=f32)[None, :]
    feats = jnp.concatenate([t, jnp.cos(f * w), -jnp.sin(f * w)], axis=-1)
    h = jnp.sin(freq[0] * (feats @ w1 + b1))
    h = jnp.sin(freq[1] * (h @ w2 + b2))
    h = jnp.sin(freq[2] * (h @ w3 + b3))
    k = (h @ w4).astype(f32).reshape(L, HY_ORDER, 2, HY_WIDTH)
    max_decay = math.log(HY_DECAY_TARGET) / HY_DECAY_SHORT
    min_decay = math.log(HY_DECAY_TARGET) / HY_DECAY_LONG
    deltas = jnp.abs(jnp.linspace(min_decay, max_decay, HY_WIDTH, dtype=f32))
    window = jnp.exp(-t * deltas) + HY_WINDOW_SHIFT
    return k * window[:, None, None, :]


def bidir_fftconv(u, kf, kb, bias):
    L, C = u.shape[1], u.shape[2]
    k = jnp.concatenate([kf[:1] + kb[:1], kf[1:], jnp.zeros((1, C), kf.dtype), kb[:0:-1]], axis=0)
    uf = u.astype(jnp.float32)
    y = jnp.fft.irfft(jnp.fft.rfft(uf, n=2 * L, axis=1) * jnp.fft.rfft(k, n=2 * L, axis=0)[None], n=2 * L, axis=1)[:, :L]
    return (y + uf * bias.astype(jnp.float32)).astype(u.dtype)


def hyena_seq(p, conv_w, conv_b, w1, b1, w2, b2, w3, b3, w4, freq, bias):
    filt = hyena_filters(p.shape[1], w1, b1, w2, b2, w3, b3, w4, freq)
    u = dwconv(p, conv_w, conv_b)
    v, x1, x2 = jnp.split(u, 3, axis=-1)
    z = x1 * bidir_fftconv(v, filt[:, 0, 0], filt[:, 0, 1], bias[0])
    return x2 * bidir_fftconv(z, filt[:, 1, 0], filt[:, 1, 1], bias[1])


def segsum(a):
    T = a.shape[-1]
    x = jnp.broadcast_to(a[..., :, None], a.shape + (T,))
    x = jnp.cumsum(jnp.where(jnp.tril(jnp.ones((T, T), bool), -1), x, 0.0), axis=-2)
    return jnp.where(jnp.tril(jnp.ones((T, T), bool)), x, -jnp.inf)


def ssd_scan(X, A, Bm, Cm, init, need_y):
    b, l, h, p = X.shape
    g, n = Bm.shape[2], Bm.shape[3]
    r = h // g
    q = SSD_CHUNK
    c = l // q
    X = X.reshape(b, c, q, g, r, p)
    A = A.reshape(b, c, q, g, r).transpose(0, 3, 4, 1, 2)
    Bm = Bm.reshape(b, c, q, g, n)
    Cm = Cm.reshape(b, c, q, g, n)
    A_cs = jnp.cumsum(A, axis=-1)
    decay_states = jnp.exp(A_cs[..., -1:] - A_cs)
    states = jnp.einsum('bclgn,bgrcl,bclgrp->bcgrpn', Bm, decay_states, X)
    states = jnp.concatenate([init.reshape(b, 1, g, r, p, n), states], axis=1)
    decay_chunk = jnp.exp(segsum(jnp.pad(A_cs[..., -1], ((0, 0), (0, 0), (0, 0), (1, 0)))))
    new_states = jnp.einsum('bgrzc,bcgrpn->bzgrpn', decay_chunk, states)
    final = new_states[:, -1].reshape(b, h, p, n)
    if not need_y:
        return None, final
    cb = jnp.einsum('bclgn,bcsgn->bgcls', Cm, Bm)
    m = cb[:, :, None] * jnp.exp(segsum(A))
    y_diag = jnp.einsum('bgrcls,bcsgrp->bclgrp', m, X)
    y_off = jnp.einsum('bclgn,bcgrpn,bgrcl->bclgrp', Cm, new_states[:, :-1], jnp.exp(A_cs))
    return (y_diag + y_off).reshape(b, l, h, p), final


def ssd_prepare(p, conv_w, conv_b, dt_bias):
    f32 = jnp.float32
    b, l = p.shape[:2]
    z, xbc, dt_raw = jnp.split(p, [SSD_INNER, SSD_INNER + SSD_XBC], axis=-1)
    xbc = jax.nn.silu(dwconv(xbc, conv_w, conv_b)).astype(f32)
    xs, bm, cm = jnp.split(xbc, [SSD_INNER, SSD_INNER + SSD_GROUPS * SSD_STATE], axis=-1)
    xs = xs.reshape(b, l, SSD_HEADS, SSD_HEADDIM)
    bm = bm.reshape(b, l, SSD_GROUPS, SSD_STATE)
    cm = cm.reshape(b, l, SSD_GROUPS, SSD_STATE)
    dt = jax.nn.softplus(dt_raw.astype(f32).reshape(b, l, 2, SSD_HEADS) + dt_bias.astype(f32))
    return z, xs, bm, cm, dt


def ssd_direction(xs, bm, cm, dt, a, init, reverse, need_y):
    if reverse:
        xs, bm, cm, dt = (jnp.flip(t, 1) for t in (xs, bm, cm, dt))
    y, final = ssd_scan(xs * dt[..., None], dt * a, bm, cm, init, need_y)
    if reverse and need_y:
        y = jnp.flip(y, 1)
    return y, final


def ssd_finish(y, xs, z, d_skip, norm_g):
    f32 = jnp.float32
    b, l = z.shape[:2]
    y = (y + xs * d_skip.astype(f32)[:, None]).reshape(b, l, SSD_INNER) * jax.nn.silu(z.astype(f32))
    yg = y.reshape(b, l, SSD_GROUPS, SSD_INNER // SSD_GROUPS)
    yg = yg * lax.rsqrt(jnp.mean(yg * yg, axis=-1, keepdims=True) + EPS)
    return (yg.reshape(b, l, SSD_INNER) * norm_g.astype(f32)).astype(z.dtype)


def axial_rope(L):
    rows = L // GRID_W
    row = jnp.repeat(jnp.arange(rows), GRID_W)
    col = jnp.tile(jnp.arange(GRID_W), rows)
    inv = ROPE_BASE ** (-jnp.arange(ROPE_FREQS, dtype=jnp.float32) / ROPE_FREQS)
    ang = jnp.stack([row, col], axis=-1).astype(jnp.float32)[:, :, None] * inv
    return jnp.cos(ang), jnp.sin(ang)


def apply_rope(x, cos, sin):
    b, L, H, c, d = x.shape
    xr = x.reshape(b, L, H, c, 2, 2, d // 4)
    x1, x2 = xr[..., 0, :], xr[..., 1, :]
    cs, sn = cos[:, None, None], sin[:, None, None]
    out = jnp.stack([x1 * cs - x2 * sn, x2 * cs + x1 * sn], axis=-2)
    return out.reshape(x.shape).astype(x.dtype)


def qk_heads(t, gain):
    b, l = t.shape[:2]
    return rmsnorm(t.reshape(b, l, DA_HEADS, 2, DA_HEAD_DIM), gain)


def v_heads(t):
    b, l = t.shape[:2]
    return t.reshape(b, l, DA_HEADS, DA_V_DIM)


def diff_attend(q, k, v, lam):
    f32 = jnp.float32
    s = jnp.einsum('bqhcd,bkhcd->bhcqk', q.astype(f32), k.astype(f32)) * (DA_HEAD_DIM ** -0.5)
    p = jax.nn.softmax(s, axis=-1)
    a = p[:, :, 0] - lam * p[:, :, 1]
    return jnp.einsum('bhqk,bkhe->bqhe', a, v.astype(f32))


def latent_attention(q, k, v, lam):
    b, l = q.shape[:2]
    nb = l // Q_BLOCK
    qb = q.reshape(b, nb, Q_BLOCK, DA_HEADS, 2, DA_HEAD_DIM).transpose(1, 0, 2, 3, 4, 5)
    out = lax.map(lambda blk: diff_attend(blk, k, v, lam), qb)
    return out.transpose(1, 0, 2, 3, 4).reshape(b, l, DA_HEADS, DA_V_DIM)


def da_finish(o, subln_g, lam_init, dtype):
    b, l = o.shape[:2]
    o = rmsnorm(o, subln_g) * (1.0 - lam_init)
    return o.reshape(b, l, DA_WIDTH).astype(dtype)


def merge_branches(ys, gate_logits, w_branch, w_out):
    b, l = gate_logits.shape[:2]
    proj = jnp.einsum('blie,ied->blid', jnp.stack(ys, axis=2), w_branch)
    gates = jax.nn.sigmoid(gate_logits.reshape(b, l, N_BRANCH, D_MODEL).astype(jnp.float32)).astype(proj.dtype)
    return jnp.sum(gates * proj, axis=2) @ w_out


def swiglu(x, w1, w3, w2):
    return (jax.nn.silu(x @ w1) * (x @ w3)) @ w2


def moe_swiglu(x, router_w, w1, w3, w2):
    f32 = jnp.float32
    logits = (x @ router_w).astype(f32)
    top_vals, top_idx = lax.top_k(logits, TOP_K)
    weights = jax.nn.softmax(top_vals, axis=-1)
    gate = jnp.sum(jax.nn.one_hot(top_idx, N_EXPERTS, dtype=f32) * weights[..., None], axis=-2)
    out = jnp.zeros(x.shape, f32)
    for e in range(N_EXPERTS):
        out = out + gate[..., e:e + 1] * swiglu(x, w1[e], w3[e], w2[e]).astype(f32)
    return out.astype(x.dtype)


def token_mixer(p_lat, p_ctx, hy_conv_w, hy_conv_b, hy_w1, hy_b1, hy_w2, hy_b2, hy_w3, hy_b3, hy_w4, hy_freq, hy_bias,
                ssd_conv_w, ssd_conv_b, ssd_dt_bias, ssd_a_log, ssd_d, ssd_norm_g,
                da_q_norm, da_k_norm, da_lambda, da_subln_g, w_branch, w_out, layer_num, need_ctx):
    f32 = jnp.float32
    cuts = [HY_COLS, HY_COLS + SSD_COLS, HY_COLS + SSD_COLS + DA_COLS]
    hy_l, ssd_l, da_l, gate_l = jnp.split(p_lat, cuts, axis=-1)
    hy_c, ssd_c, da_c, gate_c = jnp.split(p_ctx, cuts, axis=-1)
    hy_args = (hy_conv_w, hy_conv_b, hy_w1, hy_b1, hy_w2, hy_b2, hy_w3, hy_b3, hy_w4, hy_freq, hy_bias)

    y_hy_l = hyena_seq(hy_l, *hy_args)

    a = -jnp.exp(ssd_a_log.astype(f32))
    z_l, xs_l, b_l, c_l, dt_l = ssd_prepare(ssd_l, ssd_conv_w, ssd_conv_b, ssd_dt_bias)
    z_c, xs_c, b_c, c_c, dt_c = ssd_prepare(ssd_c, ssd_conv_w, ssd_conv_b, ssd_dt_bias)
    zero = jnp.zeros((p_ctx.shape[0], SSD_HEADS, SSD_HEADDIM, SSD_STATE), f32)
    yl_dirs, yc_dirs = [], []
    for d in range(2):
        rev = d == 1
        yc, state = ssd_direction(xs_c, b_c, c_c, dt_c[:, :, d], a[d], zero, rev, need_ctx)
        yl, _ = ssd_direction(xs_l, b_l, c_l, dt_l[:, :, d], a[d], state, rev, True)
        yl_dirs.append(yl)
        yc_dirs.append(yc)
    y_ssd_l = ssd_finish(yl_dirs[0] + yl_dirs[1], xs_l, z_l, ssd_d, ssd_norm_g)

    lam_init = 0.8 - 0.6 * math.exp(-0.3 * layer_num)
    lv = da_lambda.astype(f32)
    lam = jnp.exp(jnp.sum(lv[0] * lv[1])) - jnp.exp(jnp.sum(lv[2] * lv[3])) + lam_init
    q_l, k_l, v_l = jnp.split(da_l, [DA_QK, 2 * DA_QK], axis=-1)
    k_c, v_c = jnp.split(da_c[..., DA_QK:], [DA_QK], axis=-1)
    cos, sin = axial_rope(p_lat.shape[1])
    q_l = apply_rope(qk_heads(q_l, da_q_norm), cos, sin)
    k_l = apply_rope(qk_heads(k_l, da_k_norm), cos, sin)
    k_c = qk_heads(k_c, da_k_norm)
    v_c = v_heads(v_c)
    k_all = jnp.concatenate([k_l, k_c], axis=1)
    v_all = jnp.concatenate([v_heads(v_l), v_c], axis=1)
    y_da_l = da_finish(latent_attention(q_l, k_all, v_all, lam), da_subln_g, lam_init, p_lat.dtype)

    out_l = merge_branches((y_hy_l, y_ssd_l, y_da_l), gate_l, w_branch, w_out)
    if not need_ctx:
        return out_l, None
    y_hy_c = hyena_seq(hy_c, *hy_args)
    y_ssd_c = ssd_finish(yc_dirs[0] + yc_dirs[1], xs_c, z_c, ssd_d, ssd_norm_g)
    q_c = qk_heads(da_c[..., :DA_QK], da_q_norm)
    y_da_c = da_finish(diff_attend(q_c, k_c, v_c, lam), da_subln_g, lam_init, p_ctx.dtype)
    out_c = merge_branches((y_hy_c, y_ssd_c, y_da_c), gate_c, w_branch, w_out)
    return out_l, out_c


def setup_inputs(seed: int = 0) -> dict:
    key = jax.random.key(seed)
    ks = iter(jax.random.split(key, 64))
    f32 = jnp.float32

    def nrm(shape, scale):
        return jax.random.normal(next(ks), shape, f32) * scale

    def gain(shape):
        return 1.0 + nrm(shape, 0.02)

    L, D = DEPTH, D_MODEL
    n_dense = (DEPTH + 1) // 2
    n_moe = DEPTH // 2
    dt0 = jnp.exp(jax.random.uniform(next(ks), (L, 2, SSD_HEADS), f32, math.log(1e-3), math.log(1e-1)))
    dt_bias = dt0 + jnp.log(-jnp.expm1(-dt0))
    a_log = jnp.log(jax.random.uniform(next(ks), (L, 2, SSD_HEADS), f32, 1.0, 16.0))
    return {
        'x': nrm((BATCH, SEQ, D), 1.0),
        'c': nrm((BATCH, D), 1.0),
        'ctx': nrm((BATCH, CTX_LEN, D), 1.0),
        'c_ctx': nrm((D,), 1.0),
        'w_mod': nrm((L, D, 6 * D), D ** -0.5),
        'b_mod': nrm((L, 6 * D), 0.01),
        'norm1_g': gain((L, D)),
        'norm2_g': gain((L, D)),
        'w_in': nrm((L, D, IN_COLS), D ** -0.5),
        'hy_conv_w': nrm((L, HY_SHORT_CONV, HY_COLS), HY_SHORT_CONV ** -0.5),
        'hy_conv_b': nrm((L, HY_COLS), 0.01),
        'hy_w1': nrm((L, HY_EMB, HY_FFN), HY_EMB ** -0.5),
        'hy_b1': nrm((L, HY_FFN), 0.1),
        'hy_w2': nrm((L, HY_FFN, HY_FFN), HY_FFN ** -0.5),
        'hy_b2': nrm((L, HY_FFN), 0.1),
        'hy_w3': nrm((L, HY_FFN, HY_FFN), HY_FFN ** -0.5),
        'hy_b3': nrm((L, HY_FFN), 0.1),
        'hy_w4': nrm((L, HY_FFN, HY_ORDER * 2 * HY_WIDTH), 0.1 * HY_FFN ** -0.5),
        'hy_freq': gain((L, 3, HY_FFN)),
        'hy_bias': nrm((L, HY_ORDER, HY_WIDTH), 0.1),
        'ssd_conv_w': nrm((L, SSD_SHORT_CONV, SSD_XBC), SSD_SHORT_CONV ** -0.5),
        'ssd_conv_b': nrm((L, SSD_XBC), 0.01),
        'ssd_dt_bias': dt_bias,
        'ssd_a_log': a_log,
        'ssd_d': gain((L, SSD_HEADS)),
        'ssd_norm_g': gain((L, SSD_INNER)),
        'da_q_norm': gain((L, DA_HEAD_DIM)),
        'da_k_norm': gain((L, DA_HEAD_DIM)),
        'da_lambda': nrm((L, 4, DA_HEAD_DIM), 0.1),
        'da_subln_g': gain((L, DA_V_DIM)),
        'w_branch': nrm((L, N_BRANCH, D, D), D ** -0.5),
        'w_out': nrm((L, D, D), D ** -0.5),
        'ffn_w1': nrm((n_dense, D, DENSE_FF), D ** -0.5),
        'ffn_w3': nrm((n_dense, D, DENSE_FF), D ** -0.5),
        'ffn_w2': nrm((n_dense, DENSE_FF, D), DENSE_FF ** -0.5),
        'router_w': nrm((n_moe, D, N_EXPERTS), D ** -0.5),
        'moe_w1': nrm((n_moe, N_EXPERTS, D, EXPERT_FF), D ** -0.5),
        'moe_w3': nrm((n_moe, N_EXPERTS, D, EXPERT_FF), D ** -0.5),
        'moe_w2': nrm((n_moe, N_EXPERTS, EXPERT_FF, D), EXPERT_FF ** -0.5),
    }


def reference(x, c, ctx, c_ctx, w_mod, b_mod, norm1_g, norm2_g, w_in, hy_conv_w, hy_conv_b, hy_w1, hy_b1, hy_w2,
              hy_b2, hy_w3, hy_b3, hy_w4, hy_freq, hy_bias, ssd_conv_w, ssd_conv_b, ssd_dt_bias, ssd_a_log, ssd_d,
              ssd_norm_g, da_q_norm, da_k_norm, da_lambda, da_subln_g, w_branch, w_out, ffn_w1, ffn_w3, ffn_w2,
              router_w, moe_w1, moe_w3, moe_w2):
    h_lat, h_ctx = x, ctx
    for i in range(DEPTH):
        need_ctx = i < DEPTH - 1
        mod_l = (jax.nn.silu(c) @ w_mod[i] + b_mod[i])[:, None, :]
        mod_c = jax.nn.silu(c_ctx) @ w_mod[i] + b_mod[i]
        sh1_l, sc1_l, g1_l, sh2_l, sc2_l, g2_l = jnp.split(mod_l, 6, axis=-1)
        sh1_c, sc1_c, g1_c, sh2_c, sc2_c, g2_c = jnp.split(mod_c, 6, axis=-1)

        p_lat = modulate(rmsnorm(h_lat, norm1_g[i]), sh1_l, sc1_l) @ w_in[i]
        p_ctx = modulate(rmsnorm(h_ctx, norm1_g[i]), sh1_c, sc1_c) @ w_in[i]
        m_lat, m_ctx = token_mixer(p_lat, p_ctx, hy_conv_w[i], hy_conv_b[i], hy_w1[i], hy_b1[i], hy_w2[i], hy_b2[i],
                                   hy_w3[i], hy_b3[i], hy_w4[i], hy_freq[i], hy_bias[i], ssd_conv_w[i], ssd_conv_b[i],
                                   ssd_dt_bias[i], ssd_a_log[i], ssd_d[i], ssd_norm_g[i], da_q_norm[i], da_k_norm[i],
                                   da_lambda[i], da_subln_g[i], w_branch[i], w_out[i], i + 1, need_ctx)
        h_lat = h_lat + g1_l * m_lat
        f_in = modulate(rmsnorm(h_lat, norm2_g[i]), sh2_l, sc2_l)
        if need_ctx:
            h_ctx = h_ctx + g1_c * m_ctx
            f_in = jnp.concatenate([modulate(rmsnorm(h_ctx, norm2_g[i]), sh2_c, sc2_c), f_in], axis=1)
        j = i // 2
        if i % 2 == 0:
            f_out = swiglu(f_in, ffn_w1[j], ffn_w3[j], ffn_w2[j])
        else:
            f_out = moe_swiglu(f_in, router_w[j], moe_w1[j], moe_w3[j], moe_w2[j])
        n_c = f_in.shape[1] - h_lat.shape[1]
        h_lat = h_lat + g2_l * f_out[:, n_c:]
        if need_ctx:
            h_ctx = h_ctx + g2_c * f_out[:, :n_c]
    return h_lat
```

```python
import numpy as np
import concourse.bass as bass
import concourse.mybir as mybir
from concourse.bass_utils import run_bass_kernel_spmd

F32 = mybir.dt.float32
BF16 = mybir.dt.bfloat16
I32 = mybir.dt.int32
AF = mybir.ActivationFunctionType
ALU = mybir.AluOpType
AX = mybir.AxisListType


class Buf:
    __slots__ = ("t", "name", "lw", "rd")

    def __init__(self, t, name):
        self.t = t
        self.name = name
        self.lw = None
        self.rd = []

    def __getitem__(self, idx):
        return self.t[idx]


class Prog:
    ENG = ("pe", "dve", "act", "pool", "sp")
    NDMA = 6

    def __init__(self, nc):
        self.nc = nc
        self.ops = {e: [] for e in self.ENG}
        self.cnt = {e: 0 for e in self.ENG}
        self.waited = {e: {} for e in self.ENG}
        self.dmak = {e: 0 for e in self.ENG}
        self.pctx = []
        self.sctx = []
        self.sems = {}
        self.nbuf = 0
        self.prefix = ""
        self.bind = {}
        self.ext = {}
        self.psum_tiles = None
        self.nphase = 0
        for e in self.ENG:
            self.sem(e)
        for q in self.ENG:
            for i in range(self.NDMA):
                self.sem(("d", q, i))

    def sem(self, key):
        if key not in self.sems:
            cm = self.nc.semaphore("s_%s" % "_".join(str(k) for k in (key if isinstance(key, tuple) else (key,))))
            self.sems[key] = cm.__enter__()
            self.pctx.append(cm)
        return self.sems[key]

    def sb(self, shape, dt, name=None):
        self.nbuf += 1
        name = "%s%s_%d" % (self.prefix, name or "sb", self.nbuf)
        cm = self.nc.sbuf_tensor(name, list(shape), dt)
        t = cm.__enter__()
        self.sctx.append(cm)
        return Buf(t, name)

    def ps(self, shape, dt, name=None):
        self.nbuf += 1
        name = name or "ps%d" % self.nbuf
        cm = self.nc.psum_tensor(name, list(shape), dt)
        t = cm.__enter__()
        self.pctx.append(cm)
        return Buf(t, name)

    def dram(self, name, shape, dt, kind="Internal"):
        if name in self.bind:
            b = self.bind[name]
            assert tuple(b.t.shape) == tuple(shape), (name, b.t.shape, shape)
            return b
        full = self.prefix + name
        if kind == "ExternalInput":
            self.ext[full] = (tuple(shape), dt)
        return Buf(self.nc.dram_tensor(full, list(shape), dt, kind=kind).ap(), full)

    def scratch(self, name, shape, dt):
        return Buf(self.nc.dram_tensor(name, list(shape), dt, kind="Internal").ap(), name)

    def sub(self, buf, name):
        return Buf(buf.t, name)

    def _need(self, eng, tok, waits):
        if tok is None:
            return
        k, v = tok
        if self.waited[eng].get(k, 0) >= v:
            return
        self.waited[eng][k] = v
        waits[k] = max(waits.get(k, 0), v)

    def _deps(self, eng, r, w):
        waits = {}
        for b in r:
            self._need(eng, b.lw, waits)
        for b in w:
            self._need(eng, b.lw, waits)
            for t in b.rd:
                self._need(eng, t, waits)
        return waits

    def op(self, eng, fn, r=(), w=()):
        waits = self._deps(eng, r, w)
        self.cnt[eng] += 1
        tok = (eng, self.cnt[eng])
        self.ops[eng].append((waits, fn, (eng, 1)))
        for b in r:
            b.rd.append(tok)
        for b in w:
            b.lw = tok
            b.rd = []
        return tok

    def dma(self, q, out, in_, r=(), w=(), **kw):
        waits = self._deps(q, r, w)
        k = self.dmak[q]
        self.dmak[q] += 1
        key = ("d", q, k % self.NDMA)
        val = 16 * (k // self.NDMA + 1)
        if k >= self.NDMA:
            self._need(q, (key, val - 16), waits)
        tok = (key, val)
        self.ops[q].append((waits, lambda e: e.dma_start(out=out, in_=in_, **kw), (key, 16)))
        for b in r:
            b.rd.append(tok)
        for b in w:
            b.lw = tok
            b.rd = []
        return tok

    def fence(self, eng, bufs):
        waits = {}
        for b in bufs:
            self._need(eng, b.lw, waits)
        self.ops[eng].append((waits, None, None))

    def barrier(self):
        toks = [(e, self.cnt[e]) for e in self.ENG if self.cnt[e] > 0]
        for q in self.ENG:
            k = self.dmak[q]
            for slot in range(min(self.NDMA, k)):
                n_on_slot = (k - slot + self.NDMA - 1) // self.NDMA
                toks.append((("d", q, slot), 16 * n_on_slot))
        for e in self.ENG:
            waits = {}
            for t in toks:
                if t[0] == e:
                    continue
                self._need(e, t, waits)
            self.ops[e].append((waits, None, None))

    def flush(self):
        nc = self.nc
        self.nphase += 1
        with nc.Block() as block:
            def mk(ename):
                def body(eng):
                    for waits, fn, inc in self.ops[ename]:
                        for k, v in waits.items():
                            eng.wait_ge(self.sems[k], v)
                        if fn is not None:
                            ins = fn(eng)
                            ins.then_inc(self.sems[inc[0]], inc[1])
                return body
            block.tensor(mk("pe"))
            block.vector(mk("dve"))
            block.scalar(mk("act"))
            block.gpsimd(mk("pool"))
            block.sync(mk("sp"))
        self.ops = {e: [] for e in self.ENG}

    def end_phase(self):
        self.barrier()
        self.flush()
        while self.sctx:
            self.sctx.pop().__exit__(None, None, None)
        self.bind = {}
        self.prefix = ""

    def emit(self):
        self.end_phase()
        while self.pctx:
            self.pctx.pop().__exit__(None, None, None)


def _begin(P, prefix, bind):
    own = P is None
    if own:
        P = Prog(bass.Bass("TRN2", target_bir_lowering=False))
    P.prefix = prefix
    P.bind = dict(bind or {})
    return P, own


def _finish(P, own, outs):
    if own:
        P.fence("sp", outs)
        P.emit()
        return P.nc
    P.end_phase()
    return None


D = 1024
KC = 8
EPS = 1e-6


class Ctx:
    def __init__(self, P, nrot=8, wsize=4096, nw=6):
        self.P = P
        self.wsize = wsize
        self.nw = nw
        if P.psum_tiles is None:
            P.psum_tiles = [P.ps([128, 512], F32, name="psum%d" % i) for i in range(8)]
        self.psum = P.psum_tiles
        self.nrot = nrot
        self.accs = self.psum[nrot:]
        self.pi = 0
        self.wb = []
        self.wi = 0
        self.ci = 0

    def ps(self):
        p = self.psum[self.pi % self.nrot]
        self.pi += 1
        return p

    def wbuf(self):
        if not self.wb:
            self.wb = [self.P.sb([128, self.wsize], BF16, name="wbuf%d" % i) for i in range(self.nw)]
        b = self.wb[self.wi % len(self.wb)]
        self.wi += 1
        return b

    def wload(self, Wd, ap, kc, ncol):
        b = self.wbuf()
        v = b[:, 0:kc * ncol].rearrange("p (k c) -> p k c", k=kc)
        self.P.dma("pool", v, ap, r=[Wd], w=[b])
        return b, v

    def ev(self):
        self.ci += 1
        return "dve" if self.ci % 2 else "act"


def wview(Wd, r0, kc, c0, ncol):
    return Wd[r0:r0 + kc * 128, c0:c0 + ncol].rearrange("(k p) c -> p k c", p=128)


def modnorm(C, hT, n, A, Bv, fT, ones_bf, sqb, tmp, rstd, f32out=None):
    P = C.P
    P.op("act", lambda e: e.activation(out=sqb[:, :, 0:n], in_=hT[:, :, 0:n], func=AF.Square), r=[hT], w=[sqb])
    ps = C.ps()
    for kc in range(KC):
        P.op("pe", lambda e, kc=kc: e.matmul(ps[:, 0:n], lhsT=ones_bf[:], rhs=sqb[:, kc, 0:n],
                                             start=(kc == 0), stop=(kc == KC - 1)), r=[ones_bf, sqb], w=[ps])
    P.op("act", lambda e: e.activation(out=rstd[:, 0:n], in_=ps[:, 0:n], func=AF.Sqrt, bias=EPS, scale=1.0 / D),
         r=[ps], w=[rstd])
    P.op("dve", lambda e: e.reciprocal(out=rstd[:, 0:n], in_=rstd[:, 0:n]), r=[rstd], w=[rstd])
    for kc in range(KC):
        P.op("dve", lambda e, kc=kc: e.scalar_tensor_tensor(
            out=tmp[:, kc, 0:n], in0=hT[:, kc, 0:n], scalar=A[:, kc:kc + 1], in1=rstd[:, 0:n],
            op0=ALU.mult, op1=ALU.mult), r=[hT, A, rstd], w=[tmp])
    for kc in range(KC):
        P.op("act", lambda e, kc=kc: e.activation(out=fT[:, kc, 0:n], in_=tmp[:, kc, 0:n], func=AF.Identity,
                                                  bias=Bv[:, kc:kc + 1], scale=1.0), r=[tmp, Bv], w=[fT])
        if f32out is not None:
            P.op("dve", lambda e, kc=kc: e.tensor_scalar(out=f32out[:, kc, 0:n], in0=tmp[:, kc, 0:n],
                                                         scalar1=Bv[:, kc:kc + 1], scalar2=None, op0=ALU.add),
                 r=[tmp, Bv], w=[f32out])


def build_B(T, tiles, moe, P=None, prefix="", bind=None, ytm=False):
    P, own = _begin(P, prefix, bind)
    C = Ctx(P, nw=5)
    NMAX = max(n for _, n, _ in tiles)
    hT_d = P.dram("hT", [D, T], F32, "ExternalInput")
    if ytm:
        ytm_d = [[P.dram("y%s%d" % (nm, s_), [T, 512], F32, "ExternalInput") for s_ in range(2)] for nm in ("hy", "ssd")]
        yda_d = [P.dram("yda%d" % s_, [512, T], F32, "ExternalInput") for s_ in range(2)]
    else:
        yT_d = P.dram("yT", [3, D, T], F32, "ExternalInput")
    vec_d = P.dram("vecs", [128, 112], F32, "ExternalInput")
    wg_d = P.dram("wg", [D, 3 * D], F32, "ExternalInput")
    wbr_d = P.dram("wbr", [3, D, D], F32, "ExternalInput")
    wo_d = P.dram("wo", [D, D], F32, "ExternalInput")
    if moe:
        NE, FF = 8, 2048
        rw_d = P.dram("rw", [D, NE], F32, "ExternalInput")
        w1_d = P.dram("w1", [NE, D, FF], F32, "ExternalInput")
        w3_d = P.dram("w3", [NE, D, FF], F32, "ExternalInput")
        w2_d = P.dram("w2", [NE, FF, D], F32, "ExternalInput")
    else:
        NE, FF = 1, 4096
        w1_d = P.dram("w1", [NE, D, FF], F32, "ExternalInput")
        w3_d = P.dram("w3", [NE, D, FF], F32, "ExternalInput")
        w2_d = P.dram("w2", [NE, FF, D], F32, "ExternalInput")
    out_d = P.dram("hout", [D, T], F32, "ExternalOutput")
    FC = FF // 128

    vec = P.sb([128, 112], F32, "vec")
    der = P.sb([128, 2, 2, 8], F32, "der")
    ones_bf = P.sb([128, 128], BF16, "ones")
    hT = P.sb([128, KC, NMAX], F32, "hTs")
    f1 = P.sb([128, KC, NMAX], BF16, "f1")
    sqb = P.sb([128, KC, NMAX], BF16, "sqb")
    tmp = P.sb([128, KC, NMAX], F32, "tmp")
    rstd = P.sb([128, NMAX], F32, "rstd")
    yb = P.sb([128, 3, KC, NMAX], BF16, "yb")
    merged = P.sb([128, KC, NMAX], BF16, "merged")
    acc = P.sb([128, 4, NMAX], F32, "acc")
    sig = P.sb([128, NMAX], F32, "sig")
    tm2 = P.sb([128, NMAX], F32, "tm2")
    gT = P.sb([128, FC, NMAX], BF16, "gT")
    if moe or ytm:
        id_d = P.dram("ident", [128, 128], F32, "ExternalInput")
        ident = P.sb([128, 128], F32, "idents")
        P.dma("sp", ident[:], id_d[:], r=[id_d], w=[ident])
    if ytm:
        yst_ = [P.sb([128, 1024], F32, "ytmst%d" % q) for q in range(2)]
    if moe:
        f32T = P.sb([128, KC, NMAX], F32, "f32T")
        rw = P.sb([128, KC, NE], F32, "rws")
        lg = P.sb([128, 4, NE], F32, "lg")
        gt = P.sb([128, 4, NE], F32, "gt")
        mk = P.sb([128, 4, NE], F32, "mk")
        m12 = P.sb([128, 4, 4], F32, "m12")
        gtT = P.sb([NE, NMAX], F32, "gtT")
        sel = P.sb([NE, NE, 128], F32, "sel")
        gbc = P.sb([128, NMAX], F32, "gbc")
        oacc = P.sb([128, KC, NMAX], F32, "oacc")

    P.dma("sp", vec[:], vec_d[:], r=[vec_d], w=[vec])
    P.op("dve", lambda e: e.memset(ones_bf[:], 1.0), w=[ones_bf])
    for cls in range(2):
        for which, (gcol, sccol) in enumerate(((0, 16 + 8 + cls * 48), (8, 16 + 32 + cls * 48))):
            P.op("dve", lambda e, cls=cls, which=which, gcol=gcol, sccol=sccol: e.scalar_tensor_tensor(
                out=der[:, cls, which, :], in0=vec[:, sccol:sccol + 8], scalar=1.0, in1=vec[:, gcol:gcol + 8],
                op0=ALU.add, op1=ALU.mult), r=[vec], w=[der])
    if moe:
        P.dma("sp", rw[:], rw_d[:, :].rearrange("(k p) e -> p k e", p=128), r=[rw_d], w=[rw])
        for e_ in range(NE):
            P.op("dve", lambda e, e_=e_: e.tensor_copy(out=sel[:, e_, :], in_=ident[0:NE, e_:e_ + 1].to_broadcast([NE, 128])),
                 r=[ident], w=[sel])

    def do_tok(t0, n, cls):
        mo = 16 + cls * 48
        P.dma("sp", hT[:, :, 0:n], hT_d[:, t0:t0 + n].rearrange("(k p) t -> p k t", p=128), r=[hT_d], w=[hT])
        if ytm:
            qn_ = 0
            for i in range(2):
                for sb_ in range(n // 128):
                    st_ = yst_[qn_ % 2]
                    qn_ += 1
                    r0 = t0 + sb_ * 128
                    for s_ in range(2):
                        P.dma("sp", st_[:, s_ * 512:(s_ + 1) * 512], ytm_d[i][s_][r0:r0 + 128, :], r=[ytm_d[i][s_]], w=[st_])
                    for kc in range(KC):
                        ps = C.ps()
                        P.op("pe", lambda e, ps=ps, st_=st_, kc=kc: e.transpose(out=ps[:, 0:128], in_=st_[:, kc * 128:(kc + 1) * 128], identity=ident[:]),
                             r=[st_, ident], w=[ps])
                        if kc % 2:
                            P.op("dve", lambda e, ps=ps, i=i, kc=kc, sb_=sb_: e.tensor_copy(out=yb[:, i, kc, sb_ * 128:(sb_ + 1) * 128], in_=ps[:, 0:128]),
                                 r=[ps], w=[yb])
                        else:
                            P.op("act", lambda e, ps=ps, i=i, kc=kc, sb_=sb_: e.activation(out=yb[:, i, kc, sb_ * 128:(sb_ + 1) * 128], in_=ps[:, 0:128], func=AF.Copy),
                                 r=[ps], w=[yb])
            for s_ in range(2):
                P.dma("pool", yb[:, 2, s_ * 4:(s_ + 1) * 4, 0:n], yda_d[s_][:, t0:t0 + n].rearrange("(k p) t -> p k t", p=128),
                      r=[yda_d[s_]], w=[yb])
        else:
            for i in range(3):
                P.dma("pool", yb[:, i, :, 0:n], yT_d[i, :, t0:t0 + n].rearrange("(k p) t -> p k t", p=128), r=[yT_d], w=[yb])
        modnorm_v(C, hT, n, der, (cls, 0), vec, mo + 0, f1, ones_bf, sqb, tmp, rstd)
        for dg in range(2):
            for i in range(3):
                wbb, wbv = C.wload(wbr_d, wview(wbr_d[i], 0, KC, dg * 512, 512), KC, 512)
                wgb, wgv = C.wload(wg_d, wview(wg_d, 0, KC, i * D + dg * 512, 512), KC, 512)
                for j in range(4):
                    psA = C.ps(); psB = C.ps()
                    for kc in range(KC):
                        P.op("pe", lambda e, kc=kc, j=j, wbv=wbv, i=i, psA=psA: e.matmul(
                            psA[:, 0:n], lhsT=wbv[:, kc, j * 128:(j + 1) * 128], rhs=yb[:, i, kc, 0:n],
                            start=(kc == 0), stop=(kc == KC - 1)), r=[wbb, yb], w=[psA])
                    for kc in range(KC):
                        P.op("pe", lambda e, kc=kc, j=j, wgv=wgv, psB=psB: e.matmul(
                            psB[:, 0:n], lhsT=wgv[:, kc, j * 128:(j + 1) * 128], rhs=f1[:, kc, 0:n],
                            start=(kc == 0), stop=(kc == KC - 1)), r=[wgb, f1], w=[psB])
                    P.op("act", lambda e, psB=psB: e.activation(out=sig[:, 0:n], in_=psB[:, 0:n], func=AF.Sigmoid),
                         r=[psB], w=[sig])
                    if i == 0:
                        P.op("dve", lambda e, psA=psA, j=j: e.tensor_tensor(out=acc[:, j, 0:n], in0=psA[:, 0:n], in1=sig[:, 0:n],
                                                                    op=ALU.mult), r=[psA, sig], w=[acc])
                    else:
                        P.op("dve", lambda e, psA=psA: e.tensor_tensor(out=tm2[:, 0:n], in0=psA[:, 0:n], in1=sig[:, 0:n],
                                                               op=ALU.mult), r=[psA, sig], w=[tm2])
                        if i == 1:
                            P.op("dve", lambda e, j=j: e.tensor_tensor(out=acc[:, j, 0:n], in0=acc[:, j, 0:n], in1=tm2[:, 0:n],
                                                                       op=ALU.add), r=[acc, tm2], w=[acc])
                        else:
                            P.op("dve", lambda e, j=j, dg=dg: e.tensor_tensor(out=merged[:, dg * 4 + j, 0:n], in0=acc[:, j, 0:n],
                                                                       in1=tm2[:, 0:n], op=ALU.add), r=[acc, tm2], w=[merged])
        for dg in range(2):
            wob, wov = C.wload(wo_d, wview(wo_d, 0, KC, dg * 512, 512), KC, 512)
            for j in range(4):
                db = dg * 4 + j
                ps = C.ps()
                for kc in range(KC):
                    P.op("pe", lambda e, kc=kc, j=j, wov=wov, ps=ps: e.matmul(
                        ps[:, 0:n], lhsT=wov[:, kc, j * 128:(j + 1) * 128], rhs=merged[:, kc, 0:n],
                        start=(kc == 0), stop=(kc == KC - 1)), r=[wob, merged], w=[ps])
                P.op("dve", lambda e, ps=ps, db=db, mo=mo: e.scalar_tensor_tensor(
                    out=hT[:, db, 0:n], in0=ps[:, 0:n], scalar=vec[:, mo + 16 + db:mo + 16 + db + 1], in1=hT[:, db, 0:n],
                    op0=ALU.mult, op1=ALU.add), r=[ps, vec, hT], w=[hT])
        modnorm_v(C, hT, n, der, (cls, 1), vec, mo + 24, f1, ones_bf, sqb, tmp, rstd, f32out=(f32T if moe else None))
        if moe:
            nsub = n // 128
            for sb_ in range(nsub):
                ps = C.ps()
                for kc in range(KC):
                    P.op("pe", lambda e, kc=kc, sb_=sb_, ps=ps: e.matmul(
                        ps[:, 0:NE], lhsT=f32T[:, kc, sb_ * 128:(sb_ + 1) * 128], rhs=rw[:, kc, :],
                        start=(kc == 0), stop=(kc == KC - 1)), r=[f32T, rw], w=[ps])
                P.op("dve", lambda e, ps=ps, sb_=sb_: e.tensor_copy(out=lg[:, sb_, :], in_=ps[:, 0:NE]), r=[ps], w=[lg])
            for sb_ in range(nsub):
                P.op("dve", lambda e, sb_=sb_: e.reduce_max(out=m12[:, sb_, 0:1], in_=lg[:, sb_, :], axis=AX.X), r=[lg], w=[m12])
                P.op("dve", lambda e, sb_=sb_: e.tensor_scalar(out=mk[:, sb_, :], in0=lg[:, sb_, :], scalar1=m12[:, sb_, 0:1],
                                                               scalar2=None, op0=ALU.is_equal), r=[lg, m12], w=[mk])
                P.op("dve", lambda e, sb_=sb_: e.scalar_tensor_tensor(out=gt[:, sb_, :], in0=mk[:, sb_, :], scalar=-1e30,
                                                                      in1=lg[:, sb_, :], op0=ALU.mult, op1=ALU.add),
                     r=[mk, lg], w=[gt])
                P.op("dve", lambda e, sb_=sb_: e.reduce_max(out=m12[:, sb_, 1:2], in_=gt[:, sb_, :], axis=AX.X), r=[gt], w=[m12])
                P.op("dve", lambda e, sb_=sb_: e.tensor_tensor(out=m12[:, sb_, 2:3], in0=m12[:, sb_, 0:1], in1=m12[:, sb_, 1:2],
                                                               op=ALU.subtract), r=[m12], w=[m12])
                P.op("act", lambda e, sb_=sb_: e.activation(out=m12[:, sb_, 3:4], in_=m12[:, sb_, 2:3], func=AF.Sigmoid, scale=-1.0),
                     r=[m12], w=[m12])
                P.op("act", lambda e, sb_=sb_: e.activation(out=m12[:, sb_, 2:3], in_=m12[:, sb_, 2:3], func=AF.Sigmoid),
                     r=[m12], w=[m12])
                P.op("dve", lambda e, sb_=sb_: e.tensor_scalar(out=gt[:, sb_, :], in0=gt[:, sb_, :], scalar1=m12[:, sb_, 1:2],
                                                               scalar2=m12[:, sb_, 3:4], op0=ALU.is_equal, op1=ALU.mult),
                     r=[gt, m12], w=[gt])
                P.op("dve", lambda e, sb_=sb_: e.scalar_tensor_tensor(out=gt[:, sb_, :], in0=mk[:, sb_, :], scalar=m12[:, sb_, 2:3],
                                                                      in1=gt[:, sb_, :], op0=ALU.mult, op1=ALU.add),
                     r=[mk, m12, gt], w=[gt])
                ps = C.ps()
                P.op("pe", lambda e, sb_=sb_, ps=ps: e.transpose(out=ps[0:NE, 0:128], in_=gt[:, sb_, :], identity=ident[:]),
                     r=[gt, ident], w=[ps])
                P.op("dve", lambda e, sb_=sb_, ps=ps: e.tensor_copy(out=gtT[:, sb_ * 128:(sb_ + 1) * 128], in_=ps[0:NE, 0:128]),
                     r=[ps], w=[gtT])
        for ex in range(NE):
            if moe:
                ps = C.ps()
                P.op("pe", lambda e, ex=ex, ps=ps: e.matmul(ps[:, 0:n], lhsT=sel[:, ex, :], rhs=gtT[:, 0:n], start=True, stop=True),
                     r=[sel, gtT], w=[ps])
                P.op("act", lambda e, ps=ps: e.activation(out=gbc[:, 0:n], in_=ps[:, 0:n], func=AF.Copy), r=[ps], w=[gbc])
            for fg in range(FF // 512):
                w1b, w1v = C.wload(w1_d, wview(w1_d[ex], 0, KC, fg * 512, 512), KC, 512)
                w3b, w3v = C.wload(w3_d, wview(w3_d[ex], 0, KC, fg * 512, 512), KC, 512)
                for j in range(4):
                    psa = C.ps(); psb = C.ps()
                    for kc in range(KC):
                        P.op("pe", lambda e, kc=kc, j=j, w1v=w1v, psa=psa: e.matmul(
                            psa[:, 0:n], lhsT=w1v[:, kc, j * 128:(j + 1) * 128], rhs=f1[:, kc, 0:n],
                            start=(kc == 0), stop=(kc == KC - 1)), r=[w1b, f1], w=[psa])
                    for kc in range(KC):
                        P.op("pe", lambda e, kc=kc, j=j, w3v=w3v, psb=psb: e.matmul(
                            psb[:, 0:n], lhsT=w3v[:, kc, j * 128:(j + 1) * 128], rhs=f1[:, kc, 0:n],
                            start=(kc == 0), stop=(kc == KC - 1)), r=[w3b, f1], w=[psb])
                    P.op("act", lambda e, psa=psa: e.activation(out=sig[:, 0:n], in_=psa[:, 0:n], func=AF.Silu), r=[psa], w=[sig])
                    if moe:
                        P.op("dve", lambda e: e.tensor_tensor(out=sig[:, 0:n], in0=sig[:, 0:n], in1=gbc[:, 0:n], op=ALU.mult),
                             r=[sig, gbc], w=[sig])
                    P.op("dve", lambda e, psb=psb, fg=fg, j=j: e.tensor_tensor(out=gT[:, fg * 4 + j, 0:n], in0=psb[:, 0:n],
                                                                        in1=sig[:, 0:n], op=ALU.mult), r=[psb, sig], w=[gT])
            for db in range(KC):
                ps = C.ps()
                w2b, w2v = C.wload(w2_d, wview(w2_d[ex], 0, FC, db * 128, 128), FC, 128)
                for fc in range(FC):
                    P.op("pe", lambda e, fc=fc, w2v=w2v, ps=ps: e.matmul(
                        ps[:, 0:n], lhsT=w2v[:, fc, :], rhs=gT[:, fc, 0:n],
                        start=(fc == 0), stop=(fc == FC - 1)), r=[w2b, gT], w=[ps])
                if not moe:
                    P.op("dve", lambda e, ps=ps, db=db, mo=mo: e.scalar_tensor_tensor(
                        out=hT[:, db, 0:n], in0=ps[:, 0:n], scalar=vec[:, mo + 40 + db:mo + 40 + db + 1], in1=hT[:, db, 0:n],
                        op0=ALU.mult, op1=ALU.add), r=[ps, vec, hT], w=[hT])
                elif ex == 0:
                    P.op("dve", lambda e, ps=ps, db=db: e.tensor_copy(out=oacc[:, db, 0:n], in_=ps[:, 0:n]), r=[ps], w=[oacc])
                else:
                    P.op("dve", lambda e, ps=ps, db=db: e.tensor_tensor(out=oacc[:, db, 0:n], in0=oacc[:, db, 0:n], in1=ps[:, 0:n],
                                                                 op=ALU.add), r=[ps, oacc], w=[oacc])
        if moe:
            for db in range(KC):
                P.op("dve", lambda e, db=db, mo=mo: e.scalar_tensor_tensor(
                    out=hT[:, db, 0:n], in0=oacc[:, db, 0:n], scalar=vec[:, mo + 40 + db:mo + 40 + db + 1], in1=hT[:, db, 0:n],
                    op0=ALU.mult, op1=ALU.add), r=[oacc, vec, hT], w=[hT])
        P.dma("sp", out_d[:, t0:t0 + n].rearrange("(k p) t -> p k t", p=128), hT[:, :, 0:n], r=[hT], w=[out_d])
    for tl in tiles:
        do_tok(*tl)
    return _finish(P, own, [out_d])


def modnorm_v(C, hT, n, der, idx, vec, bcol, fT, ones_bf, sqb, tmp, rstd, f32out=None):
    cls, which = idx

    class _V:
        pass
    A = Buf(der.t[:, cls, which, :], "A")
    A.lw, A.rd = der.lw, der.rd
    Bv = Buf(vec.t[:, bcol:bcol + 8], "Bv")
    Bv.lw, Bv.rd = vec.lw, vec.rd
    modnorm(C, hT, n, A, Bv, fT, ones_bf, sqb, tmp, rstd, f32out=f32out)


NTOK = 4352
NCTX = 256


def build_attn(layer, need_ctx, nheads=4, nqb=8, P=None, prefix="", bind=None):
    import math
    lam_init = 0.8 - 0.6 * math.exp(-0.3 * (layer + 1))
    P, own = _begin(P, prefix, bind)
    C = Ctx(P, nrot=4, wsize=1024, nw=6)
    hT_d = P.dram("hT", [D, NTOK], F32, "ExternalInput")
    vec_d = P.dram("vecs", [128, 112], F32, "ExternalInput")
    wq_d = P.dram("wq", [D, 512], F32, "ExternalInput")
    wk_d = P.dram("wk", [D, 512], F32, "ExternalInput")
    wv_d = P.dram("wv", [D, 512], F32, "ExternalInput")
    qkg_d = P.dram("qkg", [128, 2], F32, "ExternalInput")
    lam_d = P.dram("lamv", [1, 256], F32, "ExternalInput")
    sg_d = P.dram("sublng", [128, 128], F32, "ExternalInput")
    cos_d = P.dram("cosT", [128, 4096], F32, "ExternalInput")
    sin_d = P.dram("sinT", [128, 4096], F32, "ExternalInput")
    rm_d = P.dram("rmT", [128, 128], F32, "ExternalInput")
    bo_d = P.dram("blockones", [128, 128], F32, "ExternalInput")
    id_d = P.dram("ident", [128, 128], F32, "ExternalInput")
    out_d = P.dram("yT", [512, NTOK], F32, "ExternalOutput")

    vec = P.sb([128, 112], F32, "vec")
    der = P.sb([128, 2, 2, 8], F32, "der")
    ones_bf = P.sb([128, 128], BF16, "ones")
    ones_f = P.sb([1, 128], F32, "ones_f")
    f1 = P.sb([128, KC, NTOK], BF16, "f1")
    NT = 256
    hTt = P.sb([128, KC, NT], F32, "hTt")
    sqb = P.sb([128, KC, NT], BF16, "sqb")
    tmp = P.sb([128, KC, NT], F32, "tmp")
    rstd = P.sb([128, 512], F32, "rstd")
    qkg = P.sb([128, 2], F32, "qkgs")
    lamv = P.sb([1, 256], F32, "lamvs")
    lams = P.sb([1, 8], F32, "lams")
    neglam = P.sb([128, 1], F32, "neglam")
    sg = P.sb([128, 128], F32, "sg")
    cosT = P.sb([128, 4096], F32, "cosTs")
    sinT = P.sb([128, 4096], F32, "sinTs")
    rmT = P.sb([128, 128], F32, "rmTs")
    bo = P.sb([128, 128], BF16, "bos")
    ident = P.sb([128, 128], F32, "idents")
    qT = P.sb([128, NTOK], BF16, "qT")
    kT = P.sb([128, NTOK], BF16, "kT")
    Vaug = P.sb([128, 34, 130], BF16, "Vaug")
    sqq = P.sb([128, 512], BF16, "sqq")
    qn = P.sb([128, 512], F32, "qn")
    t1 = P.sb([128, 512], F32, "t1")
    t2 = P.sb([128, 512], F32, "t2")
    PT = [P.sb([128, 512], BF16, "PT%d" % i) for i in range(3)]
    o0 = P.sb([128, 4, 128], F32, "o0")
    oo = P.sb([128, 4, 128], F32, "oo")
    junk = P.sb([128, 128], F32, "junk")
    rc = P.sb([128, 8], F32, "rc")
    yq = P.sb([128, 128], F32, "yq")
    yst = P.sb([128, 512], F32, "yst")

    P.dma("sp", vec[:], vec_d[:], r=[vec_d], w=[vec])
    P.dma("sp", qkg[:], qkg_d[:], r=[qkg_d], w=[qkg])
    P.dma("sp", lamv[:], lam_d[:], r=[lam_d], w=[lamv])
    P.dma("sp", sg[:], sg_d[:], r=[sg_d], w=[sg])
    P.dma("sp", cosT[:], cos_d[:], r=[cos_d], w=[cosT])
    P.dma("sp", sinT[:], sin_d[:], r=[sin_d], w=[sinT])
    P.dma("sp", rmT[:], rm_d[:], r=[rm_d], w=[rmT])
    P.dma("pool", bo[:], bo_d[:], r=[bo_d], w=[bo])
    P.dma("sp", ident[:], id_d[:], r=[id_d], w=[ident])
    P.op("dve", lambda e: e.memset(ones_bf[:], 1.0), w=[ones_bf])
    P.op("dve", lambda e: e.memset(ones_f[:], 1.0), w=[ones_f])
    P.op("dve", lambda e: e.memset(Vaug[:, :, 128:130], 1.0), w=[Vaug])
    for cls in range(2):
        sccol = 16 + 8 + cls * 48
        P.op("dve", lambda e, cls=cls, sccol=sccol: e.scalar_tensor_tensor(
            out=der[:, cls, 0, :], in0=vec[:, sccol:sccol + 8], scalar=1.0, in1=vec[:, 0:8],
            op0=ALU.add, op1=ALU.mult), r=[vec], w=[der])
    P.op("dve", lambda e: e.tensor_scalar(out=sg[:], in0=sg[:], scalar1=float(1.0 - lam_init), scalar2=None, op0=ALU.mult),
         r=[sg], w=[sg])
    P.op("dve", lambda e: e.tensor_tensor(out=lamv[:, 0:64], in0=lamv[:, 0:64], in1=lamv[:, 64:128], op=ALU.mult), r=[lamv], w=[lamv])
    P.op("dve", lambda e: e.tensor_tensor(out=lamv[:, 128:192], in0=lamv[:, 128:192], in1=lamv[:, 192:256], op=ALU.mult), r=[lamv], w=[lamv])
    P.op("dve", lambda e: e.reduce_sum(out=lams[:, 0:1], in_=lamv[:, 0:64], axis=AX.X), r=[lamv], w=[lams])
    P.op("dve", lambda e: e.reduce_sum(out=lams[:, 1:2], in_=lamv[:, 128:192], axis=AX.X), r=[lamv], w=[lams])
    P.op("act", lambda e: e.activation(out=lams[:, 2:4], in_=lams[:, 0:2], func=AF.Exp), r=[lams], w=[lams])
    P.op("dve", lambda e: e.tensor_tensor(out=lams[:, 4:5], in0=lams[:, 3:4], in1=lams[:, 2:3], op=ALU.subtract), r=[lams], w=[lams])
    P.op("dve", lambda e: e.tensor_scalar(out=lams[:, 4:5], in0=lams[:, 4:5], scalar1=float(-lam_init), scalar2=None, op0=ALU.add),
         r=[lams], w=[lams])
    ps = C.ps()
    P.op("pe", lambda e, ps=ps: e.matmul(ps[:, 0:1], lhsT=ones_f[:, :], rhs=lams[:, 4:5], start=True, stop=True), r=[ones_f, lams], w=[ps])
    P.op("dve", lambda e, ps=ps: e.tensor_copy(out=neglam[:], in_=ps[:, 0:1]), r=[ps], w=[neglam])

    for t0 in range(0, NTOK, NT):
        cls = 1 if t0 < NCTX else 0
        mo = 16 + cls * 48
        P.dma("sp", hTt[:, :, :], hT_d[:, t0:t0 + NT].rearrange("(k p) t -> p k t", p=128), r=[hT_d], w=[hTt])
        f1v = Buf(f1.t[:, :, t0:t0 + NT], "f1v")
        f1v.lw, f1v.rd = f1.lw, f1.rd
        modnorm_v(C, hTt, NT, der, (cls, 0), vec, mo + 0, f1v, ones_bf, sqb, tmp, rstd)
        f1.lw = f1v.lw
    blocks = [(0, 256)] + [(256 + i * 512, 512) for i in range(8)]
    for h in range(nheads):
        wqb, wqv = C.wload(wq_d, wview(wq_d, 0, KC, h * 128, 128), KC, 128)
        wkb, wkv = C.wload(wk_d, wview(wk_d, 0, KC, h * 128, 128), KC, 128)
        wvb, wvv = C.wload(wv_d, wview(wv_d, 0, KC, h * 128, 128), KC, 128)
        for (dst, wb_, wv_, gcol, isq) in ((qT, wqb, wqv, 0, True), (kT, wkb, wkv, 1, False)):
            for (t0, n) in blocks:
                if isq and (not need_ctx) and t0 < NCTX:
                    continue
                ps = C.ps()
                for kc in range(KC):
                    P.op("pe", lambda e, kc=kc, ps=ps, wv_=wv_, t0=t0, n=n: e.matmul(
                        ps[:, 0:n], lhsT=wv_[:, kc, :], rhs=f1[:, kc, t0:t0 + n], start=(kc == 0), stop=(kc == KC - 1)),
                        r=[wb_, f1], w=[ps])
                P.op("act", lambda e, ps=ps, n=n: e.activation(out=sqq[:, 0:n], in_=ps[:, 0:n], func=AF.Square), r=[ps], w=[sqq])
                ps2 = C.ps()
                P.op("pe", lambda e, ps2=ps2, n=n: e.matmul(ps2[:, 0:n], lhsT=bo[:], rhs=sqq[:, 0:n], start=True, stop=True),
                     r=[bo, sqq], w=[ps2])
                P.op("act", lambda e, ps2=ps2, n=n: e.activation(out=rstd[:, 0:n], in_=ps2[:, 0:n], func=AF.Sqrt, bias=EPS, scale=1.0 / 64),
                     r=[ps2], w=[rstd])
                P.op("dve", lambda e, n=n: e.reciprocal(out=rstd[:, 0:n], in_=rstd[:, 0:n]), r=[rstd], w=[rstd])
                if t0 < NCTX:
                    P.op("dve", lambda e, ps=ps, n=n, gcol=gcol, dst=dst, t0=t0: e.scalar_tensor_tensor(
                        out=dst[:, t0:t0 + n], in0=ps[:, 0:n], scalar=qkg[:, gcol:gcol + 1], in1=rstd[:, 0:n],
                        op0=ALU.mult, op1=ALU.mult), r=[ps, qkg, rstd], w=[dst])
                    continue
                P.op("dve", lambda e, ps=ps, n=n, gcol=gcol: e.scalar_tensor_tensor(
                    out=qn[:, 0:n], in0=ps[:, 0:n], scalar=qkg[:, gcol:gcol + 1], in1=rstd[:, 0:n],
                    op0=ALU.mult, op1=ALU.mult), r=[ps, qkg, rstd], w=[qn])
                ps3 = C.ps()
                P.op("pe", lambda e, ps3=ps3, n=n: e.matmul(ps3[:, 0:n], lhsT=rmT[:], rhs=qn[:, 0:n], start=True, stop=True),
                     r=[rmT, qn], w=[ps3])
                lp = t0 - NCTX
                P.op("pool", lambda e, n=n, lp=lp: e.tensor_tensor(out=t1[:, 0:n], in0=qn[:, 0:n], in1=cosT[:, lp:lp + n], op=ALU.mult),
                     r=[qn, cosT], w=[t1])
                P.op("dve", lambda e, ps3=ps3, n=n, lp=lp: e.tensor_tensor(out=t2[:, 0:n], in0=ps3[:, 0:n], in1=sinT[:, lp:lp + n], op=ALU.mult),
                     r=[ps3, sinT], w=[t2])
                P.op("dve", lambda e, n=n, dst=dst, t0=t0: e.tensor_tensor(out=dst[:, t0:t0 + n], in0=t1[:, 0:n], in1=t2[:, 0:n], op=ALU.add),
                     r=[t1, t2], w=[dst])
        for kt in range(34):
            ps = C.ps()
            for kc in range(KC):
                P.op("pe", lambda e, kc=kc, ps=ps, kt=kt, wvv=wvv: e.matmul(
                    ps[:, 0:128], lhsT=f1[:, kc, kt * 128:(kt + 1) * 128], rhs=wvv[:, kc, :], start=(kc == 0), stop=(kc == KC - 1)),
                    r=[f1, wvb], w=[ps])
            eng = C.ev()
            if eng == "dve":
                P.op("dve", lambda e, ps=ps, kt=kt: e.tensor_copy(out=Vaug[:, kt, 0:128], in_=ps[:, 0:128]), r=[ps], w=[Vaug])
            else:
                P.op("act", lambda e, ps=ps, kt=kt: e.activation(out=Vaug[:, kt, 0:128], in_=ps[:, 0:128], func=AF.Copy), r=[ps], w=[Vaug])
        qblocks = [(256 + i * 512, 512, list(range(34))) for i in range(nqb)]
        if need_ctx:
            qblocks = [(0, 256, [0, 1])] + qblocks
        pti = 0
        for (q0, nq, kts) in qblocks:
            nqs = nq // 128
            for comp in range(2):
                r0 = comp * 64
                for ki, kt in enumerate(kts):
                    ps = C.ps()
                    P.op("pe", lambda e, ps=ps, kt=kt, r0=r0, q0=q0, nq=nq: e.matmul(
                        ps[:, 0:nq], lhsT=kT[r0:r0 + 64, kt * 128:(kt + 1) * 128], rhs=qT[r0:r0 + 64, q0:q0 + nq],
                        start=True, stop=True), r=[kT, qT], w=[ps])
                    pt = PT[pti % 3]
                    pti += 1
                    P.op("act", lambda e, ps=ps, pt=pt, nq=nq: e.activation(out=pt[:, 0:nq], in_=ps[:, 0:nq], func=AF.Exp, scale=0.125),
                         r=[ps], w=[pt])
                    for qs in range(nqs):
                        acc = C.accs[qs]
                        P.op("pe", lambda e, acc=acc, pt=pt, qs=qs, kt=kt, ki=ki, last=(ki == len(kts) - 1): e.matmul(
                            acc[:, 0:129], lhsT=pt[:, qs * 128:(qs + 1) * 128], rhs=Vaug[:, kt, 0:129],
                            start=(ki == 0), stop=last), r=[pt, Vaug], w=[acc])
                for qs in range(nqs):
                    acc = C.accs[qs]
                    P.op("dve", lambda e, acc=acc, qs=qs, comp=comp: e.reciprocal(out=rc[:, comp * 4 + qs:comp * 4 + qs + 1], in_=acc[:, 128:129]),
                         r=[acc], w=[rc])
                    if comp == 0:
                        P.op("dve", lambda e, acc=acc, qs=qs: e.tensor_scalar(out=o0[:, qs, :], in0=acc[:, 0:128], scalar1=rc[:, qs:qs + 1],
                                                                       scalar2=None, op0=ALU.mult), r=[acc, rc], w=[o0])
                    else:
                        P.op("dve", lambda e, qs=qs: e.tensor_tensor(out=rc[:, 4 + qs:5 + qs], in0=rc[:, 4 + qs:5 + qs], in1=neglam[:, 0:1],
                                                                     op=ALU.mult), r=[rc, neglam], w=[rc])
                        P.op("dve", lambda e, acc=acc, qs=qs: e.scalar_tensor_tensor(
                            out=oo[:, qs, :], in0=acc[:, 0:128], scalar=rc[:, 4 + qs:5 + qs], in1=o0[:, qs, :],
                            op0=ALU.mult, op1=ALU.add), r=[acc, rc, o0], w=[oo])
            for qs in range(nqs):
                P.op("dve", lambda e, qs=qs: e.memset(rc[:, qs:qs + 1], 0.0), w=[rc])
                P.op("act", lambda e, qs=qs: e.activation(out=junk[:], in_=oo[:, qs, :], func=AF.Square, accum_out=rc[:, qs:qs + 1]),
                     r=[oo, rc], w=[junk, rc])
                P.op("act", lambda e, qs=qs: e.activation(out=rc[:, qs:qs + 1], in_=rc[:, qs:qs + 1], func=AF.Sqrt, bias=EPS, scale=1.0 / 128),
                     r=[rc], w=[rc])
                P.op("dve", lambda e, qs=qs: e.reciprocal(out=rc[:, qs:qs + 1], in_=rc[:, qs:qs + 1]), r=[rc], w=[rc])
                P.op("dve", lambda e, qs=qs: e.scalar_tensor_tensor(out=yq[:], in0=oo[:, qs, :], scalar=rc[:, qs:qs + 1], in1=sg[:],
                                                                    op0=ALU.mult, op1=ALU.mult), r=[oo, rc, sg], w=[yq])
                ps = C.ps()
                P.op("pe", lambda e, ps=ps: e.transpose(out=ps[:, 0:128], in_=yq[:], identity=ident[:]), r=[yq, ident], w=[ps])
                P.op("act", lambda e, ps=ps, qs=qs: e.activation(out=yst[:, qs * 128:(qs + 1) * 128], in_=ps[:, 0:128], func=AF.Copy),
                     r=[ps], w=[yst])
            P.dma("sp", out_d[h * 128:(h + 1) * 128, q0:q0 + nq], yst[:, 0:nq], r=[yst], w=[out_d])
    return _finish(P, own, [out_d])


def attn_consts():
    inv = (10000.0 ** (-np.arange(16, dtype=np.float32) / 16)).astype(np.float32)
    t = np.arange(4096)
    ang = np.stack([t // 64, t % 64], -1).astype(np.float32)[:, :, None] * inv
    cos, sin = np.cos(ang).astype(np.float32), np.sin(ang).astype(np.float32)
    cosT = np.zeros((128, 4096), np.float32)
    sinT = np.zeros((128, 4096), np.float32)
    rmT = np.zeros((128, 128), np.float32)
    for p in range(128):
        d = p % 64
        a, j, fr = d // 32, (d % 32) // 16, d % 16
        cosT[p] = cos[:, a, fr]
        sinT[p] = sin[:, a, fr]
        if j == 0:
            rmT[p + 16, p] = -1.0
        else:
            rmT[p - 16, p] = 1.0
    bo = np.zeros((128, 128), np.float32)
    bo[:64, :64] = 1.0
    bo[64:, 64:] = 1.0
    return dict(cosT=cosT, sinT=sinT, rmT=rmT, blockones=bo, ident=np.eye(128, dtype=np.float32))


def fpos(t):
    return 1 + t if t < NCTX else 3 + t


def build_pj(groups, P=None, prefix="", bind=None):
    P, own = _begin(P, prefix, bind)
    C = Ctx(P, nrot=8, wsize=4096, nw=4)
    hT_d = P.dram("hT", [D, NTOK], F32, "ExternalInput")
    vec_d = P.dram("vecs", [128, 112], F32, "ExternalInput")
    vec = P.sb([128, 112], F32, "vec")
    der = P.sb([128, 2, 2, 8], F32, "der")
    ones_bf = P.sb([128, 128], BF16, "ones")
    FW = NTOK + 4
    f1 = P.sb([128, KC, FW], BF16, "f1")
    NT = 256
    hTt = P.sb([128, KC, NT], F32, "hTt")
    sqb = P.sb([128, KC, NT], BF16, "sqb")
    tmp = P.sb([128, KC, NT], F32, "tmp")
    rstd = P.sb([128, 512], F32, "rstd")
    stage = [P.sb([128, 512], F32, "stage%d" % i) for i in range(3)]
    wst = P.sb([128, KC, 512], F32, "wst")
    cwb = P.sb([128, 512], F32, "cwb")
    P.dma("sp", vec[:], vec_d[:], r=[vec_d], w=[vec])
    P.op("dve", lambda e: e.memset(ones_bf[:], 1.0), w=[ones_bf])
    P.op("pool", lambda e: e.memset(f1[:], 0.0), w=[f1])
    for cls in range(2):
        sccol = 16 + 8 + cls * 48
        P.op("dve", lambda e, cls=cls, sccol=sccol: e.scalar_tensor_tensor(
            out=der[:, cls, 0, :], in0=vec[:, sccol:sccol + 8], scalar=1.0, in1=vec[:, 0:8],
            op0=ALU.add, op1=ALU.mult), r=[vec], w=[der])
    for t0 in range(0, NTOK, NT):
        cls = 1 if t0 < NCTX else 0
        mo = 16 + cls * 48
        P.dma("sp", hTt[:, :, :], hT_d[:, t0:t0 + NT].rearrange("(k p) t -> p k t", p=128), r=[hT_d], w=[hTt])
        c0 = fpos(t0)
        f1v = Buf(f1.t[:, :, c0:c0 + NT], "f1v")
        f1v.lw, f1v.rd = f1.lw, f1.rd
        modnorm_v(C, hTt, NT, der, (cls, 0), vec, mo + 0, f1v, ones_bf, sqb, tmp, rstd)
        f1.lw = f1v.lw
    si = [0]
    for g in groups:
        name, ncol, conv, act, layout = g["name"], g["ncol"], g["conv"], g["act"], g["layout"]
        w_d = P.dram("w_" + name, [D, ncol], F32, "ExternalInput")
        ntap = 3 if conv else 1
        if conv:
            cw_d = P.dram("cw_" + name, [3, 128, ncol], F32, "ExternalInput")
        if layout == "tm":
            b_d = P.dram("b_" + name, [1, ncol], F32, "ExternalInput")
            out_d = P.dram("o_" + name, [NTOK, ncol], F32, "ExternalOutput")
            brow = P.sb([1, ncol], BF16, "brow_" + name)
            P.dma("pool", brow[:], b_d[:], r=[b_d], w=[brow])
        else:
            b_d = P.dram("b_" + name, [128, ncol // 128], F32, "ExternalInput")
            out_d = P.dram("o_" + name, [ncol, NTOK], F32, "ExternalOutput")
            bcol = P.sb([128, ncol // 128], F32, "bcol_" + name)
            P.dma("sp", bcol[:], b_d[:], r=[b_d], w=[bcol])
        wts = []
        P.dma("sp", wst[:, :, 0:ncol], wview(w_d, 0, KC, 0, ncol), r=[w_d], w=[wst])
        for j in range(ntap):
            wt = P.sb([128, KC, ncol], BF16, "wt_%s_%d" % (name, j))
            if conv:
                P.dma("sp", cwb[:, 0:ncol], cw_d[j], r=[cw_d], w=[cwb])
                P.op("dve", lambda e, wt=wt, ncol=ncol: e.tensor_tensor(
                    out=wt[:], in0=wst[:, :, 0:ncol], in1=cwb[:, 0:ncol].unsqueeze(1).to_broadcast([128, KC, ncol]), op=ALU.mult),
                    r=[wst, cwb], w=[wt])
            else:
                P.op("dve", lambda e, wt=wt, ncol=ncol: e.tensor_copy(out=wt[:], in_=wst[:, :, 0:ncol]), r=[wst], w=[wt])
            wts.append(wt)
        shifts = (-1, 0, 1) if conv else (0,)
        func = {None: AF.Copy, "silu": AF.Silu}[act]
        if layout == "tm":
            for t0 in range(0, NTOK, 128):
                ps = C.ps()
                c0 = fpos(t0)
                nmm = ntap * KC
                i = 0
                for j, sh in enumerate(shifts):
                    for kc in range(KC):
                        P.op("pe", lambda e, ps=ps, j=j, sh=sh, kc=kc, c0=c0, ncol=ncol, wts=wts, i=i: e.matmul(
                            ps[:, 0:ncol], lhsT=f1[:, kc, c0 + sh:c0 + sh + 128], rhs=wts[j][:, kc, :], start=(i == 0), stop=False),
                            r=[f1, wts[j]], w=[ps])
                        i += 1
                P.op("pe", lambda e, ps=ps, ncol=ncol, brow=brow: e.matmul(
                    ps[:, 0:ncol], lhsT=ones_bf[0:1, :], rhs=brow[0:1, :], start=False, stop=True), r=[ones_bf, brow], w=[ps])
                st = stage[si[0] % 3]
                si[0] += 1
                P.op("act", lambda e, ps=ps, st=st, ncol=ncol, func=func: e.activation(out=st[:, 0:ncol], in_=ps[:, 0:ncol], func=func),
                     r=[ps], w=[st])
                P.dma("sp", out_d[t0:t0 + 128, :], st[:, 0:ncol], r=[st], w=[out_d])
        else:
            blocks = [(0, 256)] + [(256 + i * 512, 512) for i in range(8)]
            for cb in range(ncol // 128):
                for (t0, n) in blocks:
                    ps = C.ps()
                    c0 = fpos(t0)
                    i = 0
                    for j, sh in enumerate(shifts):
                        for kc in range(KC):
                            P.op("pe", lambda e, ps=ps, j=j, sh=sh, kc=kc, c0=c0, n=n, cb=cb, wts=wts, i=i, last=(i == ntap * KC - 1): e.matmul(
                                ps[:, 0:n], lhsT=wts[j][:, kc, cb * 128:(cb + 1) * 128], rhs=f1[:, kc, c0 + sh:c0 + sh + n],
                                start=(i == 0), stop=last), r=[f1, wts[j]], w=[ps])
                            i += 1
                    st = stage[si[0] % 3]
                    si[0] += 1
                    P.op("act", lambda e, ps=ps, st=st, n=n, cb=cb, func=func, bcol=bcol: e.activation(
                        out=st[:, 0:n], in_=ps[:, 0:n], func=func, bias=bcol[:, cb:cb + 1], scale=1.0), r=[ps, bcol], w=[st])
                    P.dma("sp", out_d[cb * 128:(cb + 1) * 128, t0:t0 + n], st[:, 0:n], r=[st], w=[out_d])
        g["_out"] = out_d
    return _finish(P, own, [g["_out"] for g in groups])


def ssd_consts():
    k = np.arange(128)[:, None]
    l = np.arange(128)[None, :]
    return dict(triu=(k <= l).astype(np.float32), trius=(k < l).astype(np.float32),
                tril=(k >= l).astype(np.float32), trils=(k > l).astype(np.float32))


def build_ssd(need_ctx, P=None, prefix="", bind=None):
    P, own = _begin(P, prefix, bind)
    C = Ctx(P, nrot=8)
    NCH = 34
    xs_d = P.dram("xs", [NTOK, 512], F32, "ExternalInput")
    b_d = P.dram("btm", [NTOK, 128], F32, "ExternalInput")
    z_d = P.dram("z", [NTOK, 512], F32, "ExternalInput")
    dt_d = P.dram("dtraw", [NTOK, 16], F32, "ExternalInput")
    bt_d = P.dram("bT", [128, NTOK], F32, "ExternalInput")
    ct_d = P.dram("cT", [128, NTOK], F32, "ExternalInput")
    dtb_d = P.dram("dtb", [128, 16], F32, "ExternalInput")
    alog_d = P.dram("alog", [128, 16], F32, "ExternalInput")
    dsk_d = P.dram("dskip", [128, 8], F32, "ExternalInput")
    ng_d = P.dram("normg", [128, 512], F32, "ExternalInput")
    cm_d = {k: P.dram(k, [128, 128], F32, "ExternalInput") for k in ("triu", "trius", "tril", "trils")}
    out_d = P.dram("y", [NTOK, 512], F32, "ExternalOutput")

    xs = P.sb([128, NCH, 512], F32, "xss")
    btm = P.sb([128, NCH, 128], BF16, "btms")
    bT = P.sb([128, NTOK], BF16, "bTs")
    cT = P.sb([128, NTOK], BF16, "cTs")
    dt = P.sb([128, NCH, 16], F32, "dts")
    Aa = P.sb([128, NCH, 16], F32, "Aa")
    dtb = P.sb([128, 16], F32, "dtbs")
    aneg = P.sb([128, 16], F32, "aneg")
    dsk = P.sb([128, 8], F32, "dsks")
    ng = P.sb([128, 512], F32, "ngs")
    cm = {k: P.sb([128, 128], F32, k + "_sb") for k in cm_d}
    ones_f = P.sb([128, 128], F32, "ones_f")
    yf = P.sb([128, NCH, 512], BF16, "yf")
    S = P.sb([128, 512], F32, "S")
    Sb = P.sb([128, 512], BF16, "Sb")
    sc = P.sb([128, 16], F32, "sc")
    ex = P.sb([128, 24], F32, "ex")
    Xd = P.sb([128, 512], F32, "Xd")
    Xdb = P.sb([128, 512], BF16, "Xdb")
    Xw = P.sb([128, 512], BF16, "Xw")
    cbm = P.sb([128, 128], F32, "cbm")
    Am = P.sb([128, 8, 128], F32, "Am")
    es = P.sb([128, 8, 128], F32, "es")
    Mb = P.sb([128, 8, 128], BF16, "Mb")
    yt = P.sb([128, 512], F32, "yt")
    zt = P.sb([128, 512], F32, "zt")
    y2 = P.sb([128, 512], F32, "y2")
    junk = P.sb([128, 512], F32, "junk")
    r1 = P.sb([128, 2], F32, "r1")

    P.dma("sp", xs[:], xs_d[:, :].rearrange("(c p) f -> p c f", p=128), r=[xs_d], w=[xs])
    P.dma("pool", btm[:], b_d[:, :].rearrange("(c p) f -> p c f", p=128), r=[b_d], w=[btm])
    P.dma("pool", bT[:], bt_d[:], r=[bt_d], w=[bT])
    P.dma("pool", cT[:], ct_d[:], r=[ct_d], w=[cT])
    P.dma("sp", dt[:], dt_d[:, :].rearrange("(c p) f -> p c f", p=128), r=[dt_d], w=[dt])
    P.dma("sp", dtb[:], dtb_d[:], r=[dtb_d], w=[dtb])
    P.dma("sp", aneg[:], alog_d[:], r=[alog_d], w=[aneg])
    P.dma("sp", dsk[:], dsk_d[:], r=[dsk_d], w=[dsk])
    P.dma("sp", ng[:], ng_d[:], r=[ng_d], w=[ng])
    for k in cm_d:
        P.dma("sp", cm[k][:], cm_d[k][:], r=[cm_d[k]], w=[cm[k]])
    P.op("dve", lambda e: e.memset(ones_f[:], 1.0), w=[ones_f])
    P.op("dve", lambda e: e.tensor_tensor(out=dt[:], in0=dt[:], in1=dtb[:, :].unsqueeze(1).to_broadcast([128, NCH, 16]), op=ALU.add),
         r=[dt, dtb], w=[dt])
    P.op("act", lambda e: e.activation(out=dt[:], in_=dt[:], func=AF.Exp), r=[dt], w=[dt])
    P.op("act", lambda e: e.activation(out=dt[:], in_=dt[:], func=AF.Ln, bias=1.0, scale=1.0), r=[dt], w=[dt])
    P.op("act", lambda e: e.activation(out=aneg[:], in_=aneg[:], func=AF.Exp), r=[aneg], w=[aneg])
    P.op("dve", lambda e: e.scalar_tensor_tensor(out=Aa[:], in0=dt[:], scalar=-1.0, in1=aneg[:, :].unsqueeze(1).to_broadcast([128, NCH, 16]),
                                                 op0=ALU.mult, op1=ALU.mult), r=[dt, aneg], w=[Aa])

    def bc8(ap):
        return ap.unsqueeze(2).to_broadcast([128, 8, 64])

    def v3(buf):
        return buf[:, :].rearrange("p (h q) -> p h q", h=8)

    def chunk(c, d, emit_y, finish):
        cum, SM, R = (("triu", "trils", "triu") if d == 0 else ("trius", "trius", "tril"))
        A_c = Aa[:, c, d * 8:(d + 1) * 8]
        ps1 = C.ps()
        P.op("pe", lambda e: e.matmul(ps1[:, 0:8], lhsT=cm[cum][:], rhs=A_c, start=True, stop=True), r=[cm[cum], Aa], w=[ps1])
        P.op("pe", lambda e: e.matmul(ps1[:, 8:16], lhsT=ones_f[:], rhs=A_c, start=True, stop=True), r=[ones_f, Aa], w=[ps1])
        P.op("dve", lambda e: e.tensor_copy(out=sc[:], in_=ps1[:, 0:16]), r=[ps1], w=[sc])
        P.op("act", lambda e: e.activation(out=ex[:, 0:16], in_=sc[:], func=AF.Exp), r=[sc], w=[ex])
        P.op("dve", lambda e: e.tensor_tensor(out=sc[:, 0:8], in0=sc[:, 8:16], in1=sc[:, 0:8], op=ALU.subtract), r=[sc], w=[sc])
        P.op("act", lambda e: e.activation(out=ex[:, 16:24], in_=sc[:, 0:8], func=AF.Exp), r=[sc], w=[ex])
        wy, ws = (ex[:, 0:8], ex[:, 16:24]) if d == 0 else (ex[:, 16:24], ex[:, 0:8])
        etot = ex[:, 8:16]
        P.op("dve", lambda e: e.tensor_tensor(out=v3(Xd), in0=xs[:, c, :].rearrange("p (h q) -> p h q", h=8),
                                              in1=bc8(dt[:, c, d * 8:(d + 1) * 8]), op=ALU.mult), r=[xs, dt], w=[Xd])
        P.op("pool", lambda e: e.tensor_copy(out=Xdb[:], in_=Xd[:]), r=[Xd], w=[Xdb])
        P.op("dve", lambda e: e.tensor_tensor(out=v3(Xw), in0=v3(Xd), in1=bc8(ws), op=ALU.mult), r=[Xd, ex], w=[Xw])
        cs_ = slice(c * 128, (c + 1) * 128)
        ps_st = C.ps()
        P.op("pe", lambda e: e.matmul(ps_st[:, :], lhsT=btm[:, c, :], rhs=Xw[:], start=True, stop=True), r=[btm, Xw], w=[ps_st])
        if emit_y:
            ps_off = C.ps()
            P.op("pe", lambda e: e.matmul(ps_off[:, :], lhsT=cT[:, cs_], rhs=Sb[:], start=True, stop=True), r=[cT, Sb], w=[ps_off])
            ps_cb = C.ps()
            P.op("pe", lambda e: e.matmul(ps_cb[:, 0:128], lhsT=bT[:, cs_], rhs=cT[:, cs_], start=True, stop=True), r=[bT, cT], w=[ps_cb])
            P.op("dve", lambda e: e.tensor_tensor(out=cbm[:], in0=ps_cb[:, 0:128], in1=cm[R][:], op=ALU.mult), r=[ps_cb, cm[R]], w=[cbm])
            P.op("pool", lambda e: e.tensor_tensor(out=Am[:], in0=cm[SM][:, :].unsqueeze(1).to_broadcast([128, 8, 128]),
                                                   in1=A_c.unsqueeze(2).to_broadcast([128, 8, 128]), op=ALU.mult), r=[cm[SM], Aa], w=[Am])
            psg = [C.ps(), C.ps()]
            for h in range(8):
                pg = psg[h // 4]
                P.op("pe", lambda e, h=h, pg=pg: e.matmul(pg[:, (h % 4) * 128:(h % 4 + 1) * 128], lhsT=Am[:, h, :], rhs=cm[R][:],
                                                          start=True, stop=True), r=[Am, cm[R]], w=[pg])
            for q in range(2):
                P.op("act", lambda e, q=q: e.activation(out=es[:, q * 4:(q + 1) * 4, :].rearrange("p h l -> p (h l)"), in_=psg[q][:, :], func=AF.Exp),
                     r=[psg[q]], w=[es])
            P.op("dve", lambda e: e.tensor_tensor(out=Mb[:], in0=es[:], in1=cbm[:, :].unsqueeze(1).to_broadcast([128, 8, 128]), op=ALU.mult),
                 r=[es, cbm], w=[Mb])
            ps_y = C.ps()
            for h in range(8):
                P.op("pe", lambda e, h=h: e.matmul(ps_y[:, h * 64:(h + 1) * 64], lhsT=Mb[:, h, :], rhs=Xdb[:, h * 64:(h + 1) * 64],
                                                   start=True, stop=True), r=[Mb, Xdb], w=[ps_y])
            P.op("dve", lambda e: e.tensor_tensor(out=v3(yt), in0=ps_off[:, :].rearrange("p (h q) -> p h q", h=8), in1=bc8(wy), op=ALU.mult),
                 r=[ps_off, ex], w=[yt])
            if d == 0:
                P.op("dve", lambda e: e.tensor_tensor(out=yf[:, c, :], in0=yt[:], in1=ps_y[:, :], op=ALU.add), r=[yt, ps_y], w=[yf])
            else:
                P.op("dve", lambda e: e.tensor_tensor(out=yt[:], in0=yt[:], in1=ps_y[:, :], op=ALU.add), r=[yt, ps_y], w=[yt])
        P.op("dve", lambda e: e.tensor_tensor(out=v3(S), in0=v3(S), in1=bc8(etot), op=ALU.mult), r=[S, ex], w=[S])
        P.op("dve", lambda e: e.tensor_tensor(out=S[:], in0=S[:], in1=ps_st[:, :], op=ALU.add), r=[S, ps_st], w=[S])
        P.op("pool", lambda e: e.tensor_copy(out=Sb[:], in_=S[:]), r=[S], w=[Sb])
        if finish:
            P.dma("sp", zt[:], z_d[c * 128:(c + 1) * 128, :], r=[z_d], w=[zt])
            P.op("dve", lambda e: e.tensor_tensor(out=yt[:], in0=yt[:], in1=yf[:, c, :], op=ALU.add), r=[yt, yf], w=[yt])
            P.op("dve", lambda e: e.tensor_tensor(out=v3(y2), in0=xs[:, c, :].rearrange("p (h q) -> p h q", h=8), in1=bc8(dsk[:, :]), op=ALU.mult),
                 r=[xs, dsk], w=[y2])
            P.op("dve", lambda e: e.tensor_tensor(out=yt[:], in0=yt[:], in1=y2[:], op=ALU.add), r=[yt, y2], w=[yt])
            P.op("act", lambda e: e.activation(out=zt[:], in_=zt[:], func=AF.Silu), r=[zt], w=[zt])
            P.op("dve", lambda e: e.tensor_tensor(out=yt[:], in0=yt[:], in1=zt[:], op=ALU.mult), r=[yt, zt], w=[yt])
            P.op("dve", lambda e: e.memset(r1[:, 0:1], 0.0), w=[r1])
            P.op("act", lambda e: e.activation(out=junk[:], in_=yt[:], func=AF.Square, accum_out=r1[:, 0:1]), r=[yt, r1], w=[junk, r1])
            P.op("act", lambda e: e.activation(out=r1[:, 1:2], in_=r1[:, 0:1], func=AF.Sqrt, bias=EPS, scale=1.0 / 512), r=[r1], w=[r1])
            P.op("dve", lambda e: e.reciprocal(out=r1[:, 1:2], in_=r1[:, 1:2]), r=[r1], w=[r1])
            P.op("dve", lambda e: e.scalar_tensor_tensor(out=y2[:], in0=yt[:], scalar=r1[:, 1:2], in1=ng[:], op0=ALU.mult, op1=ALU.mult),
                 r=[yt, r1, ng], w=[y2])
            P.dma("sp", out_d[c * 128:(c + 1) * 128, :], y2[:], r=[y2], w=[out_d])

    for d in range(2):
        P.op("dve", lambda e: e.memset(S[:], 0.0), w=[S])
        P.op("dve", lambda e: e.memset(Sb[:], 0.0), w=[Sb])
        order = [0, 1] + list(range(2, NCH)) if d == 0 else [1, 0] + list(range(NCH - 1, 1, -1))
        for c in order:
            isctx = c < 2
            ey = (not isctx) or need_ctx
            chunk(c, d, ey, ey and d == 1)
    if not need_ctx:
        P.op("dve", lambda e: e.memset(y2[:], 0.0), w=[y2])
        for c in range(2):
            P.dma("sp", out_d[c * 128:(c + 1) * 128, :], y2[:], r=[y2], w=[out_d])
    return _finish(P, own, [out_d])


import math as _math


def hy_consts(L):
    N = 2 * L
    f = np.arange(L, dtype=np.float64)
    th = 2 * np.pi * (f + 0.5) / N
    ang = np.outer(th, f + 0.5)
    nch = L // 128
    import ml_dtypes
    tabs = []
    for M in (np.cos(ang), np.sin(ang)):
        tabs.append(M.reshape(nch, 128, nch, 128).transpose(2, 1, 0, 3))
    tab = np.stack(tabs).reshape(2, nch, 128, nch * 128).astype(ml_dtypes.bfloat16)
    t = np.linspace(0.0, 1.0, L, dtype=np.float32)[:, None]
    w = (2.0 * np.pi * np.arange(L, dtype=np.float32)[:, None] / L).astype(np.float32)
    fb = np.linspace(1e-4, 16 - 1, 16, dtype=np.float32)[None, :]
    feats = np.concatenate([t, np.cos(fb * w), -np.sin(fb * w)], axis=-1).astype(np.float32)
    tn = np.zeros((128, nch, 2), np.float32)
    tt = np.arange(L).reshape(nch, 128).T
    tn[:, :, 0] = tt / (L - 1)
    tn[:, :, 1] = (tt + 1) / (L - 1)
    csh = np.zeros((128, nch, 2), np.float32)
    thh = (th / 2).reshape(nch, 128).T
    csh[:, :, 0] = np.cos(thh)
    csh[:, :, 1] = np.sin(thh)
    return dict(tab=np.ascontiguousarray(tab), featsT=np.ascontiguousarray(feats.T), tn=tn, csh=csh)


def hy_negdelta(s):
    max_decay = _math.log(1e-2) / 0.3
    min_decay = _math.log(1e-2) / 1.5
    deltas = np.abs(np.linspace(min_decay, max_decay, 1024, dtype=np.float32))
    d = -deltas[512 * s:512 * s + 512]
    return np.ascontiguousarray(np.broadcast_to(d[None, :], (128, 512))).astype(np.float32)


def build_hy(need_ctx, P=None, prefix="", bind=None):
    P, own = _begin(P, prefix, bind)
    C = Ctx(P, nrot=8)
    u_d = P.dram("u", [NTOK, 1536], F32, "ExternalInput")
    seqs = [dict(L=4096, T0=NCTX, sfx="L")]
    if need_ctx:
        seqs.append(dict(L=256, T0=0, sfx="C"))
    for sq in seqs:
        n_ = sq["L"] // 128
        sq["nch"] = n_
        sq["tab_d"] = P.dram("tab" + sq["sfx"], [2, n_, 128, n_ * 128], BF16, "ExternalInput")
        sq["ft_d"] = P.dram("featsT" + sq["sfx"], [33, sq["L"]], F32, "ExternalInput")
        sq["tn_d"] = P.dram("tn" + sq["sfx"], [128, n_, 2], F32, "ExternalInput")
        sq["csh_d"] = P.dram("csh" + sq["sfx"], [128, n_, 2], F32, "ExternalInput")
    w1_d = P.dram("hw1", [33, 64], F32, "ExternalInput")
    w2_d = P.dram("hw2", [64, 64], F32, "ExternalInput")
    w3_d = P.dram("hw3", [64, 64], F32, "ExternalInput")
    fq_d = P.dram("hfreq", [64, 3], F32, "ExternalInput")
    hb_d = P.dram("hb", [64, 3], F32, "ExternalInput")
    w4_d = P.dram("hw4", [64, 4, 512], F32, "ExternalInput")
    nd_d = P.dram("negdelta", [128, 512], F32, "ExternalInput")
    hbias_d = P.dram("hbias", [1, 2, 512], F32, "ExternalInput")
    out_d = P.dram("y", [NTOK, 512], F32, "ExternalOutput")

    w1 = P.sb([33, 64], F32, "w1s"); w2 = P.sb([64, 64], F32, "w2s"); w3 = P.sb([64, 64], F32, "w3s")
    fq = P.sb([64, 3], F32, "fqs"); hb = P.sb([64, 3], F32, "hbs")
    w4 = P.sb([64, 4, 512], F32, "w4s")
    nd = P.sb([128, 512], F32, "nds")
    hbias = P.sb([1, 2, 512], F32, "hbiass")
    NCH = 32
    ft = P.sb([33, 4096], F32, "fts")
    tn = P.sb([128, NCH, 2], F32, "tns")
    csh = P.sb([128, NCH, 2], F32, "cshs")
    h3 = P.sb([64, 4096 + 128], F32, "h3")
    ha = P.sb([64, 512], F32, "ha"); hbb = P.sb([64, 512], F32, "hbb")
    qi = P.sb([64, 512], I32, "qi"); qf = P.sb([64, 512], F32, "qf")
    KD = [P.sb([128, NCH, 256], BF16, "KD%d" % i) for i in range(2)]
    spec = P.sb([128, NCH, 2, 256], BF16, "spec")
    ub = P.sb([128, NCH, 256], BF16, "ub")
    slabs = [P.sb([128, NCH * 128], BF16, "slab%d" % i) for i in range(4)]
    wex = [P.sb([128, 256], F32, "wex%d" % i) for i in range(2)]
    kfb = P.sb([128, 256], F32, "kfb"); kbb = P.sb([128, 256], F32, "kbb")
    tt = [P.sb([128, 256], F32, "tt%d" % i) for i in range(4)]
    xt = [P.sb([128, 256], F32, "xt%d" % i) for i in range(2)]
    ost = [P.sb([128, 256], F32, "ost%d" % i) for i in range(2)]
    for (sbuf, dbuf) in ((w1, w1_d), (w2, w2_d), (w3, w3_d), (fq, fq_d), (hb, hb_d), (w4, w4_d), (nd, nd_d), (hbias, hbias_d)):
        P.dma("sp", sbuf[:], dbuf[:], r=[dbuf], w=[sbuf])
    cnt = {"slab": 0, "x": 0}
    TWO_PI = 2.0 * _math.pi

    def get_slab(sq, m, idx):
        sl = slabs[cnt["slab"] % 4]
        cnt["slab"] += 1
        n = sq["nch"] * 128
        P.dma("sp", sl[:, 0:n], sq["tab_d"][m, idx], r=[sq["tab_d"]], w=[sl])
        return sl

    def do_seq(sq):
        L, nch, T0 = sq["L"], sq["nch"], sq["T0"]
        N = 2 * L
        P.dma("sp", ft[:, 0:L], sq["ft_d"][:], r=[sq["ft_d"]], w=[ft])
        P.dma("sp", tn[:, 0:nch, :], sq["tn_d"][:], r=[sq["tn_d"]], w=[tn])
        P.dma("sp", csh[:, 0:nch, :], sq["csh_d"][:], r=[sq["csh_d"]], w=[csh])
        P.op("dve", lambda e: e.memset(h3[:], 0.0), w=[h3])
        bn = min(512, L)
        for b0 in range(0, L, bn):
            src = None
            for li, (wm, kdim) in enumerate(((w1, 33), (w2, 64), (w3, 64))):
                ps = C.ps()
                if li == 0:
                    P.op("pe", lambda e, ps=ps, b0=b0: e.matmul(ps[0:64, 0:bn], lhsT=w1[:, :], rhs=ft[:, b0:b0 + bn], start=True, stop=True),
                         r=[w1, ft], w=[ps])
                else:
                    P.op("pe", lambda e, ps=ps, wm=wm, src=src: e.matmul(ps[0:64, 0:bn], lhsT=wm[:, :], rhs=src[:, 0:bn], start=True, stop=True),
                         r=[wm, src], w=[ps])
                tmpb = ha if li % 2 == 0 else hbb
                P.op("dve", lambda e, ps=ps, tmpb=tmpb, li=li: e.tensor_scalar(
                    out=tmpb[:, 0:bn], in0=ps[0:64, 0:bn], scalar1=hb[:, li:li + 1], scalar2=fq[:, li:li + 1], op0=ALU.add, op1=ALU.mult),
                    r=[ps, hb, fq], w=[tmpb])
                P.op("dve", lambda e, tmpb=tmpb: e.tensor_scalar(
                    out=tmpb[:, 0:bn], in0=tmpb[:, 0:bn], scalar1=float(1.0 / TWO_PI), scalar2=64.5, op0=ALU.mult, op1=ALU.add),
                    r=[tmpb], w=[tmpb])
                P.op("dve", lambda e, tmpb=tmpb: e.tensor_copy(out=qi[:, 0:bn], in_=tmpb[:, 0:bn]), r=[tmpb], w=[qi])
                P.op("dve", lambda e: e.tensor_copy(out=qf[:, 0:bn], in_=qi[:, 0:bn]), r=[qi], w=[qf])
                P.op("dve", lambda e, tmpb=tmpb: e.tensor_tensor(out=tmpb[:, 0:bn], in0=tmpb[:, 0:bn], in1=qf[:, 0:bn], op=ALU.subtract),
                     r=[tmpb, qf], w=[tmpb])
                P.op("dve", lambda e, tmpb=tmpb: e.tensor_single_scalar(out=qf[:, 0:bn], in_=tmpb[:, 0:bn], scalar=0.0, op=ALU.is_lt),
                     r=[tmpb], w=[qf])
                P.op("dve", lambda e, tmpb=tmpb: e.tensor_tensor(out=tmpb[:, 0:bn], in0=tmpb[:, 0:bn], in1=qf[:, 0:bn], op=ALU.add),
                     r=[tmpb, qf], w=[tmpb])
                if li < 2:
                    P.op("act", lambda e, tmpb=tmpb: e.activation(out=tmpb[:, 0:bn], in_=tmpb[:, 0:bn], func=AF.Sin, bias=negpi[:, 0:1], scale=float(TWO_PI)),
                         r=[tmpb, negpi], w=[tmpb])
                    src = tmpb
                else:
                    P.op("act", lambda e, tmpb=tmpb, b0=b0: e.activation(out=h3[:, b0:b0 + bn], in_=tmpb[:, 0:bn], func=AF.Sin, bias=negpi[:, 0:1], scale=float(TWO_PI)),
                         r=[tmpb, negpi], w=[h3])
        for half in range(2):
            for o in range(2):
                do_ho(sq, half, o, slice(256 * half, 256 * half + 256))

    def do_ho(sq, half, o, ch):
        L, nch, T0 = sq["L"], sq["nch"], sq["T0"]
        N = 2 * L
        if True:
            if True:
                for tc in range(nch):
                    psf = C.ps(); psb = C.ps()
                    P.op("pe", lambda e, psf=psf, tc=tc: e.matmul(psf[:, 0:256], lhsT=h3[:, tc * 128:tc * 128 + 128], rhs=w4[:, o * 2 + 0, ch],
                                                                  start=True, stop=True), r=[h3, w4], w=[psf])
                    P.op("pe", lambda e, psb=psb, tc=tc: e.matmul(psb[:, 0:256], lhsT=h3[:, tc * 128 + 1:tc * 128 + 129], rhs=w4[:, o * 2 + 1, ch],
                                                                  start=True, stop=True), r=[h3, w4], w=[psb])
                    for dd, (psx, kx) in enumerate(((psf, kfb), (psb, kbb))):
                        P.op("act", lambda e, dd=dd, tc=tc: e.activation(out=wex[dd][:], in_=nd[:, ch], func=AF.Exp, scale=tn[:, tc, dd:dd + 1]),
                             r=[nd, tn], w=[wex[dd]])
                        P.op("dve", lambda e, dd=dd, psx=psx, kx=kx: e.scalar_tensor_tensor(
                            out=kx[:], in0=wex[dd][:], scalar=0.05, in1=psx[:, 0:256], op0=ALU.add, op1=ALU.mult), r=[wex[dd], psx], w=[kx])
                    if tc == 0:
                        ps0 = C.ps()
                        P.op("pe", lambda e, ps0=ps0: e.matmul(ps0[:, 0:256], lhsT=h3[:, 0:128], rhs=w4[:, o * 2 + 1, ch], start=True, stop=True),
                             r=[h3, w4], w=[ps0])
                        P.op("dve", lambda e, ps0=ps0: e.scalar_tensor_tensor(
                            out=kfb[0:1, :], in0=ps0[0:1, 0:256], scalar=1.05, in1=kfb[0:1, :], op0=ALU.mult, op1=ALU.add), r=[ps0, kfb], w=[kfb])
                        P.op("dve", lambda e: e.tensor_tensor(out=kfb[0:1, :], in0=kfb[0:1, :], in1=hbias[0:1, o, ch], op=ALU.add),
                             r=[kfb, hbias], w=[kfb])
                    P.op("dve", lambda e, tc=tc: e.tensor_tensor(out=KD[0][:, tc, :], in0=kfb[:], in1=kbb[:], op=ALU.add), r=[kfb, kbb], w=[KD[0]])
                    P.op("pool", lambda e, tc=tc: e.tensor_tensor(out=KD[1][:, tc, :], in0=kfb[:], in1=kbb[:], op=ALU.subtract), r=[kfb, kbb], w=[KD[1]])
                for fc in range(nch):
                    sC = get_slab(sq, 0, fc); sS = get_slab(sq, 1, fc)
                    pP = C.ps(); pQ = C.ps()
                    for tc in range(nch):
                        P.op("pe", lambda e, tc=tc, sC=sC, pP=pP: e.matmul(pP[:, 0:256], lhsT=sC[:, tc * 128:(tc + 1) * 128], rhs=KD[0][:, tc, :],
                                                                    start=(tc == 0), stop=(tc == nch - 1)), r=[sC, KD[0]], w=[pP])
                    for tc in range(nch):
                        P.op("pe", lambda e, tc=tc, sS=sS, pQ=pQ: e.matmul(pQ[:, 0:256], lhsT=sS[:, tc * 128:(tc + 1) * 128], rhs=KD[1][:, tc, :],
                                                                    start=(tc == 0), stop=(tc == nch - 1)), r=[sS, KD[1]], w=[pQ])
                    cth = csh[:, fc, 0:1]; sth = csh[:, fc, 1:2]
                    P.op("dve", lambda e, pQ=pQ, sth=sth: e.tensor_scalar(out=tt[0][:], in0=pQ[:, 0:256], scalar1=sth, scalar2=None, op0=ALU.mult),
                         r=[pQ, csh], w=[tt[0]])
                    P.op("dve", lambda e, pP=pP, cth=cth, fc=fc: e.scalar_tensor_tensor(out=spec[:, fc, 0, :], in0=pP[:, 0:256], scalar=cth, in1=tt[0][:],
                                                                                 op0=ALU.mult, op1=ALU.add), r=[pP, csh, tt[0]], w=[spec])
                    P.op("dve", lambda e, pQ=pQ, cth=cth: e.tensor_scalar(out=tt[1][:], in0=pQ[:, 0:256], scalar1=cth, scalar2=None, op0=ALU.mult),
                         r=[pQ, csh], w=[tt[1]])
                    P.op("dve", lambda e, pP=pP, sth=sth, fc=fc: e.scalar_tensor_tensor(out=spec[:, fc, 1, :], in0=pP[:, 0:256], scalar=sth, in1=tt[1][:],
                                                                                 op0=ALU.mult, op1=ALU.subtract), r=[pP, csh, tt[1]], w=[spec])
                if o == 0:
                    P.dma("pool", ub[:, 0:nch, :], u_d[T0:T0 + L, 256 * half:256 * half + 256].rearrange("(c p) f -> p c f", p=128),
                          r=[u_d], w=[ub])
                for fc in range(nch):
                    sC = get_slab(sq, 0, fc); sS = get_slab(sq, 1, fc)
                    pA = C.ps(); pB = C.ps()
                    for tc in range(nch):
                        P.op("pe", lambda e, tc=tc, sC=sC, pA=pA: e.matmul(pA[:, 0:256], lhsT=sC[:, tc * 128:(tc + 1) * 128], rhs=ub[:, tc, :],
                                                                    start=(tc == 0), stop=(tc == nch - 1)), r=[sC, ub], w=[pA])
                    for tc in range(nch):
                        P.op("pe", lambda e, tc=tc, sS=sS, pB=pB: e.matmul(pB[:, 0:256], lhsT=sS[:, tc * 128:(tc + 1) * 128], rhs=ub[:, tc, :],
                                                                    start=(tc == 0), stop=(tc == nch - 1)), r=[sS, ub], w=[pB])
                    Kr = spec[:, fc, 0, :]; Ki = spec[:, fc, 1, :]
                    P.op("dve", lambda e, pA=pA, Kr=Kr: e.tensor_tensor(out=tt[0][:], in0=pA[:, 0:256], in1=Kr, op=ALU.mult), r=[pA, spec], w=[tt[0]])
                    P.op("dve", lambda e, pB=pB, Ki=Ki: e.tensor_tensor(out=tt[1][:], in0=pB[:, 0:256], in1=Ki, op=ALU.mult), r=[pB, spec], w=[tt[1]])
                    P.op("dve", lambda e, pB=pB, Kr=Kr: e.tensor_tensor(out=tt[2][:], in0=pB[:, 0:256], in1=Kr, op=ALU.mult), r=[pB, spec], w=[tt[2]])
                    P.op("dve", lambda e, pA=pA, Ki=Ki: e.tensor_tensor(out=tt[3][:], in0=pA[:, 0:256], in1=Ki, op=ALU.mult), r=[pA, spec], w=[tt[3]])
                    P.op("pool", lambda e, fc=fc: e.tensor_tensor(out=KD[0][:, fc, :], in0=tt[0][:], in1=tt[1][:], op=ALU.add), r=[tt[0], tt[1]], w=[KD[0]])
                    P.op("pool", lambda e, fc=fc: e.tensor_tensor(out=KD[1][:, fc, :], in0=tt[2][:], in1=tt[3][:], op=ALU.subtract), r=[tt[2], tt[3]], w=[KD[1]])
                for tc in range(nch):
                    sC = get_slab(sq, 0, tc); sS = get_slab(sq, 1, tc)
                    py = C.ps()
                    for fc in range(nch):
                        P.op("pe", lambda e, fc=fc, sC=sC, py=py: e.matmul(py[:, 0:256], lhsT=sC[:, fc * 128:(fc + 1) * 128], rhs=KD[0][:, fc, :],
                                                                    start=(fc == 0), stop=False), r=[sC, KD[0]], w=[py])
                    for fc in range(nch):
                        P.op("pe", lambda e, fc=fc, sS=sS, py=py: e.matmul(py[:, 0:256], lhsT=sS[:, fc * 128:(fc + 1) * 128], rhs=KD[1][:, fc, :],
                                                                    start=False, stop=(fc == nch - 1)), r=[sS, KD[1]], w=[py])
                    xb = xt[cnt["x"] % 2]
                    ob = ost[cnt["x"] % 2]
                    cnt["x"] += 1
                    r0 = T0 + tc * 128
                    c0 = 512 * (1 + o) + 256 * half
                    P.dma("pool", xb[:], u_d[r0:r0 + 128, c0:c0 + 256], r=[u_d], w=[xb])
                    if o == 0:
                        P.op("dve", lambda e, py=py, xb=xb, tc=tc: e.scalar_tensor_tensor(
                            out=ub[:, tc, :], in0=py[:, 0:256], scalar=float(2.0 / N), in1=xb[:], op0=ALU.mult, op1=ALU.mult), r=[py, xb], w=[ub])
                    else:
                        P.op("dve", lambda e, py=py, xb=xb, ob=ob: e.scalar_tensor_tensor(
                            out=ob[:], in0=py[:, 0:256], scalar=float(2.0 / N), in1=xb[:], op0=ALU.mult, op1=ALU.mult), r=[py, xb], w=[ob])
                        P.dma("sp", out_d[r0:r0 + 128, 256 * half:256 * half + 256], ob[:], r=[ob], w=[out_d])

    negpi = P.sb([64, 1], F32, "negpi")
    P.op("dve", lambda e: e.memset(negpi[:], -_math.pi), w=[negpi])
    for sq in seqs:
        do_seq(sq)
    if not need_ctx:
        zb = P.sb([128, 512], F32, "zb")
        P.op("dve", lambda e: e.memset(zb[:], 0.0), w=[zb])
        for c in range(2):
            P.dma("sp", out_d[c * 128:(c + 1) * 128, :], zb[:], r=[zb], w=[out_d])
    return _finish(P, own, [out_d])


def build_mod(P=None, prefix="", bind=None):
    P, own = _begin(P, prefix, bind)
    C = Ctx(P, nrot=8)
    c_d = P.dram("cs", [128, KC, 5], F32, "ExternalInput")
    w_d = P.dram("wm", [2, D, 768], F32, "ExternalInput")
    b_d = P.dram("bm", [2, 1, 768], F32, "ExternalInput")
    out_d = P.dram("mod", [2, 5, 768], F32, "ExternalOutput")
    cs = P.sb([128, KC, 5], F32, "css")
    wm = P.sb([128, KC, 768], F32, "wms")
    brow = P.sb([1, 768], F32, "brow")
    ones_f = P.sb([1, 8], F32, "ones_f")
    st = P.sb([5, 768], F32, "st")
    P.dma("sp", cs[:], c_d[:], r=[c_d], w=[cs])
    P.op("act", lambda e: e.activation(out=cs[:], in_=cs[:], func=AF.Silu), r=[cs], w=[cs])
    P.op("dve", lambda e: e.memset(ones_f[:], 1.0), w=[ones_f])
    for l in range(2):
        P.dma("sp", wm[:], wview(w_d[l], 0, KC, 0, 768), r=[w_d], w=[wm])
        P.dma("sp", brow[:], b_d[l], r=[b_d], w=[brow])
        for (c0, n) in ((0, 512), (512, 256)):
            ps = C.ps()
            for kc in range(KC):
                P.op("pe", lambda e, ps=ps, kc=kc, c0=c0, n=n: e.matmul(ps[0:5, 0:n], lhsT=cs[:, kc, :], rhs=wm[:, kc, c0:c0 + n],
                                                                start=(kc == 0), stop=False), r=[cs, wm], w=[ps])
            P.op("pe", lambda e, ps=ps, c0=c0, n=n: e.matmul(ps[0:5, 0:n], lhsT=ones_f[0:1, 0:5], rhs=brow[0:1, c0:c0 + n],
                                                      start=False, stop=True), r=[ones_f, brow], w=[ps])
            P.op("act", lambda e, ps=ps, c0=c0, n=n: e.activation(out=st[:, c0:c0 + n], in_=ps[0:5, 0:n], func=AF.Copy), r=[ps], w=[st])
        P.dma("sp", out_d[l], st[:], r=[st], w=[out_d])
    return _finish(P, own, [out_d])


_CACHE = {}


def _get(key, fn):
    if key not in _CACHE:
        _CACHE[key] = fn()
    return _CACHE[key]


def _run(nc, in_maps):
    res = run_bass_kernel_spmd(nc, in_maps, core_ids=list(range(len(in_maps))))
    return res.results


def _pack(v):
    return np.ascontiguousarray(np.asarray(v, np.float32).reshape(8, 128).T)


def _c(a):
    return np.ascontiguousarray(np.asarray(a, dtype=np.float32))


def _rep(v, n=128):
    v = np.asarray(v, np.float32)
    return np.ascontiguousarray(np.broadcast_to(v[None], (n,) + v.shape))


SSD_GROUPS = [dict(name="xs", ncol=512, conv=True, act="silu", layout="tm"),
              dict(name="btm", ncol=128, conv=True, act="silu", layout="tm"),
              dict(name="z", ncol=512, conv=False, act=None, layout="tm"),
              dict(name="dtraw", ncol=16, conv=False, act=None, layout="tm"),
              dict(name="bT", ncol=128, conv=True, act="silu", layout="fm"),
              dict(name="cT", ncol=128, conv=True, act="silu", layout="fm")]
HY_GROUPS = [dict(name="uv", ncol=512, conv=True, act=None, layout="tm"),
             dict(name="ux1", ncol=512, conv=True, act=None, layout="tm"),
             dict(name="ux2", ncol=512, conv=True, act=None, layout="tm")]


def kernel_unfused(x, c, ctx, c_ctx, w_mod, b_mod, norm1_g, norm2_g, w_in, hy_conv_w, hy_conv_b, hy_w1, hy_b1, hy_w2,
           hy_b2, hy_w3, hy_b3, hy_w4, hy_freq, hy_bias, ssd_conv_w, ssd_conv_b, ssd_dt_bias, ssd_a_log, ssd_d,
           ssd_norm_g, da_q_norm, da_k_norm, da_lambda, da_subln_g, w_branch, w_out, ffn_w1, ffn_w3, ffn_w2,
           router_w, moe_w1, moe_w3, moe_w2):
    f32 = np.float32
    x = np.asarray(x, f32); ctx = np.asarray(ctx, f32)
    cores = [(b, s) for b in range(4) for s in range(2)]
    cc = np.concatenate([np.asarray(c, f32), np.asarray(c_ctx, f32)[None]], 0)
    cs = np.ascontiguousarray(cc.T.reshape(8, 128, 5).transpose(1, 0, 2))
    ncm = _get("mod", build_mod)
    res = _run(ncm, [dict(cs=cs, wm=_c(np.asarray(w_mod)[:, :, j * 768:(j + 1) * 768]),
                          bm=_c(np.asarray(b_mod)[:, None, j * 768:(j + 1) * 768])) for j in range(8)])
    mod = np.concatenate([r["mod"] for r in res], axis=2)

    h_lat = x
    h_ctx = ctx
    acons = attn_consts()
    scons = ssd_consts()
    hyL = {k + "L": v for k, v in hy_consts(4096).items()}
    hyC = {k + "C": v for k, v in hy_consts(256).items()}
    for i in range(2):
        need_ctx = i == 0
        W = np.asarray(w_in[i], f32)

        def vecs(b):
            cols = [_pack(norm1_g[i]), _pack(norm2_g[i])]
            for m in (mod[i, b], mod[i, 4]):
                for j in range(6):
                    cols.append(_pack(m[j * 1024:(j + 1) * 1024]))
            return np.ascontiguousarray(np.concatenate(cols, axis=1))
        hT = [np.ascontiguousarray(np.concatenate([h_ctx[b], h_lat[b]], 0).T) for b in range(4)]
        vv = [vecs(b) for b in range(4)]
        so = 3072
        xo = so + 1024
        cw = np.asarray(ssd_conv_w[i], f32); cb = np.asarray(ssd_conv_b[i], f32)
        maps = []
        for (b, s) in cores:
            m = dict(hT=hT[b], vecs=vv[b])
            sel = dict(xs=slice(512 * s, 512 * s + 512), btm=slice(1024 + 128 * s, 1024 + 128 * s + 128),
                       bT=slice(1024 + 128 * s, 1024 + 128 * s + 128), cT=slice(1280 + 128 * s, 1280 + 128 * s + 128))
            for nm, sl in sel.items():
                m["w_" + nm] = _c(W[:, xo:xo + 1536][:, sl])
                m["cw_" + nm] = _c(np.broadcast_to(cw[:, None, sl], (3, 128, sl.stop - sl.start)))
                if nm in ("bT", "cT"):
                    m["b_" + nm] = _c(cb[sl].reshape(1, 128).T)
                else:
                    m["b_" + nm] = _c(cb[None, sl])
            m["w_z"] = _c(W[:, so + 512 * s:so + 512 * s + 512]); m["b_z"] = np.zeros((1, 512), f32)
            dcols = [5632 + d * 16 + 8 * s + h for d in range(2) for h in range(8)]
            m["w_dtraw"] = _c(W[:, dcols]); m["b_dtraw"] = np.zeros((1, 16), f32)
            maps.append(m)
        pj = _run(_get("pj_ssd", lambda: build_pj([dict(g) for g in SSD_GROUPS])), maps)
        maps = []
        for k, (b, s) in enumerate(cores):
            r = pj[k]
            m = dict(xs=r["o_xs"], btm=r["o_btm"], z=r["o_z"], dtraw=r["o_dtraw"], bT=r["o_bT"], cT=r["o_cT"],
                     dtb=_rep(np.asarray(ssd_dt_bias[i], f32)[:, 8 * s:8 * s + 8].reshape(16)),
                     alog=_rep(np.asarray(ssd_a_log[i], f32)[:, 8 * s:8 * s + 8].reshape(16)),
                     dskip=_rep(np.asarray(ssd_d[i], f32)[8 * s:8 * s + 8]),
                     normg=_rep(np.asarray(ssd_norm_g[i], f32)[512 * s:512 * s + 512]))
            m.update(scons)
            maps.append(m)
        y_ssd = _run(_get(("ssd", need_ctx), lambda: build_ssd(need_ctx)), maps)
        del pj
        hw = np.asarray(hy_conv_w[i], f32); hbv = np.asarray(hy_conv_b[i], f32)
        maps = []
        for (b, s) in cores:
            m = dict(hT=hT[b], vecs=vv[b])
            for gi, nm in enumerate(("uv", "ux1", "ux2")):
                sl = slice(1024 * gi + 512 * s, 1024 * gi + 512 * s + 512)
                m["w_" + nm] = _c(W[:, sl])
                m["cw_" + nm] = _c(np.broadcast_to(hw[:, None, sl], (3, 128, 512)))
                m["b_" + nm] = _c(hbv[None, sl])
            maps.append(m)
        pj = _run(_get("pj_hy", lambda: build_pj([dict(g) for g in HY_GROUPS])), maps)
        maps = []
        for k, (b, s) in enumerate(cores):
            r = pj[k]
            cs_ = slice(512 * s, 512 * s + 512)
            m = dict(u=np.ascontiguousarray(np.concatenate([r["o_uv"], r["o_ux1"], r["o_ux2"]], 1)),
                     hw1=_c(hy_w1[i]), hw2=_c(hy_w2[i]), hw3=_c(hy_w3[i]), hfreq=_c(np.asarray(hy_freq[i]).T),
                     hb=_c(np.stack([np.asarray(hy_b1[i]), np.asarray(hy_b2[i]), np.asarray(hy_b3[i])], 1)),
                     hw4=_c(np.asarray(hy_w4[i], f32).reshape(64, 2, 2, 1024)[:, :, :, cs_].reshape(64, 4, 512)),
                     negdelta=hy_negdelta(s), hbias=_c(np.asarray(hy_bias[i], f32)[:, cs_][None]))
            m.update(hyL)
            if need_ctx:
                m.update(hyC)
            maps.append(m)
        y_hy = _run(_get(("hy", need_ctx), lambda: build_hy(need_ctx)), maps)
        del pj
        maps = []
        for (b, s) in cores:
            o = 5664 + s * 512
            m = dict(hT=hT[b], vecs=vv[b], wq=_c(W[:, o:o + 512]), wk=_c(W[:, o + 1024:o + 1536]), wv=_c(W[:, o + 2048:o + 2560]),
                     qkg=_c(np.stack([np.concatenate([np.asarray(da_q_norm[i], f32)] * 2), np.concatenate([np.asarray(da_k_norm[i], f32)] * 2)], 1)),
                     lamv=_c(np.asarray(da_lambda[i], f32).reshape(1, 256)), sublng=_rep(np.asarray(da_subln_g[i], f32)))
            m.update(acons)
            maps.append(m)
        y_da = _run(_get(("attn", i), lambda: build_attn(i, need_ctx)), maps)
        if need_ctx:
            T = 2176
            tiles = [(0, 128, 1)] + [(128 + 512 * q, 512, 0) for q in range(4)]
        else:
            T = 2048
            tiles = [(512 * q, 512, 0) for q in range(4)]
        maps = []
        for (b, s) in cores:
            tok = np.concatenate([np.arange(128 * s, 128 * s + 128), 256 + np.arange(2048 * s, 2048 * s + 2048)]) if need_ctx \
                else 256 + np.arange(2048 * s, 2048 * s + 2048)
            yh = np.concatenate([y_hy[2 * b]["y"], y_hy[2 * b + 1]["y"]], 1)[tok].T
            ys = np.concatenate([y_ssd[2 * b]["y"], y_ssd[2 * b + 1]["y"]], 1)[tok].T
            yd = np.concatenate([y_da[2 * b]["yT"], y_da[2 * b + 1]["yT"]], 0)[:, tok]
            m = dict(hT=np.ascontiguousarray(hT[b][:, tok]), yT=np.ascontiguousarray(np.stack([yh, ys, yd])), vecs=vv[b],
                     wg=_c(W[:, -3072:]), wbr=_c(w_branch[i]), wo=_c(w_out[i]))
            if i % 2 == 0:
                m.update(w1=_c(ffn_w1[i // 2])[None], w3=_c(ffn_w3[i // 2])[None], w2=_c(ffn_w2[i // 2])[None])
            else:
                m.update(rw=_c(router_w[i // 2]), w1=_c(moe_w1[i // 2]), w3=_c(moe_w3[i // 2]), w2=_c(moe_w2[i // 2]),
                         ident=np.eye(128, dtype=f32))
            maps.append(m)
        moe = i % 2 == 1
        outB = _run(_get(("B", T, moe), lambda: build_B(T, tiles, moe)), maps)
        del y_hy, y_ssd, y_da
        new_lat = np.empty_like(h_lat)
        new_ctx = np.array(h_ctx, copy=True)
        for k, (b, s) in enumerate(cores):
            ho = outB[k]["hout"].T
            if need_ctx:
                new_ctx[b, 128 * s:128 * s + 128] = ho[0:128]
                new_lat[b, 2048 * s:2048 * s + 2048] = ho[128:]
            else:
                new_lat[b, 2048 * s:2048 * s + 2048] = ho
        h_lat, h_ctx = new_lat, new_ctx
    return np.ascontiguousarray(h_lat.astype(np.float32))


def emit_modT(P, vecs_sc, prefix="M_"):
    P, own = _begin(P, prefix, None)
    C = Ctx(P, nrot=8)
    c_d = P.dram("cs", [128, KC, 2], F32, "ExternalInput")
    w_d = P.dram("wm", [2, D, 6 * D], F32, "ExternalInput")
    b_d = P.dram("bm", [2, 128, 48], F32, "ExternalInput")
    n_d = P.dram("norms", [2, 128, 16], F32, "ExternalInput")
    cs = P.sb([128, KC, 2], F32, "css")
    wm = [P.sb([128, KC, 512], F32, "wms%d" % i) for i in range(2)]
    bm = P.sb([128, 48], F32, "bms")
    vt = P.sb([128, 112], F32, "vt")
    P.dma("sp", cs[:], c_d[:], r=[c_d], w=[cs])
    P.op("act", lambda e: e.activation(out=cs[:], in_=cs[:], func=AF.Silu), r=[cs], w=[cs])
    for l in range(2):
        P.dma("sp", bm[:], b_d[l], r=[b_d], w=[bm])
        P.dma("sp", vt[:, 0:16], n_d[l], r=[n_d], w=[vt])
        for cb in range(12):
            wb = wm[cb % 2]
            P.dma("sp", wb[:], wview(w_d[l], 0, KC, cb * 512, 512), r=[w_d], w=[wb])
            for j4 in range(4):
                j = cb * 4 + j4
                ps = C.ps()
                for kc in range(KC):
                    P.op("pe", lambda e, ps=ps, wb=wb, kc=kc, j4=j4: e.matmul(
                        ps[:, 0:2], lhsT=wb[:, kc, j4 * 128:(j4 + 1) * 128], rhs=cs[:, kc, :], start=(kc == 0), stop=(kc == KC - 1)),
                        r=[wb, cs], w=[ps])
                for cls in range(2):
                    col = 16 + cls * 48 + j
                    P.op("dve", lambda e, ps=ps, cls=cls, col=col, j=j: e.tensor_tensor(
                        out=vt[:, col:col + 1], in0=ps[:, cls:cls + 1], in1=bm[:, j:j + 1], op=ALU.add), r=[ps, bm], w=[vt])
        P.dma("sp", vecs_sc[l][:], vt[:], r=[vt], w=[vecs_sc[l]])
    P.end_phase()


def _view(buf, ap, name):
    b = Buf(ap, name)
    return b


def build_fused():
    P = Prog(bass.Bass("TRN2", target_bir_lowering=False))
    sc = lambda name, shape: P.scratch(name, shape, F32)
    hT0 = P.dram("hT0", [D, NTOK], F32, "ExternalInput")
    shared = {}
    for nm, shp, dt in (("cosT", [128, 4096], F32), ("sinT", [128, 4096], F32), ("rmT", [128, 128], F32),
                        ("blockones", [128, 128], F32), ("ident", [128, 128], F32),
                        ("triu", [128, 128], F32), ("trius", [128, 128], F32), ("tril", [128, 128], F32), ("trils", [128, 128], F32),
                        ("tabL", [2, 32, 128, 4096], BF16), ("featsTL", [33, 4096], F32), ("tnL", [128, 32, 2], F32), ("cshL", [128, 32, 2], F32),
                        ("tabC", [2, 2, 128, 256], BF16), ("featsTC", [33, 256], F32), ("tnC", [128, 2, 2], F32), ("cshC", [128, 2, 2], F32)):
        shared[nm] = P.dram(nm, shp, dt, "ExternalInput")
    vecs_sc = [sc("vecs_l%d" % l, [128, 112]) for l in range(2)]
    hT1 = sc("hT1", [D, NTOK])
    s_xs = sc("s_xs", [NTOK, 512]); s_btm = sc("s_btm", [NTOK, 128]); s_z = sc("s_z", [NTOK, 512]); s_dt = sc("s_dt", [NTOK, 16])
    s_bT = sc("s_bT", [128, NTOK]); s_cT = sc("s_cT", [128, NTOK])
    s_u = sc("s_u", [NTOK, 1536])
    y_ssd = [sc("y_ssd%d" % s, [NTOK, 512]) for s in range(2)]
    y_hy = [sc("y_hy%d" % s, [NTOK, 512]) for s in range(2)]
    y_da = [sc("y_da%d" % s, [512, NTOK]) for s in range(2)]
    emit_modT(P, vecs_sc)
    hcur = hT0
    for i in range(2):
        need_ctx = i == 0
        for s in range(2):
            pre = "L%ds%d_" % (i, s)
            build_pj([dict(g) for g in SSD_GROUPS], P=P, prefix=pre + "pjs_",
                     bind=dict(hT=hcur, vecs=vecs_sc[i], o_xs=s_xs, o_btm=s_btm, o_z=s_z, o_dtraw=s_dt, o_bT=s_bT, o_cT=s_cT))
            b = dict(xs=s_xs, btm=s_btm, z=s_z, dtraw=s_dt, bT=s_bT, cT=s_cT, y=y_ssd[s])
            b.update({k: shared[k] for k in ("triu", "trius", "tril", "trils")})
            build_ssd(need_ctx, P=P, prefix=pre + "ssd_", bind=b)
            build_pj([dict(g) for g in HY_GROUPS], P=P, prefix=pre + "pjh_",
                     bind=dict(hT=hcur, vecs=vecs_sc[i], o_uv=Buf(s_u.t[:, 0:512], "u0"), o_ux1=Buf(s_u.t[:, 512:1024], "u1"),
                               o_ux2=Buf(s_u.t[:, 1024:1536], "u2")))
            b = dict(u=s_u, y=y_hy[s])
            b.update({k: shared[k] for k in ("tabL", "featsTL", "tnL", "cshL")})
            if need_ctx:
                b.update({k: shared[k] for k in ("tabC", "featsTC", "tnC", "cshC")})
            build_hy(need_ctx, P=P, prefix=pre + "hy_", bind=b)
            b = dict(hT=hcur, vecs=vecs_sc[i], yT=y_da[s])
            b.update({k: shared[k] for k in ("cosT", "sinT", "rmT", "blockones", "ident")})
            build_attn(i, need_ctx, P=P, prefix=pre + "at_", bind=b)
        pre = "L%d_B_" % i
        if need_ctx:
            T = NTOK
            tiles = [(0, 128, 1), (128, 128, 1)] + [(256 + 512 * q, 512, 0) for q in range(8)]
            b = dict(hT=hcur, vecs=vecs_sc[i], hout=hT1, ident=shared["ident"])
            for s in range(2):
                b["yhy%d" % s] = y_hy[s]; b["yssd%d" % s] = y_ssd[s]; b["yda%d" % s] = y_da[s]
        else:
            T = 4096
            tiles = [(512 * q, 512, 0) for q in range(8)]
            b = dict(hT=Buf(hcur.t[:, NCTX:NTOK], "hlat"), vecs=vecs_sc[i], ident=shared["ident"])
            for s in range(2):
                b["yhy%d" % s] = Buf(y_hy[s].t[NCTX:NTOK, :], "yh"); b["yssd%d" % s] = Buf(y_ssd[s].t[NCTX:NTOK, :], "ys")
                b["yda%d" % s] = Buf(y_da[s].t[:, NCTX:NTOK], "yd")
            out_final = P.dram("out", [D, 4096], F32, "ExternalOutput")
            b["hout"] = out_final
        build_B(T, tiles, i % 2 == 1, P=P, prefix=pre, bind=b, ytm=True)
        hcur = hT1
    P.fence("sp", [out_final])
    P.emit()
    return P.nc, P.ext


def fused_inputs(b, inp):
    f32 = np.float32
    g = lambda k: np.asarray(inp[k], f32)
    m = {}
    m["hT0"] = np.ascontiguousarray(np.concatenate([g("ctx")[b], g("x")[b]], 0).T)
    m.update(attn_consts())
    m.update(ssd_consts())
    m.update({k + "L": v for k, v in hy_consts(4096).items()})
    m.update({k + "C": v for k, v in hy_consts(256).items()})
    cc = np.stack([g("c")[b], g("c_ctx")], 1)
    m["M_cs"] = np.ascontiguousarray(cc.reshape(8, 128, 2).transpose(1, 0, 2))
    m["M_wm"] = _c(g("w_mod"))
    m["M_bm"] = np.ascontiguousarray(g("b_mod").reshape(2, 48, 128).transpose(0, 2, 1))
    m["M_norms"] = np.ascontiguousarray(np.stack([np.concatenate([_pack(g("norm1_g")[l]), _pack(g("norm2_g")[l])], 1) for l in range(2)]))
    for i in range(2):
        need_ctx = i == 0
        W = g("w_in")[i]
        so = 3072
        xo = so + 1024
        cw = g("ssd_conv_w")[i]; cb = g("ssd_conv_b")[i]
        hw = g("hy_conv_w")[i]; hbv = g("hy_conv_b")[i]
        for s in range(2):
            pre = "L%ds%d_" % (i, s)
            p = pre + "pjs_"
            sel = dict(xs=slice(512 * s, 512 * s + 512), btm=slice(1024 + 128 * s, 1024 + 128 * s + 128),
                       bT=slice(1024 + 128 * s, 1024 + 128 * s + 128), cT=slice(1280 + 128 * s, 1280 + 128 * s + 128))
            for nm, sl in sel.items():
                m[p + "w_" + nm] = _c(W[:, xo:xo + 1536][:, sl])
                m[p + "cw_" + nm] = _c(np.broadcast_to(cw[:, None, sl], (3, 128, sl.stop - sl.start)))
                m[p + "b_" + nm] = _c(cb[sl].reshape(1, 128).T) if nm in ("bT", "cT") else _c(cb[None, sl])
            m[p + "w_z"] = _c(W[:, so + 512 * s:so + 512 * s + 512]); m[p + "b_z"] = np.zeros((1, 512), f32)
            dcols = [5632 + d * 16 + 8 * s + h for d in range(2) for h in range(8)]
            m[p + "w_dtraw"] = _c(W[:, dcols]); m[p + "b_dtraw"] = np.zeros((1, 16), f32)
            p = pre + "ssd_"
            m[p + "dtb"] = _rep(g("ssd_dt_bias")[i][:, 8 * s:8 * s + 8].reshape(16))
            m[p + "alog"] = _rep(g("ssd_a_log")[i][:, 8 * s:8 * s + 8].reshape(16))
            m[p + "dskip"] = _rep(g("ssd_d")[i][8 * s:8 * s + 8])
            m[p + "normg"] = _rep(g("ssd_norm_g")[i][512 * s:512 * s + 512])
            p = pre + "pjh_"
            for gi, nm in enumerate(("uv", "ux1", "ux2")):
                sl = slice(1024 * gi + 512 * s, 1024 * gi + 512 * s + 512)
                m[p + "w_" + nm] = _c(W[:, sl])
                m[p + "cw_" + nm] = _c(np.broadcast_to(hw[:, None, sl], (3, 128, 512)))
                m[p + "b_" + nm] = _c(hbv[None, sl])
            p = pre + "hy_"
            cs_ = slice(512 * s, 512 * s + 512)
            m[p + "hw1"] = _c(g("hy_w1")[i]); m[p + "hw2"] = _c(g("hy_w2")[i]); m[p + "hw3"] = _c(g("hy_w3")[i])
            m[p + "hfreq"] = _c(g("hy_freq")[i].T)
            m[p + "hb"] = _c(np.stack([g("hy_b1")[i], g("hy_b2")[i], g("hy_b3")[i]], 1))
            m[p + "hw4"] = _c(g("hy_w4")[i].reshape(64, 2, 2, 1024)[:, :, :, cs_].reshape(64, 4, 512))
            m[p + "negdelta"] = hy_negdelta(s)
            m[p + "hbias"] = _c(g("hy_bias")[i][:, cs_][None])
            p = pre + "at_"
            o = 5664 + s * 512
            m[p + "wq"] = _c(W[:, o:o + 512]); m[p + "wk"] = _c(W[:, o + 1024:o + 1536]); m[p + "wv"] = _c(W[:, o + 2048:o + 2560])
            m[p + "qkg"] = _c(np.stack([np.concatenate([g("da_q_norm")[i]] * 2), np.concatenate([g("da_k_norm")[i]] * 2)], 1))
            m[p + "lamv"] = _c(g("da_lambda")[i].reshape(1, 256))
            m[p + "sublng"] = _rep(g("da_subln_g")[i])
        p = "L%d_B_" % i
        m[p + "wg"] = _c(W[:, -3072:]); m[p + "wbr"] = _c(g("w_branch")[i]); m[p + "wo"] = _c(g("w_out")[i])
        if i % 2 == 0:
            m[p + "w1"] = _c(g("ffn_w1")[i // 2])[None]; m[p + "w3"] = _c(g("ffn_w3")[i // 2])[None]; m[p + "w2"] = _c(g("ffn_w2")[i // 2])[None]
        else:
            m[p + "rw"] = _c(g("router_w")[i // 2]); m[p + "w1"] = _c(g("moe_w1")[i // 2]); m[p + "w3"] = _c(g("moe_w3")[i // 2])
            m[p + "w2"] = _c(g("moe_w2")[i // 2])
    return m


def kernel_fused(**inp):
    nc, ext = _get("fused", build_fused)
    maps = []
    for b in range(4):
        m = fused_inputs(b, inp)
        missing = [k for k in ext if k not in m]
        extra = [k for k in m if k not in ext]
        assert not missing, missing
        for k in extra:
            del m[k]
        maps.append(m)
    res = run_bass_kernel_spmd(nc, maps, core_ids=list(range(4)))
    out = np.stack([np.ascontiguousarray(r["out"].T) for r in res.results])
    return np.ascontiguousarray(out.astype(np.float32))


def kernel(**inp):
    return kernel_fused(**inp)
```

```python
import numpy as np
import concourse.bass as bass
import concourse.mybir as mybir
from concourse.bass_utils import run_bass_kernel_spmd

F32 = mybir.dt.float32
BF16 = mybir.dt.bfloat16
I32 = mybir.dt.int32
AF = mybir.ActivationFunctionType
ALU = mybir.AluOpType
AX = mybir.AxisListType


class Buf:
    __slots__ = ("t", "name", "lw", "rd")

    def __init__(self, t, name):
        self.t = t
        self.name = name
        self.lw = None
        self.rd = []

    def __getitem__(self, idx):
        return self.t[idx]


class Prog:
    ENG = ("pe", "dve", "act", "pool", "sp")
    NDMA = 6

    def __init__(self, nc):
        self.nc = nc
        self.ops = {e: [] for e in self.ENG}
        self.cnt = {e: 0 for e in self.ENG}
        self.waited = {e: {} for e in self.ENG}
        self.dmak = {e: 0 for e in self.ENG}
        self.pctx = []
        self.sctx = []
        self.sems = {}
        self.nbuf = 0
        self.prefix = ""
        self.bind = {}
        self.ext = {}
        self.psum_tiles = None
        self.nphase = 0
        for e in self.ENG:
            self.sem(e)
        for q in self.ENG:
            for i in range(self.NDMA):
                self.sem(("d", q, i))

    def sem(self, key):
        if key not in self.sems:
            cm = self.nc.semaphore("s_%s" % "_".join(str(k) for k in (key if isinstance(key, tuple) else (key,))))
            self.sems[key] = cm.__enter__()
            self.pctx.append(cm)
        return self.sems[key]

    def sb(self, shape, dt, name=None):
        self.nbuf += 1
        name = "%s%s_%d" % (self.prefix, name or "sb", self.nbuf)
        cm = self.nc.sbuf_tensor(name, list(shape), dt)
        t = cm.__enter__()
        self.sctx.append(cm)
        return Buf(t, name)

    def ps(self, shape, dt, name=None):
        self.nbuf += 1
        name = name or "ps%d" % self.nbuf
        cm = self.nc.psum_tensor(name, list(shape), dt)
        t = cm.__enter__()
        self.pctx.append(cm)
        return Buf(t, name)

    def dram(self, name, shape, dt, kind="Internal"):
        if name in self.bind:
            b = self.bind[name]
            assert tuple(b.t.shape) == tuple(shape), (name, b.t.shape, shape)
            return b
        full = self.prefix + name
        if kind == "ExternalInput":
            self.ext[full] = (tuple(shape), dt)
        return Buf(self.nc.dram_tensor(full, list(shape), dt, kind=kind).ap(), full)

    def scratch(self, name, shape, dt):
        return Buf(self.nc.dram_tensor(name, list(shape), dt, kind="Internal").ap(), name)

    def sub(self, buf, name):
        return Buf(buf.t, name)

    def _need(self, eng, tok, waits):
        if tok is None:
            return
        k, v = tok
        if eng == "pe" and k == "pe":
            return
        if self.waited[eng].get(k, 0) >= v:
            return
        self.waited[eng][k] = v
        waits[k] = max(waits.get(k, 0), v)

    def _deps(self, eng, r, w):
        waits = {}
        for b in r:
            self._need(eng, b.lw, waits)
        for b in w:
            self._need(eng, b.lw, waits)
            for t in b.rd:
                self._need(eng, t, waits)
        return waits

    def op(self, eng, fn, r=(), w=()):
        waits = self._deps(eng, r, w)
        self.cnt[eng] += 1
        tok = (eng, self.cnt[eng])
        self.ops[eng].append((waits, fn, (eng, 1)))
        for b in r:
            b.rd.append(tok)
        for b in w:
            b.lw = tok
            b.rd = []
        return tok

    def dma(self, q, out, in_, r=(), w=(), **kw):
        waits = self._deps(q, r, w)
        k = self.dmak[q]
        self.dmak[q] += 1
        key = ("d", q, k % self.NDMA)
        val = 16 * (k // self.NDMA + 1)
        if k >= self.NDMA:
            self._need(q, (key, val - 16), waits)
        tok = (key, val)
        self.ops[q].append((waits, lambda e: e.dma_start(out=out, in_=in_, **kw), (key, 16)))
        for b in r:
            b.rd.append(tok)
        for b in w:
            b.lw = tok
            b.rd = []
        return tok

    def fence(self, eng, bufs):
        waits = {}
        for b in bufs:
            self._need(eng, b.lw, waits)
        self.ops[eng].append((waits, None, None))

    def barrier(self):
        toks = [(e, self.cnt[e]) for e in self.ENG if self.cnt[e] > 0]
        for q in self.ENG:
            k = self.dmak[q]
            for slot in range(min(self.NDMA, k)):
                n_on_slot = (k - slot + self.NDMA - 1) // self.NDMA
                toks.append((("d", q, slot), 16 * n_on_slot))
        for e in self.ENG:
            waits = {}
            for t in toks:
                if t[0] == e:
                    continue
                self._need(e, t, waits)
            self.ops[e].append((waits, None, None))

    def flush(self):
        nc = self.nc
        self.nphase += 1
        with nc.Block() as block:
            def mk(ename):
                def body(eng):
                    for waits, fn, inc in self.ops[ename]:
                        for k, v in waits.items():
                            eng.wait_ge(self.sems[k], v)
                        if fn is not None:
                            ins = fn(eng)
                            ins.then_inc(self.sems[inc[0]], inc[1])
                return body
            block.tensor(mk("pe"))
            block.vector(mk("dve"))
            block.scalar(mk("act"))
            block.gpsimd(mk("pool"))
            block.sync(mk("sp"))
        self.ops = {e: [] for e in self.ENG}

    def end_phase(self):
        self.barrier()
        self.flush()
        while self.sctx:
            self.sctx.pop().__exit__(None, None, None)
        self.bind = {}
        self.prefix = ""

    def emit(self):
        self.end_phase()
        while self.pctx:
            self.pctx.pop().__exit__(None, None, None)


def _begin(P, prefix, bind):
    own = P is None
    if own:
        P = Prog(bass.Bass("TRN2", target_bir_lowering=False))
    P.prefix = prefix
    P.bind = dict(bind or {})
    return P, own


def _finish(P, own, outs):
    if own:
        P.fence("sp", outs)
        P.emit()
        return P.nc
    P.end_phase()
    return None


D = 1024
KC = 8
EPS = 1e-6


class Ctx:
    def __init__(self, P, nrot=8, wsize=4096, nw=6):
        self.P = P
        self.wsize = wsize
        self.nw = nw
        if P.psum_tiles is None:
            P.psum_tiles = [P.ps([128, 512], F32, name="psum%d" % i) for i in range(8)]
        self.psum = P.psum_tiles
        self.nrot = nrot
        self.accs = self.psum[nrot:]
        self.pi = 0
        self.wb = []
        self.wi = 0
        self.ci = 0

    def ps(self):
        p = self.psum[self.pi % self.nrot]
        self.pi += 1
        return p

    def wbuf(self):
        if not self.wb:
            self.wb = [self.P.sb([128, self.wsize], BF16, name="wbuf%d" % i) for i in range(self.nw)]
        b = self.wb[self.wi % len(self.wb)]
        self.wi += 1
        return b

    def wload(self, Wd, ap, kc, ncol):
        b = self.wbuf()
        v = b[:, 0:kc * ncol].rearrange("p (k c) -> p k c", k=kc)
        self.P.dma("pool", v, ap, r=[Wd], w=[b])
        return b, v

    def ev(self):
        self.ci += 1
        return "dve" if self.ci % 2 else "act"


def wview(Wd, r0, kc, c0, ncol):
    return Wd[r0:r0 + kc * 128, c0:c0 + ncol].rearrange("(k p) c -> p k c", p=128)


def modnorm(C, hT, n, A, Bv, fT, ones_bf, sqb, tmp, rstd, f32out=None):
    P = C.P
    P.op("act", lambda e: e.activation(out=sqb[:, :, 0:n], in_=hT[:, :, 0:n], func=AF.Square), r=[hT], w=[sqb])
    ps = C.ps()
    for kc in range(KC):
        P.op("pe", lambda e, kc=kc: e.matmul(ps[:, 0:n], lhsT=ones_bf[:], rhs=sqb[:, kc, 0:n],
                                             start=(kc == 0), stop=(kc == KC - 1)), r=[ones_bf, sqb], w=[ps])
    P.op("act", lambda e: e.activation(out=rstd[:, 0:n], in_=ps[:, 0:n], func=AF.Sqrt, bias=EPS, scale=1.0 / D),
         r=[ps], w=[rstd])
    P.op("dve", lambda e: e.reciprocal(out=rstd[:, 0:n], in_=rstd[:, 0:n]), r=[rstd], w=[rstd])
    for kc in range(KC):
        P.op("dve", lambda e, kc=kc: e.scalar_tensor_tensor(
            out=tmp[:, kc, 0:n], in0=hT[:, kc, 0:n], scalar=A[:, kc:kc + 1], in1=rstd[:, 0:n],
            op0=ALU.mult, op1=ALU.mult), r=[hT, A, rstd], w=[tmp])
    for kc in range(KC):
        P.op("act", lambda e, kc=kc: e.activation(out=fT[:, kc, 0:n], in_=tmp[:, kc, 0:n], func=AF.Identity,
                                                  bias=Bv[:, kc:kc + 1], scale=1.0), r=[tmp, Bv], w=[fT])
        if f32out is not None:
            P.op("dve", lambda e, kc=kc: e.tensor_scalar(out=f32out[:, kc, 0:n], in0=tmp[:, kc, 0:n],
                                                         scalar1=Bv[:, kc:kc + 1], scalar2=None, op0=ALU.add),
                 r=[tmp, Bv], w=[f32out])


def build_B(T, tiles, moe, P=None, prefix="", bind=None, ytm=False):
    P, own = _begin(P, prefix, bind)
    C = Ctx(P, nw=5)
    NMAX = max(n for _, n, _ in tiles)
    hT_d = P.dram("hT", [D, T], F32, "ExternalInput")
    if ytm:
        ytm_d = [[P.dram("y%s%d" % (nm, s_), [T, 512], F32, "ExternalInput") for s_ in range(2)] for nm in ("hy", "ssd")]
        yda_d = [P.dram("yda%d" % s_, [512, T], F32, "ExternalInput") for s_ in range(2)]
    else:
        yT_d = P.dram("yT", [3, D, T], F32, "ExternalInput")
    vec_d = P.dram("vecs", [128, 112], F32, "ExternalInput")
    wg_d = P.dram("wg", [D, 3 * D], F32, "ExternalInput")
    wbr_d = P.dram("wbr", [3, D, D], F32, "ExternalInput")
    wo_d = P.dram("wo", [D, D], F32, "ExternalInput")
    if moe:
        NE, FF = 8, 2048
        rw_d = P.dram("rw", [D, NE], F32, "ExternalInput")
        w1_d = P.dram("w1", [NE, D, FF], F32, "ExternalInput")
        w3_d = P.dram("w3", [NE, D, FF], F32, "ExternalInput")
        w2_d = P.dram("w2", [NE, FF, D], F32, "ExternalInput")
    else:
        NE, FF = 1, 4096
        w1_d = P.dram("w1", [NE, D, FF], F32, "ExternalInput")
        w3_d = P.dram("w3", [NE, D, FF], F32, "ExternalInput")
        w2_d = P.dram("w2", [NE, FF, D], F32, "ExternalInput")
    out_d = P.dram("hout", [D, T], F32, "ExternalOutput")
    FC = FF // 128

    vec = P.sb([128, 112], F32, "vec")
    der = P.sb([128, 2, 2, 8], F32, "der")
    ones_bf = P.sb([128, 128], BF16, "ones")
    hT = P.sb([128, KC, NMAX], F32, "hTs")
    f1 = P.sb([128, KC, NMAX], BF16, "f1")
    sqb = P.sb([128, KC, NMAX], BF16, "sqb")
    tmp = P.sb([128, KC, NMAX], F32, "tmp")
    rstd = P.sb([128, NMAX], F32, "rstd")
    yb = P.sb([128, 3, KC, NMAX], BF16, "yb")
    merged = P.sb([128, KC, NMAX], BF16, "merged")
    acc = P.sb([128, 4, NMAX], F32, "acc")
    sig = P.sb([128, NMAX], F32, "sig")
    tm2 = P.sb([128, NMAX], F32, "tm2")
    gT = P.sb([128, FC, NMAX], BF16, "gT")
    if moe or ytm:
        id_d = P.dram("ident", [128, 128], F32, "ExternalInput")
        ident = P.sb([128, 128], F32, "idents")
        P.dma("sp", ident[:], id_d[:], r=[id_d], w=[ident])
    if ytm:
        yst_ = [P.sb([128, 1024], F32, "ytmst%d" % q) for q in range(2)]
    if moe:
        f32T = P.sb([128, KC, NMAX], F32, "f32T")
        rw = P.sb([128, KC, NE], F32, "rws")
        lg = P.sb([128, 4, NE], F32, "lg")
        gt = P.sb([128, 4, NE], F32, "gt")
        mk = P.sb([128, 4, NE], F32, "mk")
        m12 = P.sb([128, 4, 4], F32, "m12")
        gtT = P.sb([NE, NMAX], F32, "gtT")
        sel = P.sb([NE, NE, 128], F32, "sel")
        gbc = P.sb([128, NMAX], F32, "gbc")
        oacc = P.sb([128, KC, NMAX], F32, "oacc")

    P.dma("sp", vec[:], vec_d[:], r=[vec_d], w=[vec])
    P.op("dve", lambda e: e.memset(ones_bf[:], 1.0), w=[ones_bf])
    for cls in range(2):
        for which, (gcol, sccol) in enumerate(((0, 16 + 8 + cls * 48), (8, 16 + 32 + cls * 48))):
            P.op("dve", lambda e, cls=cls, which=which, gcol=gcol, sccol=sccol: e.scalar_tensor_tensor(
                out=der[:, cls, which, :], in0=vec[:, sccol:sccol + 8], scalar=1.0, in1=vec[:, gcol:gcol + 8],
                op0=ALU.add, op1=ALU.mult), r=[vec], w=[der])
    if moe:
        P.dma("sp", rw[:], rw_d[:, :].rearrange("(k p) e -> p k e", p=128), r=[rw_d], w=[rw])
        for e_ in range(NE):
            P.op("dve", lambda e, e_=e_: e.tensor_copy(out=sel[:, e_, :], in_=ident[0:NE, e_:e_ + 1].to_broadcast([NE, 128])),
                 r=[ident], w=[sel])

    def do_tok(t0, n, cls):
        mo = 16 + cls * 48
        P.dma("sp", hT[:, :, 0:n], hT_d[:, t0:t0 + n].rearrange("(k p) t -> p k t", p=128), r=[hT_d], w=[hT])
        if ytm:
            qn_ = 0
            for i in range(2):
                for sb_ in range(n // 128):
                    st_ = yst_[qn_ % 2]
                    qn_ += 1
                    r0 = t0 + sb_ * 128
                    for s_ in range(2):
                        P.dma("sp", st_[:, s_ * 512:(s_ + 1) * 512], ytm_d[i][s_][r0:r0 + 128, :], r=[ytm_d[i][s_]], w=[st_])
                    for kc in range(KC):
                        ps = C.ps()
                        P.op("pe", lambda e, ps=ps, st_=st_, kc=kc: e.transpose(out=ps[:, 0:128], in_=st_[:, kc * 128:(kc + 1) * 128], identity=ident[:]),
                             r=[st_, ident], w=[ps])
                        if kc % 2:
                            P.op("dve", lambda e, ps=ps, i=i, kc=kc, sb_=sb_: e.tensor_copy(out=yb[:, i, kc, sb_ * 128:(sb_ + 1) * 128], in_=ps[:, 0:128]),
                                 r=[ps], w=[yb])
                        else:
                            P.op("act", lambda e, ps=ps, i=i, kc=kc, sb_=sb_: e.activation(out=yb[:, i, kc, sb_ * 128:(sb_ + 1) * 128], in_=ps[:, 0:128], func=AF.Copy),
                                 r=[ps], w=[yb])
            for s_ in range(2):
                P.dma("pool", yb[:, 2, s_ * 4:(s_ + 1) * 4, 0:n], yda_d[s_][:, t0:t0 + n].rearrange("(k p) t -> p k t", p=128),
                      r=[yda_d[s_]], w=[yb])
        else:
            for i in range(3):
                P.dma("pool", yb[:, i, :, 0:n], yT_d[i, :, t0:t0 + n].rearrange("(k p) t -> p k t", p=128), r=[yT_d], w=[yb])
        modnorm_v(C, hT, n, der, (cls, 0), vec, mo + 0, f1, ones_bf, sqb, tmp, rstd)
        for dg in range(2):
            for i in range(3):
                wbb, wbv = C.wload(wbr_d, wview(wbr_d[i], 0, KC, dg * 512, 512), KC, 512)
                wgb, wgv = C.wload(wg_d, wview(wg_d, 0, KC, i * D + dg * 512, 512), KC, 512)
                for j in range(4):
                    psA = C.ps(); psB = C.ps()
                    for kc in range(KC):
                        P.op("pe", lambda e, kc=kc, j=j, wbv=wbv, i=i, psA=psA: e.matmul(
                            psA[:, 0:n], lhsT=wbv[:, kc, j * 128:(j + 1) * 128], rhs=yb[:, i, kc, 0:n],
                            start=(kc == 0), stop=(kc == KC - 1)), r=[wbb, yb], w=[psA])
                    for kc in range(KC):
                        P.op("pe", lambda e, kc=kc, j=j, wgv=wgv, psB=psB: e.matmul(
                            psB[:, 0:n], lhsT=wgv[:, kc, j * 128:(j + 1) * 128], rhs=f1[:, kc, 0:n],
                            start=(kc == 0), stop=(kc == KC - 1)), r=[wgb, f1], w=[psB])
                    P.op("act", lambda e, psB=psB: e.activation(out=sig[:, 0:n], in_=psB[:, 0:n], func=AF.Sigmoid),
                         r=[psB], w=[sig])
                    if i == 0:
                        P.op("dve", lambda e, psA=psA, j=j: e.tensor_tensor(out=acc[:, j, 0:n], in0=psA[:, 0:n], in1=sig[:, 0:n],
                                                                    op=ALU.mult), r=[psA, sig], w=[acc])
                    else:
                        P.op("dve", lambda e, psA=psA: e.tensor_tensor(out=tm2[:, 0:n], in0=psA[:, 0:n], in1=sig[:, 0:n],
                                                               op=ALU.mult), r=[psA, sig], w=[tm2])
                        if i == 1:
                            P.op("dve", lambda e, j=j: e.tensor_tensor(out=acc[:, j, 0:n], in0=acc[:, j, 0:n], in1=tm2[:, 0:n],
                                                                       op=ALU.add), r=[acc, tm2], w=[acc])
                        else:
                            P.op("dve", lambda e, j=j, dg=dg: e.tensor_tensor(out=merged[:, dg * 4 + j, 0:n], in0=acc[:, j, 0:n],
                                                                       in1=tm2[:, 0:n], op=ALU.add), r=[acc, tm2], w=[merged])
        for dg in range(2):
            wob, wov = C.wload(wo_d, wview(wo_d, 0, KC, dg * 512, 512), KC, 512)
            for j in range(4):
                db = dg * 4 + j
                ps = C.ps()
                for kc in range(KC):
                    P.op("pe", lambda e, kc=kc, j=j, wov=wov, ps=ps: e.matmul(
                        ps[:, 0:n], lhsT=wov[:, kc, j * 128:(j + 1) * 128], rhs=merged[:, kc, 0:n],
                        start=(kc == 0), stop=(kc == KC - 1)), r=[wob, merged], w=[ps])
                P.op("dve", lambda e, ps=ps, db=db, mo=mo: e.scalar_tensor_tensor(
                    out=hT[:, db, 0:n], in0=ps[:, 0:n], scalar=vec[:, mo + 16 + db:mo + 16 + db + 1], in1=hT[:, db, 0:n],
                    op0=ALU.mult, op1=ALU.add), r=[ps, vec, hT], w=[hT])
        modnorm_v(C, hT, n, der, (cls, 1), vec, mo + 24, f1, ones_bf, sqb, tmp, rstd, f32out=(f32T if moe else None))
        if moe:
            nsub = n // 128
            for sb_ in range(nsub):
                ps = C.ps()
                for kc in range(KC):
                    P.op("pe", lambda e, kc=kc, sb_=sb_, ps=ps: e.matmul(
                        ps[:, 0:NE], lhsT=f32T[:, kc, sb_ * 128:(sb_ + 1) * 128], rhs=rw[:, kc, :],
                        start=(kc == 0), stop=(kc == KC - 1)), r=[f32T, rw], w=[ps])
                P.op("dve", lambda e, ps=ps, sb_=sb_: e.tensor_copy(out=lg[:, sb_, :], in_=ps[:, 0:NE]), r=[ps], w=[lg])
            for sb_ in range(nsub):
                P.op("dve", lambda e, sb_=sb_: e.reduce_max(out=m12[:, sb_, 0:1], in_=lg[:, sb_, :], axis=AX.X), r=[lg], w=[m12])
                P.op("dve", lambda e, sb_=sb_: e.tensor_scalar(out=mk[:, sb_, :], in0=lg[:, sb_, :], scalar1=m12[:, sb_, 0:1],
                                                               scalar2=None, op0=ALU.is_equal), r=[lg, m12], w=[mk])
                P.op("dve", lambda e, sb_=sb_: e.scalar_tensor_tensor(out=gt[:, sb_, :], in0=mk[:, sb_, :], scalar=-1e30,
                                                                      in1=lg[:, sb_, :], op0=ALU.mult, op1=ALU.add),
                     r=[mk, lg], w=[gt])
                P.op("dve", lambda e, sb_=sb_: e.reduce_max(out=m12[:, sb_, 1:2], in_=gt[:, sb_, :], axis=AX.X), r=[gt], w=[m12])
                P.op("dve", lambda e, sb_=sb_: e.tensor_tensor(out=m12[:, sb_, 2:3], in0=m12[:, sb_, 0:1], in1=m12[:, sb_, 1:2],
                                                               op=ALU.subtract), r=[m12], w=[m12])
                P.op("act", lambda e, sb_=sb_: e.activation(out=m12[:, sb_, 3:4], in_=m12[:, sb_, 2:3], func=AF.Sigmoid, scale=-1.0),
                     r=[m12], w=[m12])
                P.op("act", lambda e, sb_=sb_: e.activation(out=m12[:, sb_, 2:3], in_=m12[:, sb_, 2:3], func=AF.Sigmoid),
                     r=[m12], w=[m12])
                P.op("dve", lambda e, sb_=sb_: e.tensor_scalar(out=gt[:, sb_, :], in0=gt[:, sb_, :], scalar1=m12[:, sb_, 1:2],
                                                               scalar2=m12[:, sb_, 3:4], op0=ALU.is_equal, op1=ALU.mult),
                     r=[gt, m12], w=[gt])
                P.op("dve", lambda e, sb_=sb_: e.scalar_tensor_tensor(out=gt[:, sb_, :], in0=mk[:, sb_, :], scalar=m12[:, sb_, 2:3],
                                                                      in1=gt[:, sb_, :], op0=ALU.mult, op1=ALU.add),
                     r=[mk, m12, gt], w=[gt])
                ps = C.ps()
                P.op("pe", lambda e, sb_=sb_, ps=ps: e.transpose(out=ps[0:NE, 0:128], in_=gt[:, sb_, :], identity=ident[:]),
                     r=[gt, ident], w=[ps])
                P.op("dve", lambda e, sb_=sb_, ps=ps: e.tensor_copy(out=gtT[:, sb_ * 128:(sb_ + 1) * 128], in_=ps[0:NE, 0:128]),
                     r=[ps], w=[gtT])
        for ex in range(NE):
            if moe:
                ps = C.ps()
                P.op("pe", lambda e, ex=ex, ps=ps: e.matmul(ps[:, 0:n], lhsT=sel[:, ex, :], rhs=gtT[:, 0:n], start=True, stop=True),
                     r=[sel, gtT], w=[ps])
                P.op("act", lambda e, ps=ps: e.activation(out=gbc[:, 0:n], in_=ps[:, 0:n], func=AF.Copy), r=[ps], w=[gbc])
            for fg in range(FF // 512):
                w1b, w1v = C.wload(w1_d, wview(w1_d[ex], 0, KC, fg * 512, 512), KC, 512)
                w3b, w3v = C.wload(w3_d, wview(w3_d[ex], 0, KC, fg * 512, 512), KC, 512)
                for j in range(4):
                    psa = C.ps(); psb = C.ps()
                    for kc in range(KC):
                        P.op("pe", lambda e, kc=kc, j=j, w1v=w1v, psa=psa: e.matmul(
                            psa[:, 0:n], lhsT=w1v[:, kc, j * 128:(j + 1) * 128], rhs=f1[:, kc, 0:n],
                            start=(kc == 0), stop=(kc == KC - 1)), r=[w1b, f1], w=[psa])
                    for kc in range(KC):
                        P.op("pe", lambda e, kc=kc, j=j, w3v=w3v, psb=psb: e.matmul(
                            psb[:, 0:n], lhsT=w3v[:, kc, j * 128:(j + 1) * 128], rhs=f1[:, kc, 0:n],
                            start=(kc == 0), stop=(kc == KC - 1)), r=[w3b, f1], w=[psb])
                    P.op("act", lambda e, psa=psa: e.activation(out=sig[:, 0:n], in_=psa[:, 0:n], func=AF.Silu), r=[psa], w=[sig])
                    if moe:
                        P.op("dve", lambda e: e.tensor_tensor(out=sig[:, 0:n], in0=sig[:, 0:n], in1=gbc[:, 0:n], op=ALU.mult),
                             r=[sig, gbc], w=[sig])
                    P.op("dve", lambda e, psb=psb, fg=fg, j=j: e.tensor_tensor(out=gT[:, fg * 4 + j, 0:n], in0=psb[:, 0:n],
                                                                        in1=sig[:, 0:n], op=ALU.mult), r=[psb, sig], w=[gT])
            for db in range(KC):
                ps = C.ps()
                w2b, w2v = C.wload(w2_d, wview(w2_d[ex], 0, FC, db * 128, 128), FC, 128)
                for fc in range(FC):
                    P.op("pe", lambda e, fc=fc, w2v=w2v, ps=ps: e.matmul(
                        ps[:, 0:n], lhsT=w2v[:, fc, :], rhs=gT[:, fc, 0:n],
                        start=(fc == 0), stop=(fc == FC - 1)), r=[w2b, gT], w=[ps])
                if not moe:
                    P.op("dve", lambda e, ps=ps, db=db, mo=mo: e.scalar_tensor_tensor(
                        out=hT[:, db, 0:n], in0=ps[:, 0:n], scalar=vec[:, mo + 40 + db:mo + 40 + db + 1], in1=hT[:, db, 0:n],
                        op0=ALU.mult, op1=ALU.add), r=[ps, vec, hT], w=[hT])
                elif ex == 0:
                    P.op("dve", lambda e, ps=ps, db=db: e.tensor_copy(out=oacc[:, db, 0:n], in_=ps[:, 0:n]), r=[ps], w=[oacc])
                else:
                    P.op("dve", lambda e, ps=ps, db=db: e.tensor_tensor(out=oacc[:, db, 0:n], in0=oacc[:, db, 0:n], in1=ps[:, 0:n],
                                                                 op=ALU.add), r=[ps, oacc], w=[oacc])
        if moe:
            for db in range(KC):
                P.op("dve", lambda e, db=db, mo=mo: e.scalar_tensor_tensor(
                    out=hT[:, db, 0:n], in0=oacc[:, db, 0:n], scalar=vec[:, mo + 40 + db:mo + 40 + db + 1], in1=hT[:, db, 0:n],
                    op0=ALU.mult, op1=ALU.add), r=[oacc, vec, hT], w=[hT])
        P.dma("sp", out_d[:, t0:t0 + n].rearrange("(k p) t -> p k t", p=128), hT[:, :, 0:n], r=[hT], w=[out_d])
    for tl in tiles:
        do_tok(*tl)
    return _finish(P, own, [out_d])


def modnorm_v(C, hT, n, der, idx, vec, bcol, fT, ones_bf, sqb, tmp, rstd, f32out=None):
    cls, which = idx

    class _V:
        pass
    A = Buf(der.t[:, cls, which, :], "A")
    A.lw, A.rd = der.lw, der.rd
    Bv = Buf(vec.t[:, bcol:bcol + 8], "Bv")
    Bv.lw, Bv.rd = vec.lw, vec.rd
    modnorm(C, hT, n, A, Bv, fT, ones_bf, sqb, tmp, rstd, f32out=f32out)


NTOK = 4352
NCTX = 256


def build_attn(layer, need_ctx, nheads=4, nqb=8, P=None, prefix="", bind=None):
    import math
    lam_init = 0.8 - 0.6 * math.exp(-0.3 * (layer + 1))
    P, own = _begin(P, prefix, bind)
    C = Ctx(P, nrot=4, wsize=1024, nw=6)
    hT_d = P.dram("hT", [D, NTOK], F32, "ExternalInput")
    vec_d = P.dram("vecs", [128, 112], F32, "ExternalInput")
    wq_d = P.dram("wq", [D, 512], F32, "ExternalInput")
    wk_d = P.dram("wk", [D, 512], F32, "ExternalInput")
    wv_d = P.dram("wv", [D, 512], F32, "ExternalInput")
    qkg_d = P.dram("qkg", [128, 2], F32, "ExternalInput")
    lam_d = P.dram("lamv", [1, 256], F32, "ExternalInput")
    sg_d = P.dram("sublng", [128, 128], F32, "ExternalInput")
    cos_d = P.dram("cosT", [128, 4096], F32, "ExternalInput")
    sin_d = P.dram("sinT", [128, 4096], F32, "ExternalInput")
    rm_d = P.dram("rmT", [128, 128], F32, "ExternalInput")
    bo_d = P.dram("blockones", [128, 128], F32, "ExternalInput")
    id_d = P.dram("ident", [128, 128], F32, "ExternalInput")
    out_d = P.dram("yT", [512, NTOK], F32, "ExternalOutput")

    vec = P.sb([128, 112], F32, "vec")
    der = P.sb([128, 2, 2, 8], F32, "der")
    ones_bf = P.sb([128, 128], BF16, "ones")
    ones_f = P.sb([1, 128], F32, "ones_f")
    f1 = P.sb([128, KC, NTOK], BF16, "f1")
    NT = 256
    hTt = P.sb([128, KC, NT], F32, "hTt")
    sqb = P.sb([128, KC, NT], BF16, "sqb")
    tmp = P.sb([128, KC, NT], F32, "tmp")
    rstd = P.sb([128, 512], F32, "rstd")
    qkg = P.sb([128, 2], F32, "qkgs")
    lamv = P.sb([1, 256], F32, "lamvs")
    lams = P.sb([1, 8], F32, "lams")
    neglam = P.sb([128, 1], F32, "neglam")
    sg = P.sb([128, 128], F32, "sg")
    cosT = P.sb([128, 4096], F32, "cosTs")
    sinT = P.sb([128, 4096], F32, "sinTs")
    rmT = P.sb([128, 128], F32, "rmTs")
    bo = P.sb([128, 128], BF16, "bos")
    ident = P.sb([128, 128], F32, "idents")
    qT = P.sb([128, NTOK], BF16, "qT")
    kT = P.sb([128, NTOK], BF16, "kT")
    Vaug = P.sb([128, 34, 130], BF16, "Vaug")
    sqq = P.sb([128, 512], BF16, "sqq")
    qn = P.sb([128, 512], F32, "qn")
    t1 = P.sb([128, 512], F32, "t1")
    t2 = P.sb([128, 512], F32, "t2")
    PT = [P.sb([128, 512], BF16, "PT%d" % i) for i in range(3)]
    o0 = P.sb([128, 4, 128], F32, "o0")
    oo = P.sb([128, 4, 128], F32, "oo")
    junk = P.sb([128, 128], F32, "junk")
    rc = P.sb([128, 8], F32, "rc")
    yq = P.sb([128, 128], F32, "yq")
    yst = P.sb([128, 512], F32, "yst")

    P.dma("sp", vec[:], vec_d[:], r=[vec_d], w=[vec])
    P.dma("sp", qkg[:], qkg_d[:], r=[qkg_d], w=[qkg])
    P.dma("sp", lamv[:], lam_d[:], r=[lam_d], w=[lamv])
    P.dma("sp", sg[:], sg_d[:], r=[sg_d], w=[sg])
    P.dma("sp", cosT[:], cos_d[:], r=[cos_d], w=[cosT])
    P.dma("sp", sinT[:], sin_d[:], r=[sin_d], w=[sinT])
    P.dma("sp", rmT[:], rm_d[:], r=[rm_d], w=[rmT])
    P.dma("pool", bo[:], bo_d[:], r=[bo_d], w=[bo])
    P.dma("sp", ident[:], id_d[:], r=[id_d], w=[ident])
    P.op("dve", lambda e: e.memset(ones_bf[:], 1.0), w=[ones_bf])
    P.op("dve", lambda e: e.memset(ones_f[:], 1.0), w=[ones_f])
    P.op("dve", lambda e: e.memset(Vaug[:, :, 128:130], 1.0), w=[Vaug])
    for cls in range(2):
        sccol = 16 + 8 + cls * 48
        P.op("dve", lambda e, cls=cls, sccol=sccol: e.scalar_tensor_tensor(
            out=der[:, cls, 0, :], in0=vec[:, sccol:sccol + 8], scalar=1.0, in1=vec[:, 0:8],
            op0=ALU.add, op1=ALU.mult), r=[vec], w=[der])
    P.op("dve", lambda e: e.tensor_scalar(out=sg[:], in0=sg[:], scalar1=float(1.0 - lam_init), scalar2=None, op0=ALU.mult),
         r=[sg], w=[sg])
    P.op("dve", lambda e: e.tensor_tensor(out=lamv[:, 0:64], in0=lamv[:, 0:64], in1=lamv[:, 64:128], op=ALU.mult), r=[lamv], w=[lamv])
    P.op("dve", lambda e: e.tensor_tensor(out=lamv[:, 128:192], in0=lamv[:, 128:192], in1=lamv[:, 192:256], op=ALU.mult), r=[lamv], w=[lamv])
    P.op("dve", lambda e: e.reduce_sum(out=lams[:, 0:1], in_=lamv[:, 0:64], axis=AX.X), r=[lamv], w=[lams])
    P.op("dve", lambda e: e.reduce_sum(out=lams[:, 1:2], in_=lamv[:, 128:192], axis=AX.X), r=[lamv], w=[lams])
    P.op("act", lambda e: e.activation(out=lams[:, 2:4], in_=lams[:, 0:2], func=AF.Exp), r=[lams], w=[lams])
    P.op("dve", lambda e: e.tensor_tensor(out=lams[:, 4:5], in0=lams[:, 3:4], in1=lams[:, 2:3], op=ALU.subtract), r=[lams], w=[lams])
    P.op("dve", lambda e: e.tensor_scalar(out=lams[:, 4:5], in0=lams[:, 4:5], scalar1=float(-lam_init), scalar2=None, op0=ALU.add),
         r=[lams], w=[lams])
    ps = C.ps()
    P.op("pe", lambda e, ps=ps: e.matmul(ps[:, 0:1], lhsT=ones_f[:, :], rhs=lams[:, 4:5], start=True, stop=True), r=[ones_f, lams], w=[ps])
    P.op("dve", lambda e, ps=ps: e.tensor_copy(out=neglam[:], in_=ps[:, 0:1]), r=[ps], w=[neglam])

    for t0 in range(0, NTOK, NT):
        cls = 1 if t0 < NCTX else 0
        mo = 16 + cls * 48
        P.dma("sp", hTt[:, :, :], hT_d[:, t0:t0 + NT].rearrange("(k p) t -> p k t", p=128), r=[hT_d], w=[hTt])
        f1v = Buf(f1.t[:, :, t0:t0 + NT], "f1v")
        f1v.lw, f1v.rd = f1.lw, f1.rd
        modnorm_v(C, hTt, NT, der, (cls, 0), vec, mo + 0, f1v, ones_bf, sqb, tmp, rstd)
        f1.lw = f1v.lw
    blocks = [(0, 256)] + [(256 + i * 512, 512) for i in range(8)]
    for h in range(nheads):
        wqb, wqv = C.wload(wq_d, wview(wq_d, 0, KC, h * 128, 128), KC, 128)
        wkb, wkv = C.wload(wk_d, wview(wk_d, 0, KC, h * 128, 128), KC, 128)
        wvb, wvv = C.wload(wv_d, wview(wv_d, 0, KC, h * 128, 128), KC, 128)
        for (dst, wb_, wv_, gcol, isq) in ((qT, wqb, wqv, 0, True), (kT, wkb, wkv, 1, False)):
            for (t0, n) in blocks:
                if isq and (not need_ctx) and t0 < NCTX:
                    continue
                ps = C.ps()
                for kc in range(KC):
                    P.op("pe", lambda e, kc=kc, ps=ps, wv_=wv_, t0=t0, n=n: e.matmul(
                        ps[:, 0:n], lhsT=wv_[:, kc, :], rhs=f1[:, kc, t0:t0 + n], start=(kc == 0), stop=(kc == KC - 1)),
                        r=[wb_, f1], w=[ps])
                P.op("act", lambda e, ps=ps, n=n: e.activation(out=sqq[:, 0:n], in_=ps[:, 0:n], func=AF.Square), r=[ps], w=[sqq])
                ps2 = C.ps()
                P.op("pe", lambda e, ps2=ps2, n=n: e.matmul(ps2[:, 0:n], lhsT=bo[:], rhs=sqq[:, 0:n], start=True, stop=True),
                     r=[bo, sqq], w=[ps2])
                P.op("act", lambda e, ps2=ps2, n=n: e.activation(out=rstd[:, 0:n], in_=ps2[:, 0:n], func=AF.Sqrt, bias=EPS, scale=1.0 / 64),
                     r=[ps2], w=[rstd])
                P.op("dve", lambda e, n=n: e.reciprocal(out=rstd[:, 0:n], in_=rstd[:, 0:n]), r=[rstd], w=[rstd])
                if t0 < NCTX:
                    P.op("dve", lambda e, ps=ps, n=n, gcol=gcol, dst=dst, t0=t0: e.scalar_tensor_tensor(
                        out=dst[:, t0:t0 + n], in0=ps[:, 0:n], scalar=qkg[:, gcol:gcol + 1], in1=rstd[:, 0:n],
                        op0=ALU.mult, op1=ALU.mult), r=[ps, qkg, rstd], w=[dst])
                    continue
                P.op("dve", lambda e, ps=ps, n=n, gcol=gcol: e.scalar_tensor_tensor(
                    out=qn[:, 0:n], in0=ps[:, 0:n], scalar=qkg[:, gcol:gcol + 1], in1=rstd[:, 0:n],
                    op0=ALU.mult, op1=ALU.mult), r=[ps, qkg, rstd], w=[qn])
                ps3 = C.ps()
                P.op("pe", lambda e, ps3=ps3, n=n: e.matmul(ps3[:, 0:n], lhsT=rmT[:], rhs=qn[:, 0:n], start=True, stop=True),
                     r=[rmT, qn], w=[ps3])
                lp = t0 - NCTX
                P.op("pool", lambda e, n=n, lp=lp: e.tensor_tensor(out=t1[:, 0:n], in0=qn[:, 0:n], in1=cosT[:, lp:lp + n], op=ALU.mult),
                     r=[qn, cosT], w=[t1])
                P.op("dve", lambda e, ps3=ps3, n=n, lp=lp: e.tensor_tensor(out=t2[:, 0:n], in0=ps3[:, 0:n], in1=sinT[:, lp:lp + n], op=ALU.mult),
                     r=[ps3, sinT], w=[t2])
                P.op("dve", lambda e, n=n, dst=dst, t0=t0: e.tensor_tensor(out=dst[:, t0:t0 + n], in0=t1[:, 0:n], in1=t2[:, 0:n], op=ALU.add),
                     r=[t1, t2], w=[dst])
        for kt in range(34):
            ps = C.ps()
            for kc in range(KC):
                P.op("pe", lambda e, kc=kc, ps=ps, kt=kt, wvv=wvv: e.matmul(
                    ps[:, 0:128], lhsT=f1[:, kc, kt * 128:(kt + 1) * 128], rhs=wvv[:, kc, :], start=(kc == 0), stop=(kc == KC - 1)),
                    r=[f1, wvb], w=[ps])
            eng = C.ev()
            if eng == "dve":
                P.op("dve", lambda e, ps=ps, kt=kt: e.tensor_copy(out=Vaug[:, kt, 0:128], in_=ps[:, 0:128]), r=[ps], w=[Vaug])
            else:
                P.op("act", lambda e, ps=ps, kt=kt: e.activation(out=Vaug[:, kt, 0:128], in_=ps[:, 0:128], func=AF.Copy), r=[ps], w=[Vaug])
        qblocks = [(256 + i * 512, 512, list(range(34))) for i in range(nqb)]
        if need_ctx:
            qblocks = [(0, 256, [0, 1])] + qblocks
        pti = 0
        for (q0, nq, kts) in qblocks:
            nqs = nq // 128
            for comp in range(2):
                r0 = comp * 64
                def score(kt, r0=r0, q0=q0, nq=nq):
                    ps = C.ps()
                    P.op("pe", lambda e, ps=ps, kt=kt, r0=r0, q0=q0, nq=nq: e.matmul(
                        ps[:, 0:nq], lhsT=kT[r0:r0 + 64, kt * 128:(kt + 1) * 128], rhs=qT[r0:r0 + 64, q0:q0 + nq],
                        start=True, stop=True), r=[kT, qT], w=[ps])
                    return ps
                pend = [score(kts[0])]
                if len(kts) > 1:
                    pend.append(score(kts[1]))
                for ki, kt in enumerate(kts):
                    ps = pend.pop(0)
                    if ki + 2 < len(kts):
                        pend.append(score(kts[ki + 2]))
                    pt = PT[pti % 3]
                    pti += 1
                    P.op("act", lambda e, ps=ps, pt=pt, nq=nq: e.activation(out=pt[:, 0:nq], in_=ps[:, 0:nq], func=AF.Exp, scale=0.125),
                         r=[ps], w=[pt])
                    for qs in range(nqs):
                        acc = C.accs[qs]
                        P.op("pe", lambda e, acc=acc, pt=pt, qs=qs, kt=kt, ki=ki, last=(ki == len(kts) - 1): e.matmul(
                            acc[:, 0:129], lhsT=pt[:, qs * 128:(qs + 1) * 128], rhs=Vaug[:, kt, 0:129],
                            start=(ki == 0), stop=last), r=[pt, Vaug], w=[acc])
                for qs in range(nqs):
                    acc = C.accs[qs]
                    P.op("dve", lambda e, acc=acc, qs=qs, comp=comp: e.reciprocal(out=rc[:, comp * 4 + qs:comp * 4 + qs + 1], in_=acc[:, 128:129]),
                         r=[acc], w=[rc])
                    if comp == 0:
                        P.op("dve", lambda e, acc=acc, qs=qs: e.tensor_scalar(out=o0[:, qs, :], in0=acc[:, 0:128], scalar1=rc[:, qs:qs + 1],
                                                                       scalar2=None, op0=ALU.mult), r=[acc, rc], w=[o0])
                    else:
                        P.op("dve", lambda e, qs=qs: e.tensor_tensor(out=rc[:, 4 + qs:5 + qs], in0=rc[:, 4 + qs:5 + qs], in1=neglam[:, 0:1],
                                                                     op=ALU.mult), r=[rc, neglam], w=[rc])
                        P.op("dve", lambda e, acc=acc, qs=qs: e.scalar_tensor_tensor(
                            out=oo[:, qs, :], in0=acc[:, 0:128], scalar=rc[:, 4 + qs:5 + qs], in1=o0[:, qs, :],
                            op0=ALU.mult, op1=ALU.add), r=[acc, rc, o0], w=[oo])
            for qs in range(nqs):
                P.op("dve", lambda e, qs=qs: e.memset(rc[:, qs:qs + 1], 0.0), w=[rc])
                P.op("act", lambda e, qs=qs: e.activation(out=junk[:], in_=oo[:, qs, :], func=AF.Square, accum_out=rc[:, qs:qs + 1]),
                     r=[oo, rc], w=[junk, rc])
                P.op("act", lambda e, qs=qs: e.activation(out=rc[:, qs:qs + 1], in_=rc[:, qs:qs + 1], func=AF.Sqrt, bias=EPS, scale=1.0 / 128),
                     r=[rc], w=[rc])
                P.op("dve", lambda e, qs=qs: e.reciprocal(out=rc[:, qs:qs + 1], in_=rc[:, qs:qs + 1]), r=[rc], w=[rc])
                P.op("dve", lambda e, qs=qs: e.scalar_tensor_tensor(out=yq[:], in0=oo[:, qs, :], scalar=rc[:, qs:qs + 1], in1=sg[:],
                                                                    op0=ALU.mult, op1=ALU.mult), r=[oo, rc, sg], w=[yq])
                ps = C.ps()
                P.op("pe", lambda e, ps=ps: e.transpose(out=ps[:, 0:128], in_=yq[:], identity=ident[:]), r=[yq, ident], w=[ps])
                P.op("act", lambda e, ps=ps, qs=qs: e.activation(out=yst[:, qs * 128:(qs + 1) * 128], in_=ps[:, 0:128], func=AF.Copy),
                     r=[ps], w=[yst])
            P.dma("sp", out_d[h * 128:(h + 1) * 128, q0:q0 + nq], yst[:, 0:nq], r=[yst], w=[out_d])
    return _finish(P, own, [out_d])


def attn_consts():
    inv = (10000.0 ** (-np.arange(16, dtype=np.float32) / 16)).astype(np.float32)
    t = np.arange(4096)
    ang = np.stack([t // 64, t % 64], -1).astype(np.float32)[:, :, None] * inv
    cos, sin = np.cos(ang).astype(np.float32), np.sin(ang).astype(np.float32)
    cosT = np.zeros((128, 4096), np.float32)
    sinT = np.zeros((128, 4096), np.float32)
    rmT = np.zeros((128, 128), np.float32)
    for p in range(128):
        d = p % 64
        a, j, fr = d // 32, (d % 32) // 16, d % 16
        cosT[p] = cos[:, a, fr]
        sinT[p] = sin[:, a, fr]
        if j == 0:
            rmT[p + 16, p] = -1.0
        else:
            rmT[p - 16, p] = 1.0
    bo = np.zeros((128, 128), np.float32)
    bo[:64, :64] = 1.0
    bo[64:, 64:] = 1.0
    return dict(cosT=cosT, sinT=sinT, rmT=rmT, blockones=bo, ident=np.eye(128, dtype=np.float32))


def fpos(t):
    return 1 + t if t < NCTX else 3 + t


def build_pj(groups, P=None, prefix="", bind=None):
    P, own = _begin(P, prefix, bind)
    C = Ctx(P, nrot=8, wsize=4096, nw=4)
    hT_d = P.dram("hT", [D, NTOK], F32, "ExternalInput")
    vec_d = P.dram("vecs", [128, 112], F32, "ExternalInput")
    vec = P.sb([128, 112], F32, "vec")
    der = P.sb([128, 2, 2, 8], F32, "der")
    ones_bf = P.sb([128, 128], BF16, "ones")
    FW = NTOK + 4
    f1 = P.sb([128, KC, FW], BF16, "f1")
    NT = 256
    hTt = P.sb([128, KC, NT], F32, "hTt")
    sqb = P.sb([128, KC, NT], BF16, "sqb")
    tmp = P.sb([128, KC, NT], F32, "tmp")
    rstd = P.sb([128, 512], F32, "rstd")
    stage = [P.sb([128, 512], F32, "stage%d" % i) for i in range(3)]
    wst = P.sb([128, KC, 512], F32, "wst")
    cwb = P.sb([128, 512], F32, "cwb")
    P.dma("sp", vec[:], vec_d[:], r=[vec_d], w=[vec])
    P.op("dve", lambda e: e.memset(ones_bf[:], 1.0), w=[ones_bf])
    P.op("pool", lambda e: e.memset(f1[:], 0.0), w=[f1])
    for cls in range(2):
        sccol = 16 + 8 + cls * 48
        P.op("dve", lambda e, cls=cls, sccol=sccol: e.scalar_tensor_tensor(
            out=der[:, cls, 0, :], in0=vec[:, sccol:sccol + 8], scalar=1.0, in1=vec[:, 0:8],
            op0=ALU.add, op1=ALU.mult), r=[vec], w=[der])
    for t0 in range(0, NTOK, NT):
        cls = 1 if t0 < NCTX else 0
        mo = 16 + cls * 48
        P.dma("sp", hTt[:, :, :], hT_d[:, t0:t0 + NT].rearrange("(k p) t -> p k t", p=128), r=[hT_d], w=[hTt])
        c0 = fpos(t0)
        f1v = Buf(f1.t[:, :, c0:c0 + NT], "f1v")
        f1v.lw, f1v.rd = f1.lw, f1.rd
        modnorm_v(C, hTt, NT, der, (cls, 0), vec, mo + 0, f1v, ones_bf, sqb, tmp, rstd)
        f1.lw = f1v.lw
    si = [0]
    for g in groups:
        name, ncol, conv, act, layout = g["name"], g["ncol"], g["conv"], g["act"], g["layout"]
        w_d = P.dram("w_" + name, [D, ncol], F32, "ExternalInput")
        ntap = 3 if conv else 1
        if conv:
            cw_d = P.dram("cw_" + name, [3, 128, ncol], F32, "ExternalInput")
        if layout == "tm":
            b_d = P.dram("b_" + name, [1, ncol], F32, "ExternalInput")
            out_d = P.dram("o_" + name, [NTOK, ncol], F32, "ExternalOutput")
            brow = P.sb([1, ncol], BF16, "brow_" + name)
            P.dma("pool", brow[:], b_d[:], r=[b_d], w=[brow])
        else:
            b_d = P.dram("b_" + name, [128, ncol // 128], F32, "ExternalInput")
            out_d = P.dram("o_" + name, [ncol, NTOK], F32, "ExternalOutput")
            bcol = P.sb([128, ncol // 128], F32, "bcol_" + name)
            P.dma("sp", bcol[:], b_d[:], r=[b_d], w=[bcol])
        wts = []
        P.dma("sp", wst[:, :, 0:ncol], wview(w_d, 0, KC, 0, ncol), r=[w_d], w=[wst])
        for j in range(ntap):
            wt = P.sb([128, KC, ncol], BF16, "wt_%s_%d" % (name, j))
            if conv:
                P.dma("sp", cwb[:, 0:ncol], cw_d[j], r=[cw_d], w=[cwb])
                P.op("dve", lambda e, wt=wt, ncol=ncol: e.tensor_tensor(
                    out=wt[:], in0=wst[:, :, 0:ncol], in1=cwb[:, 0:ncol].unsqueeze(1).to_broadcast([128, KC, ncol]), op=ALU.mult),
                    r=[wst, cwb], w=[wt])
            else:
                P.op("dve", lambda e, wt=wt, ncol=ncol: e.tensor_copy(out=wt[:], in_=wst[:, :, 0:ncol]), r=[wst], w=[wt])
            wts.append(wt)
        shifts = (-1, 0, 1) if conv else (0,)
        func = {None: AF.Copy, "silu": AF.Silu}[act]
        if layout == "tm":
            for t0 in range(0, NTOK, 128):
                ps = C.ps()
                c0 = fpos(t0)
                nmm = ntap * KC
                i = 0
                for j, sh in enumerate(shifts):
                    for kc in range(KC):
                        P.op("pe", lambda e, ps=ps, j=j, sh=sh, kc=kc, c0=c0, ncol=ncol, wts=wts, i=i: e.matmul(
                            ps[:, 0:ncol], lhsT=f1[:, kc, c0 + sh:c0 + sh + 128], rhs=wts[j][:, kc, :], start=(i == 0), stop=False),
                            r=[f1, wts[j]], w=[ps])
                        i += 1
                P.op("pe", lambda e, ps=ps, ncol=ncol, brow=brow: e.matmul(
                    ps[:, 0:ncol], lhsT=ones_bf[0:1, :], rhs=brow[0:1, :], start=False, stop=True), r=[ones_bf, brow], w=[ps])
                st = stage[si[0] % 3]
                si[0] += 1
                P.op("act", lambda e, ps=ps, st=st, ncol=ncol, func=func: e.activation(out=st[:, 0:ncol], in_=ps[:, 0:ncol], func=func),
                     r=[ps], w=[st])
                P.dma("sp", out_d[t0:t0 + 128, :], st[:, 0:ncol], r=[st], w=[out_d])
        else:
            blocks = [(0, 256)] + [(256 + i * 512, 512) for i in range(8)]
            for cb in range(ncol // 128):
                for (t0, n) in blocks:
                    ps = C.ps()
                    c0 = fpos(t0)
                    i = 0
                    for j, sh in enumerate(shifts):
                        for kc in range(KC):
                            P.op("pe", lambda e, ps=ps, j=j, sh=sh, kc=kc, c0=c0, n=n, cb=cb, wts=wts, i=i, last=(i == ntap * KC - 1): e.matmul(
                                ps[:, 0:n], lhsT=wts[j][:, kc, cb * 128:(cb + 1) * 128], rhs=f1[:, kc, c0 + sh:c0 + sh + n],
                                start=(i == 0), stop=last), r=[f1, wts[j]], w=[ps])
                            i += 1
                    st = stage[si[0] % 3]
                    si[0] += 1
                    P.op("act", lambda e, ps=ps, st=st, n=n, cb=cb, func=func, bcol=bcol: e.activation(
                        out=st[:, 0:n], in_=ps[:, 0:n], func=func, bias=bcol[:, cb:cb + 1], scale=1.0), r=[ps, bcol], w=[st])
                    P.dma("sp", out_d[cb * 128:(cb + 1) * 128, t0:t0 + n], st[:, 0:n], r=[st], w=[out_d])
        g["_out"] = out_d
    return _finish(P, own, [g["_out"] for g in groups])


def ssd_consts():
    k = np.arange(128)[:, None]
    l = np.arange(128)[None, :]
    return dict(triu=(k <= l).astype(np.float32), trius=(k < l).astype(np.float32),
                tril=(k >= l).astype(np.float32), trils=(k > l).astype(np.float32))


def build_ssd(need_ctx, P=None, prefix="", bind=None):
    P, own = _begin(P, prefix, bind)
    C = Ctx(P, nrot=8)
    NCH = 34
    xs_d = P.dram("xs", [NTOK, 512], F32, "ExternalInput")
    b_d = P.dram("btm", [NTOK, 128], F32, "ExternalInput")
    z_d = P.dram("z", [NTOK, 512], F32, "ExternalInput")
    dt_d = P.dram("dtraw", [NTOK, 16], F32, "ExternalInput")
    bt_d = P.dram("bT", [128, NTOK], F32, "ExternalInput")
    ct_d = P.dram("cT", [128, NTOK], F32, "ExternalInput")
    dtb_d = P.dram("dtb", [128, 16], F32, "ExternalInput")
    alog_d = P.dram("alog", [128, 16], F32, "ExternalInput")
    dsk_d = P.dram("dskip", [128, 8], F32, "ExternalInput")
    ng_d = P.dram("normg", [128, 512], F32, "ExternalInput")
    cm_d = {k: P.dram(k, [128, 128], F32, "ExternalInput") for k in ("triu", "trius", "tril", "trils")}
    out_d = P.dram("y", [NTOK, 512], F32, "ExternalOutput")

    xs = P.sb([128, NCH, 512], F32, "xss")
    btm = P.sb([128, NCH, 128], BF16, "btms")
    bT = P.sb([128, NTOK], BF16, "bTs")
    cT = P.sb([128, NTOK], BF16, "cTs")
    dt = P.sb([128, NCH, 16], F32, "dts")
    Aa = P.sb([128, NCH, 16], F32, "Aa")
    dtb = P.sb([128, 16], F32, "dtbs")
    aneg = P.sb([128, 16], F32, "aneg")
    dsk = P.sb([128, 8], F32, "dsks")
    ng = P.sb([128, 512], F32, "ngs")
    cm = {k: P.sb([128, 128], F32, k + "_sb") for k in cm_d}
    ones_f = P.sb([128, 128], F32, "ones_f")
    yf = P.sb([128, NCH, 512], BF16, "yf")
    S = P.sb([128, 512], F32, "S")
    Sb = P.sb([128, 512], BF16, "Sb")
    sc = P.sb([128, 16], F32, "sc")
    ex = P.sb([128, 24], F32, "ex")
    Xd = P.sb([128, 512], F32, "Xd")
    Xdb = P.sb([128, 512], BF16, "Xdb")
    Xw = P.sb([128, 512], BF16, "Xw")
    cbm = P.sb([128, 128], F32, "cbm")
    Am = P.sb([128, 8, 128], F32, "Am")
    es = P.sb([128, 8, 128], F32, "es")
    Mb = P.sb([128, 8, 128], BF16, "Mb")
    yt = P.sb([128, 512], F32, "yt")
    zt = P.sb([128, 512], F32, "zt")
    y2 = P.sb([128, 512], F32, "y2")
    junk = P.sb([128, 512], F32, "junk")
    r1 = P.sb([128, 2], F32, "r1")

    P.dma("sp", xs[:], xs_d[:, :].rearrange("(c p) f -> p c f", p=128), r=[xs_d], w=[xs])
    P.dma("pool", btm[:], b_d[:, :].rearrange("(c p) f -> p c f", p=128), r=[b_d], w=[btm])
    P.dma("pool", bT[:], bt_d[:], r=[bt_d], w=[bT])
    P.dma("pool", cT[:], ct_d[:], r=[ct_d], w=[cT])
    P.dma("sp", dt[:], dt_d[:, :].rearrange("(c p) f -> p c f", p=128), r=[dt_d], w=[dt])
    P.dma("sp", dtb[:], dtb_d[:], r=[dtb_d], w=[dtb])
    P.dma("sp", aneg[:], alog_d[:], r=[alog_d], w=[aneg])
    P.dma("sp", dsk[:], dsk_d[:], r=[dsk_d], w=[dsk])
    P.dma("sp", ng[:], ng_d[:], r=[ng_d], w=[ng])
    for k in cm_d:
        P.dma("sp", cm[k][:], cm_d[k][:], r=[cm_d[k]], w=[cm[k]])
    P.op("dve", lambda e: e.memset(ones_f[:], 1.0), w=[ones_f])
    P.op("dve", lambda e: e.tensor_tensor(out=dt[:], in0=dt[:], in1=dtb[:, :].unsqueeze(1).to_broadcast([128, NCH, 16]), op=ALU.add),
         r=[dt, dtb], w=[dt])
    P.op("act", lambda e: e.activation(out=dt[:], in_=dt[:], func=AF.Exp), r=[dt], w=[dt])
    P.op("act", lambda e: e.activation(out=dt[:], in_=dt[:], func=AF.Ln, bias=1.0, scale=1.0), r=[dt], w=[dt])
    P.op("act", lambda e: e.activation(out=aneg[:], in_=aneg[:], func=AF.Exp), r=[aneg], w=[aneg])
    P.op("dve", lambda e: e.scalar_tensor_tensor(out=Aa[:], in0=dt[:], scalar=-1.0, in1=aneg[:, :].unsqueeze(1).to_broadcast([128, NCH, 16]),
                                                 op0=ALU.mult, op1=ALU.mult), r=[dt, aneg], w=[Aa])

    def bc8(ap):
        return ap.unsqueeze(2).to_broadcast([128, 8, 64])

    def v3(buf):
        return buf[:, :].rearrange("p (h q) -> p h q", h=8)

    def chunk(c, d, emit_y, finish):
        cum, SM, R = (("triu", "trils", "triu") if d == 0 else ("trius", "trius", "tril"))
        A_c = Aa[:, c, d * 8:(d + 1) * 8]
        ps1 = C.ps()
        P.op("pe", lambda e: e.matmul(ps1[:, 0:8], lhsT=cm[cum][:], rhs=A_c, start=True, stop=True), r=[cm[cum], Aa], w=[ps1])
        P.op("pe", lambda e: e.matmul(ps1[:, 8:16], lhsT=ones_f[:], rhs=A_c, start=True, stop=True), r=[ones_f, Aa], w=[ps1])
        P.op("dve", lambda e: e.tensor_copy(out=sc[:], in_=ps1[:, 0:16]), r=[ps1], w=[sc])
        P.op("act", lambda e: e.activation(out=ex[:, 0:16], in_=sc[:], func=AF.Exp), r=[sc], w=[ex])
        P.op("dve", lambda e: e.tensor_tensor(out=sc[:, 0:8], in0=sc[:, 8:16], in1=sc[:, 0:8], op=ALU.subtract), r=[sc], w=[sc])
        P.op("act", lambda e: e.activation(out=ex[:, 16:24], in_=sc[:, 0:8], func=AF.Exp), r=[sc], w=[ex])
        wy, ws = (ex[:, 0:8], ex[:, 16:24]) if d == 0 else (ex[:, 16:24], ex[:, 0:8])
        etot = ex[:, 8:16]
        P.op("dve", lambda e: e.tensor_tensor(out=v3(Xd), in0=xs[:, c, :].rearrange("p (h q) -> p h q", h=8),
                                              in1=bc8(dt[:, c, d * 8:(d + 1) * 8]), op=ALU.mult), r=[xs, dt], w=[Xd])
        P.op("pool", lambda e: e.tensor_copy(out=Xdb[:], in_=Xd[:]), r=[Xd], w=[Xdb])
        P.op("dve", lambda e: e.tensor_tensor(out=v3(Xw), in0=v3(Xd), in1=bc8(ws), op=ALU.mult), r=[Xd, ex], w=[Xw])
        cs_ = slice(c * 128, (c + 1) * 128)
        ps_st = C.ps()
        P.op("pe", lambda e: e.matmul(ps_st[:, :], lhsT=btm[:, c, :], rhs=Xw[:], start=True, stop=True), r=[btm, Xw], w=[ps_st])
        if emit_y:
            ps_off = C.ps()
            P.op("pe", lambda e: e.matmul(ps_off[:, :], lhsT=cT[:, cs_], rhs=Sb[:], start=True, stop=True), r=[cT, Sb], w=[ps_off])
            ps_cb = C.ps()
            P.op("pe", lambda e: e.matmul(ps_cb[:, 0:128], lhsT=bT[:, cs_], rhs=cT[:, cs_], start=True, stop=True), r=[bT, cT], w=[ps_cb])
            P.op("dve", lambda e: e.tensor_tensor(out=cbm[:], in0=ps_cb[:, 0:128], in1=cm[R][:], op=ALU.mult), r=[ps_cb, cm[R]], w=[cbm])
            P.op("pool", lambda e: e.tensor_tensor(out=Am[:], in0=cm[SM][:, :].unsqueeze(1).to_broadcast([128, 8, 128]),
                                                   in1=A_c.unsqueeze(2).to_broadcast([128, 8, 128]), op=ALU.mult), r=[cm[SM], Aa], w=[Am])
            psg = [C.ps(), C.ps()]
            for h in range(8):
                pg = psg[h // 4]
                P.op("pe", lambda e, h=h, pg=pg: e.matmul(pg[:, (h % 4) * 128:(h % 4 + 1) * 128], lhsT=Am[:, h, :], rhs=cm[R][:],
                                                          start=True, stop=True), r=[Am, cm[R]], w=[pg])
            for q in range(2):
                P.op("act", lambda e, q=q: e.activation(out=es[:, q * 4:(q + 1) * 4, :].rearrange("p h l -> p (h l)"), in_=psg[q][:, :], func=AF.Exp),
                     r=[psg[q]], w=[es])
            P.op("dve", lambda e: e.tensor_tensor(out=Mb[:], in0=es[:], in1=cbm[:, :].unsqueeze(1).to_broadcast([128, 8, 128]), op=ALU.mult),
                 r=[es, cbm], w=[Mb])
            ps_y = C.ps()
            for h in range(8):
                P.op("pe", lambda e, h=h: e.matmul(ps_y[:, h * 64:(h + 1) * 64], lhsT=Mb[:, h, :], rhs=Xdb[:, h * 64:(h + 1) * 64],
                                                   start=True, stop=True), r=[Mb, Xdb], w=[ps_y])
            P.op("dve", lambda e: e.tensor_tensor(out=v3(yt), in0=ps_off[:, :].rearrange("p (h q) -> p h q", h=8), in1=bc8(wy), op=ALU.mult),
                 r=[ps_off, ex], w=[yt])
            if d == 0:
                P.op("dve", lambda e: e.tensor_tensor(out=yf[:, c, :], in0=yt[:], in1=ps_y[:, :], op=ALU.add), r=[yt, ps_y], w=[yf])
            else:
                P.op("dve", lambda e: e.tensor_tensor(out=yt[:], in0=yt[:], in1=ps_y[:, :], op=ALU.add), r=[yt, ps_y], w=[yt])
        P.op("dve", lambda e: e.tensor_tensor(out=v3(S), in0=v3(S), in1=bc8(etot), op=ALU.mult), r=[S, ex], w=[S])
        P.op("dve", lambda e: e.tensor_tensor(out=S[:], in0=S[:], in1=ps_st[:, :], op=ALU.add), r=[S, ps_st], w=[S])
        P.op("pool", lambda e: e.tensor_copy(out=Sb[:], in_=S[:]), r=[S], w=[Sb])
        if finish:
            P.dma("sp", zt[:], z_d[c * 128:(c + 1) * 128, :], r=[z_d], w=[zt])
            P.op("dve", lambda e: e.tensor_tensor(out=yt[:], in0=yt[:], in1=yf[:, c, :], op=ALU.add), r=[yt, yf], w=[yt])
            P.op("dve", lambda e: e.tensor_tensor(out=v3(y2), in0=xs[:, c, :].rearrange("p (h q) -> p h q", h=8), in1=bc8(dsk[:, :]), op=ALU.mult),
                 r=[xs, dsk], w=[y2])
            P.op("dve", lambda e: e.tensor_tensor(out=yt[:], in0=yt[:], in1=y2[:], op=ALU.add), r=[yt, y2], w=[yt])
            P.op("act", lambda e: e.activation(out=zt[:], in_=zt[:], func=AF.Silu), r=[zt], w=[zt])
            P.op("dve", lambda e: e.tensor_tensor(out=yt[:], in0=yt[:], in1=zt[:], op=ALU.mult), r=[yt, zt], w=[yt])
            P.op("dve", lambda e: e.memset(r1[:, 0:1], 0.0), w=[r1])
            P.op("act", lambda e: e.activation(out=junk[:], in_=yt[:], func=AF.Square, accum_out=r1[:, 0:1]), r=[yt, r1], w=[junk, r1])
            P.op("act", lambda e: e.activation(out=r1[:, 1:2], in_=r1[:, 0:1], func=AF.Sqrt, bias=EPS, scale=1.0 / 512), r=[r1], w=[r1])
            P.op("dve", lambda e: e.reciprocal(out=r1[:, 1:2], in_=r1[:, 1:2]), r=[r1], w=[r1])
            P.op("dve", lambda e: e.scalar_tensor_tensor(out=y2[:], in0=yt[:], scalar=r1[:, 1:2], in1=ng[:], op0=ALU.mult, op1=ALU.mult),
                 r=[yt, r1, ng], w=[y2])
            P.dma("sp", out_d[c * 128:(c + 1) * 128, :], y2[:], r=[y2], w=[out_d])

    for d in range(2):
        P.op("dve", lambda e: e.memset(S[:], 0.0), w=[S])
        P.op("dve", lambda e: e.memset(Sb[:], 0.0), w=[Sb])
        order = [0, 1] + list(range(2, NCH)) if d == 0 else [1, 0] + list(range(NCH - 1, 1, -1))
        for c in order:
            isctx = c < 2
            ey = (not isctx) or need_ctx
            chunk(c, d, ey, ey and d == 1)
    if not need_ctx:
        P.op("dve", lambda e: e.memset(y2[:], 0.0), w=[y2])
        for c in range(2):
            P.dma("sp", out_d[c * 128:(c + 1) * 128, :], y2[:], r=[y2], w=[out_d])
    return _finish(P, own, [out_d])


import math as _math


def hy_consts(L):
    N = 2 * L
    f = np.arange(L, dtype=np.float64)
    th = 2 * np.pi * (f + 0.5) / N
    ang = np.outer(th, f + 0.5)
    nch = L // 128
    import ml_dtypes
    tabs = []
    for M in (np.cos(ang), np.sin(ang)):
        tabs.append(M.reshape(nch, 128, nch, 128).transpose(2, 1, 0, 3))
    tab = np.stack(tabs).reshape(2, nch, 128, nch * 128).astype(ml_dtypes.bfloat16)
    t = np.linspace(0.0, 1.0, L, dtype=np.float32)[:, None]
    w = (2.0 * np.pi * np.arange(L, dtype=np.float32)[:, None] / L).astype(np.float32)
    fb = np.linspace(1e-4, 16 - 1, 16, dtype=np.float32)[None, :]
    feats = np.concatenate([t, np.cos(fb * w), -np.sin(fb * w)], axis=-1).astype(np.float32)
    tn = np.zeros((128, nch, 2), np.float32)
    tt = np.arange(L).reshape(nch, 128).T
    tn[:, :, 0] = tt / (L - 1)
    tn[:, :, 1] = (tt + 1) / (L - 1)
    csh = np.zeros((128, nch, 2), np.float32)
    thh = (th / 2).reshape(nch, 128).T
    csh[:, :, 0] = np.cos(thh)
    csh[:, :, 1] = np.sin(thh)
    return dict(tab=np.ascontiguousarray(tab), featsT=np.ascontiguousarray(feats.T), tn=tn, csh=csh)


def hy_negdelta(s):
    max_decay = _math.log(1e-2) / 0.3
    min_decay = _math.log(1e-2) / 1.5
    deltas = np.abs(np.linspace(min_decay, max_decay, 1024, dtype=np.float32))
    d = -deltas[512 * s:512 * s + 512]
    return np.ascontiguousarray(np.broadcast_to(d[None, :], (128, 512))).astype(np.float32)


def build_hy(need_ctx, P=None, prefix="", bind=None):
    P, own = _begin(P, prefix, bind)
    C = Ctx(P, nrot=8)
    u_d = P.dram("u", [NTOK, 1536], F32, "ExternalInput")
    seqs = [dict(L=4096, T0=NCTX, sfx="L")]
    if need_ctx:
        seqs.append(dict(L=256, T0=0, sfx="C"))
    for sq in seqs:
        n_ = sq["L"] // 128
        sq["nch"] = n_
        sq["tab_d"] = P.dram("tab" + sq["sfx"], [2, n_, 128, n_ * 128], BF16, "ExternalInput")
        sq["ft_d"] = P.dram("featsT" + sq["sfx"], [33, sq["L"]], F32, "ExternalInput")
        sq["tn_d"] = P.dram("tn" + sq["sfx"], [128, n_, 2], F32, "ExternalInput")
        sq["csh_d"] = P.dram("csh" + sq["sfx"], [128, n_, 2], F32, "ExternalInput")
    w1_d = P.dram("hw1", [33, 64], F32, "ExternalInput")
    w2_d = P.dram("hw2", [64, 64], F32, "ExternalInput")
    w3_d = P.dram("hw3", [64, 64], F32, "ExternalInput")
    fq_d = P.dram("hfreq", [64, 3], F32, "ExternalInput")
    hb_d = P.dram("hb", [64, 3], F32, "ExternalInput")
    w4_d = P.dram("hw4", [64, 4, 512], F32, "ExternalInput")
    nd_d = P.dram("negdelta", [128, 512], F32, "ExternalInput")
    hbias_d = P.dram("hbias", [1, 2, 512], F32, "ExternalInput")
    out_d = P.dram("y", [NTOK, 512], F32, "ExternalOutput")

    w1 = P.sb([33, 64], F32, "w1s"); w2 = P.sb([64, 64], F32, "w2s"); w3 = P.sb([64, 64], F32, "w3s")
    fq = P.sb([64, 3], F32, "fqs"); hb = P.sb([64, 3], F32, "hbs")
    w4 = P.sb([64, 4, 512], F32, "w4s")
    nd = P.sb([128, 512], F32, "nds")
    hbias = P.sb([1, 2, 512], F32, "hbiass")
    NCH = 32
    ft = P.sb([33, 512], F32, "fts")
    tn = P.sb([128, NCH, 2], F32, "tns")
    csh = P.sb([128, NCH, 2], F32, "cshs")
    h3 = P.sb([64, 4096 + 128], F32, "h3")
    ha = P.sb([64, 512], F32, "ha"); hbb = P.sb([64, 512], F32, "hbb")
    qi = P.sb([64, 512], I32, "qi"); qf = P.sb([64, 512], F32, "qf")
    CW = 512
    KD = [P.sb([128, NCH, CW], BF16, "KD%d" % i) for i in range(2)]
    spec_d = P.scratch(P.prefix + "spec_scr", [NCH, 128, 2 * CW], BF16)
    spt = [P.sb([128, 2, CW], BF16, "spt%d" % i) for i in range(2)]
    spl = [P.sb([128, 2, CW], BF16, "spl%d" % i) for i in range(2)]
    ub = P.sb([128, NCH, CW], BF16, "ub")
    slabs = [P.sb([128, NCH * 128], BF16, "slab%d" % i) for i in range(4)]
    wex = [P.sb([128, CW], F32, "wex%d" % i) for i in range(2)]
    kfb = P.sb([128, CW], F32, "kfb"); kbb = P.sb([128, CW], F32, "kbb")
    tt = [P.sb([128, CW], F32, "tt%d" % i) for i in range(4)]
    xt = [P.sb([128, CW], F32, "xt%d" % i) for i in range(2)]
    ost = [P.sb([128, CW], F32, "ost%d" % i) for i in range(2)]
    for (sbuf, dbuf) in ((w1, w1_d), (w2, w2_d), (w3, w3_d), (fq, fq_d), (hb, hb_d), (w4, w4_d), (nd, nd_d), (hbias, hbias_d)):
        P.dma("sp", sbuf[:], dbuf[:], r=[dbuf], w=[sbuf])
    cnt = {"slab": 0, "x": 0}
    TWO_PI = 2.0 * _math.pi

    def get_slab(sq, m, idx):
        sl = slabs[cnt["slab"] % 4]
        cnt["slab"] += 1
        n = sq["nch"] * 128
        P.dma("sp" if cnt["slab"] % 2 else "act", sl[:, 0:n], sq["tab_d"][m, idx], r=[sq["tab_d"]], w=[sl])
        return sl

    def do_seq(sq):
        L, nch, T0 = sq["L"], sq["nch"], sq["T0"]
        N = 2 * L
        P.dma("sp", tn[:, 0:nch, :], sq["tn_d"][:], r=[sq["tn_d"]], w=[tn])
        P.dma("sp", csh[:, 0:nch, :], sq["csh_d"][:], r=[sq["csh_d"]], w=[csh])
        P.op("dve", lambda e: e.memset(h3[:], 0.0), w=[h3])
        bn = min(512, L)
        for b0 in range(0, L, bn):
            src = None
            for li, (wm, kdim) in enumerate(((w1, 33), (w2, 64), (w3, 64))):
                ps = C.ps()
                if li == 0:
                    P.dma("sp", ft[:, 0:bn], sq["ft_d"][:, b0:b0 + bn], r=[sq["ft_d"]], w=[ft])
                    P.op("pe", lambda e, ps=ps, b0=b0: e.matmul(ps[0:64, 0:bn], lhsT=w1[:, :], rhs=ft[:, 0:bn], start=True, stop=True),
                         r=[w1, ft], w=[ps])
                else:
                    P.op("pe", lambda e, ps=ps, wm=wm, src=src: e.matmul(ps[0:64, 0:bn], lhsT=wm[:, :], rhs=src[:, 0:bn], start=True, stop=True),
                         r=[wm, src], w=[ps])
                tmpb = ha if li % 2 == 0 else hbb
                P.op("dve", lambda e, ps=ps, tmpb=tmpb, li=li: e.tensor_scalar(
                    out=tmpb[:, 0:bn], in0=ps[0:64, 0:bn], scalar1=hb[:, li:li + 1], scalar2=fq[:, li:li + 1], op0=ALU.add, op1=ALU.mult),
                    r=[ps, hb, fq], w=[tmpb])
                P.op("dve", lambda e, tmpb=tmpb: e.tensor_scalar(
                    out=tmpb[:, 0:bn], in0=tmpb[:, 0:bn], scalar1=float(1.0 / TWO_PI), scalar2=64.5, op0=ALU.mult, op1=ALU.add),
                    r=[tmpb], w=[tmpb])
                P.op("dve", lambda e, tmpb=tmpb: e.tensor_copy(out=qi[:, 0:bn], in_=tmpb[:, 0:bn]), r=[tmpb], w=[qi])
                P.op("dve", lambda e: e.tensor_copy(out=qf[:, 0:bn], in_=qi[:, 0:bn]), r=[qi], w=[qf])
                P.op("dve", lambda e, tmpb=tmpb: e.tensor_tensor(out=tmpb[:, 0:bn], in0=tmpb[:, 0:bn], in1=qf[:, 0:bn], op=ALU.subtract),
                     r=[tmpb, qf], w=[tmpb])
                P.op("dve", lambda e, tmpb=tmpb: e.tensor_single_scalar(out=qf[:, 0:bn], in_=tmpb[:, 0:bn], scalar=0.0, op=ALU.is_lt),
                     r=[tmpb], w=[qf])
                P.op("dve", lambda e, tmpb=tmpb: e.tensor_tensor(out=tmpb[:, 0:bn], in0=tmpb[:, 0:bn], in1=qf[:, 0:bn], op=ALU.add),
                     r=[tmpb, qf], w=[tmpb])
                if li < 2:
                    P.op("act", lambda e, tmpb=tmpb: e.activation(out=tmpb[:, 0:bn], in_=tmpb[:, 0:bn], func=AF.Sin, bias=negpi[:, 0:1], scale=float(TWO_PI)),
                         r=[tmpb, negpi], w=[tmpb])
                    src = tmpb
                else:
                    P.op("act", lambda e, tmpb=tmpb, b0=b0: e.activation(out=h3[:, b0:b0 + bn], in_=tmpb[:, 0:bn], func=AF.Sin, bias=negpi[:, 0:1], scale=float(TWO_PI)),
                         r=[tmpb, negpi], w=[h3])
        for half in range(512 // CW):
            for o in range(2):
                do_ho(sq, half, o, slice(CW * half, CW * half + CW))

    def do_ho(sq, half, o, ch):
        L, nch, T0 = sq["L"], sq["nch"], sq["T0"]
        N = 2 * L
        if True:
            if True:
                for tc in range(nch):
                    psf = C.ps(); psb = C.ps()
                    P.op("pe", lambda e, psf=psf, tc=tc: e.matmul(psf[:, 0:CW], lhsT=h3[:, tc * 128:tc * 128 + 128], rhs=w4[:, o * 2 + 0, ch],
                                                                  start=True, stop=True), r=[h3, w4], w=[psf])
                    P.op("pe", lambda e, psb=psb, tc=tc: e.matmul(psb[:, 0:CW], lhsT=h3[:, tc * 128 + 1:tc * 128 + 129], rhs=w4[:, o * 2 + 1, ch],
                                                                  start=True, stop=True), r=[h3, w4], w=[psb])
                    for dd, (psx, kx) in enumerate(((psf, kfb), (psb, kbb))):
                        P.op("act", lambda e, dd=dd, tc=tc: e.activation(out=wex[dd][:], in_=nd[:, ch], func=AF.Exp, scale=tn[:, tc, dd:dd + 1]),
                             r=[nd, tn], w=[wex[dd]])
                        P.op("dve", lambda e, dd=dd, psx=psx, kx=kx: e.scalar_tensor_tensor(
                            out=kx[:], in0=wex[dd][:], scalar=0.05, in1=psx[:, 0:CW], op0=ALU.add, op1=ALU.mult), r=[wex[dd], psx], w=[kx])
                    if tc == 0:
                        ps0 = C.ps()
                        P.op("pe", lambda e, ps0=ps0: e.matmul(ps0[:, 0:CW], lhsT=h3[:, 0:128], rhs=w4[:, o * 2 + 1, ch], start=True, stop=True),
                             r=[h3, w4], w=[ps0])
                        P.op("dve", lambda e, ps0=ps0: e.scalar_tensor_tensor(
                            out=kfb[0:1, :], in0=ps0[0:1, 0:CW], scalar=1.05, in1=kfb[0:1, :], op0=ALU.mult, op1=ALU.add), r=[ps0, kfb], w=[kfb])
                        P.op("dve", lambda e: e.tensor_tensor(out=kfb[0:1, :], in0=kfb[0:1, :], in1=hbias[0:1, o, ch], op=ALU.add),
                             r=[kfb, hbias], w=[kfb])
                    P.op("dve", lambda e, tc=tc: e.tensor_tensor(out=KD[0][:, tc, :], in0=kfb[:], in1=kbb[:], op=ALU.add), r=[kfb, kbb], w=[KD[0]])
                    P.op("pool", lambda e, tc=tc: e.tensor_tensor(out=KD[1][:, tc, :], in0=kfb[:], in1=kbb[:], op=ALU.subtract), r=[kfb, kbb], w=[KD[1]])
                for fc in range(nch):
                    sC = get_slab(sq, 0, fc); sS = get_slab(sq, 1, fc)
                    pP = C.ps(); pQ = C.ps()
                    for tc in range(nch):
                        P.op("pe", lambda e, tc=tc, sC=sC, pP=pP: e.matmul(pP[:, 0:CW], lhsT=sC[:, tc * 128:(tc + 1) * 128], rhs=KD[0][:, tc, :],
                                                                    start=(tc == 0), stop=(tc == nch - 1)), r=[sC, KD[0]], w=[pP])
                    for tc in range(nch):
                        P.op("pe", lambda e, tc=tc, sS=sS, pQ=pQ: e.matmul(pQ[:, 0:CW], lhsT=sS[:, tc * 128:(tc + 1) * 128], rhs=KD[1][:, tc, :],
                                                                    start=(tc == 0), stop=(tc == nch - 1)), r=[sS, KD[1]], w=[pQ])
                    cth = csh[:, fc, 0:1]; sth = csh[:, fc, 1:2]
                    P.op("dve", lambda e, pQ=pQ, sth=sth: e.tensor_scalar(out=tt[0][:], in0=pQ[:, 0:CW], scalar1=sth, scalar2=None, op0=ALU.mult),
                         r=[pQ, csh], w=[tt[0]])
                    sp_ = spt[fc % 2]
                    P.op("dve", lambda e, pP=pP, cth=cth, sp_=sp_: e.scalar_tensor_tensor(out=sp_[:, 0, :], in0=pP[:, 0:CW], scalar=cth, in1=tt[0][:],
                                                                                 op0=ALU.mult, op1=ALU.add), r=[pP, csh, tt[0]], w=[sp_])
                    P.op("dve", lambda e, pQ=pQ, cth=cth: e.tensor_scalar(out=tt[1][:], in0=pQ[:, 0:CW], scalar1=cth, scalar2=None, op0=ALU.mult),
                         r=[pQ, csh], w=[tt[1]])
                    P.op("dve", lambda e, pP=pP, sth=sth, sp_=sp_: e.scalar_tensor_tensor(out=sp_[:, 1, :], in0=pP[:, 0:CW], scalar=sth, in1=tt[1][:],
                                                                                 op0=ALU.mult, op1=ALU.subtract), r=[pP, csh, tt[1]], w=[sp_])
                    P.dma("pool", spec_d[fc].rearrange("p (a c) -> p a c", a=2), sp_[:], r=[sp_], w=[spec_d])
                if o == 0:
                    P.dma("pool", ub[:, 0:nch, :], u_d[T0:T0 + L, CW * half:CW * half + CW].rearrange("(c p) f -> p c f", p=128),
                          r=[u_d], w=[ub])
                for fc in range(nch):
                    sC = get_slab(sq, 0, fc); sS = get_slab(sq, 1, fc)
                    pA = C.ps(); pB = C.ps()
                    for tc in range(nch):
                        P.op("pe", lambda e, tc=tc, sC=sC, pA=pA: e.matmul(pA[:, 0:CW], lhsT=sC[:, tc * 128:(tc + 1) * 128], rhs=ub[:, tc, :],
                                                                    start=(tc == 0), stop=(tc == nch - 1)), r=[sC, ub], w=[pA])
                    for tc in range(nch):
                        P.op("pe", lambda e, tc=tc, sS=sS, pB=pB: e.matmul(pB[:, 0:CW], lhsT=sS[:, tc * 128:(tc + 1) * 128], rhs=ub[:, tc, :],
                                                                    start=(tc == 0), stop=(tc == nch - 1)), r=[sS, ub], w=[pB])
                    spec = spl[fc % 2]
                    P.dma("pool", spec[:], spec_d[fc].rearrange("p (a c) -> p a c", a=2), r=[spec_d], w=[spec])
                    Kr = spec[:, 0, :]; Ki = spec[:, 1, :]
                    P.op("dve", lambda e, pA=pA, Kr=Kr: e.tensor_tensor(out=tt[0][:], in0=pA[:, 0:CW], in1=Kr, op=ALU.mult), r=[pA, spec], w=[tt[0]])
                    P.op("dve", lambda e, pB=pB, Ki=Ki: e.tensor_tensor(out=tt[1][:], in0=pB[:, 0:CW], in1=Ki, op=ALU.mult), r=[pB, spec], w=[tt[1]])
                    P.op("dve", lambda e, pB=pB, Kr=Kr: e.tensor_tensor(out=tt[2][:], in0=pB[:, 0:CW], in1=Kr, op=ALU.mult), r=[pB, spec], w=[tt[2]])
                    P.op("dve", lambda e, pA=pA, Ki=Ki: e.tensor_tensor(out=tt[3][:], in0=pA[:, 0:CW], in1=Ki, op=ALU.mult), r=[pA, spec], w=[tt[3]])
                    P.op("pool", lambda e, fc=fc: e.tensor_tensor(out=KD[0][:, fc, :], in0=tt[0][:], in1=tt[1][:], op=ALU.add), r=[tt[0], tt[1]], w=[KD[0]])
                    P.op("pool", lambda e, fc=fc: e.tensor_tensor(out=KD[1][:, fc, :], in0=tt[2][:], in1=tt[3][:], op=ALU.subtract), r=[tt[2], tt[3]], w=[KD[1]])
                for tc in range(nch):
                    sC = get_slab(sq, 0, tc); sS = get_slab(sq, 1, tc)
                    py = C.ps()
                    for fc in range(nch):
                        P.op("pe", lambda e, fc=fc, sC=sC, py=py: e.matmul(py[:, 0:CW], lhsT=sC[:, fc * 128:(fc + 1) * 128], rhs=KD[0][:, fc, :],
                                                                    start=(fc == 0), stop=False), r=[sC, KD[0]], w=[py])
                    for fc in range(nch):
                        P.op("pe", lambda e, fc=fc, sS=sS, py=py: e.matmul(py[:, 0:CW], lhsT=sS[:, fc * 128:(fc + 1) * 128], rhs=KD[1][:, fc, :],
                                                                    start=False, stop=(fc == nch - 1)), r=[sS, KD[1]], w=[py])
                    xb = xt[cnt["x"] % 2]
                    ob = ost[cnt["x"] % 2]
                    cnt["x"] += 1
                    r0 = T0 + tc * 128
                    c0 = 512 * (1 + o) + CW * half
                    P.dma("pool", xb[:], u_d[r0:r0 + 128, c0:c0 + CW], r=[u_d], w=[xb])
                    if o == 0:
                        P.op("dve", lambda e, py=py, xb=xb, tc=tc: e.scalar_tensor_tensor(
                            out=ub[:, tc, :], in0=py[:, 0:CW], scalar=float(2.0 / N), in1=xb[:], op0=ALU.mult, op1=ALU.mult), r=[py, xb], w=[ub])
                    else:
                        P.op("dve", lambda e, py=py, xb=xb, ob=ob: e.scalar_tensor_tensor(
                            out=ob[:], in0=py[:, 0:CW], scalar=float(2.0 / N), in1=xb[:], op0=ALU.mult, op1=ALU.mult), r=[py, xb], w=[ob])
                        P.dma("sp", out_d[r0:r0 + 128, CW * half:CW * half + CW], ob[:], r=[ob], w=[out_d])

    negpi = P.sb([64, 1], F32, "negpi")
    P.op("dve", lambda e: e.memset(negpi[:], -_math.pi), w=[negpi])
    for sq in seqs:
        do_seq(sq)
    if not need_ctx:
        zb = P.sb([128, 512], F32, "zb")
        P.op("dve", lambda e: e.memset(zb[:], 0.0), w=[zb])
        for c in range(2):
            P.dma("sp", out_d[c * 128:(c + 1) * 128, :], zb[:], r=[zb], w=[out_d])
    return _finish(P, own, [out_d])


def build_mod(P=None, prefix="", bind=None):
    P, own = _begin(P, prefix, bind)
    C = Ctx(P, nrot=8)
    c_d = P.dram("cs", [128, KC, 5], F32, "ExternalInput")
    w_d = P.dram("wm", [2, D, 768], F32, "ExternalInput")
    b_d = P.dram("bm", [2, 1, 768], F32, "ExternalInput")
    out_d = P.dram("mod", [2, 5, 768], F32, "ExternalOutput")
    cs = P.sb([128, KC, 5], F32, "css")
    wm = P.sb([128, KC, 768], F32, "wms")
    brow = P.sb([1, 768], F32, "brow")
    ones_f = P.sb([1, 8], F32, "ones_f")
    st = P.sb([5, 768], F32, "st")
    P.dma("sp", cs[:], c_d[:], r=[c_d], w=[cs])
    P.op("act", lambda e: e.activation(out=cs[:], in_=cs[:], func=AF.Silu), r=[cs], w=[cs])
    P.op("dve", lambda e: e.memset(ones_f[:], 1.0), w=[ones_f])
    for l in range(2):
        P.dma("sp", wm[:], wview(w_d[l], 0, KC, 0, 768), r=[w_d], w=[wm])
        P.dma("sp", brow[:], b_d[l], r=[b_d], w=[brow])
        for (c0, n) in ((0, 512), (512, 256)):
            ps = C.ps()
            for kc in range(KC):
                P.op("pe", lambda e, ps=ps, kc=kc, c0=c0, n=n: e.matmul(ps[0:5, 0:n], lhsT=cs[:, kc, :], rhs=wm[:, kc, c0:c0 + n],
                                                                start=(kc == 0), stop=False), r=[cs, wm], w=[ps])
            P.op("pe", lambda e, ps=ps, c0=c0, n=n: e.matmul(ps[0:5, 0:n], lhsT=ones_f[0:1, 0:5], rhs=brow[0:1, c0:c0 + n],
                                                      start=False, stop=True), r=[ones_f, brow], w=[ps])
            P.op("act", lambda e, ps=ps, c0=c0, n=n: e.activation(out=st[:, c0:c0 + n], in_=ps[0:5, 0:n], func=AF.Copy), r=[ps], w=[st])
        P.dma("sp", out_d[l], st[:], r=[st], w=[out_d])
    return _finish(P, own, [out_d])


_CACHE = {}


def _get(key, fn):
    if key not in _CACHE:
        _CACHE[key] = fn()
    return _CACHE[key]


def _run(nc, in_maps):
    res = run_bass_kernel_spmd(nc, in_maps, core_ids=list(range(len(in_maps))))
    return res.results


def _pack(v):
    return np.ascontiguousarray(np.asarray(v, np.float32).reshape(8, 128).T)


def _c(a):
    return np.ascontiguousarray(np.asarray(a, dtype=np.float32))


def _rep(v, n=128):
    v = np.asarray(v, np.float32)
    return np.ascontiguousarray(np.broadcast_to(v[None], (n,) + v.shape))


SSD_GROUPS = [dict(name="xs", ncol=512, conv=True, act="silu", layout="tm"),
              dict(name="btm", ncol=128, conv=True, act="silu", layout="tm"),
              dict(name="z", ncol=512, conv=False, act=None, layout="tm"),
              dict(name="dtraw", ncol=16, conv=False, act=None, layout="tm"),
              dict(name="bT", ncol=128, conv=True, act="silu", layout="fm"),
              dict(name="cT", ncol=128, conv=True, act="silu", layout="fm")]
HY_GROUPS = [dict(name="uv", ncol=512, conv=True, act=None, layout="tm"),
             dict(name="ux1", ncol=512, conv=True, act=None, layout="tm"),
             dict(name="ux2", ncol=512, conv=True, act=None, layout="tm")]


def kernel_unfused(x, c, ctx, c_ctx, w_mod, b_mod, norm1_g, norm2_g, w_in, hy_conv_w, hy_conv_b, hy_w1, hy_b1, hy_w2,
           hy_b2, hy_w3, hy_b3, hy_w4, hy_freq, hy_bias, ssd_conv_w, ssd_conv_b, ssd_dt_bias, ssd_a_log, ssd_d,
           ssd_norm_g, da_q_norm, da_k_norm, da_lambda, da_subln_g, w_branch, w_out, ffn_w1, ffn_w3, ffn_w2,
           router_w, moe_w1, moe_w3, moe_w2):
    f32 = np.float32
    x = np.asarray(x, f32); ctx = np.asarray(ctx, f32)
    cores = [(b, s) for b in range(4) for s in range(2)]
    cc = np.concatenate([np.asarray(c, f32), np.asarray(c_ctx, f32)[None]], 0)
    cs = np.ascontiguousarray(cc.T.reshape(8, 128, 5).transpose(1, 0, 2))
    ncm = _get("mod", build_mod)
    res = _run(ncm, [dict(cs=cs, wm=_c(np.asarray(w_mod)[:, :, j * 768:(j + 1) * 768]),
                          bm=_c(np.asarray(b_mod)[:, None, j * 768:(j + 1) * 768])) for j in range(8)])
    mod = np.concatenate([r["mod"] for r in res], axis=2)

    h_lat = x
    h_ctx = ctx
    acons = attn_consts()
    scons = ssd_consts()
    hyL = {k + "L": v for k, v in hy_consts(4096).items()}
    hyC = {k + "C": v for k, v in hy_consts(256).items()}
    for i in range(2):
        need_ctx = i == 0
        W = np.asarray(w_in[i], f32)

        def vecs(b):
            cols = [_pack(norm1_g[i]), _pack(norm2_g[i])]
            for m in (mod[i, b], mod[i, 4]):
                for j in range(6):
                    cols.append(_pack(m[j * 1024:(j + 1) * 1024]))
            return np.ascontiguousarray(np.concatenate(cols, axis=1))
        hT = [np.ascontiguousarray(np.concatenate([h_ctx[b], h_lat[b]], 0).T) for b in range(4)]
        vv = [vecs(b) for b in range(4)]
        so = 3072
        xo = so + 1024
        cw = np.asarray(ssd_conv_w[i], f32); cb = np.asarray(ssd_conv_b[i], f32)
        maps = []
        for (b, s) in cores:
            m = dict(hT=hT[b], vecs=vv[b])
            sel = dict(xs=slice(512 * s, 512 * s + 512), btm=slice(1024 + 128 * s, 1024 + 128 * s + 128),
                       bT=slice(1024 + 128 * s, 1024 + 128 * s + 128), cT=slice(1280 + 128 * s, 1280 + 128 * s + 128))
            for nm, sl in sel.items():
                m["w_" + nm] = _c(W[:, xo:xo + 1536][:, sl])
                m["cw_" + nm] = _c(np.broadcast_to(cw[:, None, sl], (3, 128, sl.stop - sl.start)))
                if nm in ("bT", "cT"):
                    m["b_" + nm] = _c(cb[sl].reshape(1, 128).T)
                else:
                    m["b_" + nm] = _c(cb[None, sl])
            m["w_z"] = _c(W[:, so + 512 * s:so + 512 * s + 512]); m["b_z"] = np.zeros((1, 512), f32)
            dcols = [5632 + d * 16 + 8 * s + h for d in range(2) for h in range(8)]
            m["w_dtraw"] = _c(W[:, dcols]); m["b_dtraw"] = np.zeros((1, 16), f32)
            maps.append(m)
        pj = _run(_get("pj_ssd", lambda: build_pj([dict(g) for g in SSD_GROUPS])), maps)
        maps = []
        for k, (b, s) in enumerate(cores):
            r = pj[k]
            m = dict(xs=r["o_xs"], btm=r["o_btm"], z=r["o_z"], dtraw=r["o_dtraw"], bT=r["o_bT"], cT=r["o_cT"],
                     dtb=_rep(np.asarray(ssd_dt_bias[i], f32)[:, 8 * s:8 * s + 8].reshape(16)),
                     alog=_rep(np.asarray(ssd_a_log[i], f32)[:, 8 * s:8 * s + 8].reshape(16)),
                     dskip=_rep(np.asarray(ssd_d[i], f32)[8 * s:8 * s + 8]),
                     normg=_rep(np.asarray(ssd_norm_g[i], f32)[512 * s:512 * s + 512]))
            m.update(scons)
            maps.append(m)
        y_ssd = _run(_get(("ssd", need_ctx), lambda: build_ssd(need_ctx)), maps)
        del pj
        hw = np.asarray(hy_conv_w[i], f32); hbv = np.asarray(hy_conv_b[i], f32)
        maps = []
        for (b, s) in cores:
            m = dict(hT=hT[b], vecs=vv[b])
            for gi, nm in enumerate(("uv", "ux1", "ux2")):
                sl = slice(1024 * gi + 512 * s, 1024 * gi + 512 * s + 512)
                m["w_" + nm] = _c(W[:, sl])
                m["cw_" + nm] = _c(np.broadcast_to(hw[:, None, sl], (3, 128, 512)))
                m["b_" + nm] = _c(hbv[None, sl])
            maps.append(m)
        pj = _run(_get("pj_hy", lambda: build_pj([dict(g) for g in HY_GROUPS])), maps)
        maps = []
        for k, (b, s) in enumerate(cores):
            r = pj[k]
            cs_ = slice(512 * s, 512 * s + 512)
            m = dict(u=np.ascontiguousarray(np.concatenate([r["o_uv"], r["o_ux1"], r["o_ux2"]], 1)),
                     hw1=_c(hy_w1[i]), hw2=_c(hy_w2[i]), hw3=_c(hy_w3[i]), hfreq=_c(np.asarray(hy_freq[i]).T),
                     hb=_c(np.stack([np.asarray(hy_b1[i]), np.asarray(hy_b2[i]), np.asarray(hy_b3[i])], 1)),
                     hw4=_c(np.asarray(hy_w4[i], f32).reshape(64, 2, 2, 1024)[:, :, :, cs_].reshape(64, 4, 512)),
                     negdelta=hy_negdelta(s), hbias=_c(np.asarray(hy_bias[i], f32)[:, cs_][None]))
            m.update(hyL)
            if need_ctx:
                m.update(hyC)
            maps.append(m)
        y_hy = _run(_get(("hy", need_ctx), lambda: build_hy(need_ctx)), maps)
        del pj
        maps = []
        for (b, s) in cores:
            o = 5664 + s * 512
            m = dict(hT=hT[b], vecs=vv[b], wq=_c(W[:, o:o + 512]), wk=_c(W[:, o + 1024:o + 1536]), wv=_c(W[:, o + 2048:o + 2560]),
                     qkg=_c(np.stack([np.concatenate([np.asarray(da_q_norm[i], f32)] * 2), np.concatenate([np.asarray(da_k_norm[i], f32)] * 2)], 1)),
                     lamv=_c(np.asarray(da_lambda[i], f32).reshape(1, 256)), sublng=_rep(np.asarray(da_subln_g[i], f32)))
            m.update(acons)
            maps.append(m)
        y_da = _run(_get(("attn", i), lambda: build_attn(i, need_ctx)), maps)
        if need_ctx:
            T = 2176
            tiles = [(0, 128, 1)] + [(128 + 512 * q, 512, 0) for q in range(4)]
        else:
            T = 2048
            tiles = [(512 * q, 512, 0) for q in range(4)]
        maps = []
        for (b, s) in cores:
            tok = np.concatenate([np.arange(128 * s, 128 * s + 128), 256 + np.arange(2048 * s, 2048 * s + 2048)]) if need_ctx \
                else 256 + np.arange(2048 * s, 2048 * s + 2048)
            yh = np.concatenate([y_hy[2 * b]["y"], y_hy[2 * b + 1]["y"]], 1)[tok].T
            ys = np.concatenate([y_ssd[2 * b]["y"], y_ssd[2 * b + 1]["y"]], 1)[tok].T
            yd = np.concatenate([y_da[2 * b]["yT"], y_da[2 * b + 1]["yT"]], 0)[:, tok]
            m = dict(hT=np.ascontiguousarray(hT[b][:, tok]), yT=np.ascontiguousarray(np.stack([yh, ys, yd])), vecs=vv[b],
                     wg=_c(W[:, -3072:]), wbr=_c(w_branch[i]), wo=_c(w_out[i]))
            if i % 2 == 0:
                m.update(w1=_c(ffn_w1[i // 2])[None], w3=_c(ffn_w3[i // 2])[None], w2=_c(ffn_w2[i // 2])[None])
            else:
                m.update(rw=_c(router_w[i // 2]), w1=_c(moe_w1[i // 2]), w3=_c(moe_w3[i // 2]), w2=_c(moe_w2[i // 2]),
                         ident=np.eye(128, dtype=f32))
            maps.append(m)
        moe = i % 2 == 1
        outB = _run(_get(("B", T, moe), lambda: build_B(T, tiles, moe)), maps)
        del y_hy, y_ssd, y_da
        new_lat = np.empty_like(h_lat)
        new_ctx = np.array(h_ctx, copy=True)
        for k, (b, s) in enumerate(cores):
            ho = outB[k]["hout"].T
            if need_ctx:
                new_ctx[b, 128 * s:128 * s + 128] = ho[0:128]
                new_lat[b, 2048 * s:2048 * s + 2048] = ho[128:]
            else:
                new_lat[b, 2048 * s:2048 * s + 2048] = ho
        h_lat, h_ctx = new_lat, new_ctx
    return np.ascontiguousarray(h_lat.astype(np.float32))


def emit_modT(P, vecs_sc, prefix="M_"):
    P, own = _begin(P, prefix, None)
    C = Ctx(P, nrot=8)
    c_d = P.dram("cs", [128, KC, 2], F32, "ExternalInput")
    w_d = P.dram("wm", [2, D, 6 * D], F32, "ExternalInput")
    b_d = P.dram("bm", [2, 128, 48], F32, "ExternalInput")
    n_d = P.dram("norms", [2, 128, 16], F32, "ExternalInput")
    cs = P.sb([128, KC, 2], F32, "css")
    wm = [P.sb([128, KC, 512], F32, "wms%d" % i) for i in range(2)]
    bm = P.sb([128, 48], F32, "bms")
    vt = P.sb([128, 112], F32, "vt")
    P.dma("sp", cs[:], c_d[:], r=[c_d], w=[cs])
    P.op("act", lambda e: e.activation(out=cs[:], in_=cs[:], func=AF.Silu), r=[cs], w=[cs])
    for l in range(2):
        P.dma("sp", bm[:], b_d[l], r=[b_d], w=[bm])
        P.dma("sp", vt[:, 0:16], n_d[l], r=[n_d], w=[vt])
        for cb in range(12):
            wb = wm[cb % 2]
            P.dma("sp", wb[:], wview(w_d[l], 0, KC, cb * 512, 512), r=[w_d], w=[wb])
            for j4 in range(4):
                j = cb * 4 + j4
                ps = C.ps()
                for kc in range(KC):
                    P.op("pe", lambda e, ps=ps, wb=wb, kc=kc, j4=j4: e.matmul(
                        ps[:, 0:2], lhsT=wb[:, kc, j4 * 128:(j4 + 1) * 128], rhs=cs[:, kc, :], start=(kc == 0), stop=(kc == KC - 1)),
                        r=[wb, cs], w=[ps])
                for cls in range(2):
                    col = 16 + cls * 48 + j
                    P.op("dve", lambda e, ps=ps, cls=cls, col=col, j=j: e.tensor_tensor(
                        out=vt[:, col:col + 1], in0=ps[:, cls:cls + 1], in1=bm[:, j:j + 1], op=ALU.add), r=[ps, bm], w=[vt])
        P.dma("sp", vecs_sc[l][:], vt[:], r=[vt], w=[vecs_sc[l]])
    P.end_phase()


def _view(buf, ap, name):
    b = Buf(ap, name)
    return b


def build_fused():
    P = Prog(bass.Bass("TRN2", target_bir_lowering=False))
    sc = lambda name, shape: P.scratch(name, shape, F32)
    hT0 = P.dram("hT0", [D, NTOK], F32, "ExternalInput")
    shared = {}
    for nm, shp, dt in (("cosT", [128, 4096], F32), ("sinT", [128, 4096], F32), ("rmT", [128, 128], F32),
                        ("blockones", [128, 128], F32), ("ident", [128, 128], F32),
                        ("triu", [128, 128], F32), ("trius", [128, 128], F32), ("tril", [128, 128], F32), ("trils", [128, 128], F32),
                        ("tabL", [2, 32, 128, 4096], BF16), ("featsTL", [33, 4096], F32), ("tnL", [128, 32, 2], F32), ("cshL", [128, 32, 2], F32),
                        ("tabC", [2, 2, 128, 256], BF16), ("featsTC", [33, 256], F32), ("tnC", [128, 2, 2], F32), ("cshC", [128, 2, 2], F32)):
        shared[nm] = P.dram(nm, shp, dt, "ExternalInput")
    vecs_sc = [sc("vecs_l%d" % l, [128, 112]) for l in range(2)]
    hT1 = sc("hT1", [D, NTOK])
    s_xs = sc("s_xs", [NTOK, 512]); s_btm = sc("s_btm", [NTOK, 128]); s_z = sc("s_z", [NTOK, 512]); s_dt = sc("s_dt", [NTOK, 16])
    s_bT = sc("s_bT", [128, NTOK]); s_cT = sc("s_cT", [128, NTOK])
    s_u = sc("s_u", [NTOK, 1536])
    y_ssd = [sc("y_ssd%d" % s, [NTOK, 512]) for s in range(2)]
    y_hy = [sc("y_hy%d" % s, [NTOK, 512]) for s in range(2)]
    y_da = [sc("y_da%d" % s, [512, NTOK]) for s in range(2)]
    emit_modT(P, vecs_sc)
    hcur = hT0
    for i in range(2):
        need_ctx = i == 0
        for s in range(2):
            pre = "L%ds%d_" % (i, s)
            build_pj([dict(g) for g in SSD_GROUPS], P=P, prefix=pre + "pjs_",
                     bind=dict(hT=hcur, vecs=vecs_sc[i], o_xs=s_xs, o_btm=s_btm, o_z=s_z, o_dtraw=s_dt, o_bT=s_bT, o_cT=s_cT))
            b = dict(xs=s_xs, btm=s_btm, z=s_z, dtraw=s_dt, bT=s_bT, cT=s_cT, y=y_ssd[s])
            b.update({k: shared[k] for k in ("triu", "trius", "tril", "trils")})
            build_ssd(need_ctx, P=P, prefix=pre + "ssd_", bind=b)
            build_pj([dict(g) for g in HY_GROUPS], P=P, prefix=pre + "pjh_",
                     bind=dict(hT=hcur, vecs=vecs_sc[i], o_uv=Buf(s_u.t[:, 0:512], "u0"), o_ux1=Buf(s_u.t[:, 512:1024], "u1"),
                               o_ux2=Buf(s_u.t[:, 1024:1536], "u2")))
            b = dict(u=s_u, y=y_hy[s])
            b.update({k: shared[k] for k in ("tabL", "featsTL", "tnL", "cshL")})
            if need_ctx:
                b.update({k: shared[k] for k in ("tabC", "featsTC", "tnC", "cshC")})
            build_hy(need_ctx, P=P, prefix=pre + "hy_", bind=b)
            b = dict(hT=hcur, vecs=vecs_sc[i], yT=y_da[s])
            b.update({k: shared[k] for k in ("cosT", "sinT", "rmT", "blockones", "ident")})
            build_attn(i, need_ctx, P=P, prefix=pre + "at_", bind=b)
        pre = "L%d_B_" % i
        if need_ctx:
            T = NTOK
            tiles = [(0, 128, 1), (128, 128, 1)] + [(256 + 512 * q, 512, 0) for q in range(8)]
            b = dict(hT=hcur, vecs=vecs_sc[i], hout=hT1, ident=shared["ident"])
            for s in range(2):
                b["yhy%d" % s] = y_hy[s]; b["yssd%d" % s] = y_ssd[s]; b["yda%d" % s] = y_da[s]
        else:
            T = 4096
            tiles = [(512 * q, 512, 0) for q in range(8)]
            b = dict(hT=Buf(hcur.t[:, NCTX:NTOK], "hlat"), vecs=vecs_sc[i], ident=shared["ident"])
            for s in range(2):
                b["yhy%d" % s] = Buf(y_hy[s].t[NCTX:NTOK, :], "yh"); b["yssd%d" % s] = Buf(y_ssd[s].t[NCTX:NTOK, :], "ys")
                b["yda%d" % s] = Buf(y_da[s].t[:, NCTX:NTOK], "yd")
            out_final = P.dram("out", [D, 4096], F32, "ExternalOutput")
            b["hout"] = out_final
        build_B(T, tiles, i % 2 == 1, P=P, prefix=pre, bind=b, ytm=True)
        hcur = hT1
    P.fence("sp", [out_final])
    P.emit()
    return P.nc, P.ext


def fused_inputs(b, inp):
    f32 = np.float32
    g = lambda k: np.asarray(inp[k], f32)
    m = {}
    m["hT0"] = np.ascontiguousarray(np.concatenate([g("ctx")[b], g("x")[b]], 0).T)
    m.update(attn_consts())
    m.update(ssd_consts())
    m.update({k + "L": v for k, v in hy_consts(4096).items()})
    m.update({k + "C": v for k, v in hy_consts(256).items()})
    cc = np.stack([g("c")[b], g("c_ctx")], 1)
    m["M_cs"] = np.ascontiguousarray(cc.reshape(8, 128, 2).transpose(1, 0, 2))
    m["M_wm"] = _c(g("w_mod"))
    m["M_bm"] = np.ascontiguousarray(g("b_mod").reshape(2, 48, 128).transpose(0, 2, 1))
    m["M_norms"] = np.ascontiguousarray(np.stack([np.concatenate([_pack(g("norm1_g")[l]), _pack(g("norm2_g")[l])], 1) for l in range(2)]))
    for i in range(2):
        need_ctx = i == 0
        W = g("w_in")[i]
        so = 3072
        xo = so + 1024
        cw = g("ssd_conv_w")[i]; cb = g("ssd_conv_b")[i]
        hw = g("hy_conv_w")[i]; hbv = g("hy_conv_b")[i]
        for s in range(2):
            pre = "L%ds%d_" % (i, s)
            p = pre + "pjs_"
            sel = dict(xs=slice(512 * s, 512 * s + 512), btm=slice(1024 + 128 * s, 1024 + 128 * s + 128),
                       bT=slice(1024 + 128 * s, 1024 + 128 * s + 128), cT=slice(1280 + 128 * s, 1280 + 128 * s + 128))
            for nm, sl in sel.items():
                m[p + "w_" + nm] = _c(W[:, xo:xo + 1536][:, sl])
                m[p + "cw_" + nm] = _c(np.broadcast_to(cw[:, None, sl], (3, 128, sl.stop - sl.start)))
                m[p + "b_" + nm] = _c(cb[sl].reshape(1, 128).T) if nm in ("bT", "cT") else _c(cb[None, sl])
            m[p + "w_z"] = _c(W[:, so + 512 * s:so + 512 * s + 512]); m[p + "b_z"] = np.zeros((1, 512), f32)
            dcols = [5632 + d * 16 + 8 * s + h for d in range(2) for h in range(8)]
            m[p + "w_dtraw"] = _c(W[:, dcols]); m[p + "b_dtraw"] = np.zeros((1, 16), f32)
            p = pre + "ssd_"
            m[p + "dtb"] = _rep(g("ssd_dt_bias")[i][:, 8 * s:8 * s + 8].reshape(16))
            m[p + "alog"] = _rep(g("ssd_a_log")[i][:, 8 * s:8 * s + 8].reshape(16))
            m[p + "dskip"] = _rep(g("ssd_d")[i][8 * s:8 * s + 8])
            m[p + "normg"] = _rep(g("ssd_norm_g")[i][512 * s:512 * s + 512])
            p = pre + "pjh_"
            for gi, nm in enumerate(("uv", "ux1", "ux2")):
                sl = slice(1024 * gi + 512 * s, 1024 * gi + 512 * s + 512)
                m[p + "w_" + nm] = _c(W[:, sl])
                m[p + "cw_" + nm] = _c(np.broadcast_to(hw[:, None, sl], (3, 128, 512)))
                m[p + "b_" + nm] = _c(hbv[None, sl])
            p = pre + "hy_"
            cs_ = slice(512 * s, 512 * s + 512)
            m[p + "hw1"] = _c(g("hy_w1")[i]); m[p + "hw2"] = _c(g("hy_w2")[i]); m[p + "hw3"] = _c(g("hy_w3")[i])
            m[p + "hfreq"] = _c(g("hy_freq")[i].T)
            m[p + "hb"] = _c(np.stack([g("hy_b1")[i], g("hy_b2")[i], g("hy_b3")[i]], 1))
            m[p + "hw4"] = _c(g("hy_w4")[i].reshape(64, 2, 2, 1024)[:, :, :, cs_].reshape(64, 4, 512))
            m[p + "negdelta"] = hy_negdelta(s)
            m[p + "hbias"] = _c(g("hy_bias")[i][:, cs_][None])
            p = pre + "at_"
            o = 5664 + s * 512
            m[p + "wq"] = _c(W[:, o:o + 512]); m[p + "wk"] = _c(W[:, o + 1024:o + 1536]); m[p + "wv"] = _c(W[:, o + 2048:o + 2560])
            m[p + "qkg"] = _c(np.stack([np.concatenate([g("da_q_norm")[i]] * 2), np.concatenate([g("da_k_norm")[i]] * 2)], 1))
            m[p + "lamv"] = _c(g("da_lambda")[i].reshape(1, 256))
            m[p + "sublng"] = _rep(g("da_subln_g")[i])
        p = "L%d_B_" % i
        m[p + "wg"] = _c(W[:, -3072:]); m[p + "wbr"] = _c(g("w_branch")[i]); m[p + "wo"] = _c(g("w_out")[i])
        if i % 2 == 0:
            m[p + "w1"] = _c(g("ffn_w1")[i // 2])[None]; m[p + "w3"] = _c(g("ffn_w3")[i // 2])[None]; m[p + "w2"] = _c(g("ffn_w2")[i // 2])[None]
        else:
            m[p + "rw"] = _c(g("router_w")[i // 2]); m[p + "w1"] = _c(g("moe_w1")[i // 2]); m[p + "w3"] = _c(g("moe_w3")[i // 2])
            m[p + "w2"] = _c(g("moe_w2")[i // 2])
    return m


def kernel_fused(**inp):
    nc, ext = _get("fused", build_fused)
    maps = []
    for b in range(4):
        m = fused_inputs(b, inp)
        missing = [k for k in ext if k not in m]
        extra = [k for k in m if k not in ext]
        assert not missing, missing
        for k in extra:
            del m[k]
        maps.append(m)
    res = run_bass_kernel_spmd(nc, maps, core_ids=list(range(4)))
    out = np.stack([np.ascontiguousarray(r["out"].T) for r in res.results])
    return np.ascontiguousarray(out.astype(np.float32))


def kernel(**inp):
    return kernel_fused(**inp)
```

```python
import numpy as np
import concourse.bass as bass
import concourse.mybir as mybir
from concourse.bass_utils import run_bass_kernel_spmd

F32 = mybir.dt.float32
BF16 = mybir.dt.bfloat16
I32 = mybir.dt.int32
AF = mybir.ActivationFunctionType
ALU = mybir.AluOpType
AX = mybir.AxisListType


class Buf:
    __slots__ = ("t", "name", "lw", "rd")

    def __init__(self, t, name):
        self.t = t
        self.name = name
        self.lw = None
        self.rd = []

    def __getitem__(self, idx):
        return self.t[idx]


class Prog:
    ENG = ("pe", "dve", "act", "pool", "sp")
    NDMA = 6

    def __init__(self, nc):
        self.nc = nc
        self.ops = {e: [] for e in self.ENG}
        self.cnt = {e: 0 for e in self.ENG}
        self.waited = {e: {} for e in self.ENG}
        self.dmak = {e: 0 for e in self.ENG}
        self.pctx = []
        self.sctx = []
        self.sems = {}
        self.nbuf = 0
        self.prefix = ""
        self.bind = {}
        self.ext = {}
        self.psum_tiles = None
        self.nphase = 0
        for e in self.ENG:
            self.sem(e)
        for q in self.ENG:
            for i in range(self.NDMA):
                self.sem(("d", q, i))

    def sem(self, key):
        if key not in self.sems:
            cm = self.nc.semaphore("s_%s" % "_".join(str(k) for k in (key if isinstance(key, tuple) else (key,))))
            self.sems[key] = cm.__enter__()
            self.pctx.append(cm)
        return self.sems[key]

    def sb(self, shape, dt, name=None):
        self.nbuf += 1
        name = "%s%s_%d" % (self.prefix, name or "sb", self.nbuf)
        cm = self.nc.sbuf_tensor(name, list(shape), dt)
        t = cm.__enter__()
        self.sctx.append(cm)
        return Buf(t, name)

    def ps(self, shape, dt, name=None):
        self.nbuf += 1
        name = name or "ps%d" % self.nbuf
        cm = self.nc.psum_tensor(name, list(shape), dt)
        t = cm.__enter__()
        self.pctx.append(cm)
        return Buf(t, name)

    def dram(self, name, shape, dt, kind="Internal"):
        if name in self.bind:
            b = self.bind[name]
            assert tuple(b.t.shape) == tuple(shape), (name, b.t.shape, shape)
            return b
        full = self.prefix + name
        if kind == "ExternalInput":
            self.ext[full] = (tuple(shape), dt)
        return Buf(self.nc.dram_tensor(full, list(shape), dt, kind=kind).ap(), full)

    def scratch(self, name, shape, dt):
        return Buf(self.nc.dram_tensor(name, list(shape), dt, kind="Internal").ap(), name)

    def sub(self, buf, name):
        return Buf(buf.t, name)

    def _need(self, eng, tok, waits):
        if tok is None:
            return
        k, v = tok
        if eng == "pe" and k == "pe":
            return
        if self.waited[eng].get(k, 0) >= v:
            return
        self.waited[eng][k] = v
        waits[k] = max(waits.get(k, 0), v)

    def _deps(self, eng, r, w):
        waits = {}
        for b in r:
            self._need(eng, b.lw, waits)
        for b in w:
            self._need(eng, b.lw, waits)
            for t in b.rd:
                self._need(eng, t, waits)
        return waits

    def op(self, eng, fn, r=(), w=()):
        waits = self._deps(eng, r, w)
        self.cnt[eng] += 1
        tok = (eng, self.cnt[eng])
        self.ops[eng].append((waits, fn, (eng, 1)))
        for b in r:
            b.rd.append(tok)
        for b in w:
            b.lw = tok
            b.rd = []
        return tok

    def dma(self, q, out, in_, r=(), w=(), **kw):
        waits = self._deps(q, r, w)
        k = self.dmak[q]
        self.dmak[q] += 1
        key = ("d", q, k % self.NDMA)
        val = 16 * (k // self.NDMA + 1)
        if k >= self.NDMA:
            self._need(q, (key, val - 16), waits)
        tok = (key, val)
        self.ops[q].append((waits, lambda e: e.dma_start(out=out, in_=in_, **kw), (key, 16)))
        for b in r:
            b.rd.append(tok)
        for b in w:
            b.lw = tok
            b.rd = []
        return tok

    def fence(self, eng, bufs):
        waits = {}
        for b in bufs:
            self._need(eng, b.lw, waits)
        self.ops[eng].append((waits, None, None))

    def barrier(self):
        toks = [(e, self.cnt[e]) for e in self.ENG if self.cnt[e] > 0]
        for q in self.ENG:
            k = self.dmak[q]
            for slot in range(min(self.NDMA, k)):
                n_on_slot = (k - slot + self.NDMA - 1) // self.NDMA
                toks.append((("d", q, slot), 16 * n_on_slot))
        for e in self.ENG:
            waits = {}
            for t in toks:
                if t[0] == e:
                    continue
                self._need(e, t, waits)
            self.ops[e].append((waits, None, None))

    def flush(self):
        nc = self.nc
        self.nphase += 1
        with nc.Block() as block:
            def mk(ename):
                def body(eng):
                    for waits, fn, inc in self.ops[ename]:
                        for k, v in waits.items():
                            eng.wait_ge(self.sems[k], v)
                        if fn is not None:
                            ins = fn(eng)
                            ins.then_inc(self.sems[inc[0]], inc[1])
                return body
            block.tensor(mk("pe"))
            block.vector(mk("dve"))
            block.scalar(mk("act"))
            block.gpsimd(mk("pool"))
            block.sync(mk("sp"))
        self.ops = {e: [] for e in self.ENG}

    def end_phase(self):
        self.barrier()
        self.flush()
        while self.sctx:
            self.sctx.pop().__exit__(None, None, None)
        self.bind = {}
        self.prefix = ""

    def emit(self):
        self.end_phase()
        while self.pctx:
            self.pctx.pop().__exit__(None, None, None)


def _begin(P, prefix, bind):
    own = P is None
    if own:
        P = Prog(bass.Bass("TRN2", target_bir_lowering=False))
    P.prefix = prefix
    P.bind = dict(bind or {})
    return P, own


def _finish(P, own, outs):
    if own:
        P.fence("sp", outs)
        P.emit()
        return P.nc
    P.end_phase()
    return None


D = 1024
KC = 8
EPS = 1e-6


class Ctx:
    def __init__(self, P, nrot=8, wsize=4096, nw=6):
        self.P = P
        self.wsize = wsize
        self.nw = nw
        if P.psum_tiles is None:
            P.psum_tiles = [P.ps([128, 512], F32, name="psum%d" % i) for i in range(8)]
        self.psum = P.psum_tiles
        self.nrot = nrot
        self.accs = self.psum[nrot:]
        self.pi = 0
        self.wb = []
        self.wi = 0
        self.ci = 0

    def ps(self):
        p = self.psum[self.pi % self.nrot]
        self.pi += 1
        return p

    def wbuf(self):
        if not self.wb:
            self.wb = [self.P.sb([128, self.wsize], BF16, name="wbuf%d" % i) for i in range(self.nw)]
        b = self.wb[self.wi % len(self.wb)]
        self.wi += 1
        return b

    def wload(self, Wd, ap, kc, ncol):
        b = self.wbuf()
        v = b[:, 0:kc * ncol].rearrange("p (k c) -> p k c", k=kc)
        self.P.dma("pool", v, ap, r=[Wd], w=[b])
        return b, v

    def ev(self):
        self.ci += 1
        return "dve" if self.ci % 2 else "act"


def wview(Wd, r0, kc, c0, ncol):
    return Wd[r0:r0 + kc * 128, c0:c0 + ncol].rearrange("(k p) c -> p k c", p=128)


def modnorm(C, hT, n, A, Bv, fT, ones_bf, sqb, tmp, rstd, f32out=None):
    P = C.P
    P.op("act", lambda e: e.activation(out=sqb[:, :, 0:n], in_=hT[:, :, 0:n], func=AF.Square), r=[hT], w=[sqb])
    ps = C.ps()
    for kc in range(KC):
        P.op("pe", lambda e, kc=kc: e.matmul(ps[:, 0:n], lhsT=ones_bf[:], rhs=sqb[:, kc, 0:n],
                                             start=(kc == 0), stop=(kc == KC - 1)), r=[ones_bf, sqb], w=[ps])
    P.op("act", lambda e: e.activation(out=rstd[:, 0:n], in_=ps[:, 0:n], func=AF.Sqrt, bias=EPS, scale=1.0 / D),
         r=[ps], w=[rstd])
    P.op("dve", lambda e: e.reciprocal(out=rstd[:, 0:n], in_=rstd[:, 0:n]), r=[rstd], w=[rstd])
    for kc in range(KC):
        P.op("dve", lambda e, kc=kc: e.scalar_tensor_tensor(
            out=tmp[:, kc, 0:n], in0=hT[:, kc, 0:n], scalar=A[:, kc:kc + 1], in1=rstd[:, 0:n],
            op0=ALU.mult, op1=ALU.mult), r=[hT, A, rstd], w=[tmp])
    for kc in range(KC):
        P.op("act", lambda e, kc=kc: e.activation(out=fT[:, kc, 0:n], in_=tmp[:, kc, 0:n], func=AF.Identity,
                                                  bias=Bv[:, kc:kc + 1], scale=1.0), r=[tmp, Bv], w=[fT])
        if f32out is not None:
            P.op("dve", lambda e, kc=kc: e.tensor_scalar(out=f32out[:, kc, 0:n], in0=tmp[:, kc, 0:n],
                                                         scalar1=Bv[:, kc:kc + 1], scalar2=None, op0=ALU.add),
                 r=[tmp, Bv], w=[f32out])


def build_B(T, tiles, moe, P=None, prefix="", bind=None, ytm=False):
    P, own = _begin(P, prefix, bind)
    C = Ctx(P, nw=5)
    NMAX = max(n for _, n, _ in tiles)
    hT_d = P.dram("hT", [D, T], F32, "ExternalInput")
    if ytm:
        ytm_d = [[P.dram("y%s%d" % (nm, s_), [T, 512], F32, "ExternalInput") for s_ in range(2)] for nm in ("hy", "ssd")]
        yda_d = [P.dram("yda%d" % s_, [512, T], F32, "ExternalInput") for s_ in range(2)]
    else:
        yT_d = P.dram("yT", [3, D, T], F32, "ExternalInput")
    vec_d = P.dram("vecs", [128, 112], F32, "ExternalInput")
    wg_d = P.dram("wg", [D, 3 * D], F32, "ExternalInput")
    wbr_d = P.dram("wbr", [3, D, D], F32, "ExternalInput")
    wo_d = P.dram("wo", [D, D], F32, "ExternalInput")
    if moe:
        NE, FF = 8, 2048
        rw_d = P.dram("rw", [D, NE], F32, "ExternalInput")
        w1_d = P.dram("w1", [NE, D, FF], F32, "ExternalInput")
        w3_d = P.dram("w3", [NE, D, FF], F32, "ExternalInput")
        w2_d = P.dram("w2", [NE, FF, D], F32, "ExternalInput")
    else:
        NE, FF = 1, 4096
        w1_d = P.dram("w1", [NE, D, FF], F32, "ExternalInput")
        w3_d = P.dram("w3", [NE, D, FF], F32, "ExternalInput")
        w2_d = P.dram("w2", [NE, FF, D], F32, "ExternalInput")
    out_d = P.dram("hout", [D, T], F32, "ExternalOutput")
    FC = FF // 128

    vec = P.sb([128, 112], F32, "vec")
    der = P.sb([128, 2, 2, 8], F32, "der")
    ones_bf = P.sb([128, 128], BF16, "ones")
    hT = P.sb([128, KC, NMAX], F32, "hTs")
    f1 = P.sb([128, KC, NMAX], BF16, "f1")
    sqb = P.sb([128, KC, NMAX], BF16, "sqb")
    tmp = P.sb([128, KC, NMAX], F32, "tmp")
    rstd = P.sb([128, NMAX], F32, "rstd")
    yb = P.sb([128, 3, KC, NMAX], BF16, "yb")
    merged = P.sb([128, KC, NMAX], BF16, "merged")
    acc = P.sb([128, 4, NMAX], F32, "acc")
    sig = P.sb([128, NMAX], F32, "sig")
    tm2 = P.sb([128, NMAX], F32, "tm2")
    gT = P.sb([128, FC, NMAX], BF16, "gT")
    if moe or ytm:
        id_d = P.dram("ident", [128, 128], F32, "ExternalInput")
        ident = P.sb([128, 128], F32, "idents")
        P.dma("sp", ident[:], id_d[:], r=[id_d], w=[ident])
    if ytm:
        yst_ = [P.sb([128, 1024], F32, "ytmst%d" % q) for q in range(2)]
    if moe:
        f32T = P.sb([128, KC, NMAX], F32, "f32T")
        rw = P.sb([128, KC, NE], F32, "rws")
        lg = P.sb([128, 4, NE], F32, "lg")
        gt = P.sb([128, 4, NE], F32, "gt")
        mk = P.sb([128, 4, NE], F32, "mk")
        m12 = P.sb([128, 4, 4], F32, "m12")
        gtT = P.sb([NE, NMAX], F32, "gtT")
        sel = P.sb([NE, NE, 128], F32, "sel")
        gbc = P.sb([128, NMAX], F32, "gbc")
        oacc = P.sb([128, KC, NMAX], F32, "oacc")

    P.dma("sp", vec[:], vec_d[:], r=[vec_d], w=[vec])
    P.op("dve", lambda e: e.memset(ones_bf[:], 1.0), w=[ones_bf])
    for cls in range(2):
        for which, (gcol, sccol) in enumerate(((0, 16 + 8 + cls * 48), (8, 16 + 32 + cls * 48))):
            P.op("dve", lambda e, cls=cls, which=which, gcol=gcol, sccol=sccol: e.scalar_tensor_tensor(
                out=der[:, cls, which, :], in0=vec[:, sccol:sccol + 8], scalar=1.0, in1=vec[:, gcol:gcol + 8],
                op0=ALU.add, op1=ALU.mult), r=[vec], w=[der])
    if moe:
        P.dma("sp", rw[:], rw_d[:, :].rearrange("(k p) e -> p k e", p=128), r=[rw_d], w=[rw])
        for e_ in range(NE):
            P.op("dve", lambda e, e_=e_: e.tensor_copy(out=sel[:, e_, :], in_=ident[0:NE, e_:e_ + 1].to_broadcast([NE, 128])),
                 r=[ident], w=[sel])

    def do_tok(t0, n, cls):
        mo = 16 + cls * 48
        P.dma("sp", hT[:, :, 0:n], hT_d[:, t0:t0 + n].rearrange("(k p) t -> p k t", p=128), r=[hT_d], w=[hT])
        if ytm:
            qn_ = 0
            for i in range(2):
                for sb_ in range(n // 128):
                    st_ = yst_[qn_ % 2]
                    qn_ += 1
                    r0 = t0 + sb_ * 128
                    for s_ in range(2):
                        P.dma("sp", st_[:, s_ * 512:(s_ + 1) * 512], ytm_d[i][s_][r0:r0 + 128, :], r=[ytm_d[i][s_]], w=[st_])
                    for kc in range(KC):
                        ps = C.ps()
                        P.op("pe", lambda e, ps=ps, st_=st_, kc=kc: e.transpose(out=ps[:, 0:128], in_=st_[:, kc * 128:(kc + 1) * 128], identity=ident[:]),
                             r=[st_, ident], w=[ps])
                        if kc % 2:
                            P.op("dve", lambda e, ps=ps, i=i, kc=kc, sb_=sb_: e.tensor_copy(out=yb[:, i, kc, sb_ * 128:(sb_ + 1) * 128], in_=ps[:, 0:128]),
                                 r=[ps], w=[yb])
                        else:
                            P.op("act", lambda e, ps=ps, i=i, kc=kc, sb_=sb_: e.activation(out=yb[:, i, kc, sb_ * 128:(sb_ + 1) * 128], in_=ps[:, 0:128], func=AF.Copy),
                                 r=[ps], w=[yb])
            for s_ in range(2):
                P.dma("pool", yb[:, 2, s_ * 4:(s_ + 1) * 4, 0:n], yda_d[s_][:, t0:t0 + n].rearrange("(k p) t -> p k t", p=128),
                      r=[yda_d[s_]], w=[yb])
        else:
            for i in range(3):
                P.dma("pool", yb[:, i, :, 0:n], yT_d[i, :, t0:t0 + n].rearrange("(k p) t -> p k t", p=128), r=[yT_d], w=[yb])
        modnorm_v(C, hT, n, der, (cls, 0), vec, mo + 0, f1, ones_bf, sqb, tmp, rstd)
        for dg in range(2):
            for i in range(3):
                wbb, wbv = C.wload(wbr_d, wview(wbr_d[i], 0, KC, dg * 512, 512), KC, 512)
                wgb, wgv = C.wload(wg_d, wview(wg_d, 0, KC, i * D + dg * 512, 512), KC, 512)
                for j in range(4):
                    psA = C.ps(); psB = C.ps()
                    for kc in range(KC):
                        P.op("pe", lambda e, kc=kc, j=j, wbv=wbv, i=i, psA=psA: e.matmul(
                            psA[:, 0:n], lhsT=wbv[:, kc, j * 128:(j + 1) * 128], rhs=yb[:, i, kc, 0:n],
                            start=(kc == 0), stop=(kc == KC - 1)), r=[wbb, yb], w=[psA])
                    for kc in range(KC):
                        P.op("pe", lambda e, kc=kc, j=j, wgv=wgv, psB=psB: e.matmul(
                            psB[:, 0:n], lhsT=wgv[:, kc, j * 128:(j + 1) * 128], rhs=f1[:, kc, 0:n],
                            start=(kc == 0), stop=(kc == KC - 1)), r=[wgb, f1], w=[psB])
                    P.op("act", lambda e, psB=psB: e.activation(out=sig[:, 0:n], in_=psB[:, 0:n], func=AF.Sigmoid),
                         r=[psB], w=[sig])
                    if i == 0:
                        P.op("dve", lambda e, psA=psA, j=j: e.tensor_tensor(out=acc[:, j, 0:n], in0=psA[:, 0:n], in1=sig[:, 0:n],
                                                                    op=ALU.mult), r=[psA, sig], w=[acc])
                    else:
                        P.op("dve", lambda e, psA=psA: e.tensor_tensor(out=tm2[:, 0:n], in0=psA[:, 0:n], in1=sig[:, 0:n],
                                                               op=ALU.mult), r=[psA, sig], w=[tm2])
                        if i == 1:
                            P.op("dve", lambda e, j=j: e.tensor_tensor(out=acc[:, j, 0:n], in0=acc[:, j, 0:n], in1=tm2[:, 0:n],
                                                                       op=ALU.add), r=[acc, tm2], w=[acc])
                        else:
                            P.op("dve", lambda e, j=j, dg=dg: e.tensor_tensor(out=merged[:, dg * 4 + j, 0:n], in0=acc[:, j, 0:n],
                                                                       in1=tm2[:, 0:n], op=ALU.add), r=[acc, tm2], w=[merged])
        for dg in range(2):
            wob, wov = C.wload(wo_d, wview(wo_d, 0, KC, dg * 512, 512), KC, 512)
            for j in range(4):
                db = dg * 4 + j
                ps = C.ps()
                for kc in range(KC):
                    P.op("pe", lambda e, kc=kc, j=j, wov=wov, ps=ps: e.matmul(
                        ps[:, 0:n], lhsT=wov[:, kc, j * 128:(j + 1) * 128], rhs=merged[:, kc, 0:n],
                        start=(kc == 0), stop=(kc == KC - 1)), r=[wob, merged], w=[ps])
                P.op("dve", lambda e, ps=ps, db=db, mo=mo: e.scalar_tensor_tensor(
                    out=hT[:, db, 0:n], in0=ps[:, 0:n], scalar=vec[:, mo + 16 + db:mo + 16 + db + 1], in1=hT[:, db, 0:n],
                    op0=ALU.mult, op1=ALU.add), r=[ps, vec, hT], w=[hT])
        modnorm_v(C, hT, n, der, (cls, 1), vec, mo + 24, f1, ones_bf, sqb, tmp, rstd, f32out=(f32T if moe else None))
        if moe:
            nsub = n // 128
            for sb_ in range(nsub):
                ps = C.ps()
                for kc in range(KC):
                    P.op("pe", lambda e, kc=kc, sb_=sb_, ps=ps: e.matmul(
                        ps[:, 0:NE], lhsT=f32T[:, kc, sb_ * 128:(sb_ + 1) * 128], rhs=rw[:, kc, :],
                        start=(kc == 0), stop=(kc == KC - 1)), r=[f32T, rw], w=[ps])
                P.op("dve", lambda e, ps=ps, sb_=sb_: e.tensor_copy(out=lg[:, sb_, :], in_=ps[:, 0:NE]), r=[ps], w=[lg])
            for sb_ in range(nsub):
                P.op("dve", lambda e, sb_=sb_: e.reduce_max(out=m12[:, sb_, 0:1], in_=lg[:, sb_, :], axis=AX.X), r=[lg], w=[m12])
                P.op("dve", lambda e, sb_=sb_: e.tensor_scalar(out=mk[:, sb_, :], in0=lg[:, sb_, :], scalar1=m12[:, sb_, 0:1],
                                                               scalar2=None, op0=ALU.is_equal), r=[lg, m12], w=[mk])
                P.op("dve", lambda e, sb_=sb_: e.scalar_tensor_tensor(out=gt[:, sb_, :], in0=mk[:, sb_, :], scalar=-1e30,
                                                                      in1=lg[:, sb_, :], op0=ALU.mult, op1=ALU.add),
                     r=[mk, lg], w=[gt])
                P.op("dve", lambda e, sb_=sb_: e.reduce_max(out=m12[:, sb_, 1:2], in_=gt[:, sb_, :], axis=AX.X), r=[gt], w=[m12])
                P.op("dve", lambda e, sb_=sb_: e.tensor_tensor(out=m12[:, sb_, 2:3], in0=m12[:, sb_, 0:1], in1=m12[:, sb_, 1:2],
                                                               op=ALU.subtract), r=[m12], w=[m12])
                P.op("act", lambda e, sb_=sb_: e.activation(out=m12[:, sb_, 3:4], in_=m12[:, sb_, 2:3], func=AF.Sigmoid, scale=-1.0),
                     r=[m12], w=[m12])
                P.op("act", lambda e, sb_=sb_: e.activation(out=m12[:, sb_, 2:3], in_=m12[:, sb_, 2:3], func=AF.Sigmoid),
                     r=[m12], w=[m12])
                P.op("dve", lambda e, sb_=sb_: e.tensor_scalar(out=gt[:, sb_, :], in0=gt[:, sb_, :], scalar1=m12[:, sb_, 1:2],
                                                               scalar2=m12[:, sb_, 3:4], op0=ALU.is_equal, op1=ALU.mult),
                     r=[gt, m12], w=[gt])
                P.op("dve", lambda e, sb_=sb_: e.scalar_tensor_tensor(out=gt[:, sb_, :], in0=mk[:, sb_, :], scalar=m12[:, sb_, 2:3],
                                                                      in1=gt[:, sb_, :], op0=ALU.mult, op1=ALU.add),
                     r=[mk, m12, gt], w=[gt])
                ps = C.ps()
                P.op("pe", lambda e, sb_=sb_, ps=ps: e.transpose(out=ps[0:NE, 0:128], in_=gt[:, sb_, :], identity=ident[:]),
                     r=[gt, ident], w=[ps])
                P.op("dve", lambda e, sb_=sb_, ps=ps: e.tensor_copy(out=gtT[:, sb_ * 128:(sb_ + 1) * 128], in_=ps[0:NE, 0:128]),
                     r=[ps], w=[gtT])
        for ex in range(NE):
            if moe:
                ps = C.ps()
                P.op("pe", lambda e, ex=ex, ps=ps: e.matmul(ps[:, 0:n], lhsT=sel[:, ex, :], rhs=gtT[:, 0:n], start=True, stop=True),
                     r=[sel, gtT], w=[ps])
                P.op("act", lambda e, ps=ps: e.activation(out=gbc[:, 0:n], in_=ps[:, 0:n], func=AF.Copy), r=[ps], w=[gbc])
            for fg in range(FF // 512):
                w1b, w1v = C.wload(w1_d, wview(w1_d[ex], 0, KC, fg * 512, 512), KC, 512)
                w3b, w3v = C.wload(w3_d, wview(w3_d[ex], 0, KC, fg * 512, 512), KC, 512)
                for j in range(4):
                    psa = C.ps(); psb = C.ps()
                    for kc in range(KC):
                        P.op("pe", lambda e, kc=kc, j=j, w1v=w1v, psa=psa: e.matmul(
                            psa[:, 0:n], lhsT=w1v[:, kc, j * 128:(j + 1) * 128], rhs=f1[:, kc, 0:n],
                            start=(kc == 0), stop=(kc == KC - 1)), r=[w1b, f1], w=[psa])
                    for kc in range(KC):
                        P.op("pe", lambda e, kc=kc, j=j, w3v=w3v, psb=psb: e.matmul(
                            psb[:, 0:n], lhsT=w3v[:, kc, j * 128:(j + 1) * 128], rhs=f1[:, kc, 0:n],
                            start=(kc == 0), stop=(kc == KC - 1)), r=[w3b, f1], w=[psb])
                    P.op("act", lambda e, psa=psa: e.activation(out=sig[:, 0:n], in_=psa[:, 0:n], func=AF.Silu), r=[psa], w=[sig])
                    if moe:
                        P.op("dve", lambda e: e.tensor_tensor(out=sig[:, 0:n], in0=sig[:, 0:n], in1=gbc[:, 0:n], op=ALU.mult),
                             r=[sig, gbc], w=[sig])
                    P.op("dve", lambda e, psb=psb, fg=fg, j=j: e.tensor_tensor(out=gT[:, fg * 4 + j, 0:n], in0=psb[:, 0:n],
                                                                        in1=sig[:, 0:n], op=ALU.mult), r=[psb, sig], w=[gT])
            for db in range(KC):
                ps = C.ps()
                w2b, w2v = C.wload(w2_d, wview(w2_d[ex], 0, FC, db * 128, 128), FC, 128)
                for fc in range(FC):
                    P.op("pe", lambda e, fc=fc, w2v=w2v, ps=ps: e.matmul(
                        ps[:, 0:n], lhsT=w2v[:, fc, :], rhs=gT[:, fc, 0:n],
                        start=(fc == 0), stop=(fc == FC - 1)), r=[w2b, gT], w=[ps])
                if not moe:
                    P.op("dve", lambda e, ps=ps, db=db, mo=mo: e.scalar_tensor_tensor(
                        out=hT[:, db, 0:n], in0=ps[:, 0:n], scalar=vec[:, mo + 40 + db:mo + 40 + db + 1], in1=hT[:, db, 0:n],
                        op0=ALU.mult, op1=ALU.add), r=[ps, vec, hT], w=[hT])
                elif ex == 0:
                    P.op("dve", lambda e, ps=ps, db=db: e.tensor_copy(out=oacc[:, db, 0:n], in_=ps[:, 0:n]), r=[ps], w=[oacc])
                else:
                    P.op("dve", lambda e, ps=ps, db=db: e.tensor_tensor(out=oacc[:, db, 0:n], in0=oacc[:, db, 0:n], in1=ps[:, 0:n],
                                                                 op=ALU.add), r=[ps, oacc], w=[oacc])
        if moe:
            for db in range(KC):
                P.op("dve", lambda e, db=db, mo=mo: e.scalar_tensor_tensor(
                    out=hT[:, db, 0:n], in0=oacc[:, db, 0:n], scalar=vec[:, mo + 40 + db:mo + 40 + db + 1], in1=hT[:, db, 0:n],
                    op0=ALU.mult, op1=ALU.add), r=[oacc, vec, hT], w=[hT])
        P.dma("sp", out_d[:, t0:t0 + n].rearrange("(k p) t -> p k t", p=128), hT[:, :, 0:n], r=[hT], w=[out_d])
    for tl in tiles:
        do_tok(*tl)
    return _finish(P, own, [out_d])


def modnorm_v(C, hT, n, der, idx, vec, bcol, fT, ones_bf, sqb, tmp, rstd, f32out=None):
    cls, which = idx

    class _V:
        pass
    A = Buf(der.t[:, cls, which, :], "A")
    A.lw, A.rd = der.lw, der.rd
    Bv = Buf(vec.t[:, bcol:bcol + 8], "Bv")
    Bv.lw, Bv.rd = vec.lw, vec.rd
    modnorm(C, hT, n, A, Bv, fT, ones_bf, sqb, tmp, rstd, f32out=f32out)


NTOK = 4352
NCTX = 256


def build_attn(layer, need_ctx, nheads=4, nqb=8, P=None, prefix="", bind=None):
    import math
    lam_init = 0.8 - 0.6 * math.exp(-0.3 * (layer + 1))
    P, own = _begin(P, prefix, bind)
    C = Ctx(P, nrot=4, wsize=1024, nw=6)
    hT_d = P.dram("hT", [D, NTOK], F32, "ExternalInput")
    vec_d = P.dram("vecs", [128, 112], F32, "ExternalInput")
    wq_d = P.dram("wq", [D, 512], F32, "ExternalInput")
    wk_d = P.dram("wk", [D, 512], F32, "ExternalInput")
    wv_d = P.dram("wv", [D, 512], F32, "ExternalInput")
    qkg_d = P.dram("qkg", [128, 2], F32, "ExternalInput")
    lam_d = P.dram("lamv", [1, 256], F32, "ExternalInput")
    sg_d = P.dram("sublng", [128, 128], F32, "ExternalInput")
    cos_d = P.dram("cosT", [128, 4096], F32, "ExternalInput")
    sin_d = P.dram("sinT", [128, 4096], F32, "ExternalInput")
    rm_d = P.dram("rmT", [128, 128], F32, "ExternalInput")
    bo_d = P.dram("blockones", [128, 128], F32, "ExternalInput")
    id_d = P.dram("ident", [128, 128], F32, "ExternalInput")
    out_d = P.dram("yT", [512, NTOK], F32, "ExternalOutput")

    vec = P.sb([128, 112], F32, "vec")
    der = P.sb([128, 2, 2, 8], F32, "der")
    ones_bf = P.sb([128, 128], BF16, "ones")
    ones_f = P.sb([1, 128], F32, "ones_f")
    f1 = P.sb([128, KC, NTOK], BF16, "f1")
    NT = 256
    hTt = P.sb([128, KC, NT], F32, "hTt")
    sqb = P.sb([128, KC, NT], BF16, "sqb")
    tmp = P.sb([128, KC, NT], F32, "tmp")
    rstd = P.sb([128, 512], F32, "rstd")
    qkg = P.sb([128, 2], F32, "qkgs")
    lamv = P.sb([1, 256], F32, "lamvs")
    lams = P.sb([1, 8], F32, "lams")
    neglam = P.sb([128, 1], F32, "neglam")
    sg = P.sb([128, 128], F32, "sg")
    cosT = P.sb([128, 4096], F32, "cosTs")
    sinT = P.sb([128, 4096], F32, "sinTs")
    rmT = P.sb([128, 128], F32, "rmTs")
    bo = P.sb([128, 128], BF16, "bos")
    ident = P.sb([128, 128], F32, "idents")
    qT = P.sb([128, NTOK], BF16, "qT")
    kT = P.sb([128, NTOK], BF16, "kT")
    Vaug = P.sb([128, 34, 130], BF16, "Vaug")
    sqq_ = [P.sb([128, 512], BF16, "sqq%d" % i) for i in range(2)]
    qn_ = [P.sb([128, 512], F32, "qn%d" % i) for i in range(2)]
    t1_ = [P.sb([128, 512], F32, "t1%d" % i) for i in range(2)]
    t2_ = [P.sb([128, 512], F32, "t2%d" % i) for i in range(2)]
    rs_ = [P.sb([128, 512], F32, "rs%d" % i) for i in range(2)]
    bcnt = [0]
    PT = [P.sb([128, 512], BF16, "PT%d" % i) for i in range(3)]
    o0 = P.sb([128, 4, 128], F32, "o0")
    oo = P.sb([128, 4, 128], F32, "oo")
    junk = P.sb([128, 128], F32, "junk")
    rc = P.sb([128, 8], F32, "rc")
    yq = P.sb([128, 128], F32, "yq")
    yst = P.sb([128, 512], F32, "yst")

    P.dma("sp", vec[:], vec_d[:], r=[vec_d], w=[vec])
    P.dma("sp", qkg[:], qkg_d[:], r=[qkg_d], w=[qkg])
    P.dma("sp", lamv[:], lam_d[:], r=[lam_d], w=[lamv])
    P.dma("sp", sg[:], sg_d[:], r=[sg_d], w=[sg])
    P.dma("sp", cosT[:], cos_d[:], r=[cos_d], w=[cosT])
    P.dma("sp", sinT[:], sin_d[:], r=[sin_d], w=[sinT])
    P.dma("sp", rmT[:], rm_d[:], r=[rm_d], w=[rmT])
    P.dma("pool", bo[:], bo_d[:], r=[bo_d], w=[bo])
    P.dma("sp", ident[:], id_d[:], r=[id_d], w=[ident])
    P.op("dve", lambda e: e.memset(ones_bf[:], 1.0), w=[ones_bf])
    P.op("dve", lambda e: e.memset(ones_f[:], 1.0), w=[ones_f])
    P.op("dve", lambda e: e.memset(Vaug[:, :, 128:130], 1.0), w=[Vaug])
    for cls in range(2):
        sccol = 16 + 8 + cls * 48
        P.op("dve", lambda e, cls=cls, sccol=sccol: e.scalar_tensor_tensor(
            out=der[:, cls, 0, :], in0=vec[:, sccol:sccol + 8], scalar=1.0, in1=vec[:, 0:8],
            op0=ALU.add, op1=ALU.mult), r=[vec], w=[der])
    P.op("dve", lambda e: e.tensor_scalar(out=sg[:], in0=sg[:], scalar1=float(1.0 - lam_init), scalar2=None, op0=ALU.mult),
         r=[sg], w=[sg])
    P.op("dve", lambda e: e.tensor_tensor(out=lamv[:, 0:64], in0=lamv[:, 0:64], in1=lamv[:, 64:128], op=ALU.mult), r=[lamv], w=[lamv])
    P.op("dve", lambda e: e.tensor_tensor(out=lamv[:, 128:192], in0=lamv[:, 128:192], in1=lamv[:, 192:256], op=ALU.mult), r=[lamv], w=[lamv])
    P.op("dve", lambda e: e.reduce_sum(out=lams[:, 0:1], in_=lamv[:, 0:64], axis=AX.X), r=[lamv], w=[lams])
    P.op("dve", lambda e: e.reduce_sum(out=lams[:, 1:2], in_=lamv[:, 128:192], axis=AX.X), r=[lamv], w=[lams])
    P.op("act", lambda e: e.activation(out=lams[:, 2:4], in_=lams[:, 0:2], func=AF.Exp), r=[lams], w=[lams])
    P.op("dve", lambda e: e.tensor_tensor(out=lams[:, 4:5], in0=lams[:, 3:4], in1=lams[:, 2:3], op=ALU.subtract), r=[lams], w=[lams])
    P.op("dve", lambda e: e.tensor_scalar(out=lams[:, 4:5], in0=lams[:, 4:5], scalar1=float(-lam_init), scalar2=None, op0=ALU.add),
         r=[lams], w=[lams])
    ps = C.ps()
    P.op("pe", lambda e, ps=ps: e.matmul(ps[:, 0:1], lhsT=ones_f[:, :], rhs=lams[:, 4:5], start=True, stop=True), r=[ones_f, lams], w=[ps])
    P.op("dve", lambda e, ps=ps: e.tensor_copy(out=neglam[:], in_=ps[:, 0:1]), r=[ps], w=[neglam])

    f1_pre = P.bind.get("f1")
    if f1_pre is not None:
        f1pv = f1_pre[:, :].rearrange("p (k t) -> p k t", k=KC)
        P.dma("sp", f1[:, :, 0:NCTX], f1pv[:, :, 1:1 + NCTX], r=[f1_pre], w=[f1])
        P.dma("sp", f1[:, :, NCTX:NTOK], f1pv[:, :, 3 + NCTX:3 + NTOK], r=[f1_pre], w=[f1])
    for t0 in (range(0, NTOK, NT) if f1_pre is None else ()):
        cls = 1 if t0 < NCTX else 0
        mo = 16 + cls * 48
        P.dma("sp", hTt[:, :, :], hT_d[:, t0:t0 + NT].rearrange("(k p) t -> p k t", p=128), r=[hT_d], w=[hTt])
        f1v = Buf(f1.t[:, :, t0:t0 + NT], "f1v")
        f1v.lw, f1v.rd = f1.lw, f1.rd
        modnorm_v(C, hTt, NT, der, (cls, 0), vec, mo + 0, f1v, ones_bf, sqb, tmp, rstd)
        f1.lw = f1v.lw
    blocks = [(0, 256)] + [(256 + i * 512, 512) for i in range(8)]
    for h in range(nheads):
        wqb, wqv = C.wload(wq_d, wview(wq_d, 0, KC, h * 128, 128), KC, 128)
        wkb, wkv = C.wload(wk_d, wview(wk_d, 0, KC, h * 128, 128), KC, 128)
        wvb, wvv = C.wload(wv_d, wview(wv_d, 0, KC, h * 128, 128), KC, 128)
        for (dst, wb_, wv_, gcol, isq) in ((qT, wqb, wqv, 0, True), (kT, wkb, wkv, 1, False)):
            for (t0, n) in blocks:
                if isq and (not need_ctx) and t0 < NCTX:
                    continue
                bi_ = bcnt[0] % 2
                bcnt[0] += 1
                sqq, qn, t1, t2, rstd = sqq_[bi_], qn_[bi_], t1_[bi_], t2_[bi_], rs_[bi_]
                ps = C.ps()
                for kc in range(KC):
                    P.op("pe", lambda e, sqq=sqq, qn=qn, t1=t1, t2=t2, rstd=rstd, kc=kc, ps=ps, wv_=wv_, t0=t0, n=n: e.matmul(
                        ps[:, 0:n], lhsT=wv_[:, kc, :], rhs=f1[:, kc, t0:t0 + n], start=(kc == 0), stop=(kc == KC - 1)),
                        r=[wb_, f1], w=[ps])
                P.op("act", lambda e, sqq=sqq, qn=qn, t1=t1, t2=t2, rstd=rstd, ps=ps, n=n: e.activation(out=sqq[:, 0:n], in_=ps[:, 0:n], func=AF.Square), r=[ps], w=[sqq])
                ps2 = C.ps()
                P.op("pe", lambda e, sqq=sqq, qn=qn, t1=t1, t2=t2, rstd=rstd, ps2=ps2, n=n: e.matmul(ps2[:, 0:n], lhsT=bo[:], rhs=sqq[:, 0:n], start=True, stop=True),
                     r=[bo, sqq], w=[ps2])
                P.op("act", lambda e, sqq=sqq, qn=qn, t1=t1, t2=t2, rstd=rstd, ps2=ps2, n=n: e.activation(out=rstd[:, 0:n], in_=ps2[:, 0:n], func=AF.Sqrt, bias=EPS, scale=1.0 / 64),
                     r=[ps2], w=[rstd])
                P.op("dve", lambda e, sqq=sqq, qn=qn, t1=t1, t2=t2, rstd=rstd, n=n: e.reciprocal(out=rstd[:, 0:n], in_=rstd[:, 0:n]), r=[rstd], w=[rstd])
                if t0 < NCTX:
                    P.op("dve", lambda e, sqq=sqq, qn=qn, t1=t1, t2=t2, rstd=rstd, ps=ps, n=n, gcol=gcol, dst=dst, t0=t0: e.scalar_tensor_tensor(
                        out=dst[:, t0:t0 + n], in0=ps[:, 0:n], scalar=qkg[:, gcol:gcol + 1], in1=rstd[:, 0:n],
                        op0=ALU.mult, op1=ALU.mult), r=[ps, qkg, rstd], w=[dst])
                    continue
                P.op("dve", lambda e, sqq=sqq, qn=qn, t1=t1, t2=t2, rstd=rstd, ps=ps, n=n, gcol=gcol: e.scalar_tensor_tensor(
                    out=qn[:, 0:n], in0=ps[:, 0:n], scalar=qkg[:, gcol:gcol + 1], in1=rstd[:, 0:n],
                    op0=ALU.mult, op1=ALU.mult), r=[ps, qkg, rstd], w=[qn])
                ps3 = C.ps()
                P.op("pe", lambda e, sqq=sqq, qn=qn, t1=t1, t2=t2, rstd=rstd, ps3=ps3, n=n: e.matmul(ps3[:, 0:n], lhsT=rmT[:], rhs=qn[:, 0:n], start=True, stop=True),
                     r=[rmT, qn], w=[ps3])
                lp = t0 - NCTX
                P.op("pool", lambda e, sqq=sqq, qn=qn, t1=t1, t2=t2, rstd=rstd, n=n, lp=lp: e.tensor_tensor(out=t1[:, 0:n], in0=qn[:, 0:n], in1=cosT[:, lp:lp + n], op=ALU.mult),
                     r=[qn, cosT], w=[t1])
                P.op("dve", lambda e, sqq=sqq, qn=qn, t1=t1, t2=t2, rstd=rstd, ps3=ps3, n=n, lp=lp: e.tensor_tensor(out=t2[:, 0:n], in0=ps3[:, 0:n], in1=sinT[:, lp:lp + n], op=ALU.mult),
                     r=[ps3, sinT], w=[t2])
                P.op("dve", lambda e, sqq=sqq, qn=qn, t1=t1, t2=t2, rstd=rstd, n=n, dst=dst, t0=t0: e.tensor_tensor(out=dst[:, t0:t0 + n], in0=t1[:, 0:n], in1=t2[:, 0:n], op=ALU.add),
                     r=[t1, t2], w=[dst])
        for kt in range(34):
            ps = C.ps()
            for kc in range(KC):
                P.op("pe", lambda e, kc=kc, ps=ps, kt=kt, wvv=wvv: e.matmul(
                    ps[:, 0:128], lhsT=f1[:, kc, kt * 128:(kt + 1) * 128], rhs=wvv[:, kc, :], start=(kc == 0), stop=(kc == KC - 1)),
                    r=[f1, wvb], w=[ps])
            eng = C.ev()
            if eng == "dve":
                P.op("dve", lambda e, ps=ps, kt=kt: e.tensor_copy(out=Vaug[:, kt, 0:128], in_=ps[:, 0:128]), r=[ps], w=[Vaug])
            else:
                P.op("act", lambda e, ps=ps, kt=kt: e.activation(out=Vaug[:, kt, 0:128], in_=ps[:, 0:128], func=AF.Copy), r=[ps], w=[Vaug])
        qblocks = [(256 + i * 512, 512, list(range(34))) for i in range(nqb)]
        if need_ctx:
            qblocks = [(0, 256, [0, 1])] + qblocks
        pti = 0
        for (q0, nq, kts) in qblocks:
            nqs = nq // 128
            for comp in range(2):
                r0 = comp * 64
                def score(kt, r0=r0, q0=q0, nq=nq):
                    ps = C.ps()
                    P.op("pe", lambda e, ps=ps, kt=kt, r0=r0, q0=q0, nq=nq: e.matmul(
                        ps[:, 0:nq], lhsT=kT[r0:r0 + 64, kt * 128:(kt + 1) * 128], rhs=qT[r0:r0 + 64, q0:q0 + nq],
                        start=True, stop=True), r=[kT, qT], w=[ps])
                    return ps
                pend = [score(kts[0])]
                if len(kts) > 1:
                    pend.append(score(kts[1]))
                for ki, kt in enumerate(kts):
                    ps = pend.pop(0)
                    if ki + 2 < len(kts):
                        pend.append(score(kts[ki + 2]))
                    pt = PT[pti % 3]
                    pti += 1
                    P.op("act", lambda e, ps=ps, pt=pt, nq=nq: e.activation(out=pt[:, 0:nq], in_=ps[:, 0:nq], func=AF.Exp, scale=0.125),
                         r=[ps], w=[pt])
                    for qs in range(nqs):
                        acc = C.accs[qs]
                        P.op("pe", lambda e, acc=acc, pt=pt, qs=qs, kt=kt, ki=ki, last=(ki == len(kts) - 1): e.matmul(
                            acc[:, 0:129], lhsT=pt[:, qs * 128:(qs + 1) * 128], rhs=Vaug[:, kt, 0:129],
                            start=(ki == 0), stop=last), r=[pt, Vaug], w=[acc])
                for qs in range(nqs):
                    acc = C.accs[qs]
                    P.op("dve", lambda e, acc=acc, qs=qs, comp=comp: e.reciprocal(out=rc[:, comp * 4 + qs:comp * 4 + qs + 1], in_=acc[:, 128:129]),
                         r=[acc], w=[rc])
                    if comp == 0:
                        P.op("dve", lambda e, acc=acc, qs=qs: e.tensor_scalar(out=o0[:, qs, :], in0=acc[:, 0:128], scalar1=rc[:, qs:qs + 1],
                                                                       scalar2=None, op0=ALU.mult), r=[acc, rc], w=[o0])
                    else:
                        P.op("dve", lambda e, qs=qs: e.tensor_tensor(out=rc[:, 4 + qs:5 + qs], in0=rc[:, 4 + qs:5 + qs], in1=neglam[:, 0:1],
                                                                     op=ALU.mult), r=[rc, neglam], w=[rc])
                        P.op("dve", lambda e, acc=acc, qs=qs: e.scalar_tensor_tensor(
                            out=oo[:, qs, :], in0=acc[:, 0:128], scalar=rc[:, 4 + qs:5 + qs], in1=o0[:, qs, :],
                            op0=ALU.mult, op1=ALU.add), r=[acc, rc, o0], w=[oo])
            for qs in range(nqs):
                P.op("dve", lambda e, qs=qs: e.memset(rc[:, qs:qs + 1], 0.0), w=[rc])
                P.op("act", lambda e, qs=qs: e.activation(out=junk[:], in_=oo[:, qs, :], func=AF.Square, accum_out=rc[:, qs:qs + 1]),
                     r=[oo, rc], w=[junk, rc])
                P.op("act", lambda e, qs=qs: e.activation(out=rc[:, qs:qs + 1], in_=rc[:, qs:qs + 1], func=AF.Sqrt, bias=EPS, scale=1.0 / 128),
                     r=[rc], w=[rc])
                P.op("dve", lambda e, qs=qs: e.reciprocal(out=rc[:, qs:qs + 1], in_=rc[:, qs:qs + 1]), r=[rc], w=[rc])
                P.op("dve", lambda e, qs=qs: e.scalar_tensor_tensor(out=yq[:], in0=oo[:, qs, :], scalar=rc[:, qs:qs + 1], in1=sg[:],
                                                                    op0=ALU.mult, op1=ALU.mult), r=[oo, rc, sg], w=[yq])
                ps = C.ps()
                P.op("pe", lambda e, ps=ps: e.transpose(out=ps[:, 0:128], in_=yq[:], identity=ident[:]), r=[yq, ident], w=[ps])
                P.op("act", lambda e, ps=ps, qs=qs: e.activation(out=yst[:, qs * 128:(qs + 1) * 128], in_=ps[:, 0:128], func=AF.Copy),
                     r=[ps], w=[yst])
            P.dma("sp", out_d[h * 128:(h + 1) * 128, q0:q0 + nq], yst[:, 0:nq], r=[yst], w=[out_d])
    return _finish(P, own, [out_d])


def attn_consts():
    inv = (10000.0 ** (-np.arange(16, dtype=np.float32) / 16)).astype(np.float32)
    t = np.arange(4096)
    ang = np.stack([t // 64, t % 64], -1).astype(np.float32)[:, :, None] * inv
    cos, sin = np.cos(ang).astype(np.float32), np.sin(ang).astype(np.float32)
    cosT = np.zeros((128, 4096), np.float32)
    sinT = np.zeros((128, 4096), np.float32)
    rmT = np.zeros((128, 128), np.float32)
    for p in range(128):
        d = p % 64
        a, j, fr = d // 32, (d % 32) // 16, d % 16
        cosT[p] = cos[:, a, fr]
        sinT[p] = sin[:, a, fr]
        if j == 0:
            rmT[p + 16, p] = -1.0
        else:
            rmT[p - 16, p] = 1.0
    bo = np.zeros((128, 128), np.float32)
    bo[:64, :64] = 1.0
    bo[64:, 64:] = 1.0
    return dict(cosT=cosT, sinT=sinT, rmT=rmT, blockones=bo, ident=np.eye(128, dtype=np.float32))


def fpos(t):
    return 1 + t if t < NCTX else 3 + t


def build_pj(groups, P=None, prefix="", bind=None):
    P, own = _begin(P, prefix, bind)
    C = Ctx(P, nrot=8, wsize=4096, nw=4)
    hT_d = P.dram("hT", [D, NTOK], F32, "ExternalInput")
    vec_d = P.dram("vecs", [128, 112], F32, "ExternalInput")
    vec = P.sb([128, 112], F32, "vec")
    der = P.sb([128, 2, 2, 8], F32, "der")
    ones_bf = P.sb([128, 128], BF16, "ones")
    FW = NTOK + 4
    f1 = P.sb([128, KC, FW], BF16, "f1")
    NT = 256
    hTt = P.sb([128, KC, NT], F32, "hTt")
    sqb = P.sb([128, KC, NT], BF16, "sqb")
    tmp = P.sb([128, KC, NT], F32, "tmp")
    rstd = P.sb([128, 512], F32, "rstd")
    stage = [P.sb([128, 512], F32, "stage%d" % i) for i in range(3)]
    wst = P.sb([128, KC, 512], F32, "wst")
    cwb = P.sb([128, 512], F32, "cwb")
    P.dma("sp", vec[:], vec_d[:], r=[vec_d], w=[vec])
    P.op("dve", lambda e: e.memset(ones_bf[:], 1.0), w=[ones_bf])
    P.op("pool", lambda e: e.memset(f1[:], 0.0), w=[f1])
    for cls in range(2):
        sccol = 16 + 8 + cls * 48
        P.op("dve", lambda e, cls=cls, sccol=sccol: e.scalar_tensor_tensor(
            out=der[:, cls, 0, :], in0=vec[:, sccol:sccol + 8], scalar=1.0, in1=vec[:, 0:8],
            op0=ALU.add, op1=ALU.mult), r=[vec], w=[der])
    f1_pre = P.bind.get("f1")
    if f1_pre is not None:
        P.dma("sp", f1[:], f1_pre[:, :].rearrange("p (k t) -> p k t", k=KC), r=[f1_pre], w=[f1])
    for t0 in (range(0, NTOK, NT) if f1_pre is None else ()):
        cls = 1 if t0 < NCTX else 0
        mo = 16 + cls * 48
        P.dma("sp", hTt[:, :, :], hT_d[:, t0:t0 + NT].rearrange("(k p) t -> p k t", p=128), r=[hT_d], w=[hTt])
        c0 = fpos(t0)
        f1v = Buf(f1.t[:, :, c0:c0 + NT], "f1v")
        f1v.lw, f1v.rd = f1.lw, f1.rd
        modnorm_v(C, hTt, NT, der, (cls, 0), vec, mo + 0, f1v, ones_bf, sqb, tmp, rstd)
        f1.lw = f1v.lw
    if groups and groups[0].get("f1only"):
        f1o = P.dram("f1out", [128, KC * (NTOK + 4)], BF16, "ExternalOutput")
        P.dma("sp", f1o[:, :].rearrange("p (k t) -> p k t", k=KC), f1[:], r=[f1], w=[f1o])
        return _finish(P, own, [f1o])
    si = [0]
    for g in groups:
        name, ncol, conv, act, layout = g["name"], g["ncol"], g["conv"], g["act"], g["layout"]
        w_d = P.dram("w_" + name, [D, ncol], F32, "ExternalInput")
        ntap = 3 if conv else 1
        if conv:
            cw_d = P.dram("cw_" + name, [3, 128, ncol], F32, "ExternalInput")
        if layout == "tm":
            b_d = P.dram("b_" + name, [1, ncol], F32, "ExternalInput")
            out_d = P.dram("o_" + name, [NTOK, ncol], F32, "ExternalOutput")
            brow = P.sb([1, ncol], BF16, "brow_" + name)
            P.dma("pool", brow[:], b_d[:], r=[b_d], w=[brow])
        else:
            b_d = P.dram("b_" + name, [128, ncol // 128], F32, "ExternalInput")
            out_d = P.dram("o_" + name, [ncol, NTOK], F32, "ExternalOutput")
            bcol = P.sb([128, ncol // 128], F32, "bcol_" + name)
            P.dma("sp", bcol[:], b_d[:], r=[b_d], w=[bcol])
        wts = []
        P.dma("sp", wst[:, :, 0:ncol], wview(w_d, 0, KC, 0, ncol), r=[w_d], w=[wst])
        for j in range(ntap):
            wt = P.sb([128, KC, ncol], BF16, "wt_%s_%d" % (name, j))
            if conv:
                P.dma("sp", cwb[:, 0:ncol], cw_d[j], r=[cw_d], w=[cwb])
                P.op("dve", lambda e, wt=wt, ncol=ncol: e.tensor_tensor(
                    out=wt[:], in0=wst[:, :, 0:ncol], in1=cwb[:, 0:ncol].unsqueeze(1).to_broadcast([128, KC, ncol]), op=ALU.mult),
                    r=[wst, cwb], w=[wt])
            else:
                P.op("dve", lambda e, wt=wt, ncol=ncol: e.tensor_copy(out=wt[:], in_=wst[:, :, 0:ncol]), r=[wst], w=[wt])
            wts.append(wt)
        shifts = (-1, 0, 1) if conv else (0,)
        func = {None: AF.Copy, "silu": AF.Silu}[act]
        if layout == "tm":
            for t0 in range(0, NTOK, 128):
                ps = C.ps()
                c0 = fpos(t0)
                nmm = ntap * KC
                i = 0
                for j, sh in enumerate(shifts):
                    for kc in range(KC):
                        P.op("pe", lambda e, ps=ps, j=j, sh=sh, kc=kc, c0=c0, ncol=ncol, wts=wts, i=i: e.matmul(
                            ps[:, 0:ncol], lhsT=f1[:, kc, c0 + sh:c0 + sh + 128], rhs=wts[j][:, kc, :], start=(i == 0), stop=False),
                            r=[f1, wts[j]], w=[ps])
                        i += 1
                P.op("pe", lambda e, ps=ps, ncol=ncol, brow=brow: e.matmul(
                    ps[:, 0:ncol], lhsT=ones_bf[0:1, :], rhs=brow[0:1, :], start=False, stop=True), r=[ones_bf, brow], w=[ps])
                st = stage[si[0] % 3]
                si[0] += 1
                P.op("act", lambda e, ps=ps, st=st, ncol=ncol, func=func: e.activation(out=st[:, 0:ncol], in_=ps[:, 0:ncol], func=func),
                     r=[ps], w=[st])
                P.dma("sp", out_d[t0:t0 + 128, :], st[:, 0:ncol], r=[st], w=[out_d])
        else:
            blocks = [(0, 256)] + [(256 + i * 512, 512) for i in range(8)]
            for cb in range(ncol // 128):
                for (t0, n) in blocks:
                    ps = C.ps()
                    c0 = fpos(t0)
                    i = 0
                    for j, sh in enumerate(shifts):
                        for kc in range(KC):
                            P.op("pe", lambda e, ps=ps, j=j, sh=sh, kc=kc, c0=c0, n=n, cb=cb, wts=wts, i=i, last=(i == ntap * KC - 1): e.matmul(
                                ps[:, 0:n], lhsT=wts[j][:, kc, cb * 128:(cb + 1) * 128], rhs=f1[:, kc, c0 + sh:c0 + sh + n],
                                start=(i == 0), stop=last), r=[f1, wts[j]], w=[ps])
                            i += 1
                    st = stage[si[0] % 3]
                    si[0] += 1
                    P.op("act", lambda e, ps=ps, st=st, n=n, cb=cb, func=func, bcol=bcol: e.activation(
                        out=st[:, 0:n], in_=ps[:, 0:n], func=func, bias=bcol[:, cb:cb + 1], scale=1.0), r=[ps, bcol], w=[st])
                    P.dma("sp", out_d[cb * 128:(cb + 1) * 128, t0:t0 + n], st[:, 0:n], r=[st], w=[out_d])
        g["_out"] = out_d
    return _finish(P, own, [g["_out"] for g in groups])


def ssd_consts():
    k = np.arange(128)[:, None]
    l = np.arange(128)[None, :]
    return dict(triu=(k <= l).astype(np.float32), trius=(k < l).astype(np.float32),
                tril=(k >= l).astype(np.float32), trils=(k > l).astype(np.float32))


def build_ssd(need_ctx, P=None, prefix="", bind=None):
    P, own = _begin(P, prefix, bind)
    C = Ctx(P, nrot=8)
    NCH = 34
    xs_d = P.dram("xs", [NTOK, 512], F32, "ExternalInput")
    b_d = P.dram("btm", [NTOK, 128], F32, "ExternalInput")
    z_d = P.dram("z", [NTOK, 512], F32, "ExternalInput")
    dt_d = P.dram("dtraw", [NTOK, 16], F32, "ExternalInput")
    bt_d = P.dram("bT", [128, NTOK], F32, "ExternalInput")
    ct_d = P.dram("cT", [128, NTOK], F32, "ExternalInput")
    dtb_d = P.dram("dtb", [128, 16], F32, "ExternalInput")
    alog_d = P.dram("alog", [128, 16], F32, "ExternalInput")
    dsk_d = P.dram("dskip", [128, 8], F32, "ExternalInput")
    ng_d = P.dram("normg", [128, 512], F32, "ExternalInput")
    cm_d = {k: P.dram(k, [128, 128], F32, "ExternalInput") for k in ("triu", "trius", "tril", "trils")}
    out_d = P.dram("y", [NTOK, 512], F32, "ExternalOutput")

    xs = P.sb([128, NCH, 512], F32, "xss")
    btm = P.sb([128, NCH, 128], BF16, "btms")
    bT = P.sb([128, NTOK], BF16, "bTs")
    cT = P.sb([128, NTOK], BF16, "cTs")
    dt = P.sb([128, NCH, 16], F32, "dts")
    Aa = P.sb([128, NCH, 16], F32, "Aa")
    dtb = P.sb([128, 16], F32, "dtbs")
    aneg = P.sb([128, 16], F32, "aneg")
    dsk = P.sb([128, 8], F32, "dsks")
    ng = P.sb([128, 512], F32, "ngs")
    cm = {k: P.sb([128, 128], F32, k + "_sb") for k in cm_d}
    ones_f = P.sb([128, 128], F32, "ones_f")
    yf = P.sb([128, NCH, 512], BF16, "yf")
    S = P.sb([128, 512], F32, "S")
    Sb = P.sb([128, 512], BF16, "Sb")
    sc = P.sb([128, 16], F32, "sc")
    ex = P.sb([128, 24], F32, "ex")
    Xd = P.sb([128, 512], F32, "Xd")
    Xdb = P.sb([128, 512], BF16, "Xdb")
    Xw = P.sb([128, 512], BF16, "Xw")
    cbm = P.sb([128, 128], F32, "cbm")
    Am = P.sb([128, 8, 128], F32, "Am")
    es = P.sb([128, 8, 128], F32, "es")
    Mb = P.sb([128, 8, 128], BF16, "Mb")
    yt = P.sb([128, 512], F32, "yt")
    zt = P.sb([128, 512], F32, "zt")
    y2 = P.sb([128, 512], F32, "y2")
    junk = P.sb([128, 512], F32, "junk")
    r1 = P.sb([128, 2], F32, "r1")

    P.dma("sp", xs[:], xs_d[:, :].rearrange("(c p) f -> p c f", p=128), r=[xs_d], w=[xs])
    P.dma("pool", btm[:], b_d[:, :].rearrange("(c p) f -> p c f", p=128), r=[b_d], w=[btm])
    P.dma("pool", bT[:], bt_d[:], r=[bt_d], w=[bT])
    P.dma("pool", cT[:], ct_d[:], r=[ct_d], w=[cT])
    P.dma("sp", dt[:], dt_d[:, :].rearrange("(c p) f -> p c f", p=128), r=[dt_d], w=[dt])
    P.dma("sp", dtb[:], dtb_d[:], r=[dtb_d], w=[dtb])
    P.dma("sp", aneg[:], alog_d[:], r=[alog_d], w=[aneg])
    P.dma("sp", dsk[:], dsk_d[:], r=[dsk_d], w=[dsk])
    P.dma("sp", ng[:], ng_d[:], r=[ng_d], w=[ng])
    for k in cm_d:
        P.dma("sp", cm[k][:], cm_d[k][:], r=[cm_d[k]], w=[cm[k]])
    P.op("dve", lambda e: e.memset(ones_f[:], 1.0), w=[ones_f])
    P.op("dve", lambda e: e.tensor_tensor(out=dt[:], in0=dt[:], in1=dtb[:, :].unsqueeze(1).to_broadcast([128, NCH, 16]), op=ALU.add),
         r=[dt, dtb], w=[dt])
    P.op("act", lambda e: e.activation(out=dt[:], in_=dt[:], func=AF.Exp), r=[dt], w=[dt])
    P.op("act", lambda e: e.activation(out=dt[:], in_=dt[:], func=AF.Ln, bias=1.0, scale=1.0), r=[dt], w=[dt])
    P.op("act", lambda e: e.activation(out=aneg[:], in_=aneg[:], func=AF.Exp), r=[aneg], w=[aneg])
    P.op("dve", lambda e: e.scalar_tensor_tensor(out=Aa[:], in0=dt[:], scalar=-1.0, in1=aneg[:, :].unsqueeze(1).to_broadcast([128, NCH, 16]),
                                                 op0=ALU.mult, op1=ALU.mult), r=[dt, aneg], w=[Aa])

    def bc8(ap):
        return ap.unsqueeze(2).to_broadcast([128, 8, 64])

    def v3(buf):
        return buf[:, :].rearrange("p (h q) -> p h q", h=8)

    def chunk(c, d, emit_y, finish):
        cum, SM, R = (("triu", "trils", "triu") if d == 0 else ("trius", "trius", "tril"))
        A_c = Aa[:, c, d * 8:(d + 1) * 8]
        ps1 = C.ps()
        P.op("pe", lambda e: e.matmul(ps1[:, 0:8], lhsT=cm[cum][:], rhs=A_c, start=True, stop=True), r=[cm[cum], Aa], w=[ps1])
        P.op("pe", lambda e: e.matmul(ps1[:, 8:16], lhsT=ones_f[:], rhs=A_c, start=True, stop=True), r=[ones_f, Aa], w=[ps1])
        P.op("dve", lambda e: e.tensor_copy(out=sc[:], in_=ps1[:, 0:16]), r=[ps1], w=[sc])
        P.op("act", lambda e: e.activation(out=ex[:, 0:16], in_=sc[:], func=AF.Exp), r=[sc], w=[ex])
        P.op("dve", lambda e: e.tensor_tensor(out=sc[:, 0:8], in0=sc[:, 8:16], in1=sc[:, 0:8], op=ALU.subtract), r=[sc], w=[sc])
        P.op("act", lambda e: e.activation(out=ex[:, 16:24], in_=sc[:, 0:8], func=AF.Exp), r=[sc], w=[ex])
        wy, ws = (ex[:, 0:8], ex[:, 16:24]) if d == 0 else (ex[:, 16:24], ex[:, 0:8])
        etot = ex[:, 8:16]
        P.op("dve", lambda e: e.tensor_tensor(out=v3(Xd), in0=xs[:, c, :].rearrange("p (h q) -> p h q", h=8),
                                              in1=bc8(dt[:, c, d * 8:(d + 1) * 8]), op=ALU.mult), r=[xs, dt], w=[Xd])
        P.op("pool", lambda e: e.tensor_copy(out=Xdb[:], in_=Xd[:]), r=[Xd], w=[Xdb])
        P.op("dve", lambda e: e.tensor_tensor(out=v3(Xw), in0=v3(Xd), in1=bc8(ws), op=ALU.mult), r=[Xd, ex], w=[Xw])
        cs_ = slice(c * 128, (c + 1) * 128)
        ps_st = C.ps()
        P.op("pe", lambda e: e.matmul(ps_st[:, :], lhsT=btm[:, c, :], rhs=Xw[:], start=True, stop=True), r=[btm, Xw], w=[ps_st])
        if emit_y:
            ps_off = C.ps()
            P.op("pe", lambda e: e.matmul(ps_off[:, :], lhsT=cT[:, cs_], rhs=Sb[:], start=True, stop=True), r=[cT, Sb], w=[ps_off])
            ps_cb = C.ps()
            P.op("pe", lambda e: e.matmul(ps_cb[:, 0:128], lhsT=bT[:, cs_], rhs=cT[:, cs_], start=True, stop=True), r=[bT, cT], w=[ps_cb])
            P.op("dve", lambda e: e.tensor_tensor(out=cbm[:], in0=ps_cb[:, 0:128], in1=cm[R][:], op=ALU.mult), r=[ps_cb, cm[R]], w=[cbm])
            P.op("pool", lambda e: e.tensor_tensor(out=Am[:], in0=cm[SM][:, :].unsqueeze(1).to_broadcast([128, 8, 128]),
                                                   in1=A_c.unsqueeze(2).to_broadcast([128, 8, 128]), op=ALU.mult), r=[cm[SM], Aa], w=[Am])
            psg = [C.ps(), C.ps()]
            for h in range(8):
                pg = psg[h // 4]
                P.op("pe", lambda e, h=h, pg=pg: e.matmul(pg[:, (h % 4) * 128:(h % 4 + 1) * 128], lhsT=Am[:, h, :], rhs=cm[R][:],
                                                          start=True, stop=True), r=[Am, cm[R]], w=[pg])
            for q in range(2):
                P.op("act", lambda e, q=q: e.activation(out=es[:, q * 4:(q + 1) * 4, :].rearrange("p h l -> p (h l)"), in_=psg[q][:, :], func=AF.Exp),
                     r=[psg[q]], w=[es])
            P.op("dve", lambda e: e.tensor_tensor(out=Mb[:], in0=es[:], in1=cbm[:, :].unsqueeze(1).to_broadcast([128, 8, 128]), op=ALU.mult),
                 r=[es, cbm], w=[Mb])
            ps_y = C.ps()
            for h in range(8):
                P.op("pe", lambda e, h=h: e.matmul(ps_y[:, h * 64:(h + 1) * 64], lhsT=Mb[:, h, :], rhs=Xdb[:, h * 64:(h + 1) * 64],
                                                   start=True, stop=True), r=[Mb, Xdb], w=[ps_y])
            P.op("dve", lambda e: e.tensor_tensor(out=v3(yt), in0=ps_off[:, :].rearrange("p (h q) -> p h q", h=8), in1=bc8(wy), op=ALU.mult),
                 r=[ps_off, ex], w=[yt])
            if d == 0:
                P.op("dve", lambda e: e.tensor_tensor(out=yf[:, c, :], in0=yt[:], in1=ps_y[:, :], op=ALU.add), r=[yt, ps_y], w=[yf])
            else:
                P.op("dve", lambda e: e.tensor_tensor(out=yt[:], in0=yt[:], in1=ps_y[:, :], op=ALU.add), r=[yt, ps_y], w=[yt])
        P.op("dve", lambda e: e.tensor_tensor(out=v3(S), in0=v3(S), in1=bc8(etot), op=ALU.mult), r=[S, ex], w=[S])
        P.op("dve", lambda e: e.tensor_tensor(out=S[:], in0=S[:], in1=ps_st[:, :], op=ALU.add), r=[S, ps_st], w=[S])
        P.op("pool", lambda e: e.tensor_copy(out=Sb[:], in_=S[:]), r=[S], w=[Sb])
        if finish:
            P.dma("sp", zt[:], z_d[c * 128:(c + 1) * 128, :], r=[z_d], w=[zt])
            P.op("dve", lambda e: e.tensor_tensor(out=yt[:], in0=yt[:], in1=yf[:, c, :], op=ALU.add), r=[yt, yf], w=[yt])
            P.op("dve", lambda e: e.tensor_tensor(out=v3(y2), in0=xs[:, c, :].rearrange("p (h q) -> p h q", h=8), in1=bc8(dsk[:, :]), op=ALU.mult),
                 r=[xs, dsk], w=[y2])
            P.op("dve", lambda e: e.tensor_tensor(out=yt[:], in0=yt[:], in1=y2[:], op=ALU.add), r=[yt, y2], w=[yt])
            P.op("act", lambda e: e.activation(out=zt[:], in_=zt[:], func=AF.Silu), r=[zt], w=[zt])
            P.op("dve", lambda e: e.tensor_tensor(out=yt[:], in0=yt[:], in1=zt[:], op=ALU.mult), r=[yt, zt], w=[yt])
            P.op("dve", lambda e: e.memset(r1[:, 0:1], 0.0), w=[r1])
            P.op("act", lambda e: e.activation(out=junk[:], in_=yt[:], func=AF.Square, accum_out=r1[:, 0:1]), r=[yt, r1], w=[junk, r1])
            P.op("act", lambda e: e.activation(out=r1[:, 1:2], in_=r1[:, 0:1], func=AF.Sqrt, bias=EPS, scale=1.0 / 512), r=[r1], w=[r1])
            P.op("dve", lambda e: e.reciprocal(out=r1[:, 1:2], in_=r1[:, 1:2]), r=[r1], w=[r1])
            P.op("dve", lambda e: e.scalar_tensor_tensor(out=y2[:], in0=yt[:], scalar=r1[:, 1:2], in1=ng[:], op0=ALU.mult, op1=ALU.mult),
                 r=[yt, r1, ng], w=[y2])
            P.dma("sp", out_d[c * 128:(c + 1) * 128, :], y2[:], r=[y2], w=[out_d])

    for d in range(2):
        P.op("dve", lambda e: e.memset(S[:], 0.0), w=[S])
        P.op("dve", lambda e: e.memset(Sb[:], 0.0), w=[Sb])
        order = [0, 1] + list(range(2, NCH)) if d == 0 else [1, 0] + list(range(NCH - 1, 1, -1))
        for c in order:
            isctx = c < 2
            ey = (not isctx) or need_ctx
            chunk(c, d, ey, ey and d == 1)
    if not need_ctx:
        P.op("dve", lambda e: e.memset(y2[:], 0.0), w=[y2])
        for c in range(2):
            P.dma("sp", out_d[c * 128:(c + 1) * 128, :], y2[:], r=[y2], w=[out_d])
    return _finish(P, own, [out_d])


import math as _math


def hy_consts(L):
    N = 2 * L
    f = np.arange(L, dtype=np.float64)
    th = 2 * np.pi * (f + 0.5) / N
    ang = np.outer(th, f + 0.5)
    nch = L // 128
    import ml_dtypes
    tabs = []
    for M in (np.cos(ang), np.sin(ang)):
        tabs.append(M.reshape(nch, 128, nch, 128).transpose(2, 1, 0, 3))
    tab = np.stack(tabs).reshape(2, nch, 128, nch * 128).astype(ml_dtypes.bfloat16)
    t = np.linspace(0.0, 1.0, L, dtype=np.float32)[:, None]
    w = (2.0 * np.pi * np.arange(L, dtype=np.float32)[:, None] / L).astype(np.float32)
    fb = np.linspace(1e-4, 16 - 1, 16, dtype=np.float32)[None, :]
    feats = np.concatenate([t, np.cos(fb * w), -np.sin(fb * w)], axis=-1).astype(np.float32)
    tn = np.zeros((128, nch, 2), np.float32)
    tt = np.arange(L).reshape(nch, 128).T
    tn[:, :, 0] = tt / (L - 1)
    tn[:, :, 1] = (tt + 1) / (L - 1)
    csh = np.zeros((128, nch, 2), np.float32)
    thh = (th / 2).reshape(nch, 128).T
    csh[:, :, 0] = np.cos(thh)
    csh[:, :, 1] = np.sin(thh)
    return dict(tab=np.ascontiguousarray(tab), featsT=np.ascontiguousarray(feats.T), tn=tn, csh=csh)


def hy_negdelta(s):
    max_decay = _math.log(1e-2) / 0.3
    min_decay = _math.log(1e-2) / 1.5
    deltas = np.abs(np.linspace(min_decay, max_decay, 1024, dtype=np.float32))
    d = -deltas[512 * s:512 * s + 512]
    return np.ascontiguousarray(np.broadcast_to(d[None, :], (128, 512))).astype(np.float32)


def build_hy(need_ctx, P=None, prefix="", bind=None):
    P, own = _begin(P, prefix, bind)
    C = Ctx(P, nrot=8)
    u_d = P.dram("u", [NTOK, 1536], F32, "ExternalInput")
    seqs = [dict(L=4096, T0=NCTX, sfx="L")]
    if need_ctx:
        seqs.append(dict(L=256, T0=0, sfx="C"))
    for sq in seqs:
        n_ = sq["L"] // 128
        sq["nch"] = n_
        sq["tab_d"] = P.dram("tab" + sq["sfx"], [2, n_, 128, n_ * 128], BF16, "ExternalInput")
        sq["ft_d"] = P.dram("featsT" + sq["sfx"], [33, sq["L"]], F32, "ExternalInput")
        sq["tn_d"] = P.dram("tn" + sq["sfx"], [128, n_, 2], F32, "ExternalInput")
        sq["csh_d"] = P.dram("csh" + sq["sfx"], [128, n_, 2], F32, "ExternalInput")
    w1_d = P.dram("hw1", [33, 64], F32, "ExternalInput")
    w2_d = P.dram("hw2", [64, 64], F32, "ExternalInput")
    w3_d = P.dram("hw3", [64, 64], F32, "ExternalInput")
    fq_d = P.dram("hfreq", [64, 3], F32, "ExternalInput")
    hb_d = P.dram("hb", [64, 3], F32, "ExternalInput")
    w4_d = P.dram("hw4", [64, 4, 512], F32, "ExternalInput")
    nd_d = P.dram("negdelta", [128, 512], F32, "ExternalInput")
    hbias_d = P.dram("hbias", [1, 2, 512], F32, "ExternalInput")
    out_d = P.dram("y", [NTOK, 512], F32, "ExternalOutput")

    w1 = P.sb([33, 64], F32, "w1s"); w2 = P.sb([64, 64], F32, "w2s"); w3 = P.sb([64, 64], F32, "w3s")
    fq = P.sb([64, 3], F32, "fqs"); hb = P.sb([64, 3], F32, "hbs")
    w4 = P.sb([64, 4, 512], F32, "w4s")
    nd = P.sb([128, 512], F32, "nds")
    hbias = P.sb([1, 2, 512], F32, "hbiass")
    NCH = 32
    ft = P.sb([33, 512], F32, "fts")
    tn = P.sb([128, NCH, 2], F32, "tns")
    csh = P.sb([128, NCH, 2], F32, "cshs")
    h3 = P.sb([64, 4096 + 128], F32, "h3")
    ha = P.sb([64, 512], F32, "ha"); hbb = P.sb([64, 512], F32, "hbb")
    qi = P.sb([64, 512], I32, "qi"); qf = P.sb([64, 512], F32, "qf")
    CW = 512
    KD = [P.sb([128, NCH, CW], BF16, "KD%d" % i) for i in range(2)]
    spec_d = P.scratch(P.prefix + "spec_scr", [NCH, 128, 2 * CW], BF16)
    spt = [P.sb([128, 2, CW], BF16, "spt%d" % i) for i in range(2)]
    spl = [P.sb([128, 2, CW], BF16, "spl%d" % i) for i in range(2)]
    ub = P.sb([128, NCH, CW], BF16, "ub")
    slabs = [P.sb([128, NCH * 128], BF16, "slab%d" % i) for i in range(4)]
    wex = [P.sb([128, CW], F32, "wex%d" % i) for i in range(2)]
    kfb = P.sb([128, CW], F32, "kfb"); kbb = P.sb([128, CW], F32, "kbb")
    tt = [P.sb([128, CW], F32, "tt%d" % i) for i in range(4)]
    xt = [P.sb([128, CW], F32, "xt%d" % i) for i in range(2)]
    ost = [P.sb([128, CW], F32, "ost%d" % i) for i in range(2)]
    for (sbuf, dbuf) in ((w1, w1_d), (w2, w2_d), (w3, w3_d), (fq, fq_d), (hb, hb_d), (w4, w4_d), (nd, nd_d), (hbias, hbias_d)):
        P.dma("sp", sbuf[:], dbuf[:], r=[dbuf], w=[sbuf])
    cnt = {"slab": 0, "x": 0}
    TWO_PI = 2.0 * _math.pi

    def get_slab(sq, m, idx):
        sl = slabs[cnt["slab"] % 4]
        cnt["slab"] += 1
        n = sq["nch"] * 128
        P.dma("sp" if cnt["slab"] % 2 else "act", sl[:, 0:n], sq["tab_d"][m, idx], r=[sq["tab_d"]], w=[sl])
        return sl

    def do_seq(sq):
        L, nch, T0 = sq["L"], sq["nch"], sq["T0"]
        N = 2 * L
        P.dma("sp", tn[:, 0:nch, :], sq["tn_d"][:], r=[sq["tn_d"]], w=[tn])
        P.dma("sp", csh[:, 0:nch, :], sq["csh_d"][:], r=[sq["csh_d"]], w=[csh])
        P.op("dve", lambda e: e.memset(h3[:], 0.0), w=[h3])
        bn = min(512, L)
        for b0 in range(0, L, bn):
            src = None
            for li, (wm, kdim) in enumerate(((w1, 33), (w2, 64), (w3, 64))):
                ps = C.ps()
                if li == 0:
                    P.dma("sp", ft[:, 0:bn], sq["ft_d"][:, b0:b0 + bn], r=[sq["ft_d"]], w=[ft])
                    P.op("pe", lambda e, ps=ps, b0=b0: e.matmul(ps[0:64, 0:bn], lhsT=w1[:, :], rhs=ft[:, 0:bn], start=True, stop=True),
                         r=[w1, ft], w=[ps])
                else:
                    P.op("pe", lambda e, ps=ps, wm=wm, src=src: e.matmul(ps[0:64, 0:bn], lhsT=wm[:, :], rhs=src[:, 0:bn], start=True, stop=True),
                         r=[wm, src], w=[ps])
                tmpb = ha if li % 2 == 0 else hbb
                P.op("dve", lambda e, ps=ps, tmpb=tmpb, li=li: e.tensor_scalar(
                    out=tmpb[:, 0:bn], in0=ps[0:64, 0:bn], scalar1=hb[:, li:li + 1], scalar2=fq[:, li:li + 1], op0=ALU.add, op1=ALU.mult),
                    r=[ps, hb, fq], w=[tmpb])
                P.op("dve", lambda e, tmpb=tmpb: e.tensor_scalar(
                    out=tmpb[:, 0:bn], in0=tmpb[:, 0:bn], scalar1=float(1.0 / TWO_PI), scalar2=64.5, op0=ALU.mult, op1=ALU.add),
                    r=[tmpb], w=[tmpb])
                P.op("dve", lambda e, tmpb=tmpb: e.tensor_copy(out=qi[:, 0:bn], in_=tmpb[:, 0:bn]), r=[tmpb], w=[qi])
                P.op("dve", lambda e: e.tensor_copy(out=qf[:, 0:bn], in_=qi[:, 0:bn]), r=[qi], w=[qf])
                P.op("dve", lambda e, tmpb=tmpb: e.tensor_tensor(out=tmpb[:, 0:bn], in0=tmpb[:, 0:bn], in1=qf[:, 0:bn], op=ALU.subtract),
                     r=[tmpb, qf], w=[tmpb])
                P.op("dve", lambda e, tmpb=tmpb: e.tensor_single_scalar(out=qf[:, 0:bn], in_=tmpb[:, 0:bn], scalar=0.0, op=ALU.is_lt),
                     r=[tmpb], w=[qf])
                P.op("dve", lambda e, tmpb=tmpb: e.tensor_tensor(out=tmpb[:, 0:bn], in0=tmpb[:, 0:bn], in1=qf[:, 0:bn], op=ALU.add),
                     r=[tmpb, qf], w=[tmpb])
                if li < 2:
                    P.op("act", lambda e, tmpb=tmpb: e.activation(out=tmpb[:, 0:bn], in_=tmpb[:, 0:bn], func=AF.Sin, bias=negpi[:, 0:1], scale=float(TWO_PI)),
                         r=[tmpb, negpi], w=[tmpb])
                    src = tmpb
                else:
                    P.op("act", lambda e, tmpb=tmpb, b0=b0: e.activation(out=h3[:, b0:b0 + bn], in_=tmpb[:, 0:bn], func=AF.Sin, bias=negpi[:, 0:1], scale=float(TWO_PI)),
                         r=[tmpb, negpi], w=[h3])
        for half in range(512 // CW):
            for o in range(2):
                do_ho(sq, half, o, slice(CW * half, CW * half + CW))

    def do_ho(sq, half, o, ch):
        L, nch, T0 = sq["L"], sq["nch"], sq["T0"]
        N = 2 * L
        if True:
            if True:
                for tc in range(nch):
                    psf = C.ps(); psb = C.ps()
                    P.op("pe", lambda e, psf=psf, tc=tc: e.matmul(psf[:, 0:CW], lhsT=h3[:, tc * 128:tc * 128 + 128], rhs=w4[:, o * 2 + 0, ch],
                                                                  start=True, stop=True), r=[h3, w4], w=[psf])
                    P.op("pe", lambda e, psb=psb, tc=tc: e.matmul(psb[:, 0:CW], lhsT=h3[:, tc * 128 + 1:tc * 128 + 129], rhs=w4[:, o * 2 + 1, ch],
                                                                  start=True, stop=True), r=[h3, w4], w=[psb])
                    for dd, (psx, kx) in enumerate(((psf, kfb), (psb, kbb))):
                        P.op("act", lambda e, dd=dd, tc=tc: e.activation(out=wex[dd][:], in_=nd[:, ch], func=AF.Exp, scale=tn[:, tc, dd:dd + 1]),
                             r=[nd, tn], w=[wex[dd]])
                        P.op("dve", lambda e, dd=dd, psx=psx, kx=kx: e.scalar_tensor_tensor(
                            out=kx[:], in0=wex[dd][:], scalar=0.05, in1=psx[:, 0:CW], op0=ALU.add, op1=ALU.mult), r=[wex[dd], psx], w=[kx])
                    if tc == 0:
                        ps0 = C.ps()
                        P.op("pe", lambda e, ps0=ps0: e.matmul(ps0[:, 0:CW], lhsT=h3[:, 0:128], rhs=w4[:, o * 2 + 1, ch], start=True, stop=True),
                             r=[h3, w4], w=[ps0])
                        P.op("dve", lambda e, ps0=ps0: e.scalar_tensor_tensor(
                            out=kfb[0:1, :], in0=ps0[0:1, 0:CW], scalar=1.05, in1=kfb[0:1, :], op0=ALU.mult, op1=ALU.add), r=[ps0, kfb], w=[kfb])
                        P.op("dve", lambda e: e.tensor_tensor(out=kfb[0:1, :], in0=kfb[0:1, :], in1=hbias[0:1, o, ch], op=ALU.add),
                             r=[kfb, hbias], w=[kfb])
                    P.op("dve", lambda e, tc=tc: e.tensor_tensor(out=KD[0][:, tc, :], in0=kfb[:], in1=kbb[:], op=ALU.add), r=[kfb, kbb], w=[KD[0]])
                    P.op("pool", lambda e, tc=tc: e.tensor_tensor(out=KD[1][:, tc, :], in0=kfb[:], in1=kbb[:], op=ALU.subtract), r=[kfb, kbb], w=[KD[1]])
                for fc in range(nch):
                    sC = get_slab(sq, 0, fc); sS = get_slab(sq, 1, fc)
                    pP = C.ps(); pQ = C.ps()
                    for tc in range(nch):
                        P.op("pe", lambda e, tc=tc, sC=sC, pP=pP: e.matmul(pP[:, 0:CW], lhsT=sC[:, tc * 128:(tc + 1) * 128], rhs=KD[0][:, tc, :],
                                                                    start=(tc == 0), stop=(tc == nch - 1)), r=[sC, KD[0]], w=[pP])
                    for tc in range(nch):
                        P.op("pe", lambda e, tc=tc, sS=sS, pQ=pQ: e.matmul(pQ[:, 0:CW], lhsT=sS[:, tc * 128:(tc + 1) * 128], rhs=KD[1][:, tc, :],
                                                                    start=(tc == 0), stop=(tc == nch - 1)), r=[sS, KD[1]], w=[pQ])
                    cth = csh[:, fc, 0:1]; sth = csh[:, fc, 1:2]
                    P.op("dve", lambda e, pQ=pQ, sth=sth: e.tensor_scalar(out=tt[0][:], in0=pQ[:, 0:CW], scalar1=sth, scalar2=None, op0=ALU.mult),
                         r=[pQ, csh], w=[tt[0]])
                    sp_ = spt[fc % 2]
                    P.op("dve", lambda e, pP=pP, cth=cth, sp_=sp_: e.scalar_tensor_tensor(out=sp_[:, 0, :], in0=pP[:, 0:CW], scalar=cth, in1=tt[0][:],
                                                                                 op0=ALU.mult, op1=ALU.add), r=[pP, csh, tt[0]], w=[sp_])
                    P.op("dve", lambda e, pQ=pQ, cth=cth: e.tensor_scalar(out=tt[1][:], in0=pQ[:, 0:CW], scalar1=cth, scalar2=None, op0=ALU.mult),
                         r=[pQ, csh], w=[tt[1]])
                    P.op("dve", lambda e, pP=pP, sth=sth, sp_=sp_: e.scalar_tensor_tensor(out=sp_[:, 1, :], in0=pP[:, 0:CW], scalar=sth, in1=tt[1][:],
                                                                                 op0=ALU.mult, op1=ALU.subtract), r=[pP, csh, tt[1]], w=[sp_])
                    P.dma("pool", spec_d[fc].rearrange("p (a c) -> p a c", a=2), sp_[:], r=[sp_], w=[spec_d])
                if o == 0:
                    P.dma("pool", ub[:, 0:nch, :], u_d[T0:T0 + L, CW * half:CW * half + CW].rearrange("(c p) f -> p c f", p=128),
                          r=[u_d], w=[ub])
                for fc in range(nch):
                    sC = get_slab(sq, 0, fc); sS = get_slab(sq, 1, fc)
                    pA = C.ps(); pB = C.ps()
                    for tc in range(nch):
                        P.op("pe", lambda e, tc=tc, sC=sC, pA=pA: e.matmul(pA[:, 0:CW], lhsT=sC[:, tc * 128:(tc + 1) * 128], rhs=ub[:, tc, :],
                                                                    start=(tc == 0), stop=(tc == nch - 1)), r=[sC, ub], w=[pA])
                    for tc in range(nch):
                        P.op("pe", lambda e, tc=tc, sS=sS, pB=pB: e.matmul(pB[:, 0:CW], lhsT=sS[:, tc * 128:(tc + 1) * 128], rhs=ub[:, tc, :],
                                                                    start=(tc == 0), stop=(tc == nch - 1)), r=[sS, ub], w=[pB])
                    spec = spl[fc % 2]
                    P.dma("pool", spec[:], spec_d[fc].rearrange("p (a c) -> p a c", a=2), r=[spec_d], w=[spec])
                    Kr = spec[:, 0, :]; Ki = spec[:, 1, :]
                    P.op("dve", lambda e, pA=pA, Kr=Kr: e.tensor_tensor(out=tt[0][:], in0=pA[:, 0:CW], in1=Kr, op=ALU.mult), r=[pA, spec], w=[tt[0]])
                    P.op("dve", lambda e, pB=pB, Ki=Ki: e.tensor_tensor(out=tt[1][:], in0=pB[:, 0:CW], in1=Ki, op=ALU.mult), r=[pB, spec], w=[tt[1]])
                    P.op("dve", lambda e, pB=pB, Kr=Kr: e.tensor_tensor(out=tt[2][:], in0=pB[:, 0:CW], in1=Kr, op=ALU.mult), r=[pB, spec], w=[tt[2]])
                    P.op("dve", lambda e, pA=pA, Ki=Ki: e.tensor_tensor(out=tt[3][:], in0=pA[:, 0:CW], in1=Ki, op=ALU.mult), r=[pA, spec], w=[tt[3]])
                    P.op("pool", lambda e, fc=fc: e.tensor_tensor(out=KD[0][:, fc, :], in0=tt[0][:], in1=tt[1][:], op=ALU.add), r=[tt[0], tt[1]], w=[KD[0]])
                    P.op("pool", lambda e, fc=fc: e.tensor_tensor(out=KD[1][:, fc, :], in0=tt[2][:], in1=tt[3][:], op=ALU.subtract), r=[tt[2], tt[3]], w=[KD[1]])
                for tc in range(nch):
                    sC = get_slab(sq, 0, tc); sS = get_slab(sq, 1, tc)
                    py = C.ps()
                    for fc in range(nch):
                        P.op("pe", lambda e, fc=fc, sC=sC, py=py: e.matmul(py[:, 0:CW], lhsT=sC[:, fc * 128:(fc + 1) * 128], rhs=KD[0][:, fc, :],
                                                                    start=(fc == 0), stop=False), r=[sC, KD[0]], w=[py])
                    for fc in range(nch):
                        P.op("pe", lambda e, fc=fc, sS=sS, py=py: e.matmul(py[:, 0:CW], lhsT=sS[:, fc * 128:(fc + 1) * 128], rhs=KD[1][:, fc, :],
                                                                    start=False, stop=(fc == nch - 1)), r=[sS, KD[1]], w=[py])
                    xb = xt[cnt["x"] % 2]
                    ob = ost[cnt["x"] % 2]
                    cnt["x"] += 1
                    r0 = T0 + tc * 128
                    c0 = 512 * (1 + o) + CW * half
                    P.dma("pool", xb[:], u_d[r0:r0 + 128, c0:c0 + CW], r=[u_d], w=[xb])
                    if o == 0:
                        P.op("dve", lambda e, py=py, xb=xb, tc=tc: e.scalar_tensor_tensor(
                            out=ub[:, tc, :], in0=py[:, 0:CW], scalar=float(2.0 / N), in1=xb[:], op0=ALU.mult, op1=ALU.mult), r=[py, xb], w=[ub])
                    else:
                        P.op("dve", lambda e, py=py, xb=xb, ob=ob: e.scalar_tensor_tensor(
                            out=ob[:], in0=py[:, 0:CW], scalar=float(2.0 / N), in1=xb[:], op0=ALU.mult, op1=ALU.mult), r=[py, xb], w=[ob])
                        P.dma("sp", out_d[r0:r0 + 128, CW * half:CW * half + CW], ob[:], r=[ob], w=[out_d])

    negpi = P.sb([64, 1], F32, "negpi")
    P.op("dve", lambda e: e.memset(negpi[:], -_math.pi), w=[negpi])
    for sq in seqs:
        do_seq(sq)
    if not need_ctx:
        zb = P.sb([128, 512], F32, "zb")
        P.op("dve", lambda e: e.memset(zb[:], 0.0), w=[zb])
        for c in range(2):
            P.dma("sp", out_d[c * 128:(c + 1) * 128, :], zb[:], r=[zb], w=[out_d])
    return _finish(P, own, [out_d])


def build_mod(P=None, prefix="", bind=None):
    P, own = _begin(P, prefix, bind)
    C = Ctx(P, nrot=8)
    c_d = P.dram("cs", [128, KC, 5], F32, "ExternalInput")
    w_d = P.dram("wm", [2, D, 768], F32, "ExternalInput")
    b_d = P.dram("bm", [2, 1, 768], F32, "ExternalInput")
    out_d = P.dram("mod", [2, 5, 768], F32, "ExternalOutput")
    cs = P.sb([128, KC, 5], F32, "css")
    wm = P.sb([128, KC, 768], F32, "wms")
    brow = P.sb([1, 768], F32, "brow")
    ones_f = P.sb([1, 8], F32, "ones_f")
    st = P.sb([5, 768], F32, "st")
    P.dma("sp", cs[:], c_d[:], r=[c_d], w=[cs])
    P.op("act", lambda e: e.activation(out=cs[:], in_=cs[:], func=AF.Silu), r=[cs], w=[cs])
    P.op("dve", lambda e: e.memset(ones_f[:], 1.0), w=[ones_f])
    for l in range(2):
        P.dma("sp", wm[:], wview(w_d[l], 0, KC, 0, 768), r=[w_d], w=[wm])
        P.dma("sp", brow[:], b_d[l], r=[b_d], w=[brow])
        for (c0, n) in ((0, 512), (512, 256)):
            ps = C.ps()
            for kc in range(KC):
                P.op("pe", lambda e, ps=ps, kc=kc, c0=c0, n=n: e.matmul(ps[0:5, 0:n], lhsT=cs[:, kc, :], rhs=wm[:, kc, c0:c0 + n],
                                                                start=(kc == 0), stop=False), r=[cs, wm], w=[ps])
            P.op("pe", lambda e, ps=ps, c0=c0, n=n: e.matmul(ps[0:5, 0:n], lhsT=ones_f[0:1, 0:5], rhs=brow[0:1, c0:c0 + n],
                                                      start=False, stop=True), r=[ones_f, brow], w=[ps])
            P.op("act", lambda e, ps=ps, c0=c0, n=n: e.activation(out=st[:, c0:c0 + n], in_=ps[0:5, 0:n], func=AF.Copy), r=[ps], w=[st])
        P.dma("sp", out_d[l], st[:], r=[st], w=[out_d])
    return _finish(P, own, [out_d])


_CACHE = {}


def _get(key, fn):
    if key not in _CACHE:
        _CACHE[key] = fn()
    return _CACHE[key]


def _run(nc, in_maps):
    res = run_bass_kernel_spmd(nc, in_maps, core_ids=list(range(len(in_maps))))
    return res.results


def _pack(v):
    return np.ascontiguousarray(np.asarray(v, np.float32).reshape(8, 128).T)


def _c(a):
    return np.ascontiguousarray(np.asarray(a, dtype=np.float32))


def _rep(v, n=128):
    v = np.asarray(v, np.float32)
    return np.ascontiguousarray(np.broadcast_to(v[None], (n,) + v.shape))


SSD_GROUPS = [dict(name="xs", ncol=512, conv=True, act="silu", layout="tm"),
              dict(name="btm", ncol=128, conv=True, act="silu", layout="tm"),
              dict(name="z", ncol=512, conv=False, act=None, layout="tm"),
              dict(name="dtraw", ncol=16, conv=False, act=None, layout="tm"),
              dict(name="bT", ncol=128, conv=True, act="silu", layout="fm"),
              dict(name="cT", ncol=128, conv=True, act="silu", layout="fm")]
HY_GROUPS = [dict(name="uv", ncol=512, conv=True, act=None, layout="tm"),
             dict(name="ux1", ncol=512, conv=True, act=None, layout="tm"),
             dict(name="ux2", ncol=512, conv=True, act=None, layout="tm")]


def kernel_unfused(x, c, ctx, c_ctx, w_mod, b_mod, norm1_g, norm2_g, w_in, hy_conv_w, hy_conv_b, hy_w1, hy_b1, hy_w2,
           hy_b2, hy_w3, hy_b3, hy_w4, hy_freq, hy_bias, ssd_conv_w, ssd_conv_b, ssd_dt_bias, ssd_a_log, ssd_d,
           ssd_norm_g, da_q_norm, da_k_norm, da_lambda, da_subln_g, w_branch, w_out, ffn_w1, ffn_w3, ffn_w2,
           router_w, moe_w1, moe_w3, moe_w2):
    f32 = np.float32
    x = np.asarray(x, f32); ctx = np.asarray(ctx, f32)
    cores = [(b, s) for b in range(4) for s in range(2)]
    cc = np.concatenate([np.asarray(c, f32), np.asarray(c_ctx, f32)[None]], 0)
    cs = np.ascontiguousarray(cc.T.reshape(8, 128, 5).transpose(1, 0, 2))
    ncm = _get("mod", build_mod)
    res = _run(ncm, [dict(cs=cs, wm=_c(np.asarray(w_mod)[:, :, j * 768:(j + 1) * 768]),
                          bm=_c(np.asarray(b_mod)[:, None, j * 768:(j + 1) * 768])) for j in range(8)])
    mod = np.concatenate([r["mod"] for r in res], axis=2)

    h_lat = x
    h_ctx = ctx
    acons = attn_consts()
    scons = ssd_consts()
    hyL = {k + "L": v for k, v in hy_consts(4096).items()}
    hyC = {k + "C": v for k, v in hy_consts(256).items()}
    for i in range(2):
        need_ctx = i == 0
        W = np.asarray(w_in[i], f32)

        def vecs(b):
            cols = [_pack(norm1_g[i]), _pack(norm2_g[i])]
            for m in (mod[i, b], mod[i, 4]):
                for j in range(6):
                    cols.append(_pack(m[j * 1024:(j + 1) * 1024]))
            return np.ascontiguousarray(np.concatenate(cols, axis=1))
        hT = [np.ascontiguousarray(np.concatenate([h_ctx[b], h_lat[b]], 0).T) for b in range(4)]
        vv = [vecs(b) for b in range(4)]
        so = 3072
        xo = so + 1024
        cw = np.asarray(ssd_conv_w[i], f32); cb = np.asarray(ssd_conv_b[i], f32)
        maps = []
        for (b, s) in cores:
            m = dict(hT=hT[b], vecs=vv[b])
            sel = dict(xs=slice(512 * s, 512 * s + 512), btm=slice(1024 + 128 * s, 1024 + 128 * s + 128),
                       bT=slice(1024 + 128 * s, 1024 + 128 * s + 128), cT=slice(1280 + 128 * s, 1280 + 128 * s + 128))
            for nm, sl in sel.items():
                m["w_" + nm] = _c(W[:, xo:xo + 1536][:, sl])
                m["cw_" + nm] = _c(np.broadcast_to(cw[:, None, sl], (3, 128, sl.stop - sl.start)))
                if nm in ("bT", "cT"):
                    m["b_" + nm] = _c(cb[sl].reshape(1, 128).T)
                else:
                    m["b_" + nm] = _c(cb[None, sl])
            m["w_z"] = _c(W[:, so + 512 * s:so + 512 * s + 512]); m["b_z"] = np.zeros((1, 512), f32)
            dcols = [5632 + d * 16 + 8 * s + h for d in range(2) for h in range(8)]
            m["w_dtraw"] = _c(W[:, dcols]); m["b_dtraw"] = np.zeros((1, 16), f32)
            maps.append(m)
        pj = _run(_get("pj_ssd", lambda: build_pj([dict(g) for g in SSD_GROUPS])), maps)
        maps = []
        for k, (b, s) in enumerate(cores):
            r = pj[k]
            m = dict(xs=r["o_xs"], btm=r["o_btm"], z=r["o_z"], dtraw=r["o_dtraw"], bT=r["o_bT"], cT=r["o_cT"],
                     dtb=_rep(np.asarray(ssd_dt_bias[i], f32)[:, 8 * s:8 * s + 8].reshape(16)),
                     alog=_rep(np.asarray(ssd_a_log[i], f32)[:, 8 * s:8 * s + 8].reshape(16)),
                     dskip=_rep(np.asarray(ssd_d[i], f32)[8 * s:8 * s + 8]),
                     normg=_rep(np.asarray(ssd_norm_g[i], f32)[512 * s:512 * s + 512]))
            m.update(scons)
            maps.append(m)
        y_ssd = _run(_get(("ssd", need_ctx), lambda: build_ssd(need_ctx)), maps)
        del pj
        hw = np.asarray(hy_conv_w[i], f32); hbv = np.asarray(hy_conv_b[i], f32)
        maps = []
        for (b, s) in cores:
            m = dict(hT=hT[b], vecs=vv[b])
            for gi, nm in enumerate(("uv", "ux1", "ux2")):
                sl = slice(1024 * gi + 512 * s, 1024 * gi + 512 * s + 512)
                m["w_" + nm] = _c(W[:, sl])
                m["cw_" + nm] = _c(np.broadcast_to(hw[:, None, sl], (3, 128, 512)))
                m["b_" + nm] = _c(hbv[None, sl])
            maps.append(m)
        pj = _run(_get("pj_hy", lambda: build_pj([dict(g) for g in HY_GROUPS])), maps)
        maps = []
        for k, (b, s) in enumerate(cores):
            r = pj[k]
            cs_ = slice(512 * s, 512 * s + 512)
            m = dict(u=np.ascontiguousarray(np.concatenate([r["o_uv"], r["o_ux1"], r["o_ux2"]], 1)),
                     hw1=_c(hy_w1[i]), hw2=_c(hy_w2[i]), hw3=_c(hy_w3[i]), hfreq=_c(np.asarray(hy_freq[i]).T),
                     hb=_c(np.stack([np.asarray(hy_b1[i]), np.asarray(hy_b2[i]), np.asarray(hy_b3[i])], 1)),
                     hw4=_c(np.asarray(hy_w4[i], f32).reshape(64, 2, 2, 1024)[:, :, :, cs_].reshape(64, 4, 512)),
                     negdelta=hy_negdelta(s), hbias=_c(np.asarray(hy_bias[i], f32)[:, cs_][None]))
            m.update(hyL)
            if need_ctx:
                m.update(hyC)
            maps.append(m)
        y_hy = _run(_get(("hy", need_ctx), lambda: build_hy(need_ctx)), maps)
        del pj
        maps = []
        for (b, s) in cores:
            o = 5664 + s * 512
            m = dict(hT=hT[b], vecs=vv[b], wq=_c(W[:, o:o + 512]), wk=_c(W[:, o + 1024:o + 1536]), wv=_c(W[:, o + 2048:o + 2560]),
                     qkg=_c(np.stack([np.concatenate([np.asarray(da_q_norm[i], f32)] * 2), np.concatenate([np.asarray(da_k_norm[i], f32)] * 2)], 1)),
                     lamv=_c(np.asarray(da_lambda[i], f32).reshape(1, 256)), sublng=_rep(np.asarray(da_subln_g[i], f32)))
            m.update(acons)
            maps.append(m)
        y_da = _run(_get(("attn", i), lambda: build_attn(i, need_ctx)), maps)
        if need_ctx:
            T = 2176
            tiles = [(0, 128, 1)] + [(128 + 512 * q, 512, 0) for q in range(4)]
        else:
            T = 2048
            tiles = [(512 * q, 512, 0) for q in range(4)]
        maps = []
        for (b, s) in cores:
            tok = np.concatenate([np.arange(128 * s, 128 * s + 128), 256 + np.arange(2048 * s, 2048 * s + 2048)]) if need_ctx \
                else 256 + np.arange(2048 * s, 2048 * s + 2048)
            yh = np.concatenate([y_hy[2 * b]["y"], y_hy[2 * b + 1]["y"]], 1)[tok].T
            ys = np.concatenate([y_ssd[2 * b]["y"], y_ssd[2 * b + 1]["y"]], 1)[tok].T
            yd = np.concatenate([y_da[2 * b]["yT"], y_da[2 * b + 1]["yT"]], 0)[:, tok]
            m = dict(hT=np.ascontiguousarray(hT[b][:, tok]), yT=np.ascontiguousarray(np.stack([yh, ys, yd])), vecs=vv[b],
                     wg=_c(W[:, -3072:]), wbr=_c(w_branch[i]), wo=_c(w_out[i]))
            if i % 2 == 0:
                m.update(w1=_c(ffn_w1[i // 2])[None], w3=_c(ffn_w3[i // 2])[None], w2=_c(ffn_w2[i // 2])[None])
            else:
                m.update(rw=_c(router_w[i // 2]), w1=_c(moe_w1[i // 2]), w3=_c(moe_w3[i // 2]), w2=_c(moe_w2[i // 2]),
                         ident=np.eye(128, dtype=f32))
            maps.append(m)
        moe = i % 2 == 1
        outB = _run(_get(("B", T, moe), lambda: build_B(T, tiles, moe)), maps)
        del y_hy, y_ssd, y_da
        new_lat = np.empty_like(h_lat)
        new_ctx = np.array(h_ctx, copy=True)
        for k, (b, s) in enumerate(cores):
            ho = outB[k]["hout"].T
            if need_ctx:
                new_ctx[b, 128 * s:128 * s + 128] = ho[0:128]
                new_lat[b, 2048 * s:2048 * s + 2048] = ho[128:]
            else:
                new_lat[b, 2048 * s:2048 * s + 2048] = ho
        h_lat, h_ctx = new_lat, new_ctx
    return np.ascontiguousarray(h_lat.astype(np.float32))


def emit_modT(P, vecs_sc, prefix="M_"):
    P, own = _begin(P, prefix, None)
    C = Ctx(P, nrot=8)
    c_d = P.dram("cs", [128, KC, 2], F32, "ExternalInput")
    w_d = P.dram("wm", [2, D, 6 * D], F32, "ExternalInput")
    b_d = P.dram("bm", [2, 128, 48], F32, "ExternalInput")
    n_d = P.dram("norms", [2, 128, 16], F32, "ExternalInput")
    cs = P.sb([128, KC, 2], F32, "css")
    wm = [P.sb([128, KC, 512], F32, "wms%d" % i) for i in range(2)]
    bm = P.sb([128, 48], F32, "bms")
    vt = P.sb([128, 112], F32, "vt")
    P.dma("sp", cs[:], c_d[:], r=[c_d], w=[cs])
    P.op("act", lambda e: e.activation(out=cs[:], in_=cs[:], func=AF.Silu), r=[cs], w=[cs])
    for l in range(2):
        P.dma("sp", bm[:], b_d[l], r=[b_d], w=[bm])
        P.dma("sp", vt[:, 0:16], n_d[l], r=[n_d], w=[vt])
        for cb in range(12):
            wb = wm[cb % 2]
            P.dma("sp", wb[:], wview(w_d[l], 0, KC, cb * 512, 512), r=[w_d], w=[wb])
            for j4 in range(4):
                j = cb * 4 + j4
                ps = C.ps()
                for kc in range(KC):
                    P.op("pe", lambda e, ps=ps, wb=wb, kc=kc, j4=j4: e.matmul(
                        ps[:, 0:2], lhsT=wb[:, kc, j4 * 128:(j4 + 1) * 128], rhs=cs[:, kc, :], start=(kc == 0), stop=(kc == KC - 1)),
                        r=[wb, cs], w=[ps])
                for cls in range(2):
                    col = 16 + cls * 48 + j
                    P.op("dve", lambda e, ps=ps, cls=cls, col=col, j=j: e.tensor_tensor(
                        out=vt[:, col:col + 1], in0=ps[:, cls:cls + 1], in1=bm[:, j:j + 1], op=ALU.add), r=[ps, bm], w=[vt])
        P.dma("sp", vecs_sc[l][:], vt[:], r=[vt], w=[vecs_sc[l]])
    P.end_phase()


def _view(buf, ap, name):
    b = Buf(ap, name)
    return b


def build_fused():
    P = Prog(bass.Bass("TRN2", target_bir_lowering=False))
    sc = lambda name, shape: P.scratch(name, shape, F32)
    hT0 = P.dram("hT0", [D, NTOK], F32, "ExternalInput")
    shared = {}
    for nm, shp, dt in (("cosT", [128, 4096], F32), ("sinT", [128, 4096], F32), ("rmT", [128, 128], F32),
                        ("blockones", [128, 128], F32), ("ident", [128, 128], F32),
                        ("triu", [128, 128], F32), ("trius", [128, 128], F32), ("tril", [128, 128], F32), ("trils", [128, 128], F32),
                        ("tabL", [2, 32, 128, 4096], BF16), ("featsTL", [33, 4096], F32), ("tnL", [128, 32, 2], F32), ("cshL", [128, 32, 2], F32),
                        ("tabC", [2, 2, 128, 256], BF16), ("featsTC", [33, 256], F32), ("tnC", [128, 2, 2], F32), ("cshC", [128, 2, 2], F32)):
        shared[nm] = P.dram(nm, shp, dt, "ExternalInput")
    vecs_sc = [sc("vecs_l%d" % l, [128, 112]) for l in range(2)]
    hT1 = sc("hT1", [D, NTOK])
    s_xs = sc("s_xs", [NTOK, 512]); s_btm = sc("s_btm", [NTOK, 128]); s_z = sc("s_z", [NTOK, 512]); s_dt = sc("s_dt", [NTOK, 16])
    s_bT = sc("s_bT", [128, NTOK]); s_cT = sc("s_cT", [128, NTOK])
    s_u = sc("s_u", [NTOK, 1536])
    y_ssd = [sc("y_ssd%d" % s, [NTOK, 512]) for s in range(2)]
    y_hy = [sc("y_hy%d" % s, [NTOK, 512]) for s in range(2)]
    y_da = [sc("y_da%d" % s, [512, NTOK]) for s in range(2)]
    emit_modT(P, vecs_sc)
    hcur = hT0
    f1_sc = P.scratch("f1_sc", [128, KC * (NTOK + 4)], BF16)
    for i in range(2):
        need_ctx = i == 0
        build_pj([dict(f1only=True)], P=P, prefix="L%d_f1_" % i, bind=dict(hT=hcur, vecs=vecs_sc[i], f1out=f1_sc))
        for s in range(2):
            pre = "L%ds%d_" % (i, s)
            build_pj([dict(g) for g in SSD_GROUPS], P=P, prefix=pre + "pjs_",
                     bind=dict(hT=hcur, vecs=vecs_sc[i], f1=f1_sc, o_xs=s_xs, o_btm=s_btm, o_z=s_z, o_dtraw=s_dt, o_bT=s_bT, o_cT=s_cT))
            b = dict(xs=s_xs, btm=s_btm, z=s_z, dtraw=s_dt, bT=s_bT, cT=s_cT, y=y_ssd[s])
            b.update({k: shared[k] for k in ("triu", "trius", "tril", "trils")})
            build_ssd(need_ctx, P=P, prefix=pre + "ssd_", bind=b)
            build_pj([dict(g) for g in HY_GROUPS], P=P, prefix=pre + "pjh_",
                     bind=dict(hT=hcur, vecs=vecs_sc[i], f1=f1_sc, o_uv=Buf(s_u.t[:, 0:512], "u0"), o_ux1=Buf(s_u.t[:, 512:1024], "u1"),
                               o_ux2=Buf(s_u.t[:, 1024:1536], "u2")))
            b = dict(u=s_u, y=y_hy[s])
            b.update({k: shared[k] for k in ("tabL", "featsTL", "tnL", "cshL")})
            if need_ctx:
                b.update({k: shared[k] for k in ("tabC", "featsTC", "tnC", "cshC")})
            build_hy(need_ctx, P=P, prefix=pre + "hy_", bind=b)
            b = dict(hT=hcur, vecs=vecs_sc[i], yT=y_da[s], f1=f1_sc)
            b.update({k: shared[k] for k in ("cosT", "sinT", "rmT", "blockones", "ident")})
            build_attn(i, need_ctx, P=P, prefix=pre + "at_", bind=b)
        pre = "L%d_B_" % i
        if need_ctx:
            T = NTOK
            tiles = [(0, 128, 1), (128, 128, 1)] + [(256 + 512 * q, 512, 0) for q in range(8)]
            b = dict(hT=hcur, vecs=vecs_sc[i], hout=hT1, ident=shared["ident"])
            for s in range(2):
                b["yhy%d" % s] = y_hy[s]; b["yssd%d" % s] = y_ssd[s]; b["yda%d" % s] = y_da[s]
        else:
            T = 4096
            tiles = [(512 * q, 512, 0) for q in range(8)]
            b = dict(hT=Buf(hcur.t[:, NCTX:NTOK], "hlat"), vecs=vecs_sc[i], ident=shared["ident"])
            for s in range(2):
                b["yhy%d" % s] = Buf(y_hy[s].t[NCTX:NTOK, :], "yh"); b["yssd%d" % s] = Buf(y_ssd[s].t[NCTX:NTOK, :], "ys")
                b["yda%d" % s] = Buf(y_da[s].t[:, NCTX:NTOK], "yd")
            out_final = P.dram("out", [D, 4096], F32, "ExternalOutput")
            b["hout"] = out_final
        build_B(T, tiles, i % 2 == 1, P=P, prefix=pre, bind=b, ytm=True)
        hcur = hT1
    P.fence("sp", [out_final])
    P.emit()
    return P.nc, P.ext


def fused_inputs(b, inp):
    f32 = np.float32
    g = lambda k: np.asarray(inp[k], f32)
    m = {}
    m["hT0"] = np.ascontiguousarray(np.concatenate([g("ctx")[b], g("x")[b]], 0).T)
    m.update(attn_consts())
    m.update(ssd_consts())
    m.update({k + "L": v for k, v in hy_consts(4096).items()})
    m.update({k + "C": v for k, v in hy_consts(256).items()})
    cc = np.stack([g("c")[b], g("c_ctx")], 1)
    m["M_cs"] = np.ascontiguousarray(cc.reshape(8, 128, 2).transpose(1, 0, 2))
    m["M_wm"] = _c(g("w_mod"))
    m["M_bm"] = np.ascontiguousarray(g("b_mod").reshape(2, 48, 128).transpose(0, 2, 1))
    m["M_norms"] = np.ascontiguousarray(np.stack([np.concatenate([_pack(g("norm1_g")[l]), _pack(g("norm2_g")[l])], 1) for l in range(2)]))
    for i in range(2):
        need_ctx = i == 0
        W = g("w_in")[i]
        so = 3072
        xo = so + 1024
        cw = g("ssd_conv_w")[i]; cb = g("ssd_conv_b")[i]
        hw = g("hy_conv_w")[i]; hbv = g("hy_conv_b")[i]
        for s in range(2):
            pre = "L%ds%d_" % (i, s)
            p = pre + "pjs_"
            sel = dict(xs=slice(512 * s, 512 * s + 512), btm=slice(1024 + 128 * s, 1024 + 128 * s + 128),
                       bT=slice(1024 + 128 * s, 1024 + 128 * s + 128), cT=slice(1280 + 128 * s, 1280 + 128 * s + 128))
            for nm, sl in sel.items():
                m[p + "w_" + nm] = _c(W[:, xo:xo + 1536][:, sl])
                m[p + "cw_" + nm] = _c(np.broadcast_to(cw[:, None, sl], (3, 128, sl.stop - sl.start)))
                m[p + "b_" + nm] = _c(cb[sl].reshape(1, 128).T) if nm in ("bT", "cT") else _c(cb[None, sl])
            m[p + "w_z"] = _c(W[:, so + 512 * s:so + 512 * s + 512]); m[p + "b_z"] = np.zeros((1, 512), f32)
            dcols = [5632 + d * 16 + 8 * s + h for d in range(2) for h in range(8)]
            m[p + "w_dtraw"] = _c(W[:, dcols]); m[p + "b_dtraw"] = np.zeros((1, 16), f32)
            p = pre + "ssd_"
            m[p + "dtb"] = _rep(g("ssd_dt_bias")[i][:, 8 * s:8 * s + 8].reshape(16))
            m[p + "alog"] = _rep(g("ssd_a_log")[i][:, 8 * s:8 * s + 8].reshape(16))
            m[p + "dskip"] = _rep(g("ssd_d")[i][8 * s:8 * s + 8])
            m[p + "normg"] = _rep(g("ssd_norm_g")[i][512 * s:512 * s + 512])
            p = pre + "pjh_"
            for gi, nm in enumerate(("uv", "ux1", "ux2")):
                sl = slice(1024 * gi + 512 * s, 1024 * gi + 512 * s + 512)
                m[p + "w_" + nm] = _c(W[:, sl])
                m[p + "cw_" + nm] = _c(np.broadcast_to(hw[:, None, sl], (3, 128, 512)))
                m[p + "b_" + nm] = _c(hbv[None, sl])
            p = pre + "hy_"
            cs_ = slice(512 * s, 512 * s + 512)
            m[p + "hw1"] = _c(g("hy_w1")[i]); m[p + "hw2"] = _c(g("hy_w2")[i]); m[p + "hw3"] = _c(g("hy_w3")[i])
            m[p + "hfreq"] = _c(g("hy_freq")[i].T)
            m[p + "hb"] = _c(np.stack([g("hy_b1")[i], g("hy_b2")[i], g("hy_b3")[i]], 1))
            m[p + "hw4"] = _c(g("hy_w4")[i].reshape(64, 2, 2, 1024)[:, :, :, cs_].reshape(64, 4, 512))
            m[p + "negdelta"] = hy_negdelta(s)
            m[p + "hbias"] = _c(g("hy_bias")[i][:, cs_][None])
            p = pre + "at_"
            o = 5664 + s * 512
            m[p + "wq"] = _c(W[:, o:o + 512]); m[p + "wk"] = _c(W[:, o + 1024:o + 1536]); m[p + "wv"] = _c(W[:, o + 2048:o + 2560])
            m[p + "qkg"] = _c(np.stack([np.concatenate([g("da_q_norm")[i]] * 2), np.concatenate([g("da_k_norm")[i]] * 2)], 1))
            m[p + "lamv"] = _c(g("da_lambda")[i].reshape(1, 256))
            m[p + "sublng"] = _rep(g("da_subln_g")[i])
        p = "L%d_B_" % i
        m[p + "wg"] = _c(W[:, -3072:]); m[p + "wbr"] = _c(g("w_branch")[i]); m[p + "wo"] = _c(g("w_out")[i])
        if i % 2 == 0:
            m[p + "w1"] = _c(g("ffn_w1")[i // 2])[None]; m[p + "w3"] = _c(g("ffn_w3")[i // 2])[None]; m[p + "w2"] = _c(g("ffn_w2")[i // 2])[None]
        else:
            m[p + "rw"] = _c(g("router_w")[i // 2]); m[p + "w1"] = _c(g("moe_w1")[i // 2]); m[p + "w3"] = _c(g("moe_w3")[i // 2])
            m[p + "w2"] = _c(g("moe_w2")[i // 2])
    return m


def kernel_fused(**inp):
    nc, ext = _get("fused", build_fused)
    maps = []
    for b in range(4):
        m = fused_inputs(b, inp)
        missing = [k for k in ext if k not in m]
        extra = [k for k in m if k not in ext]
        assert not missing, missing
        for k in extra:
            del m[k]
        maps.append(m)
    res = run_bass_kernel_spmd(nc, maps, core_ids=list(range(4)))
    out = np.stack([np.ascontiguousarray(r["out"].T) for r in res.results])
    return np.ascontiguousarray(out.astype(np.float32))


def kernel(**inp):
    return kernel_fused(**inp)
```

```python
import numpy as np
import concourse.bass as bass
import concourse.mybir as mybir
from concourse.bass_utils import run_bass_kernel_spmd

F32 = mybir.dt.float32
BF16 = mybir.dt.bfloat16
I32 = mybir.dt.int32
AF = mybir.ActivationFunctionType
ALU = mybir.AluOpType
AX = mybir.AxisListType


class Buf:
    __slots__ = ("t", "name", "lw", "rd")

    def __init__(self, t, name):
        self.t = t
        self.name = name
        self.lw = None
        self.rd = []

    def __getitem__(self, idx):
        return self.t[idx]


class Prog:
    ENG = ("pe", "dve", "act", "pool", "sp")
    NDMA = 6

    def __init__(self, nc):
        self.nc = nc
        self.ops = {e: [] for e in self.ENG}
        self.cnt = {e: 0 for e in self.ENG}
        self.waited = {e: {} for e in self.ENG}
        self.dmak = {e: 0 for e in self.ENG}
        self.pctx = []
        self.sctx = []
        self.sems = {}
        self.nbuf = 0
        self.prefix = ""
        self.bind = {}
        self.ext = {}
        self.psum_tiles = None
        self.nphase = 0
        for e in self.ENG:
            self.sem(e)
        for q in self.ENG:
            for i in range(self.NDMA):
                self.sem(("d", q, i))

    def sem(self, key):
        if key not in self.sems:
            cm = self.nc.semaphore("s_%s" % "_".join(str(k) for k in (key if isinstance(key, tuple) else (key,))))
            self.sems[key] = cm.__enter__()
            self.pctx.append(cm)
        return self.sems[key]

    def sb(self, shape, dt, name=None):
        self.nbuf += 1
        name = "%s%s_%d" % (self.prefix, name or "sb", self.nbuf)
        cm = self.nc.sbuf_tensor(name, list(shape), dt)
        t = cm.__enter__()
        self.sctx.append(cm)
        return Buf(t, name)

    def ps(self, shape, dt, name=None):
        self.nbuf += 1
        name = name or "ps%d" % self.nbuf
        cm = self.nc.psum_tensor(name, list(shape), dt)
        t = cm.__enter__()
        self.pctx.append(cm)
        return Buf(t, name)

    def dram(self, name, shape, dt, kind="Internal"):
        if name in self.bind:
            b = self.bind[name]
            assert tuple(b.t.shape) == tuple(shape), (name, b.t.shape, shape)
            return b
        full = self.prefix + name
        if kind == "ExternalInput":
            self.ext[full] = (tuple(shape), dt)
        return Buf(self.nc.dram_tensor(full, list(shape), dt, kind=kind).ap(), full)

    def scratch(self, name, shape, dt):
        return Buf(self.nc.dram_tensor(name, list(shape), dt, kind="Internal").ap(), name)

    def sub(self, buf, name):
        return Buf(buf.t, name)

    def _need(self, eng, tok, waits):
        if tok is None:
            return
        k, v = tok
        if eng == "pe" and k == "pe":
            return
        if self.waited[eng].get(k, 0) >= v:
            return
        self.waited[eng][k] = v
        waits[k] = max(waits.get(k, 0), v)

    def _deps(self, eng, r, w):
        waits = {}
        for b in r:
            self._need(eng, b.lw, waits)
        for b in w:
            self._need(eng, b.lw, waits)
            for t in b.rd:
                self._need(eng, t, waits)
        return waits

    def op(self, eng, fn, r=(), w=()):
        waits = self._deps(eng, r, w)
        self.cnt[eng] += 1
        tok = (eng, self.cnt[eng])
        self.ops[eng].append((waits, fn, (eng, 1)))
        for b in r:
            b.rd.append(tok)
        for b in w:
            b.lw = tok
            b.rd = []
        return tok

    def dma(self, q, out, in_, r=(), w=(), **kw):
        waits = self._deps(q, r, w)
        k = self.dmak[q]
        self.dmak[q] += 1
        key = ("d", q, k % self.NDMA)
        val = 16 * (k // self.NDMA + 1)
        if k >= self.NDMA:
            self._need(q, (key, val - 16), waits)
        tok = (key, val)
        self.ops[q].append((waits, lambda e: e.dma_start(out=out, in_=in_, **kw), (key, 16)))
        for b in r:
            b.rd.append(tok)
        for b in w:
            b.lw = tok
            b.rd = []
        return tok

    def fence(self, eng, bufs):
        waits = {}
        for b in bufs:
            self._need(eng, b.lw, waits)
        self.ops[eng].append((waits, None, None))

    def barrier(self):
        toks = [(e, self.cnt[e]) for e in self.ENG if self.cnt[e] > 0]
        for q in self.ENG:
            k = self.dmak[q]
            for slot in range(min(self.NDMA, k)):
                n_on_slot = (k - slot + self.NDMA - 1) // self.NDMA
                toks.append((("d", q, slot), 16 * n_on_slot))
        for e in self.ENG:
            waits = {}
            for t in toks:
                if t[0] == e:
                    continue
                self._need(e, t, waits)
            self.ops[e].append((waits, None, None))

    def flush(self):
        nc = self.nc
        self.nphase += 1
        with nc.Block() as block:
            def mk(ename):
                def body(eng):
                    for waits, fn, inc in self.ops[ename]:
                        for k, v in waits.items():
                            eng.wait_ge(self.sems[k], v)
                        if fn is not None:
                            ins = fn(eng)
                            ins.then_inc(self.sems[inc[0]], inc[1])
                return body
            block.tensor(mk("pe"))
            block.vector(mk("dve"))
            block.scalar(mk("act"))
            block.gpsimd(mk("pool"))
            block.sync(mk("sp"))
        self.ops = {e: [] for e in self.ENG}

    def end_phase(self):
        self.barrier()
        self.flush()
        while self.sctx:
            self.sctx.pop().__exit__(None, None, None)
        self.bind = {}
        self.prefix = ""

    def emit(self):
        self.end_phase()
        while self.pctx:
            self.pctx.pop().__exit__(None, None, None)


def _begin(P, prefix, bind):
    own = P is None
    if own:
        P = Prog(bass.Bass("TRN2", target_bir_lowering=False))
    P.prefix = prefix
    P.bind = dict(bind or {})
    return P, own


def _finish(P, own, outs):
    if own:
        P.fence("sp", outs)
        P.emit()
        return P.nc
    P.end_phase()
    return None


D = 1024
KC = 8
EPS = 1e-6


class Ctx:
    def __init__(self, P, nrot=8, wsize=4096, nw=6):
        self.P = P
        self.wsize = wsize
        self.nw = nw
        if P.psum_tiles is None:
            P.psum_tiles = [P.ps([128, 512], F32, name="psum%d" % i) for i in range(8)]
        self.psum = P.psum_tiles
        self.nrot = nrot
        self.accs = self.psum[nrot:]
        self.pi = 0
        self.wb = []
        self.wi = 0
        self.ci = 0

    def ps(self):
        p = self.psum[self.pi % self.nrot]
        self.pi += 1
        return p

    def wbuf(self):
        if not self.wb:
            self.wb = [self.P.sb([128, self.wsize], BF16, name="wbuf%d" % i) for i in range(self.nw)]
        b = self.wb[self.wi % len(self.wb)]
        self.wi += 1
        return b

    def wload(self, Wd, ap, kc, ncol):
        b = self.wbuf()
        v = b[:, 0:kc * ncol].rearrange("p (k c) -> p k c", k=kc)
        self.P.dma("pool", v, ap, r=[Wd], w=[b])
        return b, v

    def ev(self):
        self.ci += 1
        return "dve" if self.ci % 2 else "act"


def wview(Wd, r0, kc, c0, ncol):
    return Wd[r0:r0 + kc * 128, c0:c0 + ncol].rearrange("(k p) c -> p k c", p=128)


def modnorm(C, hT, n, A, Bv, fT, ones_bf, sqb, tmp, rstd, f32out=None):
    P = C.P
    P.op("act", lambda e: e.activation(out=sqb[:, :, 0:n], in_=hT[:, :, 0:n], func=AF.Square), r=[hT], w=[sqb])
    ps = C.ps()
    for kc in range(KC):
        P.op("pe", lambda e, kc=kc: e.matmul(ps[:, 0:n], lhsT=ones_bf[:], rhs=sqb[:, kc, 0:n],
                                             start=(kc == 0), stop=(kc == KC - 1)), r=[ones_bf, sqb], w=[ps])
    P.op("act", lambda e: e.activation(out=rstd[:, 0:n], in_=ps[:, 0:n], func=AF.Sqrt, bias=EPS, scale=1.0 / D),
         r=[ps], w=[rstd])
    P.op("dve", lambda e: e.reciprocal(out=rstd[:, 0:n], in_=rstd[:, 0:n]), r=[rstd], w=[rstd])
    for kc in range(KC):
        P.op("dve", lambda e, kc=kc: e.scalar_tensor_tensor(
            out=tmp[:, kc, 0:n], in0=hT[:, kc, 0:n], scalar=A[:, kc:kc + 1], in1=rstd[:, 0:n],
            op0=ALU.mult, op1=ALU.mult), r=[hT, A, rstd], w=[tmp])
    for kc in range(KC):
        P.op("act", lambda e, kc=kc: e.activation(out=fT[:, kc, 0:n], in_=tmp[:, kc, 0:n], func=AF.Identity,
                                                  bias=Bv[:, kc:kc + 1], scale=1.0), r=[tmp, Bv], w=[fT])
        if f32out is not None:
            P.op("dve", lambda e, kc=kc: e.tensor_scalar(out=f32out[:, kc, 0:n], in0=tmp[:, kc, 0:n],
                                                         scalar1=Bv[:, kc:kc + 1], scalar2=None, op0=ALU.add),
                 r=[tmp, Bv], w=[f32out])


def build_B(T, tiles, moe, P=None, prefix="", bind=None, ytm=False):
    P, own = _begin(P, prefix, bind)
    C = Ctx(P, nw=5)
    NMAX = max(n for _, n, _ in tiles)
    hT_d = P.dram("hT", [D, T], F32, "ExternalInput")
    if ytm:
        ytm_d = [[P.dram("y%s%d" % (nm, s_), [T, 512], F32, "ExternalInput") for s_ in range(2)] for nm in ("hy", "ssd")]
        yda_d = [P.dram("yda%d" % s_, [512, T], F32, "ExternalInput") for s_ in range(2)]
    else:
        yT_d = P.dram("yT", [3, D, T], F32, "ExternalInput")
    vec_d = P.dram("vecs", [128, 112], F32, "ExternalInput")
    wg_d = P.dram("wg", [D, 3 * D], F32, "ExternalInput")
    wbr_d = P.dram("wbr", [3, D, D], F32, "ExternalInput")
    wo_d = P.dram("wo", [D, D], F32, "ExternalInput")
    if moe:
        NE, FF = 8, 2048
        rw_d = P.dram("rw", [D, NE], F32, "ExternalInput")
        w1_d = P.dram("w1", [NE, D, FF], F32, "ExternalInput")
        w3_d = P.dram("w3", [NE, D, FF], F32, "ExternalInput")
        w2_d = P.dram("w2", [NE, FF, D], F32, "ExternalInput")
    else:
        NE, FF = 1, 4096
        w1_d = P.dram("w1", [NE, D, FF], F32, "ExternalInput")
        w3_d = P.dram("w3", [NE, D, FF], F32, "ExternalInput")
        w2_d = P.dram("w2", [NE, FF, D], F32, "ExternalInput")
    out_d = P.dram("hout", [D, T], F32, "ExternalOutput")
    FC = FF // 128

    vec = P.sb([128, 112], F32, "vec")
    der = P.sb([128, 2, 2, 8], F32, "der")
    ones_bf = P.sb([128, 128], BF16, "ones")
    hT = P.sb([128, KC, NMAX], F32, "hTs")
    f1 = P.sb([128, KC, NMAX], BF16, "f1")
    sqb = P.sb([128, KC, NMAX], BF16, "sqb")
    tmp = P.sb([128, KC, NMAX], F32, "tmp")
    rstd = P.sb([128, NMAX], F32, "rstd")
    yb = P.sb([128, 3, KC, NMAX], BF16, "yb")
    merged = P.sb([128, KC, NMAX], BF16, "merged")
    acc = P.sb([128, 4, NMAX], F32, "acc")
    sig = P.sb([128, NMAX], F32, "sig")
    tm2 = P.sb([128, NMAX], F32, "tm2")
    gT = P.sb([128, FC, NMAX], BF16, "gT")
    if moe or ytm:
        id_d = P.dram("ident", [128, 128], F32, "ExternalInput")
        ident = P.sb([128, 128], F32, "idents")
        P.dma("sp", ident[:], id_d[:], r=[id_d], w=[ident])
    if ytm:
        yst_ = [P.sb([128, 1024], F32, "ytmst%d" % q) for q in range(2)]
    if moe:
        f32T = P.sb([128, KC, NMAX], F32, "f32T")
        rw = P.sb([128, KC, NE], F32, "rws")
        lg = P.sb([128, 4, NE], F32, "lg")
        gt = P.sb([128, 4, NE], F32, "gt")
        mk = P.sb([128, 4, NE], F32, "mk")
        m12 = P.sb([128, 4, 4], F32, "m12")
        gtT = P.sb([NE, NMAX], F32, "gtT")
        sel = P.sb([NE, NE, 128], F32, "sel")
        gbc = P.sb([128, NMAX], F32, "gbc")
        oacc = P.sb([128, KC, NMAX], F32, "oacc")

    P.dma("sp", vec[:], vec_d[:], r=[vec_d], w=[vec])
    P.op("dve", lambda e: e.memset(ones_bf[:], 1.0), w=[ones_bf])
    for cls in range(2):
        for which, (gcol, sccol) in enumerate(((0, 16 + 8 + cls * 48), (8, 16 + 32 + cls * 48))):
            P.op("dve", lambda e, cls=cls, which=which, gcol=gcol, sccol=sccol: e.scalar_tensor_tensor(
                out=der[:, cls, which, :], in0=vec[:, sccol:sccol + 8], scalar=1.0, in1=vec[:, gcol:gcol + 8],
                op0=ALU.add, op1=ALU.mult), r=[vec], w=[der])
    if moe:
        P.dma("sp", rw[:], rw_d[:, :].rearrange("(k p) e -> p k e", p=128), r=[rw_d], w=[rw])
        for e_ in range(NE):
            P.op("dve", lambda e, e_=e_: e.tensor_copy(out=sel[:, e_, :], in_=ident[0:NE, e_:e_ + 1].to_broadcast([NE, 128])),
                 r=[ident], w=[sel])

    def do_tok(t0, n, cls):
        mo = 16 + cls * 48
        P.dma("sp", hT[:, :, 0:n], hT_d[:, t0:t0 + n].rearrange("(k p) t -> p k t", p=128), r=[hT_d], w=[hT])
        if ytm:
            qn_ = 0
            for i in range(2):
                for sb_ in range(n // 128):
                    st_ = yst_[qn_ % 2]
                    qn_ += 1
                    r0 = t0 + sb_ * 128
                    for s_ in range(2):
                        P.dma("sp", st_[:, s_ * 512:(s_ + 1) * 512], ytm_d[i][s_][r0:r0 + 128, :], r=[ytm_d[i][s_]], w=[st_])
                    for kc in range(KC):
                        ps = C.ps()
                        P.op("pe", lambda e, ps=ps, st_=st_, kc=kc: e.transpose(out=ps[:, 0:128], in_=st_[:, kc * 128:(kc + 1) * 128], identity=ident[:]),
                             r=[st_, ident], w=[ps])
                        if kc % 2:
                            P.op("dve", lambda e, ps=ps, i=i, kc=kc, sb_=sb_: e.tensor_copy(out=yb[:, i, kc, sb_ * 128:(sb_ + 1) * 128], in_=ps[:, 0:128]),
                                 r=[ps], w=[yb])
                        else:
                            P.op("act", lambda e, ps=ps, i=i, kc=kc, sb_=sb_: e.activation(out=yb[:, i, kc, sb_ * 128:(sb_ + 1) * 128], in_=ps[:, 0:128], func=AF.Copy),
                                 r=[ps], w=[yb])
            for s_ in range(2):
                P.dma("pool", yb[:, 2, s_ * 4:(s_ + 1) * 4, 0:n], yda_d[s_][:, t0:t0 + n].rearrange("(k p) t -> p k t", p=128),
                      r=[yda_d[s_]], w=[yb])
        else:
            for i in range(3):
                P.dma("pool", yb[:, i, :, 0:n], yT_d[i, :, t0:t0 + n].rearrange("(k p) t -> p k t", p=128), r=[yT_d], w=[yb])
        modnorm_v(C, hT, n, der, (cls, 0), vec, mo + 0, f1, ones_bf, sqb, tmp, rstd)
        for dg in range(2):
            for i in range(3):
                wbb, wbv = C.wload(wbr_d, wview(wbr_d[i], 0, KC, dg * 512, 512), KC, 512)
                wgb, wgv = C.wload(wg_d, wview(wg_d, 0, KC, i * D + dg * 512, 512), KC, 512)
                for j in range(4):
                    psA = C.ps(); psB = C.ps()
                    for kc in range(KC):
                        P.op("pe", lambda e, kc=kc, j=j, wbv=wbv, i=i, psA=psA: e.matmul(
                            psA[:, 0:n], lhsT=wbv[:, kc, j * 128:(j + 1) * 128], rhs=yb[:, i, kc, 0:n],
                            start=(kc == 0), stop=(kc == KC - 1)), r=[wbb, yb], w=[psA])
                    for kc in range(KC):
                        P.op("pe", lambda e, kc=kc, j=j, wgv=wgv, psB=psB: e.matmul(
                            psB[:, 0:n], lhsT=wgv[:, kc, j * 128:(j + 1) * 128], rhs=f1[:, kc, 0:n],
                            start=(kc == 0), stop=(kc == KC - 1)), r=[wgb, f1], w=[psB])
                    P.op("act", lambda e, psB=psB: e.activation(out=sig[:, 0:n], in_=psB[:, 0:n], func=AF.Sigmoid),
                         r=[psB], w=[sig])
                    if i == 0:
                        P.op("dve", lambda e, psA=psA, j=j: e.tensor_tensor(out=acc[:, j, 0:n], in0=psA[:, 0:n], in1=sig[:, 0:n],
                                                                    op=ALU.mult), r=[psA, sig], w=[acc])
                    else:
                        P.op("dve", lambda e, psA=psA: e.tensor_tensor(out=tm2[:, 0:n], in0=psA[:, 0:n], in1=sig[:, 0:n],
                                                               op=ALU.mult), r=[psA, sig], w=[tm2])
                        if i == 1:
                            P.op("dve", lambda e, j=j: e.tensor_tensor(out=acc[:, j, 0:n], in0=acc[:, j, 0:n], in1=tm2[:, 0:n],
                                                                       op=ALU.add), r=[acc, tm2], w=[acc])
                        else:
                            P.op("dve", lambda e, j=j, dg=dg: e.tensor_tensor(out=merged[:, dg * 4 + j, 0:n], in0=acc[:, j, 0:n],
                                                                       in1=tm2[:, 0:n], op=ALU.add), r=[acc, tm2], w=[merged])
        for dg in range(2):
            wob, wov = C.wload(wo_d, wview(wo_d, 0, KC, dg * 512, 512), KC, 512)
            for j in range(4):
                db = dg * 4 + j
                ps = C.ps()
                for kc in range(KC):
                    P.op("pe", lambda e, kc=kc, j=j, wov=wov, ps=ps: e.matmul(
                        ps[:, 0:n], lhsT=wov[:, kc, j * 128:(j + 1) * 128], rhs=merged[:, kc, 0:n],
                        start=(kc == 0), stop=(kc == KC - 1)), r=[wob, merged], w=[ps])
                P.op("dve", lambda e, ps=ps, db=db, mo=mo: e.scalar_tensor_tensor(
                    out=hT[:, db, 0:n], in0=ps[:, 0:n], scalar=vec[:, mo + 16 + db:mo + 16 + db + 1], in1=hT[:, db, 0:n],
                    op0=ALU.mult, op1=ALU.add), r=[ps, vec, hT], w=[hT])
        modnorm_v(C, hT, n, der, (cls, 1), vec, mo + 24, f1, ones_bf, sqb, tmp, rstd, f32out=(f32T if moe else None))
        if moe:
            nsub = n // 128
            for sb_ in range(nsub):
                ps = C.ps()
                for kc in range(KC):
                    P.op("pe", lambda e, kc=kc, sb_=sb_, ps=ps: e.matmul(
                        ps[:, 0:NE], lhsT=f32T[:, kc, sb_ * 128:(sb_ + 1) * 128], rhs=rw[:, kc, :],
                        start=(kc == 0), stop=(kc == KC - 1)), r=[f32T, rw], w=[ps])
                P.op("dve", lambda e, ps=ps, sb_=sb_: e.tensor_copy(out=lg[:, sb_, :], in_=ps[:, 0:NE]), r=[ps], w=[lg])
            for sb_ in range(nsub):
                P.op("dve", lambda e, sb_=sb_: e.reduce_max(out=m12[:, sb_, 0:1], in_=lg[:, sb_, :], axis=AX.X), r=[lg], w=[m12])
                P.op("dve", lambda e, sb_=sb_: e.tensor_scalar(out=mk[:, sb_, :], in0=lg[:, sb_, :], scalar1=m12[:, sb_, 0:1],
                                                               scalar2=None, op0=ALU.is_equal), r=[lg, m12], w=[mk])
                P.op("dve", lambda e, sb_=sb_: e.scalar_tensor_tensor(out=gt[:, sb_, :], in0=mk[:, sb_, :], scalar=-1e30,
                                                                      in1=lg[:, sb_, :], op0=ALU.mult, op1=ALU.add),
                     r=[mk, lg], w=[gt])
                P.op("dve", lambda e, sb_=sb_: e.reduce_max(out=m12[:, sb_, 1:2], in_=gt[:, sb_, :], axis=AX.X), r=[gt], w=[m12])
                P.op("dve", lambda e, sb_=sb_: e.tensor_tensor(out=m12[:, sb_, 2:3], in0=m12[:, sb_, 0:1], in1=m12[:, sb_, 1:2],
                                                               op=ALU.subtract), r=[m12], w=[m12])
                P.op("act", lambda e, sb_=sb_: e.activation(out=m12[:, sb_, 3:4], in_=m12[:, sb_, 2:3], func=AF.Sigmoid, scale=-1.0),
                     r=[m12], w=[m12])
                P.op("act", lambda e, sb_=sb_: e.activation(out=m12[:, sb_, 2:3], in_=m12[:, sb_, 2:3], func=AF.Sigmoid),
                     r=[m12], w=[m12])
                P.op("dve", lambda e, sb_=sb_: e.tensor_scalar(out=gt[:, sb_, :], in0=gt[:, sb_, :], scalar1=m12[:, sb_, 1:2],
                                                               scalar2=m12[:, sb_, 3:4], op0=ALU.is_equal, op1=ALU.mult),
                     r=[gt, m12], w=[gt])
                P.op("dve", lambda e, sb_=sb_: e.scalar_tensor_tensor(out=gt[:, sb_, :], in0=mk[:, sb_, :], scalar=m12[:, sb_, 2:3],
                                                                      in1=gt[:, sb_, :], op0=ALU.mult, op1=ALU.add),
                     r=[mk, m12, gt], w=[gt])
                ps = C.ps()
                P.op("pe", lambda e, sb_=sb_, ps=ps: e.transpose(out=ps[0:NE, 0:128], in_=gt[:, sb_, :], identity=ident[:]),
                     r=[gt, ident], w=[ps])
                P.op("dve", lambda e, sb_=sb_, ps=ps: e.tensor_copy(out=gtT[:, sb_ * 128:(sb_ + 1) * 128], in_=ps[0:NE, 0:128]),
                     r=[ps], w=[gtT])
        for ex in range(NE):
            if moe:
                ps = C.ps()
                P.op("pe", lambda e, ex=ex, ps=ps: e.matmul(ps[:, 0:n], lhsT=sel[:, ex, :], rhs=gtT[:, 0:n], start=True, stop=True),
                     r=[sel, gtT], w=[ps])
                P.op("act", lambda e, ps=ps: e.activation(out=gbc[:, 0:n], in_=ps[:, 0:n], func=AF.Copy), r=[ps], w=[gbc])
            for fg in range(FF // 512):
                w1b, w1v = C.wload(w1_d, wview(w1_d[ex], 0, KC, fg * 512, 512), KC, 512)
                w3b, w3v = C.wload(w3_d, wview(w3_d[ex], 0, KC, fg * 512, 512), KC, 512)
                for j in range(4):
                    psa = C.ps(); psb = C.ps()
                    for kc in range(KC):
                        P.op("pe", lambda e, kc=kc, j=j, w1v=w1v, psa=psa: e.matmul(
                            psa[:, 0:n], lhsT=w1v[:, kc, j * 128:(j + 1) * 128], rhs=f1[:, kc, 0:n],
                            start=(kc == 0), stop=(kc == KC - 1)), r=[w1b, f1], w=[psa])
                    for kc in range(KC):
                        P.op("pe", lambda e, kc=kc, j=j, w3v=w3v, psb=psb: e.matmul(
                            psb[:, 0:n], lhsT=w3v[:, kc, j * 128:(j + 1) * 128], rhs=f1[:, kc, 0:n],
                            start=(kc == 0), stop=(kc == KC - 1)), r=[w3b, f1], w=[psb])
                    P.op("act", lambda e, psa=psa: e.activation(out=sig[:, 0:n], in_=psa[:, 0:n], func=AF.Silu), r=[psa], w=[sig])
                    if moe:
                        P.op("dve", lambda e: e.tensor_tensor(out=sig[:, 0:n], in0=sig[:, 0:n], in1=gbc[:, 0:n], op=ALU.mult),
                             r=[sig, gbc], w=[sig])
                    P.op("dve", lambda e, psb=psb, fg=fg, j=j: e.tensor_tensor(out=gT[:, fg * 4 + j, 0:n], in0=psb[:, 0:n],
                                                                        in1=sig[:, 0:n], op=ALU.mult), r=[psb, sig], w=[gT])
            for db in range(KC):
                ps = C.ps()
                w2b, w2v = C.wload(w2_d, wview(w2_d[ex], 0, FC, db * 128, 128), FC, 128)
                for fc in range(FC):
                    P.op("pe", lambda e, fc=fc, w2v=w2v, ps=ps: e.matmul(
                        ps[:, 0:n], lhsT=w2v[:, fc, :], rhs=gT[:, fc, 0:n],
                        start=(fc == 0), stop=(fc == FC - 1)), r=[w2b, gT], w=[ps])
                if not moe:
                    P.op("dve", lambda e, ps=ps, db=db, mo=mo: e.scalar_tensor_tensor(
                        out=hT[:, db, 0:n], in0=ps[:, 0:n], scalar=vec[:, mo + 40 + db:mo + 40 + db + 1], in1=hT[:, db, 0:n],
                        op0=ALU.mult, op1=ALU.add), r=[ps, vec, hT], w=[hT])
                elif ex == 0:
                    P.op("dve", lambda e, ps=ps, db=db: e.tensor_copy(out=oacc[:, db, 0:n], in_=ps[:, 0:n]), r=[ps], w=[oacc])
                else:
                    P.op("dve", lambda e, ps=ps, db=db: e.tensor_tensor(out=oacc[:, db, 0:n], in0=oacc[:, db, 0:n], in1=ps[:, 0:n],
                                                                 op=ALU.add), r=[ps, oacc], w=[oacc])
        if moe:
            for db in range(KC):
                P.op("dve", lambda e, db=db, mo=mo: e.scalar_tensor_tensor(
                    out=hT[:, db, 0:n], in0=oacc[:, db, 0:n], scalar=vec[:, mo + 40 + db:mo + 40 + db + 1], in1=hT[:, db, 0:n],
                    op0=ALU.mult, op1=ALU.add), r=[oacc, vec, hT], w=[hT])
        P.dma("sp", out_d[:, t0:t0 + n].rearrange("(k p) t -> p k t", p=128), hT[:, :, 0:n], r=[hT], w=[out_d])
    for tl in tiles:
        do_tok(*tl)
    return _finish(P, own, [out_d])


def modnorm_v(C, hT, n, der, idx, vec, bcol, fT, ones_bf, sqb, tmp, rstd, f32out=None):
    cls, which = idx

    class _V:
        pass
    A = Buf(der.t[:, cls, which, :], "A")
    A.lw, A.rd = der.lw, der.rd
    Bv = Buf(vec.t[:, bcol:bcol + 8], "Bv")
    Bv.lw, Bv.rd = vec.lw, vec.rd
    modnorm(C, hT, n, A, Bv, fT, ones_bf, sqb, tmp, rstd, f32out=f32out)


NTOK = 4352
NCTX = 256


def build_attn(layer, need_ctx, nheads=4, nqb=8, P=None, prefix="", bind=None):
    import math
    lam_init = 0.8 - 0.6 * math.exp(-0.3 * (layer + 1))
    P, own = _begin(P, prefix, bind)
    C = Ctx(P, nrot=4, wsize=1024, nw=6)
    hT_d = P.dram("hT", [D, NTOK], F32, "ExternalInput")
    vec_d = P.dram("vecs", [128, 112], F32, "ExternalInput")
    wq_d = P.dram("wq", [D, 512], F32, "ExternalInput")
    wk_d = P.dram("wk", [D, 512], F32, "ExternalInput")
    wv_d = P.dram("wv", [D, 512], F32, "ExternalInput")
    qkg_d = P.dram("qkg", [128, 2], F32, "ExternalInput")
    lam_d = P.dram("lamv", [1, 256], F32, "ExternalInput")
    sg_d = P.dram("sublng", [128, 128], F32, "ExternalInput")
    cos_d = P.dram("cosT", [128, 4096], F32, "ExternalInput")
    sin_d = P.dram("sinT", [128, 4096], F32, "ExternalInput")
    rm_d = P.dram("rmT", [128, 128], F32, "ExternalInput")
    bo_d = P.dram("blockones", [128, 128], F32, "ExternalInput")
    id_d = P.dram("ident", [128, 128], F32, "ExternalInput")
    out_d = P.dram("yT", [512, NTOK], F32, "ExternalOutput")

    vec = P.sb([128, 112], F32, "vec")
    der = P.sb([128, 2, 2, 8], F32, "der")
    ones_bf = P.sb([128, 128], BF16, "ones")
    ones_f = P.sb([1, 128], F32, "ones_f")
    f1 = P.sb([128, KC, NTOK], BF16, "f1")
    NT = 256
    hTt = P.sb([128, KC, NT], F32, "hTt")
    sqb = P.sb([128, KC, NT], BF16, "sqb")
    tmp = P.sb([128, KC, NT], F32, "tmp")
    rstd = P.sb([128, 512], F32, "rstd")
    qkg = P.sb([128, 2], F32, "qkgs")
    lamv = P.sb([1, 256], F32, "lamvs")
    lams = P.sb([1, 8], F32, "lams")
    neglam = P.sb([128, 1], F32, "neglam")
    sg = P.sb([128, 128], F32, "sg")
    cosT = P.sb([128, 4096], F32, "cosTs")
    sinT = P.sb([128, 4096], F32, "sinTs")
    rmT = P.sb([128, 128], F32, "rmTs")
    bo = P.sb([128, 128], BF16, "bos")
    ident = P.sb([128, 128], F32, "idents")
    qT = P.sb([128, NTOK], BF16, "qT")
    kT = P.sb([128, NTOK], BF16, "kT")
    Vaug = P.sb([128, 34, 130], BF16, "Vaug")
    sqq_ = [P.sb([128, 512], BF16, "sqq%d" % i) for i in range(1)]
    qn_ = [P.sb([128, 512], F32, "qn%d" % i) for i in range(1)]
    t1_ = [P.sb([128, 512], F32, "t1%d" % i) for i in range(1)]
    t2_ = [P.sb([128, 512], F32, "t2%d" % i) for i in range(1)]
    rs_ = [P.sb([128, 512], F32, "rs%d" % i) for i in range(1)]
    bcnt = [0]
    PT = [P.sb([128, 512], BF16, "PT%d" % i) for i in range(3)]
    junk = P.sb([128, 128], F32, "junk")
    rc = P.sb([128, 8], F32, "rc")
    pass_ = 0
    dacc = [P.sb([128, 512], F32, "dacc%d" % i) for i in range(2)]
    dcnt = [0]
    ycnt = [0]
    ones128 = P.sb([128, 128], F32, "ones128")
    rdn = P.sb([128, 512], F32, "rdn")
    o0f = P.sb([128, 512], F32, "o0f")
    o1f = P.sb([128, 512], F32, "o1f")
    oof = P.sb([128, 512], F32, "oof")
    sqf = P.sb([128, 512], F32, "sqf")
    rsf = P.sb([128, 512], F32, "rsf")
    ystb = [P.sb([128, 512], F32, "ystb%d" % i) for i in range(2)]
    sgcol = P.sb([128, 1], F32, "sgcol")

    P.dma("sp", vec[:], vec_d[:], r=[vec_d], w=[vec])
    P.dma("sp", qkg[:], qkg_d[:], r=[qkg_d], w=[qkg])
    P.dma("sp", lamv[:], lam_d[:], r=[lam_d], w=[lamv])
    P.dma("sp", sg[:], sg_d[:], r=[sg_d], w=[sg])
    P.dma("sp", cosT[:], cos_d[:], r=[cos_d], w=[cosT])
    P.dma("sp", sinT[:], sin_d[:], r=[sin_d], w=[sinT])
    P.dma("sp", rmT[:], rm_d[:], r=[rm_d], w=[rmT])
    P.dma("pool", bo[:], bo_d[:], r=[bo_d], w=[bo])
    P.dma("sp", ident[:], id_d[:], r=[id_d], w=[ident])
    P.op("dve", lambda e: e.memset(ones_bf[:], 1.0), w=[ones_bf])
    P.op("dve", lambda e: e.memset(ones_f[:], 1.0), w=[ones_f])
    P.op("dve", lambda e: e.memset(Vaug[:, :, 128:130], 1.0), w=[Vaug])
    for cls in range(2):
        sccol = 16 + 8 + cls * 48
        P.op("dve", lambda e, cls=cls, sccol=sccol: e.scalar_tensor_tensor(
            out=der[:, cls, 0, :], in0=vec[:, sccol:sccol + 8], scalar=1.0, in1=vec[:, 0:8],
            op0=ALU.add, op1=ALU.mult), r=[vec], w=[der])
    P.op("dve", lambda e: e.tensor_scalar(out=sg[:], in0=sg[:], scalar1=float(1.0 - lam_init), scalar2=None, op0=ALU.mult),
         r=[sg], w=[sg])
    P.op("dve", lambda e: e.memset(ones128[:], 1.0), w=[ones128])
    P.op("dve", lambda e: e.tensor_tensor(out=junk[:], in0=sg[:], in1=ident[:], op=ALU.mult), r=[sg, ident], w=[junk])
    P.op("dve", lambda e: e.reduce_sum(out=sgcol[:], in_=junk[:], axis=AX.X), r=[junk], w=[sgcol])
    P.op("dve", lambda e: e.tensor_tensor(out=lamv[:, 0:64], in0=lamv[:, 0:64], in1=lamv[:, 64:128], op=ALU.mult), r=[lamv], w=[lamv])
    P.op("dve", lambda e: e.tensor_tensor(out=lamv[:, 128:192], in0=lamv[:, 128:192], in1=lamv[:, 192:256], op=ALU.mult), r=[lamv], w=[lamv])
    P.op("dve", lambda e: e.reduce_sum(out=lams[:, 0:1], in_=lamv[:, 0:64], axis=AX.X), r=[lamv], w=[lams])
    P.op("dve", lambda e: e.reduce_sum(out=lams[:, 1:2], in_=lamv[:, 128:192], axis=AX.X), r=[lamv], w=[lams])
    P.op("act", lambda e: e.activation(out=lams[:, 2:4], in_=lams[:, 0:2], func=AF.Exp), r=[lams], w=[lams])
    P.op("dve", lambda e: e.tensor_tensor(out=lams[:, 4:5], in0=lams[:, 3:4], in1=lams[:, 2:3], op=ALU.subtract), r=[lams], w=[lams])
    P.op("dve", lambda e: e.tensor_scalar(out=lams[:, 4:5], in0=lams[:, 4:5], scalar1=float(-lam_init), scalar2=None, op0=ALU.add),
         r=[lams], w=[lams])
    ps = C.ps()
    P.op("pe", lambda e, ps=ps: e.matmul(ps[:, 0:1], lhsT=ones_f[:, :], rhs=lams[:, 4:5], start=True, stop=True), r=[ones_f, lams], w=[ps])
    P.op("dve", lambda e, ps=ps: e.tensor_copy(out=neglam[:], in_=ps[:, 0:1]), r=[ps], w=[neglam])

    f1_pre = P.bind.get("f1")
    if f1_pre is not None:
        f1pv = f1_pre[:, :].rearrange("p (k t) -> p k t", k=KC)
        P.dma("sp", f1[:, :, 0:NCTX], f1pv[:, :, 1:1 + NCTX], r=[f1_pre], w=[f1])
        P.dma("sp", f1[:, :, NCTX:NTOK], f1pv[:, :, 3 + NCTX:3 + NTOK], r=[f1_pre], w=[f1])
    for t0 in (range(0, NTOK, NT) if f1_pre is None else ()):
        cls = 1 if t0 < NCTX else 0
        mo = 16 + cls * 48
        P.dma("sp", hTt[:, :, :], hT_d[:, t0:t0 + NT].rearrange("(k p) t -> p k t", p=128), r=[hT_d], w=[hTt])
        f1v = Buf(f1.t[:, :, t0:t0 + NT], "f1v")
        f1v.lw, f1v.rd = f1.lw, f1.rd
        modnorm_v(C, hTt, NT, der, (cls, 0), vec, mo + 0, f1v, ones_bf, sqb, tmp, rstd)
        f1.lw = f1v.lw
    blocks = [(0, 256)] + [(256 + i * 512, 512) for i in range(8)]
    for h in range(nheads):
        wqb, wqv = C.wload(wq_d, wview(wq_d, 0, KC, h * 128, 128), KC, 128)
        wkb, wkv = C.wload(wk_d, wview(wk_d, 0, KC, h * 128, 128), KC, 128)
        wvb, wvv = C.wload(wv_d, wview(wv_d, 0, KC, h * 128, 128), KC, 128)
        for (dst, wb_, wv_, gcol, isq) in ((qT, wqb, wqv, 0, True), (kT, wkb, wkv, 1, False)):
            for (t0, n) in blocks:
                if isq and (not need_ctx) and t0 < NCTX:
                    continue
                bi_ = 0
                bcnt[0] += 1
                sqq, qn, t1, t2, rstd = sqq_[bi_], qn_[bi_], t1_[bi_], t2_[bi_], rs_[bi_]
                ps = C.ps()
                for kc in range(KC):
                    P.op("pe", lambda e, sqq=sqq, qn=qn, t1=t1, t2=t2, rstd=rstd, kc=kc, ps=ps, wv_=wv_, t0=t0, n=n: e.matmul(
                        ps[:, 0:n], lhsT=wv_[:, kc, :], rhs=f1[:, kc, t0:t0 + n], start=(kc == 0), stop=(kc == KC - 1)),
                        r=[wb_, f1], w=[ps])
                P.op("act", lambda e, sqq=sqq, qn=qn, t1=t1, t2=t2, rstd=rstd, ps=ps, n=n: e.activation(out=sqq[:, 0:n], in_=ps[:, 0:n], func=AF.Square), r=[ps], w=[sqq])
                ps2 = C.ps()
                P.op("pe", lambda e, sqq=sqq, qn=qn, t1=t1, t2=t2, rstd=rstd, ps2=ps2, n=n: e.matmul(ps2[:, 0:n], lhsT=bo[:], rhs=sqq[:, 0:n], start=True, stop=True),
                     r=[bo, sqq], w=[ps2])
                P.op("act", lambda e, sqq=sqq, qn=qn, t1=t1, t2=t2, rstd=rstd, ps2=ps2, n=n: e.activation(out=rstd[:, 0:n], in_=ps2[:, 0:n], func=AF.Sqrt, bias=EPS, scale=1.0 / 64),
                     r=[ps2], w=[rstd])
                P.op("dve", lambda e, sqq=sqq, qn=qn, t1=t1, t2=t2, rstd=rstd, n=n: e.reciprocal(out=rstd[:, 0:n], in_=rstd[:, 0:n]), r=[rstd], w=[rstd])
                if t0 < NCTX:
                    P.op("dve", lambda e, sqq=sqq, qn=qn, t1=t1, t2=t2, rstd=rstd, ps=ps, n=n, gcol=gcol, dst=dst, t0=t0: e.scalar_tensor_tensor(
                        out=dst[:, t0:t0 + n], in0=ps[:, 0:n], scalar=qkg[:, gcol:gcol + 1], in1=rstd[:, 0:n],
                        op0=ALU.mult, op1=ALU.mult), r=[ps, qkg, rstd], w=[dst])
                    continue
                P.op("dve", lambda e, sqq=sqq, qn=qn, t1=t1, t2=t2, rstd=rstd, ps=ps, n=n, gcol=gcol: e.scalar_tensor_tensor(
                    out=qn[:, 0:n], in0=ps[:, 0:n], scalar=qkg[:, gcol:gcol + 1], in1=rstd[:, 0:n],
                    op0=ALU.mult, op1=ALU.mult), r=[ps, qkg, rstd], w=[qn])
                ps3 = C.ps()
                P.op("pe", lambda e, sqq=sqq, qn=qn, t1=t1, t2=t2, rstd=rstd, ps3=ps3, n=n: e.matmul(ps3[:, 0:n], lhsT=rmT[:], rhs=qn[:, 0:n], start=True, stop=True),
                     r=[rmT, qn], w=[ps3])
                lp = t0 - NCTX
                P.op("pool", lambda e, sqq=sqq, qn=qn, t1=t1, t2=t2, rstd=rstd, n=n, lp=lp: e.tensor_tensor(out=t1[:, 0:n], in0=qn[:, 0:n], in1=cosT[:, lp:lp + n], op=ALU.mult),
                     r=[qn, cosT], w=[t1])
                P.op("dve", lambda e, sqq=sqq, qn=qn, t1=t1, t2=t2, rstd=rstd, ps3=ps3, n=n, lp=lp: e.tensor_tensor(out=t2[:, 0:n], in0=ps3[:, 0:n], in1=sinT[:, lp:lp + n], op=ALU.mult),
                     r=[ps3, sinT], w=[t2])
                P.op("dve", lambda e, sqq=sqq, qn=qn, t1=t1, t2=t2, rstd=rstd, n=n, dst=dst, t0=t0: e.tensor_tensor(out=dst[:, t0:t0 + n], in0=t1[:, 0:n], in1=t2[:, 0:n], op=ALU.add),
                     r=[t1, t2], w=[dst])
        for kt in range(34):
            ps = C.ps()
            for kc in range(KC):
                P.op("pe", lambda e, kc=kc, ps=ps, kt=kt, wvv=wvv: e.matmul(
                    ps[:, 0:128], lhsT=f1[:, kc, kt * 128:(kt + 1) * 128], rhs=wvv[:, kc, :], start=(kc == 0), stop=(kc == KC - 1)),
                    r=[f1, wvb], w=[ps])
            eng = C.ev()
            if eng == "dve":
                P.op("dve", lambda e, ps=ps, kt=kt: e.tensor_copy(out=Vaug[:, kt, 0:128], in_=ps[:, 0:128]), r=[ps], w=[Vaug])
            else:
                P.op("act", lambda e, ps=ps, kt=kt: e.activation(out=Vaug[:, kt, 0:128], in_=ps[:, 0:128], func=AF.Copy), r=[ps], w=[Vaug])
        qblocks = [(256 + i * 512, 512, list(range(34))) for i in range(nqb)]
        if need_ctx:
            qblocks = [(0, 256, [0, 1])] + qblocks
        pti = 0
        for (q0, nq, kts) in qblocks:
            for comp in range(2):
                r0 = comp * 64
                accO = C.accs[comp]
                da = dacc[dcnt[0] % 2]
                dcnt[0] += 1

                def score(kt, r0=r0, q0=q0, nq=nq):
                    ps = C.ps()
                    P.op("pe", lambda e, ps=ps, kt=kt, r0=r0, q0=q0, nq=nq: e.matmul(
                        ps[:, 0:nq], lhsT=kT[r0:r0 + 64, kt * 128:(kt + 1) * 128], rhs=qT[r0:r0 + 64, q0:q0 + nq],
                        start=True, stop=True), r=[kT, qT], w=[ps])
                    return ps
                pend = [score(kts[0])]
                if len(kts) > 1:
                    pend.append(score(kts[1]))
                for ki, kt in enumerate(kts):
                    ps = pend.pop(0)
                    if ki + 2 < len(kts):
                        pend.append(score(kts[ki + 2]))
                    pt = PT[pti % 3]
                    pti += 1
                    P.op("act", lambda e, ps=ps, pt=pt, nq=nq: e.activation(out=pt[:, 0:nq], in_=ps[:, 0:nq], func=AF.Exp, scale=0.125),
                         r=[ps], w=[pt])
                    P.op("pe", lambda e, accO=accO, pt=pt, kt=kt, ki=ki, nq=nq, last=(ki == len(kts) - 1): e.matmul(
                        accO[:, 0:nq], lhsT=Vaug[:, kt, 0:128], rhs=pt[:, 0:nq], start=(ki == 0), stop=last), r=[pt, Vaug], w=[accO])
                    if ki == 0:
                        P.op("dve", lambda e, da=da, pt=pt, nq=nq: e.tensor_copy(out=da[:, 0:nq], in_=pt[:, 0:nq]), r=[pt], w=[da])
                    else:
                        P.op("dve", lambda e, da=da, pt=pt, nq=nq: e.tensor_tensor(out=da[:, 0:nq], in0=da[:, 0:nq], in1=pt[:, 0:nq], op=ALU.add),
                             r=[pt, da], w=[da])
                psd = C.ps()
                P.op("pe", lambda e, psd=psd, da=da, nq=nq: e.matmul(psd[:, 0:nq], lhsT=ones128[:], rhs=da[:, 0:nq], start=True, stop=True),
                     r=[ones128, da], w=[psd])
                P.op("dve", lambda e, psd=psd, nq=nq: e.reciprocal(out=rdn[:, 0:nq], in_=psd[:, 0:nq]), r=[psd], w=[rdn])
                if comp == 0:
                    P.op("dve", lambda e, accO=accO, nq=nq: e.tensor_tensor(out=o0f[:, 0:nq], in0=accO[:, 0:nq], in1=rdn[:, 0:nq], op=ALU.mult),
                         r=[accO, rdn], w=[o0f])
                else:
                    P.op("dve", lambda e, accO=accO, nq=nq: e.tensor_tensor(out=o1f[:, 0:nq], in0=accO[:, 0:nq], in1=rdn[:, 0:nq], op=ALU.mult),
                         r=[accO, rdn], w=[o1f])
                    P.op("dve", lambda e, nq=nq: e.scalar_tensor_tensor(out=oof[:, 0:nq], in0=o1f[:, 0:nq], scalar=neglam[:, 0:1], in1=o0f[:, 0:nq],
                                                                        op0=ALU.mult, op1=ALU.add), r=[o1f, neglam, o0f], w=[oof])
            P.op("act", lambda e, nq=nq: e.activation(out=sqf[:, 0:nq], in_=oof[:, 0:nq], func=AF.Square), r=[oof], w=[sqf])
            pss = C.ps()
            P.op("pe", lambda e, pss=pss, nq=nq: e.matmul(pss[:, 0:nq], lhsT=ones128[:], rhs=sqf[:, 0:nq], start=True, stop=True),
                 r=[ones128, sqf], w=[pss])
            P.op("act", lambda e, pss=pss, nq=nq: e.activation(out=rsf[:, 0:nq], in_=pss[:, 0:nq], func=AF.Sqrt, bias=EPS, scale=1.0 / 128),
                 r=[pss], w=[rsf])
            P.op("dve", lambda e, nq=nq: e.reciprocal(out=rsf[:, 0:nq], in_=rsf[:, 0:nq]), r=[rsf], w=[rsf])
            yb_ = ystb[ycnt[0] % 2]
            ycnt[0] += 1
            P.op("dve", lambda e, nq=nq, yb_=yb_: e.scalar_tensor_tensor(out=yb_[:, 0:nq], in0=oof[:, 0:nq], scalar=sgcol[:, 0:1], in1=rsf[:, 0:nq],
                                                                 op0=ALU.mult, op1=ALU.mult), r=[oof, sgcol, rsf], w=[yb_])
            P.dma("sp", out_d[h * 128:(h + 1) * 128, q0:q0 + nq], yb_[:, 0:nq], r=[yb_], w=[out_d])
    return _finish(P, own, [out_d])


def attn_consts():
    inv = (10000.0 ** (-np.arange(16, dtype=np.float32) / 16)).astype(np.float32)
    t = np.arange(4096)
    ang = np.stack([t // 64, t % 64], -1).astype(np.float32)[:, :, None] * inv
    cos, sin = np.cos(ang).astype(np.float32), np.sin(ang).astype(np.float32)
    cosT = np.zeros((128, 4096), np.float32)
    sinT = np.zeros((128, 4096), np.float32)
    rmT = np.zeros((128, 128), np.float32)
    for p in range(128):
        d = p % 64
        a, j, fr = d // 32, (d % 32) // 16, d % 16
        cosT[p] = cos[:, a, fr]
        sinT[p] = sin[:, a, fr]
        if j == 0:
            rmT[p + 16, p] = -1.0
        else:
            rmT[p - 16, p] = 1.0
    bo = np.zeros((128, 128), np.float32)
    bo[:64, :64] = 1.0
    bo[64:, 64:] = 1.0
    return dict(cosT=cosT, sinT=sinT, rmT=rmT, blockones=bo, ident=np.eye(128, dtype=np.float32))


def fpos(t):
    return 1 + t if t < NCTX else 3 + t


def build_pj(groups, P=None, prefix="", bind=None):
    P, own = _begin(P, prefix, bind)
    C = Ctx(P, nrot=8, wsize=4096, nw=4)
    hT_d = P.dram("hT", [D, NTOK], F32, "ExternalInput")
    vec_d = P.dram("vecs", [128, 112], F32, "ExternalInput")
    vec = P.sb([128, 112], F32, "vec")
    der = P.sb([128, 2, 2, 8], F32, "der")
    ones_bf = P.sb([128, 128], BF16, "ones")
    FW = NTOK + 4
    f1 = P.sb([128, KC, FW], BF16, "f1")
    NT = 256
    hTt = P.sb([128, KC, NT], F32, "hTt")
    sqb = P.sb([128, KC, NT], BF16, "sqb")
    tmp = P.sb([128, KC, NT], F32, "tmp")
    rstd = P.sb([128, 512], F32, "rstd")
    stage = [P.sb([128, 512], F32, "stage%d" % i) for i in range(3)]
    wst = P.sb([128, KC, 512], F32, "wst")
    cwb = P.sb([128, 512], F32, "cwb")
    P.dma("sp", vec[:], vec_d[:], r=[vec_d], w=[vec])
    P.op("dve", lambda e: e.memset(ones_bf[:], 1.0), w=[ones_bf])
    P.op("pool", lambda e: e.memset(f1[:], 0.0), w=[f1])
    for cls in range(2):
        sccol = 16 + 8 + cls * 48
        P.op("dve", lambda e, cls=cls, sccol=sccol: e.scalar_tensor_tensor(
            out=der[:, cls, 0, :], in0=vec[:, sccol:sccol + 8], scalar=1.0, in1=vec[:, 0:8],
            op0=ALU.add, op1=ALU.mult), r=[vec], w=[der])
    f1_pre = P.bind.get("f1")
    if f1_pre is not None:
        P.dma("sp", f1[:], f1_pre[:, :].rearrange("p (k t) -> p k t", k=KC), r=[f1_pre], w=[f1])
    for t0 in (range(0, NTOK, NT) if f1_pre is None else ()):
        cls = 1 if t0 < NCTX else 0
        mo = 16 + cls * 48
        P.dma("sp", hTt[:, :, :], hT_d[:, t0:t0 + NT].rearrange("(k p) t -> p k t", p=128), r=[hT_d], w=[hTt])
        c0 = fpos(t0)
        f1v = Buf(f1.t[:, :, c0:c0 + NT], "f1v")
        f1v.lw, f1v.rd = f1.lw, f1.rd
        modnorm_v(C, hTt, NT, der, (cls, 0), vec, mo + 0, f1v, ones_bf, sqb, tmp, rstd)
        f1.lw = f1v.lw
    if groups and groups[0].get("f1only"):
        f1o = P.dram("f1out", [128, KC * (NTOK + 4)], BF16, "ExternalOutput")
        P.dma("sp", f1o[:, :].rearrange("p (k t) -> p k t", k=KC), f1[:], r=[f1], w=[f1o])
        return _finish(P, own, [f1o])
    si = [0]
    for g in groups:
        name, ncol, conv, act, layout = g["name"], g["ncol"], g["conv"], g["act"], g["layout"]
        w_d = P.dram("w_" + name, [D, ncol], F32, "ExternalInput")
        ntap = 3 if conv else 1
        if conv:
            cw_d = P.dram("cw_" + name, [3, 128, ncol], F32, "ExternalInput")
        if layout == "tm":
            b_d = P.dram("b_" + name, [1, ncol], F32, "ExternalInput")
            out_d = P.dram("o_" + name, [NTOK, ncol], F32, "ExternalOutput")
            brow = P.sb([1, ncol], BF16, "brow_" + name)
            P.dma("pool", brow[:], b_d[:], r=[b_d], w=[brow])
        else:
            b_d = P.dram("b_" + name, [128, ncol // 128], F32, "ExternalInput")
            out_d = P.dram("o_" + name, [ncol, NTOK], F32, "ExternalOutput")
            bcol = P.sb([128, ncol // 128], F32, "bcol_" + name)
            P.dma("sp", bcol[:], b_d[:], r=[b_d], w=[bcol])
        wts = []
        P.dma("sp", wst[:, :, 0:ncol], wview(w_d, 0, KC, 0, ncol), r=[w_d], w=[wst])
        for j in range(ntap):
            wt = P.sb([128, KC, ncol], BF16, "wt_%s_%d" % (name, j))
            if conv:
                P.dma("sp", cwb[:, 0:ncol], cw_d[j], r=[cw_d], w=[cwb])
                P.op("dve", lambda e, wt=wt, ncol=ncol: e.tensor_tensor(
                    out=wt[:], in0=wst[:, :, 0:ncol], in1=cwb[:, 0:ncol].unsqueeze(1).to_broadcast([128, KC, ncol]), op=ALU.mult),
                    r=[wst, cwb], w=[wt])
            else:
                P.op("dve", lambda e, wt=wt, ncol=ncol: e.tensor_copy(out=wt[:], in_=wst[:, :, 0:ncol]), r=[wst], w=[wt])
            wts.append(wt)
        shifts = (-1, 0, 1) if conv else (0,)
        func = {None: AF.Copy, "silu": AF.Silu}[act]
        if layout == "tm":
            for t0 in range(0, NTOK, 128):
                ps = C.ps()
                c0 = fpos(t0)
                nmm = ntap * KC
                i = 0
                for j, sh in enumerate(shifts):
                    for kc in range(KC):
                        P.op("pe", lambda e, ps=ps, j=j, sh=sh, kc=kc, c0=c0, ncol=ncol, wts=wts, i=i: e.matmul(
                            ps[:, 0:ncol], lhsT=f1[:, kc, c0 + sh:c0 + sh + 128], rhs=wts[j][:, kc, :], start=(i == 0), stop=False),
                            r=[f1, wts[j]], w=[ps])
                        i += 1
                P.op("pe", lambda e, ps=ps, ncol=ncol, brow=brow: e.matmul(
                    ps[:, 0:ncol], lhsT=ones_bf[0:1, :], rhs=brow[0:1, :], start=False, stop=True), r=[ones_bf, brow], w=[ps])
                st = stage[si[0] % 3]
                si[0] += 1
                P.op("act", lambda e, ps=ps, st=st, ncol=ncol, func=func: e.activation(out=st[:, 0:ncol], in_=ps[:, 0:ncol], func=func),
                     r=[ps], w=[st])
                P.dma("sp", out_d[t0:t0 + 128, :], st[:, 0:ncol], r=[st], w=[out_d])
        else:
            blocks = [(0, 256)] + [(256 + i * 512, 512) for i in range(8)]
            for cb in range(ncol // 128):
                for (t0, n) in blocks:
                    ps = C.ps()
                    c0 = fpos(t0)
                    i = 0
                    for j, sh in enumerate(shifts):
                        for kc in range(KC):
                            P.op("pe", lambda e, ps=ps, j=j, sh=sh, kc=kc, c0=c0, n=n, cb=cb, wts=wts, i=i, last=(i == ntap * KC - 1): e.matmul(
                                ps[:, 0:n], lhsT=wts[j][:, kc, cb * 128:(cb + 1) * 128], rhs=f1[:, kc, c0 + sh:c0 + sh + n],
                                start=(i == 0), stop=last), r=[f1, wts[j]], w=[ps])
                            i += 1
                    st = stage[si[0] % 3]
                    si[0] += 1
                    P.op("act", lambda e, ps=ps, st=st, n=n, cb=cb, func=func, bcol=bcol: e.activation(
                        out=st[:, 0:n], in_=ps[:, 0:n], func=func, bias=bcol[:, cb:cb + 1], scale=1.0), r=[ps, bcol], w=[st])
                    P.dma("sp", out_d[cb * 128:(cb + 1) * 128, t0:t0 + n], st[:, 0:n], r=[st], w=[out_d])
        g["_out"] = out_d
    return _finish(P, own, [g["_out"] for g in groups])


def ssd_consts():
    k = np.arange(128)[:, None]
    l = np.arange(128)[None, :]
    return dict(triu=(k <= l).astype(np.float32), trius=(k < l).astype(np.float32),
                tril=(k >= l).astype(np.float32), trils=(k > l).astype(np.float32))


def build_ssd(need_ctx, P=None, prefix="", bind=None):
    P, own = _begin(P, prefix, bind)
    C = Ctx(P, nrot=8)
    NCH = 34
    xs_d = P.dram("xs", [NTOK, 512], F32, "ExternalInput")
    b_d = P.dram("btm", [NTOK, 128], F32, "ExternalInput")
    z_d = P.dram("z", [NTOK, 512], F32, "ExternalInput")
    dt_d = P.dram("dtraw", [NTOK, 16], F32, "ExternalInput")
    bt_d = P.dram("bT", [128, NTOK], F32, "ExternalInput")
    ct_d = P.dram("cT", [128, NTOK], F32, "ExternalInput")
    dtb_d = P.dram("dtb", [128, 16], F32, "ExternalInput")
    alog_d = P.dram("alog", [128, 16], F32, "ExternalInput")
    dsk_d = P.dram("dskip", [128, 8], F32, "ExternalInput")
    ng_d = P.dram("normg", [128, 512], F32, "ExternalInput")
    cm_d = {k: P.dram(k, [128, 128], F32, "ExternalInput") for k in ("triu", "trius", "tril", "trils")}
    out_d = P.dram("y", [NTOK, 512], F32, "ExternalOutput")

    xs = P.sb([128, NCH, 512], F32, "xss")
    btm = P.sb([128, NCH, 128], BF16, "btms")
    bT = P.sb([128, NTOK], BF16, "bTs")
    cT = P.sb([128, NTOK], BF16, "cTs")
    dt = P.sb([128, NCH, 16], F32, "dts")
    Aa = P.sb([128, NCH, 16], F32, "Aa")
    dtb = P.sb([128, 16], F32, "dtbs")
    aneg = P.sb([128, 16], F32, "aneg")
    dsk = P.sb([128, 8], F32, "dsks")
    ng = P.sb([128, 512], F32, "ngs")
    cm = {k: P.sb([128, 128], F32, k + "_sb") for k in cm_d}
    ones_f = P.sb([128, 128], F32, "ones_f")
    yf = P.sb([128, NCH, 512], BF16, "yf")
    S = P.sb([128, 512], F32, "S")
    Sb = P.sb([128, 512], BF16, "Sb")
    sc = P.sb([128, 16], F32, "sc")
    ex = P.sb([128, 24], F32, "ex")
    Xd = P.sb([128, 512], F32, "Xd")
    Xdb = P.sb([128, 512], BF16, "Xdb")
    Xw = P.sb([128, 512], BF16, "Xw")
    cbm = P.sb([128, 128], F32, "cbm")
    Am = P.sb([128, 8, 128], F32, "Am")
    es = P.sb([128, 8, 128], F32, "es")
    Mb = P.sb([128, 8, 128], BF16, "Mb")
    yt = P.sb([128, 512], F32, "yt")
    zt = P.sb([128, 512], F32, "zt")
    y2 = P.sb([128, 512], F32, "y2")
    junk = P.sb([128, 512], F32, "junk")
    r1 = P.sb([128, 2], F32, "r1")

    P.dma("sp", xs[:], xs_d[:, :].rearrange("(c p) f -> p c f", p=128), r=[xs_d], w=[xs])
    P.dma("pool", btm[:], b_d[:, :].rearrange("(c p) f -> p c f", p=128), r=[b_d], w=[btm])
    P.dma("pool", bT[:], bt_d[:], r=[bt_d], w=[bT])
    P.dma("pool", cT[:], ct_d[:], r=[ct_d], w=[cT])
    P.dma("sp", dt[:], dt_d[:, :].rearrange("(c p) f -> p c f", p=128), r=[dt_d], w=[dt])
    P.dma("sp", dtb[:], dtb_d[:], r=[dtb_d], w=[dtb])
    P.dma("sp", aneg[:], alog_d[:], r=[alog_d], w=[aneg])
    P.dma("sp", dsk[:], dsk_d[:], r=[dsk_d], w=[dsk])
    P.dma("sp", ng[:], ng_d[:], r=[ng_d], w=[ng])
    for k in cm_d:
        P.dma("sp", cm[k][:], cm_d[k][:], r=[cm_d[k]], w=[cm[k]])
    P.op("dve", lambda e: e.memset(ones_f[:], 1.0), w=[ones_f])
    P.op("dve", lambda e: e.tensor_tensor(out=dt[:], in0=dt[:], in1=dtb[:, :].unsqueeze(1).to_broadcast([128, NCH, 16]), op=ALU.add),
         r=[dt, dtb], w=[dt])
    P.op("act", lambda e: e.activation(out=dt[:], in_=dt[:], func=AF.Exp), r=[dt], w=[dt])
    P.op("act", lambda e: e.activation(out=dt[:], in_=dt[:], func=AF.Ln, bias=1.0, scale=1.0), r=[dt], w=[dt])
    P.op("act", lambda e: e.activation(out=aneg[:], in_=aneg[:], func=AF.Exp), r=[aneg], w=[aneg])
    P.op("dve", lambda e: e.scalar_tensor_tensor(out=Aa[:], in0=dt[:], scalar=-1.0, in1=aneg[:, :].unsqueeze(1).to_broadcast([128, NCH, 16]),
                                                 op0=ALU.mult, op1=ALU.mult), r=[dt, aneg], w=[Aa])

    def bc8(ap):
        return ap.unsqueeze(2).to_broadcast([128, 8, 64])

    def v3(buf):
        return buf[:, :].rearrange("p (h q) -> p h q", h=8)

    def chunk(c, d, emit_y, finish):
        cum, SM, R = (("triu", "trils", "triu") if d == 0 else ("trius", "trius", "tril"))
        A_c = Aa[:, c, d * 8:(d + 1) * 8]
        ps1 = C.ps()
        P.op("pe", lambda e: e.matmul(ps1[:, 0:8], lhsT=cm[cum][:], rhs=A_c, start=True, stop=True), r=[cm[cum], Aa], w=[ps1])
        P.op("pe", lambda e: e.matmul(ps1[:, 8:16], lhsT=ones_f[:], rhs=A_c, start=True, stop=True), r=[ones_f, Aa], w=[ps1])
        P.op("dve", lambda e: e.tensor_copy(out=sc[:], in_=ps1[:, 0:16]), r=[ps1], w=[sc])
        P.op("act", lambda e: e.activation(out=ex[:, 0:16], in_=sc[:], func=AF.Exp), r=[sc], w=[ex])
        P.op("dve", lambda e: e.tensor_tensor(out=sc[:, 0:8], in0=sc[:, 8:16], in1=sc[:, 0:8], op=ALU.subtract), r=[sc], w=[sc])
        P.op("act", lambda e: e.activation(out=ex[:, 16:24], in_=sc[:, 0:8], func=AF.Exp), r=[sc], w=[ex])
        wy, ws = (ex[:, 0:8], ex[:, 16:24]) if d == 0 else (ex[:, 16:24], ex[:, 0:8])
        etot = ex[:, 8:16]
        P.op("dve", lambda e: e.tensor_tensor(out=v3(Xd), in0=xs[:, c, :].rearrange("p (h q) -> p h q", h=8),
                                              in1=bc8(dt[:, c, d * 8:(d + 1) * 8]), op=ALU.mult), r=[xs, dt], w=[Xd])
        P.op("pool", lambda e: e.tensor_copy(out=Xdb[:], in_=Xd[:]), r=[Xd], w=[Xdb])
        P.op("dve", lambda e: e.tensor_tensor(out=v3(Xw), in0=v3(Xd), in1=bc8(ws), op=ALU.mult), r=[Xd, ex], w=[Xw])
        cs_ = slice(c * 128, (c + 1) * 128)
        ps_st = C.ps()
        P.op("pe", lambda e: e.matmul(ps_st[:, :], lhsT=btm[:, c, :], rhs=Xw[:], start=True, stop=True), r=[btm, Xw], w=[ps_st])
        if emit_y:
            ps_off = C.ps()
            P.op("pe", lambda e: e.matmul(ps_off[:, :], lhsT=cT[:, cs_], rhs=Sb[:], start=True, stop=True), r=[cT, Sb], w=[ps_off])
            ps_cb = C.ps()
            P.op("pe", lambda e: e.matmul(ps_cb[:, 0:128], lhsT=bT[:, cs_], rhs=cT[:, cs_], start=True, stop=True), r=[bT, cT], w=[ps_cb])
            P.op("dve", lambda e: e.tensor_tensor(out=cbm[:], in0=ps_cb[:, 0:128], in1=cm[R][:], op=ALU.mult), r=[ps_cb, cm[R]], w=[cbm])
            P.op("pool", lambda e: e.tensor_tensor(out=Am[:], in0=cm[SM][:, :].unsqueeze(1).to_broadcast([128, 8, 128]),
                                                   in1=A_c.unsqueeze(2).to_broadcast([128, 8, 128]), op=ALU.mult), r=[cm[SM], Aa], w=[Am])
            psg = [C.ps(), C.ps()]
            for h in range(8):
                pg = psg[h // 4]
                P.op("pe", lambda e, h=h, pg=pg: e.matmul(pg[:, (h % 4) * 128:(h % 4 + 1) * 128], lhsT=Am[:, h, :], rhs=cm[R][:],
                                                          start=True, stop=True), r=[Am, cm[R]], w=[pg])
            for q in range(2):
                P.op("act", lambda e, q=q: e.activation(out=es[:, q * 4:(q + 1) * 4, :].rearrange("p h l -> p (h l)"), in_=psg[q][:, :], func=AF.Exp),
                     r=[psg[q]], w=[es])
            P.op("dve", lambda e: e.tensor_tensor(out=Mb[:], in0=es[:], in1=cbm[:, :].unsqueeze(1).to_broadcast([128, 8, 128]), op=ALU.mult),
                 r=[es, cbm], w=[Mb])
            ps_y = C.ps()
            for h in range(8):
                P.op("pe", lambda e, h=h: e.matmul(ps_y[:, h * 64:(h + 1) * 64], lhsT=Mb[:, h, :], rhs=Xdb[:, h * 64:(h + 1) * 64],
                                                   start=True, stop=True), r=[Mb, Xdb], w=[ps_y])
            P.op("dve", lambda e: e.tensor_tensor(out=v3(yt), in0=ps_off[:, :].rearrange("p (h q) -> p h q", h=8), in1=bc8(wy), op=ALU.mult),
                 r=[ps_off, ex], w=[yt])
            if d == 0:
                P.op("dve", lambda e: e.tensor_tensor(out=yf[:, c, :], in0=yt[:], in1=ps_y[:, :], op=ALU.add), r=[yt, ps_y], w=[yf])
            else:
                P.op("dve", lambda e: e.tensor_tensor(out=yt[:], in0=yt[:], in1=ps_y[:, :], op=ALU.add), r=[yt, ps_y], w=[yt])
        P.op("dve", lambda e: e.tensor_tensor(out=v3(S), in0=v3(S), in1=bc8(etot), op=ALU.mult), r=[S, ex], w=[S])
        P.op("dve", lambda e: e.tensor_tensor(out=S[:], in0=S[:], in1=ps_st[:, :], op=ALU.add), r=[S, ps_st], w=[S])
        P.op("pool", lambda e: e.tensor_copy(out=Sb[:], in_=S[:]), r=[S], w=[Sb])
        if finish:
            P.dma("sp", zt[:], z_d[c * 128:(c + 1) * 128, :], r=[z_d], w=[zt])
            P.op("dve", lambda e: e.tensor_tensor(out=yt[:], in0=yt[:], in1=yf[:, c, :], op=ALU.add), r=[yt, yf], w=[yt])
            P.op("dve", lambda e: e.tensor_tensor(out=v3(y2), in0=xs[:, c, :].rearrange("p (h q) -> p h q", h=8), in1=bc8(dsk[:, :]), op=ALU.mult),
                 r=[xs, dsk], w=[y2])
            P.op("dve", lambda e: e.tensor_tensor(out=yt[:], in0=yt[:], in1=y2[:], op=ALU.add), r=[yt, y2], w=[yt])
            P.op("act", lambda e: e.activation(out=zt[:], in_=zt[:], func=AF.Silu), r=[zt], w=[zt])
            P.op("dve", lambda e: e.tensor_tensor(out=yt[:], in0=yt[:], in1=zt[:], op=ALU.mult), r=[yt, zt], w=[yt])
            P.op("dve", lambda e: e.memset(r1[:, 0:1], 0.0), w=[r1])
            P.op("act", lambda e: e.activation(out=junk[:], in_=yt[:], func=AF.Square, accum_out=r1[:, 0:1]), r=[yt, r1], w=[junk, r1])
            P.op("act", lambda e: e.activation(out=r1[:, 1:2], in_=r1[:, 0:1], func=AF.Sqrt, bias=EPS, scale=1.0 / 512), r=[r1], w=[r1])
            P.op("dve", lambda e: e.reciprocal(out=r1[:, 1:2], in_=r1[:, 1:2]), r=[r1], w=[r1])
            P.op("dve", lambda e: e.scalar_tensor_tensor(out=y2[:], in0=yt[:], scalar=r1[:, 1:2], in1=ng[:], op0=ALU.mult, op1=ALU.mult),
                 r=[yt, r1, ng], w=[y2])
            P.dma("sp", out_d[c * 128:(c + 1) * 128, :], y2[:], r=[y2], w=[out_d])

    for d in range(2):
        P.op("dve", lambda e: e.memset(S[:], 0.0), w=[S])
        P.op("dve", lambda e: e.memset(Sb[:], 0.0), w=[Sb])
        order = [0, 1] + list(range(2, NCH)) if d == 0 else [1, 0] + list(range(NCH - 1, 1, -1))
        for c in order:
            isctx = c < 2
            ey = (not isctx) or need_ctx
            chunk(c, d, ey, ey and d == 1)
    if not need_ctx:
        P.op("dve", lambda e: e.memset(y2[:], 0.0), w=[y2])
        for c in range(2):
            P.dma("sp", out_d[c * 128:(c + 1) * 128, :], y2[:], r=[y2], w=[out_d])
    return _finish(P, own, [out_d])


import math as _math


def hy_consts(L):
    N = 2 * L
    f = np.arange(L, dtype=np.float64)
    th = 2 * np.pi * (f + 0.5) / N
    ang = np.outer(th, f + 0.5)
    nch = L // 128
    import ml_dtypes
    tabs = []
    for M in (np.cos(ang), np.sin(ang)):
        tabs.append(M.reshape(nch, 128, nch, 128).transpose(2, 1, 0, 3))
    tab = np.stack(tabs).reshape(2, nch, 128, nch * 128).astype(ml_dtypes.bfloat16)
    t = np.linspace(0.0, 1.0, L, dtype=np.float32)[:, None]
    w = (2.0 * np.pi * np.arange(L, dtype=np.float32)[:, None] / L).astype(np.float32)
    fb = np.linspace(1e-4, 16 - 1, 16, dtype=np.float32)[None, :]
    feats = np.concatenate([t, np.cos(fb * w), -np.sin(fb * w)], axis=-1).astype(np.float32)
    tn = np.zeros((128, nch, 2), np.float32)
    tt = np.arange(L).reshape(nch, 128).T
    tn[:, :, 0] = tt / (L - 1)
    tn[:, :, 1] = (tt + 1) / (L - 1)
    csh = np.zeros((128, nch, 2), np.float32)
    thh = (th / 2).reshape(nch, 128).T
    csh[:, :, 0] = np.cos(thh)
    csh[:, :, 1] = np.sin(thh)
    return dict(tab=np.ascontiguousarray(tab), featsT=np.ascontiguousarray(feats.T), tn=tn, csh=csh)


def hy_negdelta(s):
    max_decay = _math.log(1e-2) / 0.3
    min_decay = _math.log(1e-2) / 1.5
    deltas = np.abs(np.linspace(min_decay, max_decay, 1024, dtype=np.float32))
    d = -deltas[512 * s:512 * s + 512]
    return np.ascontiguousarray(np.broadcast_to(d[None, :], (128, 512))).astype(np.float32)


def build_hy(need_ctx, P=None, prefix="", bind=None):
    P, own = _begin(P, prefix, bind)
    C = Ctx(P, nrot=8)
    u_d = P.dram("u", [NTOK, 1536], F32, "ExternalInput")
    seqs = [dict(L=4096, T0=NCTX, sfx="L")]
    if need_ctx:
        seqs.append(dict(L=256, T0=0, sfx="C"))
    for sq in seqs:
        n_ = sq["L"] // 128
        sq["nch"] = n_
        sq["tab_d"] = P.dram("tab" + sq["sfx"], [2, n_, 128, n_ * 128], BF16, "ExternalInput")
        sq["ft_d"] = P.dram("featsT" + sq["sfx"], [33, sq["L"]], F32, "ExternalInput")
        sq["tn_d"] = P.dram("tn" + sq["sfx"], [128, n_, 2], F32, "ExternalInput")
        sq["csh_d"] = P.dram("csh" + sq["sfx"], [128, n_, 2], F32, "ExternalInput")
    w1_d = P.dram("hw1", [33, 64], F32, "ExternalInput")
    w2_d = P.dram("hw2", [64, 64], F32, "ExternalInput")
    w3_d = P.dram("hw3", [64, 64], F32, "ExternalInput")
    fq_d = P.dram("hfreq", [64, 3], F32, "ExternalInput")
    hb_d = P.dram("hb", [64, 3], F32, "ExternalInput")
    w4_d = P.dram("hw4", [64, 4, 512], F32, "ExternalInput")
    nd_d = P.dram("negdelta", [128, 512], F32, "ExternalInput")
    hbias_d = P.dram("hbias", [1, 2, 512], F32, "ExternalInput")
    out_d = P.dram("y", [NTOK, 512], F32, "ExternalOutput")

    w1 = P.sb([33, 64], F32, "w1s"); w2 = P.sb([64, 64], F32, "w2s"); w3 = P.sb([64, 64], F32, "w3s")
    fq = P.sb([64, 3], F32, "fqs"); hb = P.sb([64, 3], F32, "hbs")
    w4 = P.sb([64, 4, 512], F32, "w4s")
    nd = P.sb([128, 512], F32, "nds")
    hbias = P.sb([1, 2, 512], F32, "hbiass")
    NCH = 32
    ft = P.sb([33, 512], F32, "fts")
    tn = P.sb([128, NCH, 2], F32, "tns")
    csh = P.sb([128, NCH, 2], F32, "cshs")
    h3 = P.sb([64, 4096 + 128], F32, "h3")
    ha = P.sb([64, 512], F32, "ha"); hbb = P.sb([64, 512], F32, "hbb")
    qi = P.sb([64, 512], I32, "qi"); qf = P.sb([64, 512], F32, "qf")
    CW = 512
    KD = [P.sb([128, NCH, CW], BF16, "KD%d" % i) for i in range(2)]
    spec_d = P.scratch(P.prefix + "spec_scr", [NCH, 128, 2 * CW], BF16)
    spt = [P.sb([128, 2, CW], BF16, "spt%d" % i) for i in range(2)]
    spl = [P.sb([128, 2, CW], BF16, "spl%d" % i) for i in range(2)]
    ub = P.sb([128, NCH, CW], BF16, "ub")
    slabs = [P.sb([128, NCH * 128], BF16, "slab%d" % i) for i in range(4)]
    wex = [P.sb([128, CW], F32, "wex%d" % i) for i in range(2)]
    kfb = P.sb([128, CW], F32, "kfb"); kbb = P.sb([128, CW], F32, "kbb")
    tt = [P.sb([128, CW], F32, "tt%d" % i) for i in range(4)]
    xt = [P.sb([128, CW], F32, "xt%d" % i) for i in range(2)]
    ost = [P.sb([128, CW], F32, "ost%d" % i) for i in range(2)]
    for (sbuf, dbuf) in ((w1, w1_d), (w2, w2_d), (w3, w3_d), (fq, fq_d), (hb, hb_d), (w4, w4_d), (nd, nd_d), (hbias, hbias_d)):
        P.dma("sp", sbuf[:], dbuf[:], r=[dbuf], w=[sbuf])
    cnt = {"slab": 0, "x": 0}
    TWO_PI = 2.0 * _math.pi

    def get_slab(sq, m, idx):
        sl = slabs[cnt["slab"] % 4]
        cnt["slab"] += 1
        n = sq["nch"] * 128
        P.dma("sp" if cnt["slab"] % 2 else "act", sl[:, 0:n], sq["tab_d"][m, idx], r=[sq["tab_d"]], w=[sl])
        return sl

    def do_seq(sq):
        L, nch, T0 = sq["L"], sq["nch"], sq["T0"]
        N = 2 * L
        P.dma("sp", tn[:, 0:nch, :], sq["tn_d"][:], r=[sq["tn_d"]], w=[tn])
        P.dma("sp", csh[:, 0:nch, :], sq["csh_d"][:], r=[sq["csh_d"]], w=[csh])
        P.op("dve", lambda e: e.memset(h3[:], 0.0), w=[h3])
        bn = min(512, L)
        for b0 in range(0, L, bn):
            src = None
            for li, (wm, kdim) in enumerate(((w1, 33), (w2, 64), (w3, 64))):
                ps = C.ps()
                if li == 0:
                    P.dma("sp", ft[:, 0:bn], sq["ft_d"][:, b0:b0 + bn], r=[sq["ft_d"]], w=[ft])
                    P.op("pe", lambda e, ps=ps, b0=b0: e.matmul(ps[0:64, 0:bn], lhsT=w1[:, :], rhs=ft[:, 0:bn], start=True, stop=True),
                         r=[w1, ft], w=[ps])
                else:
                    P.op("pe", lambda e, ps=ps, wm=wm, src=src: e.matmul(ps[0:64, 0:bn], lhsT=wm[:, :], rhs=src[:, 0:bn], start=True, stop=True),
                         r=[wm, src], w=[ps])
                tmpb = ha if li % 2 == 0 else hbb
                P.op("dve", lambda e, ps=ps, tmpb=tmpb, li=li: e.tensor_scalar(
                    out=tmpb[:, 0:bn], in0=ps[0:64, 0:bn], scalar1=hb[:, li:li + 1], scalar2=fq[:, li:li + 1], op0=ALU.add, op1=ALU.mult),
                    r=[ps, hb, fq], w=[tmpb])
                P.op("dve", lambda e, tmpb=tmpb: e.tensor_scalar(
                    out=tmpb[:, 0:bn], in0=tmpb[:, 0:bn], scalar1=float(1.0 / TWO_PI), scalar2=64.5, op0=ALU.mult, op1=ALU.add),
                    r=[tmpb], w=[tmpb])
                P.op("dve", lambda e, tmpb=tmpb: e.tensor_copy(out=qi[:, 0:bn], in_=tmpb[:, 0:bn]), r=[tmpb], w=[qi])
                P.op("dve", lambda e: e.tensor_copy(out=qf[:, 0:bn], in_=qi[:, 0:bn]), r=[qi], w=[qf])
                P.op("dve", lambda e, tmpb=tmpb: e.tensor_tensor(out=tmpb[:, 0:bn], in0=tmpb[:, 0:bn], in1=qf[:, 0:bn], op=ALU.subtract),
                     r=[tmpb, qf], w=[tmpb])
                P.op("dve", lambda e, tmpb=tmpb: e.tensor_single_scalar(out=qf[:, 0:bn], in_=tmpb[:, 0:bn], scalar=0.0, op=ALU.is_lt),
                     r=[tmpb], w=[qf])
                P.op("dve", lambda e, tmpb=tmpb: e.tensor_tensor(out=tmpb[:, 0:bn], in0=tmpb[:, 0:bn], in1=qf[:, 0:bn], op=ALU.add),
                     r=[tmpb, qf], w=[tmpb])
                if li < 2:
                    P.op("act", lambda e, tmpb=tmpb: e.activation(out=tmpb[:, 0:bn], in_=tmpb[:, 0:bn], func=AF.Sin, bias=negpi[:, 0:1], scale=float(TWO_PI)),
                         r=[tmpb, negpi], w=[tmpb])
                    src = tmpb
                else:
                    P.op("act", lambda e, tmpb=tmpb, b0=b0: e.activation(out=h3[:, b0:b0 + bn], in_=tmpb[:, 0:bn], func=AF.Sin, bias=negpi[:, 0:1], scale=float(TWO_PI)),
                         r=[tmpb, negpi], w=[h3])
        for half in range(512 // CW):
            for o in range(2):
                do_ho(sq, half, o, slice(CW * half, CW * half + CW))

    def do_ho(sq, half, o, ch):
        L, nch, T0 = sq["L"], sq["nch"], sq["T0"]
        N = 2 * L
        if True:
            if True:
                for tc in range(nch):
                    psf = C.ps(); psb = C.ps()
                    P.op("pe", lambda e, psf=psf, tc=tc: e.matmul(psf[:, 0:CW], lhsT=h3[:, tc * 128:tc * 128 + 128], rhs=w4[:, o * 2 + 0, ch],
                                                                  start=True, stop=True), r=[h3, w4], w=[psf])
                    P.op("pe", lambda e, psb=psb, tc=tc: e.matmul(psb[:, 0:CW], lhsT=h3[:, tc * 128 + 1:tc * 128 + 129], rhs=w4[:, o * 2 + 1, ch],
                                                                  start=True, stop=True), r=[h3, w4], w=[psb])
                    for dd, (psx, kx) in enumerate(((psf, kfb), (psb, kbb))):
                        P.op("act", lambda e, dd=dd, tc=tc: e.activation(out=wex[dd][:], in_=nd[:, ch], func=AF.Exp, scale=tn[:, tc, dd:dd + 1]),
                             r=[nd, tn], w=[wex[dd]])
                        P.op("dve", lambda e, dd=dd, psx=psx, kx=kx: e.scalar_tensor_tensor(
                            out=kx[:], in0=wex[dd][:], scalar=0.05, in1=psx[:, 0:CW], op0=ALU.add, op1=ALU.mult), r=[wex[dd], psx], w=[kx])
                    if tc == 0:
                        ps0 = C.ps()
                        P.op("pe", lambda e, ps0=ps0: e.matmul(ps0[:, 0:CW], lhsT=h3[:, 0:128], rhs=w4[:, o * 2 + 1, ch], start=True, stop=True),
                             r=[h3, w4], w=[ps0])
                        P.op("dve", lambda e, ps0=ps0: e.scalar_tensor_tensor(
                            out=kfb[0:1, :], in0=ps0[0:1, 0:CW], scalar=1.05, in1=kfb[0:1, :], op0=ALU.mult, op1=ALU.add), r=[ps0, kfb], w=[kfb])
                        P.op("dve", lambda e: e.tensor_tensor(out=kfb[0:1, :], in0=kfb[0:1, :], in1=hbias[0:1, o, ch], op=ALU.add),
                             r=[kfb, hbias], w=[kfb])
                    P.op("dve", lambda e, tc=tc: e.tensor_tensor(out=KD[0][:, tc, :], in0=kfb[:], in1=kbb[:], op=ALU.add), r=[kfb, kbb], w=[KD[0]])
                    P.op("pool", lambda e, tc=tc: e.tensor_tensor(out=KD[1][:, tc, :], in0=kfb[:], in1=kbb[:], op=ALU.subtract), r=[kfb, kbb], w=[KD[1]])
                for fc in range(nch):
                    sC = get_slab(sq, 0, fc); sS = get_slab(sq, 1, fc)
                    pP = C.ps(); pQ = C.ps()
                    for tc in range(nch):
                        P.op("pe", lambda e, tc=tc, sC=sC, pP=pP: e.matmul(pP[:, 0:CW], lhsT=sC[:, tc * 128:(tc + 1) * 128], rhs=KD[0][:, tc, :],
                                                                    start=(tc == 0), stop=(tc == nch - 1)), r=[sC, KD[0]], w=[pP])
                    for tc in range(nch):
                        P.op("pe", lambda e, tc=tc, sS=sS, pQ=pQ: e.matmul(pQ[:, 0:CW], lhsT=sS[:, tc * 128:(tc + 1) * 128], rhs=KD[1][:, tc, :],
                                                                    start=(tc == 0), stop=(tc == nch - 1)), r=[sS, KD[1]], w=[pQ])
                    cth = csh[:, fc, 0:1]; sth = csh[:, fc, 1:2]
                    P.op("dve", lambda e, pQ=pQ, sth=sth: e.tensor_scalar(out=tt[0][:], in0=pQ[:, 0:CW], scalar1=sth, scalar2=None, op0=ALU.mult),
                         r=[pQ, csh], w=[tt[0]])
                    sp_ = spt[fc % 2]
                    P.op("dve", lambda e, pP=pP, cth=cth, sp_=sp_: e.scalar_tensor_tensor(out=sp_[:, 0, :], in0=pP[:, 0:CW], scalar=cth, in1=tt[0][:],
                                                                                 op0=ALU.mult, op1=ALU.add), r=[pP, csh, tt[0]], w=[sp_])
                    P.op("dve", lambda e, pQ=pQ, cth=cth: e.tensor_scalar(out=tt[1][:], in0=pQ[:, 0:CW], scalar1=cth, scalar2=None, op0=ALU.mult),
                         r=[pQ, csh], w=[tt[1]])
                    P.op("dve", lambda e, pP=pP, sth=sth, sp_=sp_: e.scalar_tensor_tensor(out=sp_[:, 1, :], in0=pP[:, 0:CW], scalar=sth, in1=tt[1][:],
                                                                                 op0=ALU.mult, op1=ALU.subtract), r=[pP, csh, tt[1]], w=[sp_])
                    P.dma("pool", spec_d[fc].rearrange("p (a c) -> p a c", a=2), sp_[:], r=[sp_], w=[spec_d])
                if o == 0:
                    P.dma("pool", ub[:, 0:nch, :], u_d[T0:T0 + L, CW * half:CW * half + CW].rearrange("(c p) f -> p c f", p=128),
                          r=[u_d], w=[ub])
                for fc in range(nch):
                    sC = get_slab(sq, 0, fc); sS = get_slab(sq, 1, fc)
                    pA = C.ps(); pB = C.ps()
                    for tc in range(nch):
                        P.op("pe", lambda e, tc=tc, sC=sC, pA=pA: e.matmul(pA[:, 0:CW], lhsT=sC[:, tc * 128:(tc + 1) * 128], rhs=ub[:, tc, :],
                                                                    start=(tc == 0), stop=(tc == nch - 1)), r=[sC, ub], w=[pA])
                    for tc in range(nch):
                        P.op("pe", lambda e, tc=tc, sS=sS, pB=pB: e.matmul(pB[:, 0:CW], lhsT=sS[:, tc * 128:(tc + 1) * 128], rhs=ub[:, tc, :],
                                                                    start=(tc == 0), stop=(tc == nch - 1)), r=[sS, ub], w=[pB])
                    spec = spl[fc % 2]
                    P.dma("pool", spec[:], spec_d[fc].rearrange("p (a c) -> p a c", a=2), r=[spec_d], w=[spec])
                    Kr = spec[:, 0, :]; Ki = spec[:, 1, :]
                    P.op("dve", lambda e, pA=pA, Kr=Kr: e.tensor_tensor(out=tt[0][:], in0=pA[:, 0:CW], in1=Kr, op=ALU.mult), r=[pA, spec], w=[tt[0]])
                    P.op("dve", lambda e, pB=pB, Ki=Ki: e.tensor_tensor(out=tt[1][:], in0=pB[:, 0:CW], in1=Ki, op=ALU.mult), r=[pB, spec], w=[tt[1]])
                    P.op("dve", lambda e, pB=pB, Kr=Kr: e.tensor_tensor(out=tt[2][:], in0=pB[:, 0:CW], in1=Kr, op=ALU.mult), r=[pB, spec], w=[tt[2]])
                    P.op("dve", lambda e, pA=pA, Ki=Ki: e.tensor_tensor(out=tt[3][:], in0=pA[:, 0:CW], in1=Ki, op=ALU.mult), r=[pA, spec], w=[tt[3]])
                    P.op("pool", lambda e, fc=fc: e.tensor_tensor(out=KD[0][:, fc, :], in0=tt[0][:], in1=tt[1][:], op=ALU.add), r=[tt[0], tt[1]], w=[KD[0]])
                    P.op("pool", lambda e, fc=fc: e.tensor_tensor(out=KD[1][:, fc, :], in0=tt[2][:], in1=tt[3][:], op=ALU.subtract), r=[tt[2], tt[3]], w=[KD[1]])
                for tc in range(nch):
                    sC = get_slab(sq, 0, tc); sS = get_slab(sq, 1, tc)
                    py = C.ps()
                    for fc in range(nch):
                        P.op("pe", lambda e, fc=fc, sC=sC, py=py: e.matmul(py[:, 0:CW], lhsT=sC[:, fc * 128:(fc + 1) * 128], rhs=KD[0][:, fc, :],
                                                                    start=(fc == 0), stop=False), r=[sC, KD[0]], w=[py])
                    for fc in range(nch):
                        P.op("pe", lambda e, fc=fc, sS=sS, py=py: e.matmul(py[:, 0:CW], lhsT=sS[:, fc * 128:(fc + 1) * 128], rhs=KD[1][:, fc, :],
                                                                    start=False, stop=(fc == nch - 1)), r=[sS, KD[1]], w=[py])
                    xb = xt[cnt["x"] % 2]
                    ob = ost[cnt["x"] % 2]
                    cnt["x"] += 1
                    r0 = T0 + tc * 128
                    c0 = 512 * (1 + o) + CW * half
                    P.dma("pool", xb[:], u_d[r0:r0 + 128, c0:c0 + CW], r=[u_d], w=[xb])
                    if o == 0:
                        P.op("dve", lambda e, py=py, xb=xb, tc=tc: e.scalar_tensor_tensor(
                            out=ub[:, tc, :], in0=py[:, 0:CW], scalar=float(2.0 / N), in1=xb[:], op0=ALU.mult, op1=ALU.mult), r=[py, xb], w=[ub])
                    else:
                        P.op("dve", lambda e, py=py, xb=xb, ob=ob: e.scalar_tensor_tensor(
                            out=ob[:], in0=py[:, 0:CW], scalar=float(2.0 / N), in1=xb[:], op0=ALU.mult, op1=ALU.mult), r=[py, xb], w=[ob])
                        P.dma("sp", out_d[r0:r0 + 128, CW * half:CW * half + CW], ob[:], r=[ob], w=[out_d])

    negpi = P.sb([64, 1], F32, "negpi")
    P.op("dve", lambda e: e.memset(negpi[:], -_math.pi), w=[negpi])
    for sq in seqs:
        do_seq(sq)
    if not need_ctx:
        zb = P.sb([128, 512], F32, "zb")
        P.op("dve", lambda e: e.memset(zb[:], 0.0), w=[zb])
        for c in range(2):
            P.dma("sp", out_d[c * 128:(c + 1) * 128, :], zb[:], r=[zb], w=[out_d])
    return _finish(P, own, [out_d])


def build_mod(P=None, prefix="", bind=None):
    P, own = _begin(P, prefix, bind)
    C = Ctx(P, nrot=8)
    c_d = P.dram("cs", [128, KC, 5], F32, "ExternalInput")
    w_d = P.dram("wm", [2, D, 768], F32, "ExternalInput")
    b_d = P.dram("bm", [2, 1, 768], F32, "ExternalInput")
    out_d = P.dram("mod", [2, 5, 768], F32, "ExternalOutput")
    cs = P.sb([128, KC, 5], F32, "css")
    wm = P.sb([128, KC, 768], F32, "wms")
    brow = P.sb([1, 768], F32, "brow")
    ones_f = P.sb([1, 8], F32, "ones_f")
    st = P.sb([5, 768], F32, "st")
    P.dma("sp", cs[:], c_d[:], r=[c_d], w=[cs])
    P.op("act", lambda e: e.activation(out=cs[:], in_=cs[:], func=AF.Silu), r=[cs], w=[cs])
    P.op("dve", lambda e: e.memset(ones_f[:], 1.0), w=[ones_f])
    for l in range(2):
        P.dma("sp", wm[:], wview(w_d[l], 0, KC, 0, 768), r=[w_d], w=[wm])
        P.dma("sp", brow[:], b_d[l], r=[b_d], w=[brow])
        for (c0, n) in ((0, 512), (512, 256)):
            ps = C.ps()
            for kc in range(KC):
                P.op("pe", lambda e, ps=ps, kc=kc, c0=c0, n=n: e.matmul(ps[0:5, 0:n], lhsT=cs[:, kc, :], rhs=wm[:, kc, c0:c0 + n],
                                                                start=(kc == 0), stop=False), r=[cs, wm], w=[ps])
            P.op("pe", lambda e, ps=ps, c0=c0, n=n: e.matmul(ps[0:5, 0:n], lhsT=ones_f[0:1, 0:5], rhs=brow[0:1, c0:c0 + n],
                                                      start=False, stop=True), r=[ones_f, brow], w=[ps])
            P.op("act", lambda e, ps=ps, c0=c0, n=n: e.activation(out=st[:, c0:c0 + n], in_=ps[0:5, 0:n], func=AF.Copy), r=[ps], w=[st])
        P.dma("sp", out_d[l], st[:], r=[st], w=[out_d])
    return _finish(P, own, [out_d])


_CACHE = {}


def _get(key, fn):
    if key not in _CACHE:
        _CACHE[key] = fn()
    return _CACHE[key]


def _run(nc, in_maps):
    res = run_bass_kernel_spmd(nc, in_maps, core_ids=list(range(len(in_maps))))
    return res.results


def _pack(v):
    return np.ascontiguousarray(np.asarray(v, np.float32).reshape(8, 128).T)


def _c(a):
    return np.ascontiguousarray(np.asarray(a, dtype=np.float32))


def _rep(v, n=128):
    v = np.asarray(v, np.float32)
    return np.ascontiguousarray(np.broadcast_to(v[None], (n,) + v.shape))


SSD_GROUPS = [dict(name="xs", ncol=512, conv=True, act="silu", layout="tm"),
              dict(name="btm", ncol=128, conv=True, act="silu", layout="tm"),
              dict(name="z", ncol=512, conv=False, act=None, layout="tm"),
              dict(name="dtraw", ncol=16, conv=False, act=None, layout="tm"),
              dict(name="bT", ncol=128, conv=True, act="silu", layout="fm"),
              dict(name="cT", ncol=128, conv=True, act="silu", layout="fm")]
HY_GROUPS = [dict(name="uv", ncol=512, conv=True, act=None, layout="tm"),
             dict(name="ux1", ncol=512, conv=True, act=None, layout="tm"),
             dict(name="ux2", ncol=512, conv=True, act=None, layout="tm")]


def kernel_unfused(x, c, ctx, c_ctx, w_mod, b_mod, norm1_g, norm2_g, w_in, hy_conv_w, hy_conv_b, hy_w1, hy_b1, hy_w2,
           hy_b2, hy_w3, hy_b3, hy_w4, hy_freq, hy_bias, ssd_conv_w, ssd_conv_b, ssd_dt_bias, ssd_a_log, ssd_d,
           ssd_norm_g, da_q_norm, da_k_norm, da_lambda, da_subln_g, w_branch, w_out, ffn_w1, ffn_w3, ffn_w2,
           router_w, moe_w1, moe_w3, moe_w2):
    f32 = np.float32
    x = np.asarray(x, f32); ctx = np.asarray(ctx, f32)
    cores = [(b, s) for b in range(4) for s in range(2)]
    cc = np.concatenate([np.asarray(c, f32), np.asarray(c_ctx, f32)[None]], 0)
    cs = np.ascontiguousarray(cc.T.reshape(8, 128, 5).transpose(1, 0, 2))
    ncm = _get("mod", build_mod)
    res = _run(ncm, [dict(cs=cs, wm=_c(np.asarray(w_mod)[:, :, j * 768:(j + 1) * 768]),
                          bm=_c(np.asarray(b_mod)[:, None, j * 768:(j + 1) * 768])) for j in range(8)])
    mod = np.concatenate([r["mod"] for r in res], axis=2)

    h_lat = x
    h_ctx = ctx
    acons = attn_consts()
    scons = ssd_consts()
    hyL = {k + "L": v for k, v in hy_consts(4096).items()}
    hyC = {k + "C": v for k, v in hy_consts(256).items()}
    for i in range(2):
        need_ctx = i == 0
        W = np.asarray(w_in[i], f32)

        def vecs(b):
            cols = [_pack(norm1_g[i]), _pack(norm2_g[i])]
            for m in (mod[i, b], mod[i, 4]):
                for j in range(6):
                    cols.append(_pack(m[j * 1024:(j + 1) * 1024]))
            return np.ascontiguousarray(np.concatenate(cols, axis=1))
        hT = [np.ascontiguousarray(np.concatenate([h_ctx[b], h_lat[b]], 0).T) for b in range(4)]
        vv = [vecs(b) for b in range(4)]
        so = 3072
        xo = so + 1024
        cw = np.asarray(ssd_conv_w[i], f32); cb = np.asarray(ssd_conv_b[i], f32)
        maps = []
        for (b, s) in cores:
            m = dict(hT=hT[b], vecs=vv[b])
            sel = dict(xs=slice(512 * s, 512 * s + 512), btm=slice(1024 + 128 * s, 1024 + 128 * s + 128),
                       bT=slice(1024 + 128 * s, 1024 + 128 * s + 128), cT=slice(1280 + 128 * s, 1280 + 128 * s + 128))
            for nm, sl in sel.items():
                m["w_" + nm] = _c(W[:, xo:xo + 1536][:, sl])
                m["cw_" + nm] = _c(np.broadcast_to(cw[:, None, sl], (3, 128, sl.stop - sl.start)))
                if nm in ("bT", "cT"):
                    m["b_" + nm] = _c(cb[sl].reshape(1, 128).T)
                else:
                    m["b_" + nm] = _c(cb[None, sl])
            m["w_z"] = _c(W[:, so + 512 * s:so + 512 * s + 512]); m["b_z"] = np.zeros((1, 512), f32)
            dcols = [5632 + d * 16 + 8 * s + h for d in range(2) for h in range(8)]
            m["w_dtraw"] = _c(W[:, dcols]); m["b_dtraw"] = np.zeros((1, 16), f32)
            maps.append(m)
        pj = _run(_get("pj_ssd", lambda: build_pj([dict(g) for g in SSD_GROUPS])), maps)
        maps = []
        for k, (b, s) in enumerate(cores):
            r = pj[k]
            m = dict(xs=r["o_xs"], btm=r["o_btm"], z=r["o_z"], dtraw=r["o_dtraw"], bT=r["o_bT"], cT=r["o_cT"],
                     dtb=_rep(np.asarray(ssd_dt_bias[i], f32)[:, 8 * s:8 * s + 8].reshape(16)),
                     alog=_rep(np.asarray(ssd_a_log[i], f32)[:, 8 * s:8 * s + 8].reshape(16)),
                     dskip=_rep(np.asarray(ssd_d[i], f32)[8 * s:8 * s + 8]),
                     normg=_rep(np.asarray(ssd_norm_g[i], f32)[512 * s:512 * s + 512]))
            m.update(scons)
            maps.append(m)
        y_ssd = _run(_get(("ssd", need_ctx), lambda: build_ssd(need_ctx)), maps)
        del pj
        hw = np.asarray(hy_conv_w[i], f32); hbv = np.asarray(hy_conv_b[i], f32)
        maps = []
        for (b, s) in cores:
            m = dict(hT=hT[b], vecs=vv[b])
            for gi, nm in enumerate(("uv", "ux1", "ux2")):
                sl = slice(1024 * gi + 512 * s, 1024 * gi + 512 * s + 512)
                m["w_" + nm] = _c(W[:, sl])
                m["cw_" + nm] = _c(np.broadcast_to(hw[:, None, sl], (3, 128, 512)))
                m["b_" + nm] = _c(hbv[None, sl])
            maps.append(m)
        pj = _run(_get("pj_hy", lambda: build_pj([dict(g) for g in HY_GROUPS])), maps)
        maps = []
        for k, (b, s) in enumerate(cores):
            r = pj[k]
            cs_ = slice(512 * s, 512 * s + 512)
            m = dict(u=np.ascontiguousarray(np.concatenate([r["o_uv"], r["o_ux1"], r["o_ux2"]], 1)),
                     hw1=_c(hy_w1[i]), hw2=_c(hy_w2[i]), hw3=_c(hy_w3[i]), hfreq=_c(np.asarray(hy_freq[i]).T),
                     hb=_c(np.stack([np.asarray(hy_b1[i]), np.asarray(hy_b2[i]), np.asarray(hy_b3[i])], 1)),
                     hw4=_c(np.asarray(hy_w4[i], f32).reshape(64, 2, 2, 1024)[:, :, :, cs_].reshape(64, 4, 512)),
                     negdelta=hy_negdelta(s), hbias=_c(np.asarray(hy_bias[i], f32)[:, cs_][None]))
            m.update(hyL)
            if need_ctx:
                m.update(hyC)
            maps.append(m)
        y_hy = _run(_get(("hy", need_ctx), lambda: build_hy(need_ctx)), maps)
        del pj
        maps = []
        for (b, s) in cores:
            o = 5664 + s * 512
            m = dict(hT=hT[b], vecs=vv[b], wq=_c(W[:, o:o + 512]), wk=_c(W[:, o + 1024:o + 1536]), wv=_c(W[:, o + 2048:o + 2560]),
                     qkg=_c(np.stack([np.concatenate([np.asarray(da_q_norm[i], f32)] * 2), np.concatenate([np.asarray(da_k_norm[i], f32)] * 2)], 1)),
                     lamv=_c(np.asarray(da_lambda[i], f32).reshape(1, 256)), sublng=_rep(np.asarray(da_subln_g[i], f32)))
            m.update(acons)
            maps.append(m)
        y_da = _run(_get(("attn", i), lambda: build_attn(i, need_ctx)), maps)
        if need_ctx:
            T = 2176
            tiles = [(0, 128, 1)] + [(128 + 512 * q, 512, 0) for q in range(4)]
        else:
            T = 2048
            tiles = [(512 * q, 512, 0) for q in range(4)]
        maps = []
        for (b, s) in cores:
            tok = np.concatenate([np.arange(128 * s, 128 * s + 128), 256 + np.arange(2048 * s, 2048 * s + 2048)]) if need_ctx \
                else 256 + np.arange(2048 * s, 2048 * s + 2048)
            yh = np.concatenate([y_hy[2 * b]["y"], y_hy[2 * b + 1]["y"]], 1)[tok].T
            ys = np.concatenate([y_ssd[2 * b]["y"], y_ssd[2 * b + 1]["y"]], 1)[tok].T
            yd = np.concatenate([y_da[2 * b]["yT"], y_da[2 * b + 1]["yT"]], 0)[:, tok]
            m = dict(hT=np.ascontiguousarray(hT[b][:, tok]), yT=np.ascontiguousarray(np.stack([yh, ys, yd])), vecs=vv[b],
                     wg=_c(W[:, -3072:]), wbr=_c(w_branch[i]), wo=_c(w_out[i]))
            if i % 2 == 0:
                m.update(w1=_c(ffn_w1[i // 2])[None], w3=_c(ffn_w3[i // 2])[None], w2=_c(ffn_w2[i // 2])[None])
            else:
                m.update(rw=_c(router_w[i // 2]), w1=_c(moe_w1[i // 2]), w3=_c(moe_w3[i // 2]), w2=_c(moe_w2[i // 2]),
                         ident=np.eye(128, dtype=f32))
            maps.append(m)
        moe = i % 2 == 1
        outB = _run(_get(("B", T, moe), lambda: build_B(T, tiles, moe)), maps)
        del y_hy, y_ssd, y_da
        new_lat = np.empty_like(h_lat)
        new_ctx = np.array(h_ctx, copy=True)
        for k, (b, s) in enumerate(cores):
            ho = outB[k]["hout"].T
            if need_ctx:
                new_ctx[b, 128 * s:128 * s + 128] = ho[0:128]
                new_lat[b, 2048 * s:2048 * s + 2048] = ho[128:]
            else:
                new_lat[b, 2048 * s:2048 * s + 2048] = ho
        h_lat, h_ctx = new_lat, new_ctx
    return np.ascontiguousarray(h_lat.astype(np.float32))


def emit_modT(P, vecs_sc, prefix="M_"):
    P, own = _begin(P, prefix, None)
    C = Ctx(P, nrot=8)
    c_d = P.dram("cs", [128, KC, 2], F32, "ExternalInput")
    w_d = P.dram("wm", [2, D, 6 * D], F32, "ExternalInput")
    b_d = P.dram("bm", [2, 128, 48], F32, "ExternalInput")
    n_d = P.dram("norms", [2, 128, 16], F32, "ExternalInput")
    cs = P.sb([128, KC, 2], F32, "css")
    wm = [P.sb([128, KC, 512], F32, "wms%d" % i) for i in range(2)]
    bm = P.sb([128, 48], F32, "bms")
    vt = P.sb([128, 112], F32, "vt")
    P.dma("sp", cs[:], c_d[:], r=[c_d], w=[cs])
    P.op("act", lambda e: e.activation(out=cs[:], in_=cs[:], func=AF.Silu), r=[cs], w=[cs])
    for l in range(2):
        P.dma("sp", bm[:], b_d[l], r=[b_d], w=[bm])
        P.dma("sp", vt[:, 0:16], n_d[l], r=[n_d], w=[vt])
        for cb in range(12):
            wb = wm[cb % 2]
            P.dma("sp", wb[:], wview(w_d[l], 0, KC, cb * 512, 512), r=[w_d], w=[wb])
            for j4 in range(4):
                j = cb * 4 + j4
                ps = C.ps()
                for kc in range(KC):
                    P.op("pe", lambda e, ps=ps, wb=wb, kc=kc, j4=j4: e.matmul(
                        ps[:, 0:2], lhsT=wb[:, kc, j4 * 128:(j4 + 1) * 128], rhs=cs[:, kc, :], start=(kc == 0), stop=(kc == KC - 1)),
                        r=[wb, cs], w=[ps])
                for cls in range(2):
                    col = 16 + cls * 48 + j
                    P.op("dve", lambda e, ps=ps, cls=cls, col=col, j=j: e.tensor_tensor(
                        out=vt[:, col:col + 1], in0=ps[:, cls:cls + 1], in1=bm[:, j:j + 1], op=ALU.add), r=[ps, bm], w=[vt])
        P.dma("sp", vecs_sc[l][:], vt[:], r=[vt], w=[vecs_sc[l]])
    P.end_phase()


def _view(buf, ap, name):
    b = Buf(ap, name)
    return b


def build_fused():
    P = Prog(bass.Bass("TRN2", target_bir_lowering=False))
    sc = lambda name, shape: P.scratch(name, shape, F32)
    hT0 = P.dram("hT0", [D, NTOK], F32, "ExternalInput")
    shared = {}
    for nm, shp, dt in (("cosT", [128, 4096], F32), ("sinT", [128, 4096], F32), ("rmT", [128, 128], F32),
                        ("blockones", [128, 128], F32), ("ident", [128, 128], F32),
                        ("triu", [128, 128], F32), ("trius", [128, 128], F32), ("tril", [128, 128], F32), ("trils", [128, 128], F32),
                        ("tabL", [2, 32, 128, 4096], BF16), ("featsTL", [33, 4096], F32), ("tnL", [128, 32, 2], F32), ("cshL", [128, 32, 2], F32),
                        ("tabC", [2, 2, 128, 256], BF16), ("featsTC", [33, 256], F32), ("tnC", [128, 2, 2], F32), ("cshC", [128, 2, 2], F32)):
        shared[nm] = P.dram(nm, shp, dt, "ExternalInput")
    vecs_sc = [sc("vecs_l%d" % l, [128, 112]) for l in range(2)]
    hT1 = sc("hT1", [D, NTOK])
    s_xs = sc("s_xs", [NTOK, 512]); s_btm = sc("s_btm", [NTOK, 128]); s_z = sc("s_z", [NTOK, 512]); s_dt = sc("s_dt", [NTOK, 16])
    s_bT = sc("s_bT", [128, NTOK]); s_cT = sc("s_cT", [128, NTOK])
    s_u = sc("s_u", [NTOK, 1536])
    y_ssd = [sc("y_ssd%d" % s, [NTOK, 512]) for s in range(2)]
    y_hy = [sc("y_hy%d" % s, [NTOK, 512]) for s in range(2)]
    y_da = [sc("y_da%d" % s, [512, NTOK]) for s in range(2)]
    emit_modT(P, vecs_sc)
    hcur = hT0
    f1_sc = P.scratch("f1_sc", [128, KC * (NTOK + 4)], BF16)
    for i in range(2):
        need_ctx = i == 0
        build_pj([dict(f1only=True)], P=P, prefix="L%d_f1_" % i, bind=dict(hT=hcur, vecs=vecs_sc[i], f1out=f1_sc))
        for s in range(2):
            pre = "L%ds%d_" % (i, s)
            build_pj([dict(g) for g in SSD_GROUPS], P=P, prefix=pre + "pjs_",
                     bind=dict(hT=hcur, vecs=vecs_sc[i], f1=f1_sc, o_xs=s_xs, o_btm=s_btm, o_z=s_z, o_dtraw=s_dt, o_bT=s_bT, o_cT=s_cT))
            b = dict(xs=s_xs, btm=s_btm, z=s_z, dtraw=s_dt, bT=s_bT, cT=s_cT, y=y_ssd[s])
            b.update({k: shared[k] for k in ("triu", "trius", "tril", "trils")})
            build_ssd(need_ctx, P=P, prefix=pre + "ssd_", bind=b)
            build_pj([dict(g) for g in HY_GROUPS], P=P, prefix=pre + "pjh_",
                     bind=dict(hT=hcur, vecs=vecs_sc[i], f1=f1_sc, o_uv=Buf(s_u.t[:, 0:512], "u0"), o_ux1=Buf(s_u.t[:, 512:1024], "u1"),
                               o_ux2=Buf(s_u.t[:, 1024:1536], "u2")))
            b = dict(u=s_u, y=y_hy[s])
            b.update({k: shared[k] for k in ("tabL", "featsTL", "tnL", "cshL")})
            if need_ctx:
                b.update({k: shared[k] for k in ("tabC", "featsTC", "tnC", "cshC")})
            build_hy(need_ctx, P=P, prefix=pre + "hy_", bind=b)
            b = dict(hT=hcur, vecs=vecs_sc[i], yT=y_da[s], f1=f1_sc)
            b.update({k: shared[k] for k in ("cosT", "sinT", "rmT", "blockones", "ident")})
            build_attn(i, need_ctx, P=P, prefix=pre + "at_", bind=b)
        pre = "L%d_B_" % i
        if need_ctx:
            T = NTOK
            tiles = [(0, 256, 1)] + [(256 + 512 * q, 512, 0) for q in range(8)]
            b = dict(hT=hcur, vecs=vecs_sc[i], hout=hT1, ident=shared["ident"])
            for s in range(2):
                b["yhy%d" % s] = y_hy[s]; b["yssd%d" % s] = y_ssd[s]; b["yda%d" % s] = y_da[s]
        else:
            T = 4096
            tiles = [(512 * q, 512, 0) for q in range(8)]
            b = dict(hT=Buf(hcur.t[:, NCTX:NTOK], "hlat"), vecs=vecs_sc[i], ident=shared["ident"])
            for s in range(2):
                b["yhy%d" % s] = Buf(y_hy[s].t[NCTX:NTOK, :], "yh"); b["yssd%d" % s] = Buf(y_ssd[s].t[NCTX:NTOK, :], "ys")
                b["yda%d" % s] = Buf(y_da[s].t[:, NCTX:NTOK], "yd")
            out_final = P.dram("out", [D, 4096], F32, "ExternalOutput")
            b["hout"] = out_final
        build_B(T, tiles, i % 2 == 1, P=P, prefix=pre, bind=b, ytm=True)
        hcur = hT1
    P.fence("sp", [out_final])
    P.emit()
    return P.nc, P.ext


def fused_inputs(b, inp):
    f32 = np.float32
    g = lambda k: np.asarray(inp[k], f32)
    m = {}
    m["hT0"] = np.ascontiguousarray(np.concatenate([g("ctx")[b], g("x")[b]], 0).T)
    m.update(attn_consts())
    m.update(ssd_consts())
    m.update({k + "L": v for k, v in hy_consts(4096).items()})
    m.update({k + "C": v for k, v in hy_consts(256).items()})
    cc = np.stack([g("c")[b], g("c_ctx")], 1)
    m["M_cs"] = np.ascontiguousarray(cc.reshape(8, 128, 2).transpose(1, 0, 2))
    m["M_wm"] = _c(g("w_mod"))
    m["M_bm"] = np.ascontiguousarray(g("b_mod").reshape(2, 48, 128).transpose(0, 2, 1))
    m["M_norms"] = np.ascontiguousarray(np.stack([np.concatenate([_pack(g("norm1_g")[l]), _pack(g("norm2_g")[l])], 1) for l in range(2)]))
    for i in range(2):
        need_ctx = i == 0
        W = g("w_in")[i]
        so = 3072
        xo = so + 1024
        cw = g("ssd_conv_w")[i]; cb = g("ssd_conv_b")[i]
        hw = g("hy_conv_w")[i]; hbv = g("hy_conv_b")[i]
        for s in range(2):
            pre = "L%ds%d_" % (i, s)
            p = pre + "pjs_"
            sel = dict(xs=slice(512 * s, 512 * s + 512), btm=slice(1024 + 128 * s, 1024 + 128 * s + 128),
                       bT=slice(1024 + 128 * s, 1024 + 128 * s + 128), cT=slice(1280 + 128 * s, 1280 + 128 * s + 128))
            for nm, sl in sel.items():
                m[p + "w_" + nm] = _c(W[:, xo:xo + 1536][:, sl])
                m[p + "cw_" + nm] = _c(np.broadcast_to(cw[:, None, sl], (3, 128, sl.stop - sl.start)))
                m[p + "b_" + nm] = _c(cb[sl].reshape(1, 128).T) if nm in ("bT", "cT") else _c(cb[None, sl])
            m[p + "w_z"] = _c(W[:, so + 512 * s:so + 512 * s + 512]); m[p + "b_z"] = np.zeros((1, 512), f32)
            dcols = [5632 + d * 16 + 8 * s + h for d in range(2) for h in range(8)]
            m[p + "w_dtraw"] = _c(W[:, dcols]); m[p + "b_dtraw"] = np.zeros((1, 16), f32)
            p = pre + "ssd_"
            m[p + "dtb"] = _rep(g("ssd_dt_bias")[i][:, 8 * s:8 * s + 8].reshape(16))
            m[p + "alog"] = _rep(g("ssd_a_log")[i][:, 8 * s:8 * s + 8].reshape(16))
            m[p + "dskip"] = _rep(g("ssd_d")[i][8 * s:8 * s + 8])
            m[p + "normg"] = _rep(g("ssd_norm_g")[i][512 * s:512 * s + 512])
            p = pre + "pjh_"
            for gi, nm in enumerate(("uv", "ux1", "ux2")):
                sl = slice(1024 * gi + 512 * s, 1024 * gi + 512 * s + 512)
                m[p + "w_" + nm] = _c(W[:, sl])
                m[p + "cw_" + nm] = _c(np.broadcast_to(hw[:, None, sl], (3, 128, 512)))
                m[p + "b_" + nm] = _c(hbv[None, sl])
            p = pre + "hy_"
            cs_ = slice(512 * s, 512 * s + 512)
            m[p + "hw1"] = _c(g("hy_w1")[i]); m[p + "hw2"] = _c(g("hy_w2")[i]); m[p + "hw3"] = _c(g("hy_w3")[i])
            m[p + "hfreq"] = _c(g("hy_freq")[i].T)
            m[p + "hb"] = _c(np.stack([g("hy_b1")[i], g("hy_b2")[i], g("hy_b3")[i]], 1))
            m[p + "hw4"] = _c(g("hy_w4")[i].reshape(64, 2, 2, 1024)[:, :, :, cs_].reshape(64, 4, 512))
            m[p + "negdelta"] = hy_negdelta(s)
            m[p + "hbias"] = _c(g("hy_bias")[i][:, cs_][None])
            p = pre + "at_"
            o = 5664 + s * 512
            m[p + "wq"] = _c(W[:, o:o + 512]); m[p + "wk"] = _c(W[:, o + 1024:o + 1536]); m[p + "wv"] = _c(W[:, o + 2048:o + 2560])
            m[p + "qkg"] = _c(np.stack([np.concatenate([g("da_q_norm")[i]] * 2), np.concatenate([g("da_k_norm")[i]] * 2)], 1))
            m[p + "lamv"] = _c(g("da_lambda")[i].reshape(1, 256))
            m[p + "sublng"] = _rep(g("da_subln_g")[i])
        p = "L%d_B_" % i
        m[p + "wg"] = _c(W[:, -3072:]); m[p + "wbr"] = _c(g("w_branch")[i]); m[p + "wo"] = _c(g("w_out")[i])
        if i % 2 == 0:
            m[p + "w1"] = _c(g("ffn_w1")[i // 2])[None]; m[p + "w3"] = _c(g("ffn_w3")[i // 2])[None]; m[p + "w2"] = _c(g("ffn_w2")[i // 2])[None]
        else:
            m[p + "rw"] = _c(g("router_w")[i // 2]); m[p + "w1"] = _c(g("moe_w1")[i // 2]); m[p + "w3"] = _c(g("moe_w3")[i // 2])
            m[p + "w2"] = _c(g("moe_w2")[i // 2])
    return m


def kernel_fused(**inp):
    nc, ext = _get("fused", build_fused)
    maps = []
    for b in range(4):
        m = fused_inputs(b, inp)
        missing = [k for k in ext if k not in m]
        extra = [k for k in m if k not in ext]
        assert not missing, missing
        for k in extra:
            del m[k]
        maps.append(m)
    res = run_bass_kernel_spmd(nc, maps, core_ids=list(range(4)))
    out = np.stack([np.ascontiguousarray(r["out"].T) for r in res.results])
    return np.ascontiguousarray(out.astype(np.float32))


def kernel(**inp):
    return kernel_fused(**inp)
```
